# Optimizing a Trainium2 kernel written in Bass

```python
import jax
import jax.numpy as jnp
from jax import lax
import numpy as np

D_MODEL = 2048
BATCH = 4
SEQ = 2048
DEPTH = 4

GRID_W = 64
CTX_LEN = 256
N_MIXERS = 4
NORM_EPS = 1e-6
POOL_WINDOWS = (2, 4, 8, 16)
POOL_GROUP = D_MODEL // len(POOL_WINDOWS)
NA_HEAD_DIM = 64
NA_HEADS = D_MODEL // NA_HEAD_DIM
NA_KH = 8
NA_KW = 16
ROPE_BASE = 10000.0
SG_CHUNK = 128
SG_WIDTH = D_MODEL
SG_GROUPS = 16
RW_HEAD_DIM = 64
RW_HEADS = D_MODEL // RW_HEAD_DIM
RW_DECAY_LORA = max(32, int(round(1.8 * D_MODEL ** 0.5 / 32)) * 32)
RW_ICLR_LORA = max(32, int(round(1.8 * D_MODEL ** 0.5 / 32)) * 32)
RW_GATE_LORA = max(32, int(round(0.6 * D_MODEL ** 0.8 / 32)) * 32)
RW_GN_EPS = 64e-5
D_FF = 5504

kernel_name = "hybrid_pool_natten_gmlp_rwkv7_dit"


def rms_norm(x, g):
    xf = x.astype(jnp.float32)
    y = xf * lax.rsqrt(jnp.mean(xf * xf, axis=-1, keepdims=True) + NORM_EPS)
    return (y * g.astype(jnp.float32)).astype(x.dtype)


def modulate(x, shift, scale):
    return x * (1 + scale) + shift


def conv_ffn(u, w_gate, w_up, conv_w, conv_b, w_down):
    T = u.shape[1]
    gp = jnp.pad(u @ w_gate, ((0, 0), (1, 1), (0, 0)))
    gte = gp[:, :T] * conv_w[0] + gp[:, 1:T + 1] * conv_w[1] + gp[:, 2:] * conv_w[2] + conv_b
    return (jax.nn.silu(gte) * (u @ w_up)) @ w_down


def pool_mix(u, w, b, scale):
    B, T, D = u.shape
    uf = u.astype(jnp.float32)
    t = jnp.arange(T)
    parts = []
    for g, win in enumerate(POOL_WINDOWS):
        left, right = win // 2, win - 1 - win // 2
        ug = uf[..., g * POOL_GROUP:(g + 1) * POOL_GROUP]
        cs = jnp.cumsum(jnp.pad(ug, ((0, 0), (left + 1, right), (0, 0))), axis=1)
        window_sum = cs[:, win:win + T] - cs[:, :T]
        count = (jnp.minimum(t + right, T - 1) - jnp.maximum(t - left, 0) + 1).astype(jnp.float32)
        parts.append(window_sum / count[None, :, None] - ug)
    p = jnp.stack(parts, axis=2).astype(u.dtype)
    y = jnp.einsum('btgc,gcd->btgd', p, w).reshape(B, T, D) + b
    return y * scale


def axial_rope(t):
    T, dh = t.shape[1], t.shape[-1]
    half = dh // 2
    pos = jnp.arange(T)
    inv_freq = ROPE_BASE ** (-jnp.arange(0, half, 2, dtype=jnp.float32) / half)

    def rotate(u, p):
        ang = p.astype(jnp.float32)[:, None] * inv_freq
        cos, sin = jnp.cos(ang)[:, None], jnp.sin(ang)[:, None]
        u1, u2 = jnp.split(u, 2, axis=-1)
        return jnp.concatenate([u1 * cos - u2 * sin, u1 * sin + u2 * cos], axis=-1)

    tf = t.astype(jnp.float32)
    out = jnp.concatenate([rotate(tf[..., :half], pos // GRID_W),
                           rotate(tf[..., half:], pos % GRID_W)], axis=-1)
    return out.astype(t.dtype)


def neighbourhood_attention(a_ctx, a_lat, w_qkv, rpb, w_o, need_ctx_out):
    B, T, D = a_lat.shape
    L = a_ctx.shape[1]
    H, dh = NA_HEADS, NA_HEAD_DIM
    rows = T // GRID_W
    kh, kw = min(NA_KH, rows), NA_KW
    scale = dh ** -0.5
    f32 = jnp.float32
    qkv = (a_lat @ w_qkv).reshape(B, T, 3, H, dh)
    q, k, v = axial_rope(qkv[:, :, 0]), axial_rope(qkv[:, :, 1]), qkv[:, :, 2]
    kvc = (a_ctx @ w_qkv[:, D:]).reshape(B, L, 2, H, dh)
    kc, vc = kvc[:, :, 0], kvc[:, :, 1]
    y_ctx = None
    if need_ctx_out:
        qc = (a_ctx @ w_qkv[:, :D]).reshape(B, L, H, dh)
        s = jnp.einsum('bqhd,bkhd->bhqk', qc, kc, preferred_element_type=f32) * scale
        p = jax.nn.softmax(s, axis=-1).astype(vc.dtype)
        y_ctx = jnp.einsum('bhqk,bkhd->bqhd', p, vc).reshape(B, L, D) @ w_o
    q_rows = q.reshape(B, rows, GRID_W, H, dh).transpose(1, 0, 3, 2, 4)
    to_grid = lambda z: z.reshape(B, rows, GRID_W, H, dh).transpose(0, 3, 1, 2, 4)
    k_grid, v_grid = to_grid(k), to_grid(v)
    col = jnp.arange(GRID_W)
    col_start = jnp.clip(col - kw // 2, 0, GRID_W - kw)
    col_ok = (col[None, :] >= col_start[:, None]) & (col[None, :] < col_start[:, None] + kw)
    band_ok = jnp.tile(col_ok, (1, kh))
    dc = jnp.clip(col[None, :] - col[:, None], -(kw - 1), kw - 1) + NA_KW - 1
    rpb_cols = rpb[:, :, dc]
    n_loc = kh * GRID_W

    def row_block(args):
        r, q_r = args
        rs = jnp.clip(r - kh // 2, 0, rows - kh)
        k_band = lax.dynamic_slice_in_dim(k_grid, rs, kh, axis=2).reshape(B, H, n_loc, dh)
        v_band = lax.dynamic_slice_in_dim(v_grid, rs, kh, axis=2).reshape(B, H, n_loc, dh)
        dr = rs + jnp.arange(kh) - r + NA_KH - 1
        bias = rpb_cols[:, dr].transpose(0, 2, 1, 3).reshape(H, GRID_W, n_loc)
        s_loc = jnp.einsum('bhqd,bhkd->bhqk', q_r, k_band, preferred_element_type=f32) * scale + bias
        s_loc = jnp.where(band_ok, s_loc, -jnp.inf)
        s_ctx = jnp.einsum('bhqd,bkhd->bhqk', q_r, kc, preferred_element_type=f32) * scale
        p = jax.nn.softmax(jnp.concatenate([s_loc, s_ctx], axis=-1), axis=-1).astype(v_band.dtype)
        return (jnp.einsum('bhqk,bhkd->bhqd', p[..., :n_loc], v_band)
                + jnp.einsum('bhqk,bkhd->bhqd', p[..., n_loc:], vc))

    o = lax.map(row_block, (jnp.arange(rows), q_rows))
    y_lat = o.transpose(1, 0, 3, 2, 4).reshape(B, T, D) @ w_o
    return y_ctx, y_lat


def spatial_gating_mix(u, w_in, b_in, norm_g, w_s, b_s, w_o):
    B, T, _ = u.shape
    z = jax.nn.gelu(u @ w_in + b_in)
    zu, zv = jnp.split(z, 2, axis=-1)
    zv = rms_norm(zv, norm_g).reshape(B, T // SG_CHUNK, SG_CHUNK, SG_GROUPS, SG_WIDTH // SG_GROUPS)
    zv = jnp.einsum('gpq,bnqgc->bnpgc', w_s, zv) + b_s.T[:, :, None]
    return (zu * zv.reshape(B, T, SG_WIDTH)) @ w_o


def rwkv7_project(u, mu, w_rkv, w0, w1, w2, a0, a1, a2, g1, g2, k_k, k_a, need_out):
    B, T, D = u.shape
    f32 = jnp.float32
    heads = lambda z: z.reshape(z.shape[:-1] + (RW_HEADS, RW_HEAD_DIM)).astype(f32)
    prev = jnp.pad(u, ((0, 0), (1, 0), (0, 0)))[:, :T] - u
    nxt = jnp.pad(u, ((0, 0), (0, 1), (0, 0)))[:, 1:] - u
    shifted = lambda n: u + prev * mu[0, n] + nxt * mu[1, n]
    xw, xk, xv, xa = shifted(1), shifted(2), shifted(3), shifted(4)
    k = heads(xk @ w_rkv[1])
    v = heads(xv @ w_rkv[2])
    kk = k * heads(k_k)
    kk = kk * lax.rsqrt(jnp.maximum(jnp.sum(kk * kk, axis=-1, keepdims=True), 1e-12))
    ka = heads(k_a)
    decay, key, iclr = [], [], []
    for d in range(2):
        logw = -jax.nn.softplus(-(w0[d] + jnp.tanh(xw @ w1[d]) @ w2[d])) - 0.5
        decay.append(jnp.exp(-jnp.exp(heads(logw))))
        a = jax.nn.sigmoid(heads(a0[d] + (xa @ a1[d]) @ a2[d]))
        iclr.append(a)
        key.append(k * (1.0 + (a - 1.0) * ka))
    r = g = None
    if need_out:
        r = heads(shifted(0) @ w_rkv[0])
        g = jax.nn.sigmoid(shifted(5) @ g1) @ g2
    return dict(r=r, g=g, v=v, kk=kk, k=key, w=decay, a=iclr)


def wkv_scan(state0, w, k, v, kk, a, r, reverse):
    tm = lambda z: jnp.moveaxis(z, 1, 0)
    xs = (tm(w), tm(k), tm(v), tm(kk), tm(a)) + ((tm(r),) if r is not None else ())

    def step(S, inp):
        w_t, k_t, v_t, kk_t, a_t = inp[:5]
        S = (S * w_t[:, :, None, :]
             - jnp.einsum('bhvk,bhk->bhv', S, kk_t)[..., None] * (kk_t * a_t)[:, :, None, :]
             + v_t[..., None] * k_t[:, :, None, :])
        y = jnp.einsum('bhvk,bhk->bhv', S, inp[5]) if len(inp) == 6 else None
        return S, y

    S, ys = lax.scan(step, state0, xs, reverse=reverse)
    return S, (None if r is None else jnp.moveaxis(ys, 0, 1))


def rwkv7_readout(p, ys, r_k, ln_g, ln_b, w_o, dtype):
    f32 = jnp.float32
    hs = (RW_HEADS, RW_HEAD_DIM)
    y = ys[0] + ys[1]
    mean = jnp.mean(y, axis=-1, keepdims=True)
    var = jnp.mean(jnp.square(y - mean), axis=-1, keepdims=True)
    y = (y - mean) * lax.rsqrt(var + RW_GN_EPS)
    y = y * ln_g.reshape(hs).astype(f32) + ln_b.reshape(hs).astype(f32)
    rk = r_k.astype(f32)
    bonus = (jnp.sum(p['r'] * p['k'][0] * rk, axis=-1, keepdims=True)
             + jnp.sum(p['r'] * p['k'][1] * rk, axis=-1, keepdims=True)) * p['v']
    out = (y + bonus).reshape(y.shape[:2] + (-1,)).astype(dtype) * p['g']
    return out @ w_o


def rwkv7_mix(a_ctx, a_lat, mu, w_rkv, w0, w1, w2, a0, a1, a2, g1, g2, k_k, k_a, r_k,
              ln_g, ln_b, w_o, need_ctx_out):
    proj = lambda u, need: rwkv7_project(u, mu, w_rkv, w0, w1, w2, a0, a1, a2, g1, g2, k_k, k_a, need)
    pc, pl = proj(a_ctx, need_ctx_out), proj(a_lat, True)
    B = a_lat.shape[0]
    s0 = jnp.zeros((B, RW_HEADS, RW_HEAD_DIM, RW_HEAD_DIM), jnp.float32)
    yc_dirs, yl_dirs = [], []
    for d in range(2):
        rev = d == 1
        s_ctx, yc = wkv_scan(s0, pc['w'][d], pc['k'][d], pc['v'], pc['kk'], pc['a'][d], pc['r'], rev)
        _, yl = wkv_scan(s_ctx, pl['w'][d], pl['k'][d], pl['v'], pl['kk'], pl['a'][d], pl['r'], rev)
        yc_dirs.append(yc)
        yl_dirs.append(yl)
    y_lat = rwkv7_readout(pl, yl_dirs, r_k, ln_g, ln_b, w_o, a_lat.dtype)
    y_ctx = rwkv7_readout(pc, yc_dirs, r_k, ln_g, ln_b, w_o, a_ctx.dtype) if need_ctx_out else None
    return y_ctx, y_lat


def setup_inputs(seed: int = 0) -> dict:
    key = jax.random.key(seed)
    keys = iter(jax.random.split(key, 64))
    f32 = jnp.float32
    D, F = D_MODEL, D_FF
    nrm = lambda shape, s: jax.random.normal(next(keys), shape, f32) * s
    gain = lambda shape: 1.0 + nrm(shape, 0.02)
    nA, nB, nC, nD = [len(range(m, DEPTH, N_MIXERS)) for m in range(N_MIXERS)]
    inp = {}
    inp['x'] = nrm((BATCH, SEQ, D), 1.0)
    inp['c'] = nrm((BATCH, D), 1.0)
    inp['ctx'] = nrm((BATCH, CTX_LEN, D), 1.0)
    inp['c_ctx'] = nrm((D,), 1.0)
    inp['norm1_g'] = gain((DEPTH, D))
    inp['norm2_g'] = gain((DEPTH, D))
    inp['w_mod'] = nrm((DEPTH, D, 6 * D), 0.5 * D ** -0.5)
    inp['b_mod'] = nrm((DEPTH, 6 * D), 0.02)
    inp['ffn_w_gate'] = nrm((DEPTH, D, F), D ** -0.5)
    inp['ffn_w_up'] = nrm((DEPTH, D, F), D ** -0.5)
    inp['ffn_conv_w'] = nrm((DEPTH, 3, F), 3 ** -0.5)
    inp['ffn_conv_b'] = nrm((DEPTH, F), 0.02)
    inp['ffn_w_down'] = nrm((DEPTH, F, D), F ** -0.5)
    inp['final_norm_g'] = gain((D,))
    inp['pool_w'] = nrm((nA, len(POOL_WINDOWS), POOL_GROUP, POOL_GROUP), POOL_GROUP ** -0.5)
    inp['pool_b'] = nrm((nA, D), 0.02)
    inp['pool_scale'] = gain((nA, D))
    inp['na_w_qkv'] = nrm((nB, D, 3 * D), D ** -0.5)
    inp['na_rpb'] = nrm((nB, NA_HEADS, 2 * NA_KH - 1, 2 * NA_KW - 1), 0.1)
    inp['na_w_o'] = nrm((nB, D, D), D ** -0.5)
    inp['sg_w_in'] = nrm((nC, D, 2 * SG_WIDTH), D ** -0.5)
    inp['sg_b_in'] = nrm((nC, 2 * SG_WIDTH), 0.02)
    inp['sg_norm_g'] = gain((nC, SG_WIDTH))
    inp['sg_w_s'] = nrm((nC, SG_GROUPS, SG_CHUNK, SG_CHUNK), SG_CHUNK ** -0.5)
    inp['sg_b_s'] = gain((nC, SG_GROUPS, SG_CHUNK))
    inp['sg_w_o'] = nrm((nC, SG_WIDTH, D), SG_WIDTH ** -0.5)
    inp['rw_mu'] = jax.random.uniform(next(keys), (nD, 2, 6, D), f32, 0.0, 0.5)
    inp['rw_w_rkv'] = nrm((nD, 3, D, D), D ** -0.5)
    inp['rw_w0'] = jax.random.uniform(next(keys), (nD, 2, D), f32, -6.0, 0.0)
    inp['rw_w1'] = nrm((nD, 2, D, RW_DECAY_LORA), D ** -0.5)
    inp['rw_w2'] = nrm((nD, 2, RW_DECAY_LORA, D), 0.5 * RW_DECAY_LORA ** -0.5)
    inp['rw_a0'] = nrm((nD, 2, D), 0.1)
    inp['rw_a1'] = nrm((nD, 2, D, RW_ICLR_LORA), D ** -0.5)
    inp['rw_a2'] = nrm((nD, 2, RW_ICLR_LORA, D), 0.5 * RW_ICLR_LORA ** -0.5)
    inp['rw_g1'] = nrm((nD, D, RW_GATE_LORA), D ** -0.5)
    inp['rw_g2'] = nrm((nD, RW_GATE_LORA, D), RW_GATE_LORA ** -0.5)
    inp['rw_k_k'] = 0.85 + nrm((nD, D), 0.02)
    inp['rw_k_a'] = gain((nD, D))
    inp['rw_r_k'] = nrm((nD, RW_HEADS, RW_HEAD_DIM), 0.1)
    inp['rw_ln_g'] = gain((nD, D))
    inp['rw_ln_b'] = nrm((nD, D), 0.02)
    inp['rw_w_o'] = nrm((nD, D, D), D ** -0.5)
    return inp


def reference(x, c, ctx, c_ctx, norm1_g, norm2_g, w_mod, b_mod, ffn_w_gate, ffn_w_up, ffn_conv_w,
              ffn_conv_b, ffn_w_down, final_norm_g, pool_w, pool_b, pool_scale, na_w_qkv, na_rpb,
              na_w_o, sg_w_in, sg_b_in, sg_norm_g, sg_w_s, sg_b_s, sg_w_o, rw_mu, rw_w_rkv, rw_w0,
              rw_w1, rw_w2, rw_a0, rw_a1, rw_a2, rw_g1, rw_g2, rw_k_k, rw_k_a, rw_r_k, rw_ln_g,
              rw_ln_b, rw_w_o):
    B, T, D = x.shape
    h, hc = x, ctx
    silu_c, silu_cc = jax.nn.silu(c), jax.nn.silu(c_ctx)
    for i in range(DEPTH):
        m, j = i % N_MIXERS, i // N_MIXERS
        last = i == DEPTH - 1
        ml = (silu_c @ w_mod[i] + b_mod[i]).reshape(B, 6, 1, D)
        mc = (silu_cc @ w_mod[i] + b_mod[i]).reshape(6, 1, 1, D)
        a_lat = modulate(rms_norm(h, norm1_g[i]), ml[:, 0], ml[:, 1])
        ctx_read = (not last) or m in (1, 3)
        a_ctx = modulate(rms_norm(hc, norm1_g[i]), mc[0], mc[1]) if ctx_read else None
        if m == 0:
            y_lat = pool_mix(a_lat, pool_w[j], pool_b[j], pool_scale[j])
            y_ctx = None if last else pool_mix(a_ctx, pool_w[j], pool_b[j], pool_scale[j])
        elif m == 1:
            y_ctx, y_lat = neighbourhood_attention(a_ctx, a_lat, na_w_qkv[j], na_rpb[j], na_w_o[j], not last)
        elif m == 2:
            y_lat = spatial_gating_mix(a_lat, sg_w_in[j], sg_b_in[j], sg_norm_g[j], sg_w_s[j], sg_b_s[j], sg_w_o[j])
            y_ctx = None if last else spatial_gating_mix(a_ctx, sg_w_in[j], sg_b_in[j], sg_norm_g[j],
                                                         sg_w_s[j], sg_b_s[j], sg_w_o[j])
        else:
            y_ctx, y_lat = rwkv7_mix(a_ctx, a_lat, rw_mu[j], rw_w_rkv[j], rw_w0[j], rw_w1[j], rw_w2[j],
                                     rw_a0[j], rw_a1[j], rw_a2[j], rw_g1[j], rw_g2[j], rw_k_k[j],
                                     rw_k_a[j], rw_r_k[j], rw_ln_g[j], rw_ln_b[j], rw_w_o[j], not last)
        h = h + ml[:, 2] * y_lat
        h = h + ml[:, 5] * conv_ffn(modulate(rms_norm(h, norm2_g[i]), ml[:, 3], ml[:, 4]),
                                    ffn_w_gate[i], ffn_w_up[i], ffn_conv_w[i], ffn_conv_b[i], ffn_w_down[i])
        if not last:
            hc = hc + mc[2] * y_ctx
            hc = hc + mc[5] * conv_ffn(modulate(rms_norm(hc, norm2_g[i]), mc[3], mc[4]),
                                       ffn_w_gate[i], ffn_w_up[i], ffn_conv_w[i], ffn_conv_b[i], ffn_w_down[i])
    return rms_norm(h, final_norm_g)
```

```python
import contextlib
import os
import numpy as np
import concourse.bass as bass
import concourse.mybir as mybir

F32 = mybir.dt.float32
BF16 = mybir.dt.bfloat16
AF = mybir.ActivationFunctionType
ALU = mybir.AluOpType
AX = mybir.AxisListType

ENGS = ("sync", "pe", "dve", "act", "pool")
SAME_ENGINE_SYNC = True


class Prog:
    EPOCH = 16000
    NDMA = 24

    def __init__(self):
        self.nc = bass.Bass("TRN2", target_bir_lowering=False)
        self.stack = contextlib.ExitStack()
        self.ops = {e: [] for e in ENGS}
        self.cnt = {e: 0 for e in ENGS}
        self.last_w = {}
        self.readers = {}
        self.waited = {e: {} for e in ENGS}
        self.sems = {}
        self.dma_n = 0
        self.dma_tot = [0] * self.NDMA
        self.n_uid = 0

    def dram(self, name, shape, dt=F32, kind="ExternalInput"):
        return self.nc.dram_tensor(name, list(shape), dt, kind=kind).ap()

    def sb(self, name, shape, dt=F32):
        return self.stack.enter_context(self.nc.sbuf_tensor("sb_" + name, list(shape), dt))

    def ps(self, name, shape, dt=F32):
        return self.stack.enter_context(self.nc.psum_tensor("pp_" + name, list(shape), dt))

    def _sem(self, key):
        if key not in self.sems:
            self.sems[key] = self.stack.enter_context(
                self.nc.semaphore("s_%s_%s" % (key[0], key[1])))
        return self.sems[key]

    def _deps(self, eng, reads, writes):
        toks = []
        for k in reads:
            w = self.last_w.get(k)
            if w is not None:
                toks.append(w)
        for k in writes:
            w = self.last_w.get(k)
            if w is not None:
                toks.append(w)
            toks.extend(self.readers.get(k, ()))
        need = {}
        for t in toks:
            if t[0] == "eng":
                _, e, seq = t
                if e == eng and (eng == "pe" or not SAME_ENGINE_SYNC):
                    continue
                sk = (e, seq // self.EPOCH)
                v = seq % self.EPOCH + 1
            else:
                _, s, v = t
                sk = ("dma", s)
            if need.get(sk, 0) < v:
                need[sk] = v
        out = []
        wd = self.waited[eng]
        for sk, v in need.items():
            if wd.get(sk, 0) >= v:
                continue
            wd[sk] = v
            out.append((sk, v))
        return out

    def _commit(self, tok, reads, writes):
        for k in reads:
            self.readers.setdefault(k, []).append(tok)
        for k in writes:
            self.last_w[k] = tok
            self.readers[k] = []

    @staticmethod
    def _psum_excl(reads, writes):
        r2, w2 = [], list(writes)
        for k in reads:
            if isinstance(k, tuple) and k and k[0] == "ps":
                w2.append(k)
            else:
                r2.append(k)
        return r2, w2

    def op(self, eng, fn, reads=(), writes=()):
        reads, writes = self._psum_excl(reads, writes)
        waits = self._deps(eng, reads, writes)
        seq = self.cnt[eng]
        self.cnt[eng] += 1
        tok = ("eng", eng, seq)
        self._emit(eng, waits, fn, ((eng, seq // self.EPOCH), 1))
        self._commit(tok, reads, writes)
        return tok

    def dma(self, out, in_, reads=(), writes=(), q="sync", **kw):
        if q == "pool" and "max_dma_last_dim" not in kw:
            kw["max_dma_last_dim"] = 2048
        s = self.dma_n % self.NDMA
        self.dma_n += 1
        waits = self._deps(q, reads, writes)
        prev = self.dma_tot[s]
        sk = ("dma", s)
        if prev and self.waited[q].get(sk, 0) < prev:
            self.waited[q][sk] = prev
            waits.append((sk, prev))
        self.dma_tot[s] = prev + 16
        tok = ("dma", s, prev + 16)
        fn = lambda e, out=out, in_=in_, kw=kw: e.dma_start(out=out, in_=in_, **kw)
        self._emit(q, waits, fn, (sk, 16))
        self._commit(tok, reads, writes)
        return tok

    def _emit(self, name, waits, fn, inc):
        nc = self.nc
        eng = {"sync": nc.sync, "pe": nc.tensor, "dve": nc.vector, "act": nc.scalar, "pool": nc.gpsimd}[name]
        for sk, v in waits:
            eng.wait_ge(self._sem(sk), v)
        fn(eng).then_inc(self._sem(inc[0]), inc[1])
        self.nops = getattr(self, "nops", 0) + 1 + len(waits)
        if not hasattr(self, "trace"):
            self.trace = {e: [] for e in ENGS}
        self.trace[name].append((list(waits), inc))

    def check_deadlock(self):
        tr = getattr(self, "trace", None)
        if tr is None:
            return
        val = {}
        pos = {e: 0 for e in ENGS}
        progress = True
        while progress:
            progress = False
            for e in ENGS:
                q = tr[e]
                while pos[e] < len(q):
                    waits, inc = q[pos[e]]
                    if all(val.get(sk, 0) >= v for sk, v in waits):
                        val[inc[0]] = val.get(inc[0], 0) + inc[1]
                        pos[e] += 1
                        progress = True
                    else:
                        break
        stuck = {e: (pos[e], len(tr[e])) for e in ENGS if pos[e] < len(tr[e])}
        if stuck:
            msg = []
            for e, (i, n) in stuck.items():
                waits, inc = tr[e][i]
                msg.append("%s stuck at %d/%d waiting %s (have %s)" % (
                    e, i, n, waits, [(sk, val.get(sk, 0)) for sk, v in waits]))
            raise RuntimeError("semaphore deadlock: " + "; ".join(msg))

    def build(self):
        nc = self.nc
        self.check_deadlock()
        for s in range(self.NDMA):
            if self.dma_tot[s]:
                nc.sync.wait_ge(self._sem(("dma", s)), self.dma_tot[s])
        for e in ENGS:
            if e != "sync" and self.cnt[e]:
                seq = self.cnt[e] - 1
                nc.sync.wait_ge(self._sem((e, seq // self.EPOCH)), seq % self.EPOCH + 1)
        self.stack.close()
        return nc


import concourse.bass_utils as _bu

D = 2048
KC = 16
T = 2048
L = 256
NB = 4
FF = 5504
FC = 43
EPS = 1e-6


class Banks:
    def __init__(self, p, n=8, prefix="psb"):
        self.t = [p.ps("%s%d" % (prefix, i), [128, 512], F32) for i in range(n)]
        self.keys = [("ps", prefix, i) for i in range(n)]
        self.i = 0

    def next(self):
        b = self.i % len(self.t)
        self.i += 1
        return self.t[b], self.keys[b]


def col_blocks(c0, c1, n=512):
    out = []
    while c0 < c1:
        out.append((c0, min(c1, c0 + n)))
        c0 += n
    return out


class Common:
    def __init__(self, p):
        self.ones = p.sb("c_ones", [128, 128], BF16)
        self.eps = p.sb("c_eps", [128, 1], F32)
        p.op("dve", lambda e: e.memset(self.ones[:], 1.0), writes=["c_ones"])
        p.op("dve", lambda e: e.memset(self.eps[:], EPS), writes=["c_eps"])


def load_small(p, name, shape, dram_ap, dt=F32):
    t = p.sb(name, shape, dt)
    p.dma(t[:], dram_ap, writes=[name])
    return t


def make_AB(p, name, modv, g, j_shift, j_scale, col):
    A = p.sb(name + "_A", [128, KC], F32)
    p.op("dve", lambda e: e.tensor_scalar(out=A[:], in0=modv[:, j_scale * KC:(j_scale + 1) * KC, col],
                                          scalar1=1.0, scalar2=None, op0=ALU.add),
         reads=["modv"], writes=[name + "_A"])
    p.op("dve", lambda e: e.tensor_tensor(out=A[:], in0=A[:], in1=g[:], op=ALU.mult),
         reads=[name + "_A", "gvec"], writes=[name + "_A"])
    return A


def rms_rstd(p, cm, banks, h, hkey, W, rstd, rkey, sq, nfeat_chunks=KC, dmodel=D, eps_ap=None):
    for (b0, b1) in col_blocks(0, W):
        ps, pk = banks.next()
        n = b1 - b0
        for c in range(nfeat_chunks):
            s = c % 2
            p.op("act", lambda e, c=c, s=s: e.activation(out=sq[:, s, 0:n], in_=h[:, c, b0:b1], func=AF.Square),
                 reads=[hkey], writes=[("sq", s)])
            p.op("pe", lambda e, c=c, s=s: e.matmul(ps[:, 0:n], lhsT=cm.ones[:], rhs=sq[:, s, 0:n],
                                                     start=(c == 0), stop=(c == nfeat_chunks - 1)),
                 reads=[("sq", s), "c_ones"], writes=[pk])
        p.op("act", lambda e: e.activation(out=rstd[:, b0:b1], in_=ps[:, 0:n], func=AF.Sqrt,
                                           bias=(eps_ap if eps_ap is not None else cm.eps)[:, 0:1], scale=1.0 / dmodel),
             reads=[pk, "c_eps"], writes=[rkey])
        p.op("dve", lambda e: e.reciprocal(out=rstd[:, b0:b1], in_=rstd[:, b0:b1]), reads=[rkey], writes=[rkey])


def norm_mod(p, cm, banks, h, hkey, W, segs, out_bf, okey, rstd, sq, tmp, mask=None):
    rms_rstd(p, cm, banks, h, hkey, W, rstd, "rstd", sq)
    for c in range(KC):
        s = c % 2
        p.op("dve", lambda e, c=c, s=s: e.tensor_tensor(out=tmp[:, s, 0:W], in0=h[:, c, 0:W], in1=rstd[:, 0:W], op=ALU.mult),
             reads=[hkey, "rstd"], writes=[("tmp", s)])
        for (c0, c1, A, Bfn) in segs:
            p.op("act", lambda e, c=c, s=s, c0=c0, c1=c1, A=A, Bfn=Bfn: e.activation(
                out=out_bf[:, c, c0:c1], in_=tmp[:, s, c0:c1], func=AF.Identity, scale=A[:, c:c + 1], bias=Bfn(c)),
                reads=[("tmp", s), "AB", "modv"], writes=[(okey, c)])
        if mask is not None:
            p.op("pool", lambda e, c=c: e.tensor_tensor(out=out_bf[:, c, 0:W], in0=out_bf[:, c, 0:W], in1=mask[:, 0:W], op=ALU.mult),
                 reads=[(okey, c), "mask"], writes=[(okey, c)])


WF = 1156
LAT0, LAT1 = 0, 1026
CTX0, CTX1 = 1026, 1156
FG = 4


def build_ffn(final):
    p = Prog()
    h_d = p.dram("h", [D, WF])
    y_d = p.dram("y", [2, D, WF])
    modv_d = p.dram("modv", [128, 6 * KC, 2])
    g2_d = p.dram("g2", [128, KC])
    mask_d = p.dram("mask", [128, WF])
    wg_d = p.dram("wg", [D, FF])
    wu_d = p.dram("wu", [D, FF])
    wd_d = p.dram("wd", [FF, D])
    cw_d = p.dram("cw", [128, FC, 3])
    cb_d = p.dram("cb", [128, FC])
    ho_d = p.dram("ho", [D, 1152], kind="ExternalOutput")
    if final:
        gf_d = p.dram("gf", [128, KC])
        of_d = p.dram("of", [D, 1024], kind="ExternalOutput")

    cm = Common(p)
    banks = Banks(p)
    h = p.sb("h", [128, KC, WF], F32)
    u = p.sb("u", [128, KC, WF], BF16)
    act = p.sb("actb", [128, FG, WF], BF16)
    gsb = p.sb("gsb", [128, 2, WF], F32)
    tsb = p.sb("tsb", [128, 2, WF], F32)
    sq = p.sb("sq", [128, 2, 512], BF16)
    rstd = p.sb("rstd", [128, WF], F32)
    modv = load_small(p, "modv", [128, 6 * KC, 2], modv_d)
    gvec = load_small(p, "gvec", [128, KC], g2_d)
    mask = load_small(p, "mask", [128, WF], mask_d)
    cw = load_small(p, "cw", [128, FC, 3], cw_d)
    cb = load_small(p, "cb", [128, FC], cb_d)
    gus = [p.sb("gus%d" % i, [128, KC, 256], BF16) for i in range(4)]
    wds = p.sb("wds", [128, FG, D], BF16)

    hv = h_d.rearrange("(c p) w -> p c w", p=128)
    yv = y_d.rearrange("n (c p) w -> n p c w", p=128)
    for c4 in range(0, KC, 4):
        p.dma(h[:, c4:c4 + 4, :], hv[:, c4:c4 + 4, :], writes=[("h", c) for c in range(c4, c4 + 4)])
    for c in range(KC):
        for n in range(2):
            s = (c * 2 + n) % 2
            p.dma(gsb[:, s, :], yv[n, :, c, :], writes=[("gsb", s)])
            for (c0, c1, col) in ((LAT0, LAT1, 0), (CTX0, CTX1, 1)):
                p.op("dve", lambda e, c=c, s=s, c0=c0, c1=c1, col=col: e.scalar_tensor_tensor(
                    out=h[:, c, c0:c1], in0=gsb[:, s, c0:c1], scalar=modv[:, 2 * KC + c, col:col + 1],
                    in1=h[:, c, c0:c1], op0=ALU.mult, op1=ALU.add),
                    reads=[("gsb", s), "modv", ("h", c)], writes=[("h", c)])
    A_l = make_AB(p, "ffl", modv, gvec, 3, 4, 0)
    A_c = make_AB(p, "ffc", modv, gvec, 3, 4, 1)
    hkeys = [("h", c) for c in range(KC)]

    class HK:
        pass
    segs = [(LAT0, LAT1, A_l, lambda c: modv[:, 3 * KC + c, 0:1]), (CTX0, CTX1, A_c, lambda c: modv[:, 3 * KC + c, 1:2])]
    _norm_mod_chunked(p, cm, banks, h, W=WF, segs=segs, out_bf=u, okey="u", rstd=rstd, sq=sq, tmp=tsb, mask=mask,
                      keys_A=["ffl_A", "ffc_A"])

    wgv = wg_d.rearrange("(c p) f -> p c f", p=128)
    wuv = wu_d.rearrange("(c p) f -> p c f", p=128)
    wdv = wd_d.rearrange("(f p) d -> p f d", p=128)
    blocks = col_blocks(0, WF)
    dblocks = [(1, 513, 0), (513, 1025, 0), (1025, 1155, 1)]
    nslab = 0
    slab_of = {}
    ukeys = [("u", c) for c in range(KC)]

    def ensure_slab(fo):
        nonlocal nslab
        sidx = fo // 2
        if sidx in slab_of:
            return slab_of[sidx]
        i0 = (nslab % 2) * 2
        nslab += 1
        f0 = sidx * 256
        f1 = min(FF, f0 + 256)
        p.dma(gus[i0][:, :, 0:f1 - f0], wgv[:, :, f0:f1], writes=[("gus", i0)], q="pool")
        p.dma(gus[i0 + 1][:, :, 0:f1 - f0], wuv[:, :, f0:f1], writes=[("gus", i0 + 1)], q="pool")
        slab_of[sidx] = i0
        return i0

    ngroups = (FC + FG - 1) // FG
    for g in range(ngroups):
        fos = list(range(g * FG, min(FC, (g + 1) * FG)))
        for li_ in range(len(fos)):
            p.dma(wds[:, li_, :], wdv[:, fos[0] + li_, :], writes=["wds"], q="pool")
        for li, fo in enumerate(fos):
            i0 = ensure_slab(fo)
            off = (fo % 2) * 128
            s = fo % 2
            gps = []
            for (b0, b1) in blocks:
                ps, pk = banks.next()
                for c in range(KC):
                    p.op("pe", lambda e, ps=ps, c=c, b0=b0, b1=b1: e.matmul(
                        ps[:, 0:b1 - b0], lhsT=gus[i0][:, c, off:off + 128], rhs=u[:, c, b0:b1],
                        start=(c == 0), stop=(c == KC - 1)),
                        reads=[("gus", i0), ("u", c)], writes=[pk])
                p.op("act", lambda e, ps=ps, b0=b0, b1=b1: e.activation(out=gsb[:, s, b0:b1], in_=ps[:, 0:b1 - b0], func=AF.Copy),
                     reads=[pk], writes=[("gsb", s)])
            ups = []
            for (b0, b1) in blocks:
                ps, pk = banks.next()
                for c in range(KC):
                    p.op("pe", lambda e, ps=ps, c=c, b0=b0, b1=b1: e.matmul(
                        ps[:, 0:b1 - b0], lhsT=gus[i0 + 1][:, c, off:off + 128], rhs=u[:, c, b0:b1],
                        start=(c == 0), stop=(c == KC - 1)),
                        reads=[("gus", i0 + 1), ("u", c)], writes=[pk])
                ups.append((ps, pk, b0, b1))
            n = WF - 2
            p.op("dve", lambda e: e.tensor_scalar(out=tsb[:, s, 1:1 + n], in0=gsb[:, s, 0:n], scalar1=cw[:, fo, 0:1],
                                                  scalar2=None, op0=ALU.mult),
                 reads=[("gsb", s), "cw"], writes=[("tmp", s)])
            for k in (1, 2):
                p.op("dve", lambda e, k=k: e.scalar_tensor_tensor(out=tsb[:, s, 1:1 + n], in0=gsb[:, s, k:k + n],
                                                                  scalar=cw[:, fo, k:k + 1], in1=tsb[:, s, 1:1 + n],
                                                                  op0=ALU.mult, op1=ALU.add),
                     reads=[("gsb", s), "cw", ("tmp", s)], writes=[("tmp", s)])
            p.op("act", lambda e: e.activation(out=tsb[:, s, 1:1 + n], in_=tsb[:, s, 1:1 + n], func=AF.Silu,
                                               bias=cb[:, fo:fo + 1], scale=1.0),
                 reads=[("tmp", s), "cb"], writes=[("tmp", s)])
            for (ps, pk, b0, b1) in ups:
                a0 = max(b0, 1)
                a1 = min(b1, WF - 1)
                p.op("dve", lambda e, ps=ps, b0=b0, a0=a0, a1=a1: e.tensor_tensor(
                    out=act[:, li, a0:a1], in0=tsb[:, s, a0:a1], in1=ps[:, a0 - b0:a1 - b0], op=ALU.mult),
                    reads=[("tmp", s), pk], writes=[("act", li)])
        for m in range(KC):
            for (d0, d1, col) in dblocks:
                ps, pk = banks.next()
                for li in range(len(fos)):
                    p.op("pe", lambda e, ps=ps, li=li, d0=d0, d1=d1: e.matmul(
                        ps[:, 0:d1 - d0], lhsT=wds[:, li, m * 128:(m + 1) * 128], rhs=act[:, li, d0:d1],
                        start=(li == 0), stop=(li == len(fos) - 1)),
                        reads=["wds", ("act", li)], writes=[pk])
                p.op("dve", lambda e, ps=ps, d0=d0, d1=d1, col=col: e.scalar_tensor_tensor(
                    out=h[:, m, d0:d1], in0=ps[:, 0:d1 - d0], scalar=modv[:, 5 * KC + m, col:col + 1],
                    in1=h[:, m, d0:d1], op0=ALU.mult, op1=ALU.add),
                    reads=[pk, "modv", ("h", m)], writes=[("h", m)])
    hov = ho_d.rearrange("(c p) w -> p c w", p=128)
    for c4 in range(0, KC, 4):
        ks = [("h", c) for c in range(c4, c4 + 4)]
        p.dma(hov[:, c4:c4 + 4, 0:1024], h[:, c4:c4 + 4, 1:1025], reads=ks)
        p.dma(hov[:, c4:c4 + 4, 1024:1152], h[:, c4:c4 + 4, 1027:1155], reads=ks)
    if final:
        gf = load_small(p, "gf", [128, KC], gf_d)
        rms_rstd_chunked(p, cm, banks, h, 1, 1025, rstd, sq)
        ofv = of_d.rearrange("(c p) w -> p c w", p=128)
        for c in range(KC):
            s = c % 2
            p.op("dve", lambda e, c=c, s=s: e.tensor_tensor(out=tsb[:, s, 1:1025], in0=h[:, c, 1:1025], in1=rstd[:, 1:1025], op=ALU.mult),
                 reads=[("h", c), "rstd"], writes=[("tmp", s)])
            p.op("act", lambda e, c=c, s=s: e.activation(out=tsb[:, s, 1:1025], in_=tsb[:, s, 1:1025], func=AF.Copy, scale=gf[:, c:c + 1]),
                 reads=[("tmp", s), "gf"], writes=[("tmp", s)])
            p.dma(ofv[:, c, :], tsb[:, s, 1:1025], reads=[("tmp", s)])
    return p.build()


def rms_rstd_chunked(p, cm, banks, h, w0, w1, rstd, sq, hname="h"):
    for (b0, b1) in col_blocks(w0, w1):
        ps, pk = banks.next()
        n = b1 - b0
        for c in range(KC):
            s = c % 2
            p.op("act", lambda e, c=c, s=s: e.activation(out=sq[:, s, 0:n], in_=h[:, c, b0:b1], func=AF.Square),
                 reads=[(hname, c)], writes=[("sq", s)])
            p.op("pe", lambda e, c=c, s=s: e.matmul(ps[:, 0:n], lhsT=cm.ones[:], rhs=sq[:, s, 0:n],
                                                     start=(c == 0), stop=(c == KC - 1)),
                 reads=[("sq", s), "c_ones"], writes=[pk])
        p.op("act", lambda e, ps=ps: e.activation(out=rstd[:, b0:b1], in_=ps[:, 0:n], func=AF.Sqrt,
                                                  bias=cm.eps[:, 0:1], scale=1.0 / D),
             reads=[pk, "c_eps"], writes=["rstd"])
        p.op("dve", lambda e: e.reciprocal(out=rstd[:, b0:b1], in_=rstd[:, b0:b1]), reads=["rstd"], writes=["rstd"])


def _norm_mod_chunked(p, cm, banks, h, W, segs, out_bf, okey, rstd, sq, tmp, mask, keys_A, hname="h", w0=0):
    rms_rstd_chunked(p, cm, banks, h, w0, W, rstd, sq, hname)
    for c in range(KC):
        s = c % 2
        p.op("dve", lambda e, c=c, s=s: e.tensor_tensor(out=tmp[:, s, w0:W], in0=h[:, c, w0:W], in1=rstd[:, w0:W], op=ALU.mult),
             reads=[(hname, c), "rstd"], writes=[("tmp", s)])
        for (c0, c1, A, Bfn) in segs:
            p.op("act", lambda e, c=c, s=s, c0=c0, c1=c1, A=A, Bfn=Bfn: e.activation(
                out=out_bf[:, c, c0:c1], in_=tmp[:, s, c0:c1], func=AF.Identity, scale=A[:, c:c + 1], bias=Bfn(c)),
                reads=[("tmp", s), "modv"] + keys_A, writes=[(okey, c)])
        if mask is not None:
            p.op("pool", lambda e, c=c: e.tensor_tensor(out=out_bf[:, c, w0:W], in0=out_bf[:, c, w0:W], in1=mask[:, w0:W], op=ALU.mult),
                 reads=[(okey, c), "mask"], writes=[(okey, c)])


def linear_fm(p, banks, name, x, xkey, kcin, wview, dout, blocks, evac, sw=512, q="pool", nbuf=2, wdt=BF16):
    slabs = [p.sb("%s_w%d" % (name, i), [128, kcin, sw], wdt) for i in range(nbuf)]
    ns = (dout + sw - 1) // sw
    for si in range(ns):
        sl = slabs[si % nbuf]
        sk = (name + "_w", si % nbuf)
        f0 = si * sw
        f1 = min(dout, f0 + sw)
        half = kcin // 2 if kcin >= 8 else kcin
        for k0 in range(0, kcin, half):
            p.dma(sl[:, k0:k0 + half, 0:f1 - f0], wview[:, k0:k0 + half, f0:f1], writes=[sk], q=q)
        for mi in range((f1 - f0) // 128):
            m = (f0 // 128) + mi
            for (b0, b1) in blocks:
                ps, pk = banks.next()
                for c in range(kcin):
                    p.op("pe", lambda e: e.matmul(ps[:, 0:b1 - b0], lhsT=sl[:, c, mi * 128:(mi + 1) * 128],
                                                  rhs=x[:, c, b0:b1], start=(c == 0), stop=(c == kcin - 1)),
                         reads=[sk, (xkey, c)], writes=[pk])
                evac(m, ps, pk, b0, b1)


def build_mod():
    p = Prog()
    wm_d = p.dram("wm", [D, 6144])
    bm_d = p.dram("bm", [128, 48])
    ct_d = p.dram("ct", [128, KC, 8])
    mo_d = p.dram("mo", [128, 48, 8], kind="ExternalOutput")
    banks = Banks(p)
    ct = load_small(p, "ct", [128, KC, 8], ct_d)
    bm = load_small(p, "bm", [128, 48], bm_d)
    cb16 = p.sb("cb16", [128, KC, 8], F32)
    mo = p.sb("mo", [128, 48, 8], F32)
    p.op("act", lambda e: e.activation(out=cb16[:], in_=ct[:], func=AF.Silu), reads=["ct"],
         writes=[("cb16", c) for c in range(KC)])

    def evac(m, ps, pk, b0, b1):
        p.op("dve", lambda e: e.tensor_scalar(out=mo[:, m, :], in0=ps[:, 0:8], scalar1=bm[:, m:m + 1], scalar2=None, op0=ALU.add),
             reads=[pk, "bm"], writes=["mo"])
    linear_fm(p, banks, "mod", cb16, "cb16", KC, wm_d.rearrange("(c p) f -> p c f", p=128), 6144, [(0, 8)], evac,
              q="sync", wdt=F32)
    p.dma(mo_d, mo[:], reads=["mo"])
    return p.build()


WP = 1184
P_L0, P_L1, P_C0, P_C1 = 0, 1040, 1040, 1184


def build_pool():
    p = Prog()
    h_d = p.dram("h", [D, WP])
    modv_d = p.dram("modv", [128, 6 * KC, 2])
    g1_d = p.dram("g1", [128, KC])
    mask_d = p.dram("mask", [1, WP])
    icnt_d = p.dram("icnt", [4, WP])
    pw_d = p.dram("pw", [4, 512, 512])
    pb_d = p.dram("pb", [128, KC])
    psc_d = p.dram("psc", [128, KC])
    y_d = p.dram("y", [D, 1152], kind="ExternalOutput")
    cm = Common(p)
    banks = Banks(p)
    h = p.sb("h", [128, KC, WP], F32)
    pbf = p.sb("pbf", [128, KC, WP], BF16)
    tsb = p.sb("tsb", [128, 2, WP], F32)
    t2 = p.sb("t2", [128, 2, WP], F32)
    sq = p.sb("sq", [128, 2, 512], BF16)
    rstd = p.sb("rstd", [128, WP], F32)
    yo = p.sb("yo", [128, 2, WP], F32)
    modv = load_small(p, "modv", [128, 6 * KC, 2], modv_d)
    gvec = load_small(p, "gvec", [128, KC], g1_d)
    pb = load_small(p, "pb", [128, KC], pb_d)
    psc = load_small(p, "psc", [128, KC], psc_d)
    mask = load_small(p, "mask", [128, WP], mask_d[0].partition_broadcast(128))
    icnt = p.sb("icnt", [128, 4, WP], F32)
    for g in range(4):
        p.dma(icnt[:, g, :], icnt_d[g].partition_broadcast(128), writes=["icnt"])
    hv = h_d.rearrange("(c p) w -> p c w", p=128)
    for c4 in range(0, KC, 4):
        p.dma(h[:, c4:c4 + 4, :], hv[:, c4:c4 + 4, :], writes=[("h", c) for c in range(c4, c4 + 4)])
    A_l = make_AB(p, "pl", modv, gvec, 0, 1, 0)
    A_c = make_AB(p, "pc", modv, gvec, 0, 1, 1)
    segs = [(P_L0, P_L1, A_l, lambda c: modv[:, c, 0:1]), (P_C0, P_C1, A_c, lambda c: modv[:, c, 1:2])]
    rms_rstd_chunked(p, cm, banks, h, 0, WP, rstd, sq)
    for c in range(KC):
        s = c % 2
        p.op("dve", lambda e: e.tensor_tensor(out=tsb[:, s, :], in0=h[:, c, :], in1=rstd[:], op=ALU.mult),
             reads=[("h", c), "rstd"], writes=[("tmp", s)])
        for (c0, c1, A, Bfn) in segs:
            p.op("act", lambda e: e.activation(out=h[:, c, c0:c1], in_=tsb[:, s, c0:c1], func=AF.Identity,
                                               scale=A[:, c:c + 1], bias=Bfn(c)),
                 reads=[("tmp", s), "modv", "pl_A", "pc_A"], writes=[("h", c)])
        p.op("pool", lambda e: e.tensor_tensor(out=h[:, c, :], in0=h[:, c, :], in1=mask[:], op=ALU.mult),
             reads=[("h", c), "mask"], writes=[("h", c)])
        g = c // 4
        win = (2, 4, 8, 16)[g]
        right = win - 1 - win // 2
        src, skey = h[:, c, :], ("h", c)
        sh = 1
        bufs = [tsb[:, s, :], t2[:, s, :]]
        bkeys = [("tmp", s), ("t2", s)]
        bi = 0
        while sh < win:
            dst, dkey = bufs[bi], bkeys[bi]
            p.op("dve", lambda e: e.tensor_tensor(out=dst[:, sh:WP], in0=src[:, sh:WP], in1=src[:, 0:WP - sh], op=ALU.add),
                 reads=[skey], writes=[dkey])
            p.op("dve", lambda e: e.tensor_copy(out=dst[:, 0:sh], in_=src[:, 0:sh]), reads=[skey], writes=[dkey])
            src, skey = dst, dkey
            sh *= 2
            bi ^= 1
        dst, dkey = bufs[bi], bkeys[bi]
        n = WP - 16
        p.op("dve", lambda e: e.tensor_tensor(out=dst[:, 8:8 + n], in0=src[:, 8 + right:8 + right + n],
                                              in1=icnt[:, g, 8:8 + n], op=ALU.mult),
             reads=[skey, "icnt"], writes=[dkey])
        p.op("dve", lambda e: e.tensor_tensor(out=pbf[:, c, 8:8 + n], in0=dst[:, 8:8 + n], in1=h[:, c, 8:8 + n], op=ALU.subtract),
             reads=[dkey, ("h", c)], writes=[("pbf", c)])
    yv = y_d.rearrange("(c p) w -> p c w", p=128)
    vblocks = [(8, 520), (520, 1032), (1048, 1176)]
    for g in range(4):
        wsl = p.sb("pw%d" % g, [128, 4, 512], BF16)
        p.dma(wsl[:], pw_d[g].rearrange("(c p) f -> p c f", p=128), writes=[("pw", g)], q="pool")
        for mo_ in range(4):
            m = g * 4 + mo_
            s = m % 2
            for (b0, b1) in vblocks:
                ps, pk = banks.next()
                for ci in range(4):
                    p.op("pe", lambda e: e.matmul(ps[:, 0:b1 - b0], lhsT=wsl[:, ci, mo_ * 128:(mo_ + 1) * 128],
                                                  rhs=pbf[:, g * 4 + ci, b0:b1], start=(ci == 0), stop=(ci == 3)),
                         reads=[("pw", g), ("pbf", g * 4 + ci)], writes=[pk])
                p.op("dve", lambda e: e.tensor_scalar(out=yo[:, s, b0:b1], in0=ps[:, 0:b1 - b0], scalar1=pb[:, m:m + 1],
                                                      scalar2=psc[:, m:m + 1], op0=ALU.add, op1=ALU.mult),
                     reads=[pk, "pb", "psc"], writes=[("yo", s)])
            p.dma(yv[:, m, 0:1024], yo[:, s, 8:1032], reads=[("yo", s)])
            p.dma(yv[:, m, 1024:1152], yo[:, s, 1048:1176], reads=[("yo", s)])
    return p.build()


def norm_mod_stream(p, cm, banks, hview, W, segs, out, okey, rstd, sq, hbuf, tmp, akeys, mask=None, out_chunks=None):
    blocks = col_blocks(0, W)
    pss = [banks.next() for _ in blocks]
    for c in range(KC):
        s = c % 2
        p.dma(hbuf[:, s, 0:W], hview[:, c, :], writes=[("hbuf", s)])
        for bi, (b0, b1) in enumerate(blocks):
            ps, pk = pss[bi]
            p.op("act", lambda e: e.activation(out=sq[:, s, 0:b1 - b0], in_=hbuf[:, s, b0:b1], func=AF.Square),
                 reads=[("hbuf", s)], writes=[("sq", s)])
            p.op("pe", lambda e: e.matmul(ps[:, 0:b1 - b0], lhsT=cm.ones[:], rhs=sq[:, s, 0:b1 - b0],
                                          start=(c == 0), stop=(c == KC - 1)),
                 reads=[("sq", s), "c_ones"], writes=[pk])
    for bi, (b0, b1) in enumerate(blocks):
        ps, pk = pss[bi]
        p.op("act", lambda e: e.activation(out=rstd[:, b0:b1], in_=ps[:, 0:b1 - b0], func=AF.Sqrt,
                                           bias=cm.eps[:, 0:1], scale=1.0 / D),
             reads=[pk, "c_eps"], writes=["rstd"])
        p.op("dve", lambda e: e.reciprocal(out=rstd[:, b0:b1], in_=rstd[:, b0:b1]), reads=["rstd"], writes=["rstd"])
    for c in range(KC):
        s = c % 2
        p.dma(hbuf[:, s, 0:W], hview[:, c, :], writes=[("hbuf", s)])
        p.op("dve", lambda e: e.tensor_tensor(out=tmp[:, s, 0:W], in0=hbuf[:, s, 0:W], in1=rstd[:, 0:W], op=ALU.mult),
             reads=[("hbuf", s), "rstd"], writes=[("tmp", s)])
        for (c0, c1, A, Bfn) in segs:
            p.op("act", lambda e: e.activation(out=out[:, c, c0:c1], in_=tmp[:, s, c0:c1], func=AF.Identity,
                                               scale=A[:, c:c + 1], bias=Bfn(c)),
                 reads=[("tmp", s), "modv"] + akeys, writes=[(okey, c)])
        if mask is not None:
            p.op("pool", lambda e: e.tensor_tensor(out=out[:, c, 0:W], in0=out[:, c, 0:W], in1=mask[:, 0:W], op=ALU.mult),
                 reads=[(okey, c), "mask"], writes=[(okey, c)])


def linear_fm2(p, banks, slabs, skey, x, xkey, kcin, wview, dout, blocks, evac, q="pool"):
    sw = slabs[0].shape[2]
    nbuf = len(slabs)
    ns = (dout + sw - 1) // sw
    for si in range(ns):
        sl = slabs[si % nbuf]
        sk = (skey, si % nbuf)
        f0 = si * sw
        f1 = min(dout, f0 + sw)
        half = kcin // 2 if kcin >= 8 else kcin
        for k0 in range(0, kcin, half):
            p.dma(sl[:, k0:k0 + half, 0:f1 - f0], wview[:, k0:k0 + half, f0:f1], writes=[sk], q=q)
        for mi in range((f1 - f0) // 128):
            m = (f0 // 128) + mi
            for (b0, b1) in blocks:
                ps, pk = banks.next()
                for c in range(kcin):
                    p.op("pe", lambda e: e.matmul(ps[:, 0:b1 - b0], lhsT=sl[:, c, mi * 128:(mi + 1) * 128],
                                                  rhs=x[:, c, b0:b1], start=(c == 0), stop=(c == kcin - 1)),
                         reads=[sk, (xkey, c)], writes=[pk])
                evac(m, ps, pk, b0, b1)


WG = 1152
NPC = 9


def build_gmlp():
    p = Prog()
    h_d = p.dram("h", [D, WG])
    modv_d = p.dram("modv", [128, 6 * KC, 2])
    g1_d = p.dram("g1", [128, KC])
    win_d = p.dram("win", [D, 2 * D])
    bzu_d = p.dram("bzu", [128, KC])
    bzv_d = p.dram("bzv", [1, D])
    ng_d = p.dram("ng", [1, D])
    wst_d = p.dram("wst", [128, 16, 128])
    bs_d = p.dram("bs", [1, 16 * 128])
    wo_d = p.dram("wo", [D, D])
    y_d = p.dram("y", [D, WG], kind="ExternalOutput")
    cm = Common(p)
    banks = Banks(p)
    aT = p.sb("aT", [128, KC, WG], BF16)
    zu = p.sb("zu", [128, KC, WG], BF16)
    zv = p.sb("zv", [128, NPC, D], BF16)
    hbuf = p.sb("hbuf", [128, 2, WG], F32)
    tmp = p.sb("tmp", [128, 2, WG], F32)
    sq = p.sb("sq", [128, 2, 512], BF16)
    rstd = p.sb("rstd", [128, WG], F32)
    slabs = [p.sb("slab%d" % i, [128, KC, 256], BF16) for i in range(2)]
    modv = load_small(p, "modv", [128, 6 * KC, 2], modv_d)
    gvec = load_small(p, "gvec", [128, KC], g1_d)
    bzu = load_small(p, "bzu", [128, KC], bzu_d)
    bzv = load_small(p, "bzv", [128, D], bzv_d[0].partition_broadcast(128))
    ng = load_small(p, "ng", [128, D], ng_d[0].partition_broadcast(128))
    bs = load_small(p, "bs", [128, 16 * 128], bs_d[0].partition_broadcast(128))
    wst = p.sb("wst", [128, 16, 128], BF16)
    p.dma(wst[:], wst_d, writes=["wst"], q="pool")
    ssq = p.sb("ssq", [128, NPC, 8], F32)
    rtok = p.sb("rtok", [128, NPC], F32)
    A_l = make_AB(p, "gl", modv, gvec, 0, 1, 0)
    A_c = make_AB(p, "gc", modv, gvec, 0, 1, 1)
    segs = [(0, 1024, A_l, lambda c: modv[:, c, 0:1]), (1024, WG, A_c, lambda c: modv[:, c, 1:2])]
    norm_mod_stream(p, cm, banks, h_d.rearrange("(c p) w -> p c w", p=128), WG, segs, aT, "aT", rstd, sq, hbuf, tmp,
                    ["gl_A", "gc_A"])
    blocks = col_blocks(0, WG)
    winv = win_d.rearrange("(c p) f -> p c f", p=128)
    import os
    stop = int(os.environ.get("GSTOP", "99"))
    if stop <= 0:
        return p.build()

    def evac_zu(m, ps, pk, b0, b1):
        p.op("act", lambda e: e.activation(out=zu[:, m, b0:b1], in_=ps[:, 0:b1 - b0], func=AF.Gelu_apprx_tanh,
                                           bias=bzu[:, m:m + 1], scale=1.0),
             reads=[pk, "bzu"], writes=[("zu", m)])
    linear_fm2(p, banks, slabs, "slab", aT, "aT", KC, winv[:, :, 0:D], D, blocks, evac_zu)

    if stop <= 1:
        return p.build()
    sw = 256
    for si in range(D // sw):
        sl = slabs[si % 2]
        sk = ("slab", si % 2)
        f0 = D + si * sw
        for k0 in (0, 8):
            p.dma(sl[:, k0:k0 + 8, :], winv[:, k0:k0 + 8, f0:f0 + sw], writes=[sk], q="pool")
        for n in range(NPC):
            ps, pk = banks.next()
            for c in range(KC):
                p.op("pe", lambda e: e.matmul(ps[:, 0:sw], lhsT=aT[:, c, n * 128:(n + 1) * 128], rhs=sl[:, c, :],
                                              start=(c == 0), stop=(c == KC - 1)),
                     reads=[sk, ("aT", c)], writes=[pk])
            s = (si * NPC + n) % 2
            p.op("dve", lambda e: e.tensor_tensor(out=tmp[:, s, 0:sw], in0=ps[:, 0:sw], in1=bzv[:, si * sw:(si + 1) * sw], op=ALU.add),
                 reads=[pk, "bzv"], writes=[("tmp", s)])
            p.op("act", lambda e: e.activation(out=tmp[:, s, 0:sw], in_=tmp[:, s, 0:sw], func=AF.Gelu_apprx_tanh),
                 reads=[("tmp", s)], writes=[("tmp", s)])
            p.op("act", lambda e: e.activation(out=tmp[:, s, 512:512 + sw], in_=tmp[:, s, 0:sw], func=AF.Square,
                                               accum_out=ssq[:, n, si:si + 1]),
                 reads=[("tmp", s)], writes=[("tmp", s), "ssq"])
            p.op("pool", lambda e: e.tensor_copy(out=zv[:, n, si * sw:(si + 1) * sw], in_=tmp[:, s, 0:sw]),
                 reads=[("tmp", s)], writes=[("zv", n)])
    if stop <= 2:
        return p.build()
    p.op("dve", lambda e: e.tensor_reduce(out=rtok[:], in_=ssq[:], axis=AX.X, op=ALU.add), reads=["ssq"], writes=["rtok"])
    p.op("act", lambda e: e.activation(out=rtok[:], in_=rtok[:], func=AF.Sqrt, bias=cm.eps[:, 0:1], scale=1.0 / D),
         reads=["rtok", "c_eps"], writes=["rtok"])
    p.op("dve", lambda e: e.reciprocal(out=rtok[:], in_=rtok[:]), reads=["rtok"], writes=["rtok"])
    for n in range(NPC):
        p.op("dve", lambda e: e.scalar_tensor_tensor(out=zv[:, n, :], in0=zv[:, n, :], scalar=rtok[:, n:n + 1], in1=ng[:],
                                                     op0=ALU.mult, op1=ALU.mult),
             reads=[("zv", n), "rtok", "ng"], writes=[("zv", n)])
    if stop <= 3:
        return p.build()
    bsv = bs[:].rearrange("p (g q) -> p g q", q=128)
    for n in range(NPC):
        for g4 in range(4):
            ps, pk = banks.next()
            for gi in range(4):
                g = g4 * 4 + gi
                p.op("pe", lambda e: e.matmul(ps[:, gi * 128:(gi + 1) * 128], lhsT=zv[:, n, g * 128:(g + 1) * 128],
                                              rhs=wst[:, g, :], start=True, stop=True),
                     reads=[("zv", n), "wst"], writes=[pk])
            s = (n * 4 + g4) % 2
            tv = tmp[:, s, 0:512].rearrange("p (g q) -> p g q", q=128)
            p.op("dve", lambda e: e.tensor_tensor(out=tv, in0=ps[:, 0:512].rearrange("p (g q) -> p g q", q=128),
                                                  in1=bsv[:, g4 * 4:(g4 + 1) * 4, :], op=ALU.add),
                 reads=[pk, "bs"], writes=[("tmp", s)])
            p.op("dve", lambda e: e.tensor_tensor(out=aT[:, g4 * 4:(g4 + 1) * 4, n * 128:(n + 1) * 128], in0=tv,
                                                  in1=zu[:, g4 * 4:(g4 + 1) * 4, n * 128:(n + 1) * 128], op=ALU.mult),
                 reads=[("tmp", s)] + [("zu", g4 * 4 + i) for i in range(4)],
                 writes=[("aT", g4 * 4 + i) for i in range(4)])
    if stop <= 4:
        return p.build()
    yv = y_d.rearrange("(c p) w -> p c w", p=128)
    cnt = [0]

    def evac_y(m, ps, pk, b0, b1):
        s = cnt[0] % 2
        cnt[0] += 1
        p.op("act", lambda e: e.activation(out=hbuf[:, s, b0:b1], in_=ps[:, 0:b1 - b0], func=AF.Copy),
             reads=[pk], writes=[("hbuf", s)])
        if os.environ.get("GNODMA") != "1":
            p.dma(yv[:, m, b0:b1], hbuf[:, s, b0:b1], reads=[("hbuf", s)], q="act")
    linear_fm2(p, banks, slabs, "slab", aT, "aT", KC, wo_d.rearrange("(c p) f -> p c f", p=128), D, blocks, evac_y)
    return p.build()


def build_na1():
    p = Prog()
    W = WG
    h_d = p.dram("h", [D, W])
    modv_d = p.dram("modv", [128, 6 * KC, 2])
    g1_d = p.dram("g1", [128, KC])
    w_d = p.dram("wqkv", [D, 3 * D])
    cos_d = p.dram("rcos", [128, 1024])
    sin_d = p.dram("rsin", [128, 1024])
    pm_d = p.dram("pm", [128, 128])
    q_d = p.dram("qT", [D, W], kind="ExternalOutput")
    k_d = p.dram("kT", [D, W], kind="ExternalOutput")
    v_d = p.dram("v", [W, D], kind="ExternalOutput")
    cm = Common(p)
    banks = Banks(p)
    aT = p.sb("aT", [128, KC, W], BF16)
    hbuf = p.sb("hbuf", [128, 2, W], F32)
    tmp = p.sb("tmp", [128, 2, W], F32)
    st = p.sb("st", [128, 2, W], F32)
    qb = p.sb("qb", [128, 2, 512], BF16)
    sq = p.sb("sq", [128, 2, 512], BF16)
    rstd = p.sb("rstd", [128, W], F32)
    slabs = [p.sb("slab%d" % i, [128, KC, 256], BF16) for i in range(2)]
    modv = load_small(p, "modv", [128, 6 * KC, 2], modv_d)
    gvec = load_small(p, "gvec", [128, KC], g1_d)
    rcos = load_small(p, "rcos", [128, 1024], cos_d)
    rsin = load_small(p, "rsin", [128, 1024], sin_d)
    pm = p.sb("pm", [128, 128], BF16)
    if os.environ.get("NOPM") != "1":
        p.dma(pm[:], pm_d, writes=["pm"], q="pool", max_dma_last_dim=int(os.environ.get("MDL", "512")))
    A_l = make_AB(p, "nl", modv, gvec, 0, 1, 0)
    A_c = make_AB(p, "ncx", modv, gvec, 0, 1, 1)
    segs = [(0, 1024, A_l, lambda c: modv[:, c, 0:1]), (1024, W, A_c, lambda c: modv[:, c, 1:2])]
    norm_mod_stream(p, cm, banks, h_d.rearrange("(c p) w -> p c w", p=128), W, segs, aT, "aT", rstd, sq, hbuf, tmp,
                    ["nl_A", "ncx_A"])
    wv = w_d.rearrange("(c p) f -> p c f", p=128)
    qv = q_d.rearrange("(c p) w -> p c w", p=128)
    kv = k_d.rearrange("(c p) w -> p c w", p=128)
    blocks = col_blocks(0, W)
    cnt = [0]
    stop = int(os.environ.get("GSTOP", "99"))
    if stop <= 0:
        return p.build()

    def evac_qk(m, ps, pk, b0, b1):
        isq = m < KC
        sc = 0.125 if isq else 1.0
        mm = m if isq else m - KC
        s = mm % 2
        n = b1 - b0
        ev = int(os.environ.get("EV", "9"))
        if b0 >= 1024 or ev == 0:
            p.op("act", lambda e: e.activation(out=st[:, s, b0:b1], in_=ps[:, 0:n], func=AF.Copy, scale=sc),
                 reads=[pk], writes=[("st", s, 2)])
        else:
            i = cnt[0] % 2
            cnt[0] += 1
            p.op("act", lambda e: e.activation(out=qb[:, i, 0:n], in_=ps[:, 0:n], func=AF.Copy, scale=sc),
                 reads=[pk], writes=[("qb", i)])
            ps2, pk2 = banks.next()
            p.op("pe", lambda e: e.matmul(ps2[:, 0:n], lhsT=pm[:], rhs=qb[:, i, 0:n], start=True, stop=True),
                 reads=["pm", ("qb", i)], writes=[pk2])
            if ev == 1:
                p.op("act", lambda e: e.activation(out=st[:, s, b0:b1], in_=ps2[:, 0:n], func=AF.Copy, scale=sc),
                     reads=[pk, pk2], writes=[("st", s, b0 // 512)])
                if b1 == W:
                    pass
                return
            p.op("dve", lambda e: e.scalar_tensor_tensor(out=st[:, s, b0:b1], in0=ps[:, 0:n], scalar=sc, in1=rcos[:, b0:b1],
                                                         op0=ALU.mult, op1=ALU.mult),
                 reads=[pk, "rcos"], writes=[("st", s, b0 // 512)])
            p.op("dve", lambda e: e.tensor_tensor(out=tmp[:, i, 0:n], in0=ps2[:, 0:n], in1=rsin[:, b0:b1], op=ALU.mult),
                 reads=[pk2, "rsin"], writes=[("tmp", i)])
            p.op("dve", lambda e: e.tensor_tensor(out=st[:, s, b0:b1], in0=st[:, s, b0:b1], in1=tmp[:, i, 0:n], op=ALU.add),
                 reads=[("tmp", i), ("st", s, b0 // 512)], writes=[("st", s, b0 // 512)])
        if b1 == W:
            dst = qv if isq else kv
            p.dma(dst[:, mm, :], st[:, s, :], reads=[("st", s, 0), ("st", s, 1), ("st", s, 2)], q="act")
    linear_fm2(p, banks, slabs, "slab", aT, "aT", KC, wv[:, :, 0:2 * D], 2 * D, blocks, evac_qk)
    if stop <= 1:
        return p.build()
    sw = 256
    for si in range(D // sw):
        sl = slabs[si % 2]
        sk = ("slab", si % 2)
        f0 = 2 * D + si * sw
        for k0 in (0, 8):
            p.dma(sl[:, k0:k0 + 8, :], wv[:, k0:k0 + 8, f0:f0 + sw], writes=[sk], q="pool")
        for n in range(NPC):
            ps, pk = banks.next()
            for c in range(KC):
                p.op("pe", lambda e: e.matmul(ps[:, 0:sw], lhsT=aT[:, c, n * 128:(n + 1) * 128], rhs=sl[:, c, :],
                                              start=(c == 0), stop=(c == KC - 1)),
                     reads=[sk, ("aT", c)], writes=[pk])
            s = (si * NPC + n) % 2
            p.op("act", lambda e: e.activation(out=tmp[:, s, 0:sw], in_=ps[:, 0:sw], func=AF.Copy),
                 reads=[pk], writes=[("tmp", s)])
            p.dma(v_d[n * 128:(n + 1) * 128, si * sw:(si + 1) * sw], tmp[:, s, 0:sw], reads=[("tmp", s)], q="act")
    return p.build()


NTOK = T + L


def na_rs(r):
    return min(max(r - 4, 0), 24)


def build_na2():
    p = Prog()
    q_d = p.dram("qT", [1024, NTOK])
    k_d = p.dram("kT", [1024, NTOK])
    v_d = p.dram("v", [NTOK, 1024])
    bt_d = p.dram("bt", [16, 128, 15 * 64])
    wo_d = p.dram("wo", [1024, D])
    y_d = p.dram("y", [D, NTOK], kind="ExternalOutput")
    sbanks = Banks(p, 4, "ps_s")
    abanks = Banks(p, 4, "ps_a")
    ones = p.sb("ones", [128, 64], BF16)
    p.op("dve", lambda e: e.memset(ones[:], 1.0), writes=["ones"])
    qT = p.sb("qT", [128, 8, NTOK], BF16)
    kT = p.sb("kT", [128, 8, NTOK], BF16)
    v = p.sb("v", [128, 18, 1024], BF16)
    at = p.sb("at", [128, 8, NTOK], BF16)
    bts = [p.sb("bt%d" % i, [128, 15 * 64], F32) for i in range(2)]
    pts = [p.sb("pt%d" % i, [128, 512], BF16) for i in range(4)]
    sbs = [p.sb("sbb%d" % i, [128, 512], F32) for i in range(2)]
    rec = p.sb("rec", [128, 2, 512], F32)
    st = p.sb("st", [128, 2, 512], F32)
    slabs = [p.sb("slab%d" % i, [128, 8, 512], BF16) for i in range(2)]
    qv = q_d.rearrange("(c p) w -> p c w", p=128)
    kvv = k_d.rearrange("(c p) w -> p c w", p=128)
    vv = v_d.rearrange("(n p) f -> p n f", p=128)
    for c in range(8):
        p.dma(qT[:, c, :], qv[:, c, :], writes=[("qT", c)], q="pool", max_dma_last_dim=4096)
        p.dma(kT[:, c, :], kvv[:, c, :], writes=[("kT", c)], q="pool", max_dma_last_dim=4096)
    for n in range(18):
        p.dma(v[:, n, :], vv[:, n, :], writes=[("v", n)], q="pool")
    npt = [0]
    nsb = [0]
    for h in range(16):
        c = h // 2
        base = (h % 2) * 64
        bt = bts[h % 2]
        bk = ("bt", h % 2)
        p.dma(bt[:], bt_d[h], writes=[bk])
        btv = bt[:].rearrange("p (i q) -> p i q", q=64)
        groups = [(g * 512, 512, g) for g in range(4)] + [(T, L, None)]
        for (q0, nq, g) in groups:
            O, ok = abanks.next()
            Dn, dk = abanks.next()
            items = []
            for j in range(2):
                items.append(("ctx", j))
            if g is not None:
                for kap in range(32):
                    rows = [r for r in range(8 * g, 8 * g + 8) if na_rs(r) <= kap < na_rs(r) + 8]
                    if rows:
                        items.append(("loc", kap, rows[0], rows[-1]))
            for ii, it in enumerate(items):
                first = ii == 0
                last = ii == len(items) - 1
                S, sk_ = sbanks.next()
                pt = pts[npt[0] % 4]
                ptk = ("pt", npt[0] % 4)
                npt[0] += 1
                if it[0] == "ctx":
                    j = it[1]
                    kc0 = T + j * 128
                    p.op("pe", lambda e: e.matmul(S[:, 0:nq], lhsT=kT[base:base + 64, c, kc0:kc0 + 128],
                                                  rhs=qT[base:base + 64, c, q0:q0 + nq], start=True, stop=True),
                         reads=[("kT", c), ("qT", c)], writes=[sk_])
                    p.op("act", lambda e: e.activation(out=pt[:, 0:nq], in_=S[:, 0:nq], func=AF.Exp),
                         reads=[sk_], writes=[ptk])
                    p.op("pe", lambda e: e.matmul(O[base:base + 64, 0:nq], lhsT=v[:, 16 + j, h * 64:(h + 1) * 64],
                                                  rhs=pt[:, 0:nq], start=first, stop=last),
                         reads=[("v", 16 + j), ptk], writes=[ok])
                    p.op("pe", lambda e: e.matmul(Dn[base:base + 64, 0:nq], lhsT=ones[:, :], rhs=pt[:, 0:nq],
                                                  start=first, stop=last),
                         reads=["ones", ptk], writes=[dk])
                else:
                    _, kap, ra, rb = it
                    kb = (kap % 2) * 64
                    nr = rb - ra + 1
                    nn = nr * 64
                    c0 = ra * 64 - q0
                    idx0 = ra - kap + 7
                    sb_ = sbs[nsb[0] % 2]
                    sbk = ("sbb", nsb[0] % 2)
                    nsb[0] += 1
                    p.op("pe", lambda e: e.matmul(S[kb:kb + 64, 0:nn], lhsT=kT[base:base + 64, c, kap * 64:(kap + 1) * 64],
                                                  rhs=qT[base:base + 64, c, ra * 64:(rb + 1) * 64], start=True, stop=True),
                         reads=[("kT", c), ("qT", c)], writes=[sk_])
                    p.op("dve", lambda e: e.tensor_tensor(out=sb_[kb:kb + 64, 0:nn].rearrange("p (i q) -> p i q", q=64),
                                                          in0=S[kb:kb + 64, 0:nn].rearrange("p (i q) -> p i q", q=64),
                                                          in1=btv[kb:kb + 64, idx0:idx0 + nr, :], op=ALU.add),
                         reads=[sk_, bk], writes=[sbk])
                    p.op("act", lambda e: e.activation(out=pt[kb:kb + 64, 0:nn], in_=sb_[kb:kb + 64, 0:nn], func=AF.Exp),
                         reads=[sbk], writes=[ptk])
                    p.op("pe", lambda e: e.matmul(O[base:base + 64, c0:c0 + nn], lhsT=v[kb:kb + 64, kap // 2, h * 64:(h + 1) * 64],
                                                  rhs=pt[kb:kb + 64, 0:nn], start=False, stop=last),
                         reads=[("v", kap // 2), ptk], writes=[ok])
                    p.op("pe", lambda e: e.matmul(Dn[base:base + 64, c0:c0 + nn], lhsT=ones[kb:kb + 64, :],
                                                  rhs=pt[kb:kb + 64, 0:nn], start=False, stop=last),
                         reads=["ones", ptk], writes=[dk])
            ri = (h * 5 + (g if g is not None else 4)) % 2
            p.op("dve", lambda e: e.reciprocal(out=rec[base:base + 64, ri, 0:nq], in_=Dn[base:base + 64, 0:nq]),
                 reads=[dk], writes=[("rec", ri)])
            p.op("dve", lambda e: e.tensor_tensor(out=at[base:base + 64, c, q0:q0 + nq], in0=O[base:base + 64, 0:nq],
                                                  in1=rec[base:base + 64, ri, 0:nq], op=ALU.mult),
                 reads=[ok, ("rec", ri)], writes=[("at", c)])
    yv = y_d.rearrange("(c p) w -> p c w", p=128)
    cnt = [0]

    def evac_y(m, ps, pk, b0, b1):
        s = cnt[0] % 2
        cnt[0] += 1
        p.op("act", lambda e: e.activation(out=st[:, s, 0:b1 - b0], in_=ps[:, 0:b1 - b0], func=AF.Copy),
             reads=[pk], writes=[("st", s)])
        p.dma(yv[:, m, b0:b1], st[:, s, 0:b1 - b0], reads=[("st", s)], q="act")
    linear_fm2(p, sbanks, slabs, "slab", at, "at", 8, wo_d.rearrange("(c p) f -> p c f", p=128), D, col_blocks(0, NTOK), evac_y)
    return p.build()


def _chunked(vv):
    vv = np.asarray(vv, np.float32)
    return np.ascontiguousarray(vv.reshape(-1, 128).T)


def rope_tables(t0, n):
    half = 32
    inv_freq = (10000.0 ** (-np.arange(0, half, 2, dtype=np.float32) / half)).astype(np.float32)
    pos = np.arange(t0, t0 + n)
    row = (pos // 64).astype(np.float32)
    col = (pos % 64).astype(np.float32)
    cos = np.zeros((64, n), np.float32)
    sin = np.zeros((64, n), np.float32)
    for d in range(64):
        pp = row if d < 32 else col
        ang = pp * inv_freq[d % 16]
        cos[d] = np.cos(ang)
        sgn = -1.0 if (d % 32) < 16 else 1.0
        sin[d] = sgn * np.sin(ang)
    return np.concatenate([cos, cos], 0), np.concatenate([sin, sin], 0)


def rope_perm():
    pm = np.zeros((128, 128), np.float32)
    for m in range(128):
        d = m % 32
        partner = m + 16 if d < 16 else m - 16
        pm[partner, m] = 1.0
    return pm


def na_bias_tables(rpb, heads):
    cq = np.arange(64)
    ck = np.arange(64)
    cs = np.clip(cq - 8, 0, 48)
    ok = (ck[:, None] >= cs[None, :]) & (ck[:, None] < cs[None, :] + 16)
    dc = np.clip(ck[:, None] - cq[None, :], -15, 15) + 15
    out = np.empty((len(heads), 128, 15, 64), np.float32)
    for i, h in enumerate(heads):
        for idx in range(15):
            tab = rpb[h, 14 - idx][dc]
            tab = np.where(ok, tab, np.float32(-30000.0))
            out[i, 0:64, idx] = tab
            out[i, 64:128, idx] = tab
    return out.reshape(len(heads), 128, 15 * 64)


RW_LORA = 96
RW_GATE = 256
C64 = 64


def shift_mix(p, a, akey_fn, xo, xkey, coef, n_idx, j0, n):
    for c in range(KC):
        p.op("dve", lambda e: e.tensor_scalar(out=xo[:, c, 0:n], in0=a[:, c, j0:j0 + n], scalar1=coef[:, 0, n_idx, c:c + 1],
                                              scalar2=None, op0=ALU.mult),
             reads=[akey_fn(c), "coef"], writes=[(xkey, c)])
        p.op("dve", lambda e: e.scalar_tensor_tensor(out=xo[:, c, 0:n], in0=a[:, c, j0 - 1:j0 - 1 + n], scalar=coef[:, 1, n_idx, c:c + 1],
                                                     in1=xo[:, c, 0:n], op0=ALU.mult, op1=ALU.add),
             reads=[akey_fn(c), "coef", (xkey, c)], writes=[(xkey, c)])
        p.op("dve", lambda e: e.scalar_tensor_tensor(out=xo[:, c, 0:n], in0=a[:, c, j0 + 1:j0 + 1 + n], scalar=coef[:, 2, n_idx, c:c + 1],
                                                     in1=xo[:, c, 0:n], op0=ALU.mult, op1=ALU.add),
             reads=[akey_fn(c), "coef", (xkey, c)], writes=[(xkey, c)])


def load_coef(p, coef_d):
    coef = p.sb("coef", [128, 3, 6, KC], F32)
    p.dma(coef[:, 1:3, :, :], coef_d, writes=["coef"])
    p.op("dve", lambda e: e.tensor_scalar(out=coef[:, 0, :, :], in0=coef[:, 1, :, :], scalar1=-1.0, scalar2=1.0, op0=ALU.mult, op1=ALU.add),
         reads=["coef"], writes=["coef"])
    p.op("dve", lambda e: e.tensor_tensor(out=coef[:, 0, :, :], in0=coef[:, 0, :, :], in1=coef[:, 2, :, :], op=ALU.subtract),
         reads=["coef"], writes=["coef"])
    return coef


def build_rw1():
    p = Prog()
    W = WF
    h_d = p.dram("h", [D, W])
    modv_d = p.dram("modv", [128, 6 * KC, 2])
    g1_d = p.dram("g1", [128, KC])
    mask_d = p.dram("mask", [1, W])
    coef_d = p.dram("coef", [128, 2, 6, KC])
    wrkv_d = p.dram("wrkv", [3, D, D])
    w1_d = p.dram("w1", [D, RW_LORA])
    w2_d = p.dram("w2", [RW_LORA, D])
    a1_d = p.dram("a1", [D, RW_LORA])
    a2_d = p.dram("a2", [RW_LORA, D])
    vecs_d = p.dram("vecs", [128, 5, KC])
    bones_d = p.dram("bones", [128, 128])
    rmask_d = p.dram("rmask", [1, 512])
    outs = {}
    for nm in ("at", "bt", "kt", "rt", "vt"):
        outs[nm] = p.dram(nm, [D, 1152], BF16, kind="ExternalOutput")
    bv_d = p.dram("bv", [D, 1152], kind="ExternalOutput")
    gam_d = p.dram("gam", [D, 18], kind="ExternalOutput")
    cm = Common(p)
    banks = Banks(p)
    a = p.sb("a", [128, KC, W], BF16)
    hbuf = p.sb("hbuf", [128, 2, W], F32)
    tmp = p.sb("tmp", [128, 2, W], F32)
    sq = p.sb("sq", [128, 2, 512], BF16)
    rstd = p.sb("rstd", [128, W], F32)
    modv = load_small(p, "modv", [128, 6 * KC, 2], modv_d)
    gvec = load_small(p, "gvec", [128, KC], g1_d)
    mask = load_small(p, "mask", [128, W], mask_d[0].partition_broadcast(128))
    coef = load_coef(p, coef_d)
    vecs = load_small(p, "vecs", [128, 5, KC], vecs_d)
    rmask = load_small(p, "rmask", [128, 512], rmask_d[0].partition_broadcast(128))
    bones = p.sb("bones", [128, 128], BF16)
    p.dma(bones[:], bones_d, writes=["bones"], q="pool", max_dma_last_dim=512)
    w1 = p.sb("w1", [128, KC, RW_LORA], BF16)
    a1 = p.sb("a1", [128, KC, RW_LORA], BF16)
    w2 = p.sb("w2", [RW_LORA, D], BF16)
    a2 = p.sb("a2", [RW_LORA, D], BF16)
    p.dma(w1[:], w1_d.rearrange("(c p) f -> p c f", p=128), writes=["w1"], q="pool")
    p.dma(a1[:], a1_d.rearrange("(c p) f -> p c f", p=128), writes=["a1"], q="pool")
    p.dma(w2[:], w2_d, writes=["w2"], q="pool")
    p.dma(a2[:], a2_d, writes=["a2"], q="pool")
    A_l = make_AB(p, "rl", modv, gvec, 0, 1, 0)
    A_c = make_AB(p, "rc", modv, gvec, 0, 1, 1)
    segs = [(LAT0, LAT1, A_l, lambda c: modv[:, c, 0:1]), (CTX0, CTX1, A_c, lambda c: modv[:, c, 1:2])]
    norm_mod_stream(p, cm, banks, h_d.rearrange("(c p) w -> p c w", p=128), W, segs, a, "a", rstd, sq, hbuf, tmp,
                    ["rl_A", "rc_A"], mask=mask)
    akey = lambda c: ("a", c)
    xs = {nm: p.sb("x_" + nm, [128, KC, 512], BF16) for nm in ("k", "v", "r")}
    tw = p.sb("tw", [RW_LORA, 512], BF16)
    ta = p.sb("ta", [RW_LORA, 512], BF16)
    NT_ = 14
    ft = [p.sb("ft%d" % i, [128, 512], F32) for i in range(NT_)]
    fk = [("ft", i) for i in range(NT_)]
    ob = {nm: p.sb("ob_" + nm, [128, 2, 512], BF16) for nm in ("at", "bt", "kt", "rt", "vt")}
    sqb = p.sb("sqb", [128, 2, 512], BF16)
    gam = p.sb("gamt", [128, KC, 18], F32)
    slabs = {nm: [p.sb("sl_%s%d" % (nm, i), [128, KC, 128], BF16) for i in range(2)] for nm in ("r", "k", "v")}
    widx = {"r": 0, "k": 1, "v": 2}
    wv = wrkv_d.rearrange("n (c p) f -> n p c f", p=128)
    cblocks = [(1, 512, 0), (513, 512, 512), (1027, 128, 1024)]
    nslab = [0]
    for (j0, n, o0) in cblocks:
        for (nm, lw, lkey, lt, ltkey, n_idx, fn) in (("w", w1, "w1", tw, "tw", 1, AF.Tanh), ("a", a1, "a1", ta, "ta", 4, AF.Copy)):
            shift_mix(p, a, akey, xs["r"], "x_r", coef, n_idx, j0, n)
            ps, pk = banks.next()
            for c in range(KC):
                p.op("pe", lambda e: e.matmul(ps[0:RW_LORA, 0:n], lhsT=lw[:, c, :], rhs=xs["r"][:, c, 0:n],
                                              start=(c == 0), stop=(c == KC - 1)),
                     reads=[lkey, ("x_r", c)], writes=[pk])
            p.op("act", lambda e: e.activation(out=lt[:, 0:n], in_=ps[0:RW_LORA, 0:n], func=fn), reads=[pk], writes=[ltkey])
        shift_mix(p, a, akey, xs["k"], "x_k", coef, 2, j0, n)
        shift_mix(p, a, akey, xs["v"], "x_v", coef, 3, j0, n)
        shift_mix(p, a, akey, xs["r"], "x_r", coef, 0, j0, n)
        for m in range(KC):
            cur = {}
            for nm in ("r", "k", "v"):
                i = nslab[0] % 2
                sl = slabs[nm][i]
                sk = ("sl_" + nm, i)
                for k0 in (0, 8):
                    p.dma(sl[:, k0:k0 + 8, :], wv[widx[nm], :, k0:k0 + 8, m * 128:(m + 1) * 128], writes=[sk], q="pool")
                cur[nm] = (sl, sk)
            nslab[0] += 1
            s2 = m % 2
            T_ = lambda i: ft[i][:, 0:n]
            pss = {}
            for nm in ("k", "v", "r"):
                ps, pk = banks.next()
                sl, sk = cur[nm]
                for c in range(KC):
                    p.op("pe", lambda e: e.matmul(ps[:, 0:n], lhsT=sl[:, c, :], rhs=xs[nm][:, c, 0:n],
                                                  start=(c == 0), stop=(c == KC - 1)),
                         reads=[sk, ("x_" + nm, c)], writes=[pk])
                pss[nm] = (ps, pk)
            ps_s, pk_s = banks.next()
            p.op("pe", lambda e: e.matmul(ps_s[:, 0:n], lhsT=w2[:, m * 128:(m + 1) * 128], rhs=tw[:, 0:n], start=True, stop=True),
                 reads=["w2", "tw"], writes=[pk_s])
            ps_a, pk_a = banks.next()
            p.op("pe", lambda e: e.matmul(ps_a[:, 0:n], lhsT=a2[:, m * 128:(m + 1) * 128], rhs=ta[:, 0:n], start=True, stop=True),
                 reads=["a2", "ta"], writes=[pk_a])
            p.op("act", lambda e: e.activation(out=T_(0), in_=pss["k"][0][:, 0:n], func=AF.Copy), reads=[pss["k"][1]], writes=[fk[0]])
            p.op("act", lambda e: e.activation(out=T_(1), in_=pss["v"][0][:, 0:n], func=AF.Copy), reads=[pss["v"][1]], writes=[fk[1]])
            p.op("act", lambda e: e.activation(out=T_(2), in_=pss["r"][0][:, 0:n], func=AF.Copy), reads=[pss["r"][1]], writes=[fk[2]])
            p.op("act", lambda e: e.activation(out=T_(3), in_=ps_s[:, 0:n], func=AF.Sigmoid, bias=vecs[:, 0, m:m + 1], scale=1.0),
                 reads=[pk_s, "vecs"], writes=[fk[3]])
            p.op("act", lambda e: e.activation(out=T_(4), in_=ps_a[:, 0:n], func=AF.Sigmoid, bias=vecs[:, 1, m:m + 1], scale=1.0),
                 reads=[pk_a, "vecs"], writes=[fk[4]])
            p.op("pool", lambda e: e.tensor_copy(out=ob["vt"][:, s2, 0:n], in_=T_(1)), reads=[fk[1]], writes=[("ob_vt", s2)])
            p.op("dve", lambda e: e.tensor_scalar(out=T_(5), in0=T_(0), scalar1=vecs[:, 2, m:m + 1], scalar2=None, op0=ALU.mult),
                 reads=[fk[0], "vecs"], writes=[fk[5]])
            p.op("act", lambda e: e.activation(out=sqb[:, 0, 0:n], in_=T_(5), func=AF.Square), reads=[fk[5]], writes=[("sqb", 0)])
            ps_n, pk_n = banks.next()
            p.op("pe", lambda e: e.matmul(ps_n[:, 0:n], lhsT=bones[:], rhs=sqb[:, 0, 0:n], start=True, stop=True),
                 reads=["bones", ("sqb", 0)], writes=[pk_n])
            p.op("act", lambda e: e.activation(out=T_(6), in_=ps_n[:, 0:n], func=AF.Sqrt), reads=[pk_n], writes=[fk[6]])
            p.op("dve", lambda e: e.tensor_scalar(out=T_(6), in0=T_(6), scalar1=1e-6, scalar2=None, op0=ALU.max),
                 reads=[fk[6]], writes=[fk[6]])
            p.op("dve", lambda e: e.reciprocal(out=T_(6), in_=T_(6)), reads=[fk[6]], writes=[fk[6]])
            p.op("dve", lambda e: e.tensor_tensor(out=T_(5), in0=T_(5), in1=T_(6), op=ALU.mult), reads=[fk[5], fk[6]], writes=[fk[5]])
            p.op("dve", lambda e: e.tensor_scalar(out=T_(3), in0=T_(3), scalar1=-0.6065306597126334, scalar2=None, op0=ALU.mult),
                 reads=[fk[3]], writes=[fk[3]])
            p.op("dve", lambda e: e.tensor_tensor_scan(out=T_(7), data0=rmask[:, 0:n], data1=T_(3), initial=0.0,
                                                       op0=ALU.mult, op1=ALU.add),
                 reads=["rmask", fk[3]], writes=[fk[7]])
            p.op("dve", lambda e: e.tensor_tensor(out=T_(8), in0=T_(7), in1=T_(3), op=ALU.subtract), reads=[fk[7], fk[3]], writes=[fk[8]])
            p.op("act", lambda e: e.activation(out=T_(8), in_=T_(8), func=AF.Exp), reads=[fk[8]], writes=[fk[8]])
            p.op("act", lambda e: e.activation(out=T_(9), in_=T_(7), func=AF.Exp, scale=-1.0), reads=[fk[7]], writes=[fk[9]])
            p.op("act", lambda e: e.activation(out=T_(7), in_=T_(7), func=AF.Exp), reads=[fk[7]], writes=[fk[7]])
            nch = n // C64
            p.op("pool", lambda e: e.tensor_copy(out=gam[:, m, o0 // C64:o0 // C64 + nch],
                                                 in_=ft[7][:, 0:n].rearrange("p (a b) -> p a b", b=C64)[:, :, C64 - 1]),
                 reads=[fk[7]], writes=["gam"])
            p.op("dve", lambda e: e.tensor_scalar(out=T_(10), in0=T_(4), scalar1=-1.0, scalar2=vecs[:, 3, m:m + 1], op0=ALU.add, op1=ALU.mult),
                 reads=[fk[4], "vecs"], writes=[fk[10]])
            p.op("dve", lambda e: e.scalar_tensor_tensor(out=T_(10), in0=T_(10), scalar=1.0, in1=T_(0), op0=ALU.add, op1=ALU.mult),
                 reads=[fk[10], fk[0]], writes=[fk[10]])
            p.op("dve", lambda e: e.scalar_tensor_tensor(out=ob["at"][:, s2, 0:n], in0=T_(5), scalar=-1.0, in1=T_(8), op0=ALU.mult, op1=ALU.mult),
                 reads=[fk[5], fk[8]], writes=[("ob_at", s2)])
            p.op("dve", lambda e: e.tensor_tensor(out=T_(11), in0=T_(5), in1=T_(4), op=ALU.mult), reads=[fk[5], fk[4]], writes=[fk[11]])
            p.op("dve", lambda e: e.tensor_tensor(out=ob["bt"][:, s2, 0:n], in0=T_(11), in1=T_(9), op=ALU.mult),
                 reads=[fk[11], fk[9]], writes=[("ob_bt", s2)])
            p.op("dve", lambda e: e.tensor_tensor(out=ob["kt"][:, s2, 0:n], in0=T_(10), in1=T_(9), op=ALU.mult),
                 reads=[fk[10], fk[9]], writes=[("ob_kt", s2)])
            p.op("dve", lambda e: e.tensor_tensor(out=ob["rt"][:, s2, 0:n], in0=T_(2), in1=T_(7), op=ALU.mult),
                 reads=[fk[2], fk[7]], writes=[("ob_rt", s2)])
            p.op("dve", lambda e: e.scalar_tensor_tensor(out=sqb[:, 1, 0:n], in0=T_(2), scalar=vecs[:, 4, m:m + 1], in1=T_(10), op0=ALU.mult, op1=ALU.mult),
                 reads=[fk[2], fk[10], "vecs"], writes=[("sqb", 1)])
            ps_b, pk_b = banks.next()
            p.op("pe", lambda e: e.matmul(ps_b[:, 0:n], lhsT=bones[:], rhs=sqb[:, 1, 0:n], start=True, stop=True),
                 reads=["bones", ("sqb", 1)], writes=[pk_b])
            i12 = 12 + s2
            p.op("dve", lambda e: e.tensor_tensor(out=T_(i12), in0=ps_b[:, 0:n], in1=T_(1), op=ALU.mult), reads=[pk_b, fk[1]], writes=[fk[i12]])
            for nm in ("at", "bt", "kt", "rt", "vt"):
                ov = outs[nm].rearrange("(c p) w -> p c w", p=128)
                p.dma(ov[:, m, o0:o0 + n], ob[nm][:, s2, 0:n], reads=[("ob_" + nm, s2)])
            bvv = bv_d.rearrange("(c p) w -> p c w", p=128)
            p.dma(bvv[:, m, o0:o0 + n], T_(i12), reads=[fk[i12]])
    p.dma(gam_d.rearrange("(c p) w -> p c w", p=128), gam[:], reads=["gam"])
    return p.build()


NCH = NTOK // C64
NHG = 4


def build_rw2():
    p = Prog()
    far_d = p.dram("far", [NCH, 64, 32 * 2 * 64], BF16)
    fbk_d = p.dram("fbk", [NCH, 64, 2 * 32 * 64], BF16)
    tm_d = p.dram("tm", [NCH, 64, 3 * D], BF16)
    gam_d = p.dram("gam", [64, 32, NCH])
    mka_d = p.dram("mka", [64, 2 * 64])
    mkp_d = p.dram("mkp", [64, 2 * 64])
    mkl_d = p.dram("mkl", [64, 7 * 64])
    y_d = p.dram("y", [NCH, 64, 32 * 64], kind="ExternalOutput")
    banks = Banks(p)
    FAR = [p.sb("far%d" % i, [64, 32, 2, 64], BF16) for i in range(2)]
    FBK = [p.sb("fbk%d" % i, [64, 2, 32, 64], BF16) for i in range(2)]
    TM = [p.sb("tm%d" % i, [64, 3, D], BF16) for i in range(2)]
    gam = load_small(p, "gam", [64, 32, NCH], gam_d)
    mka = load_small(p, "mka", [64, 2, 64], mka_d.rearrange("p (a b) -> p a b", b=64))
    mkp = load_small(p, "mkp", [64, 2, 64], mkp_d.rearrange("p (a b) -> p a b", b=64))
    mkl = load_small(p, "mkl", [64, 7, 64], mkl_d.rearrange("p (a b) -> p a b", b=64))
    S = p.sb("S", [64, 32, 64], F32)
    Sb = p.sb("Sb", [64, 32, 64], BF16)
    Sl = p.sb("Sl", [64, 32, 64], BF16)
    Sd = p.sb("Sd", [64, 32, 64], F32)
    yst = [p.sb("yst%d" % i, [64, 32, 64], F32) for i in range(2)]
    QA = [p.sb("QA%d" % g, [64, 8, 2, 64], BF16) for g in range(NHG)]
    QK = [p.sb("QK%d" % g, [64, 8, 2, 64], BF16) for g in range(NHG)]
    NM = [[p.sb("NM%d_%d" % (g, l), [64, 8, 64], BF16) for l in range(6)] for g in range(NHG)]
    Gf = [p.sb("Gf%d" % g, [64, 8, 64], F32) for g in range(NHG)]
    Hf = [p.sb("Hf%d" % g, [64, 8, 64], F32) for g in range(NHG)]
    Gb = [p.sb("Gb%d" % g, [64, 8, 64], BF16) for g in range(NHG)]
    Hb = [p.sb("Hb%d" % g, [64, 8, 64], BF16) for g in range(NHG)]
    Wb = [p.sb("Wb%d" % g, [64, 8, 64], BF16) for g in range(NHG)]
    Rb = [p.sb("Rb%d" % g, [64, 8, 64], BF16) for g in range(NHG)]
    Xb = [p.sb("Xb%d" % g, [64, 8, 64], BF16) for g in range(NHG)]
    p.op("dve", lambda e: e.memset(S[:], 0.0), writes=[("S", g) for g in range(NHG)])
    p.op("dve", lambda e: e.memset(Sb[:], 0.0), writes=[("Sb", g) for g in range(NHG)])
    p.op("dve", lambda e: e.memset(Sl[:], 0.0), writes=[("Sl", g) for g in range(NHG)])
    v3 = lambda ap: ap.rearrange("p (a b) -> p a b", b=64)
    bc8 = lambda ap2: ap2.unsqueeze(1).broadcast_to([64, 8, 64])
    for ci in range(NCH):
        s = ci % 2
        far, fbk, tm = FAR[s], FBK[s], TM[s]
        p.dma(far[:].rearrange("p a b c -> p (a b c)"), far_d[ci], writes=[("far", s)])
        p.dma(fbk[:].rearrange("p a b c -> p (a b c)"), fbk_d[ci], writes=[("fbk", s)])
        p.dma(tm[:].rearrange("p a b -> p (a b)"), tm_d[ci], writes=[("tm", s)])
        kfar, kfbk, ktm = ("far", s), ("fbk", s), ("tm", s)
        for g in range(NHG):
            ps, pk = banks.next()
            for hh in range(8):
                h = g * 8 + hh
                p.op("pe", lambda e: e.matmul(ps[0:64, hh * 64:(hh + 1) * 64], lhsT=far[:, h, 0, :], rhs=fbk[:, 0, h, :], start=True, stop=True),
                     reads=[kfar, kfbk], writes=[pk])
            p.op("dve", lambda e: e.tensor_tensor(out=Gf[g][:], in0=v3(ps[0:64, :]), in1=bc8(mkp[:, 0, :]), op=ALU.mult),
                 reads=[pk, "mkp"], writes=[("Gf", g)])
            p.op("pool", lambda e: e.tensor_tensor(out=Gf[g][:], in0=Gf[g][:], in1=bc8(mkp[:, 1, :]), op=ALU.add),
                 reads=[("Gf", g), "mkp"], writes=[("Gf", g)])
            p.op("act", lambda e: e.activation(out=Gb[g][:], in_=Gf[g][:], func=AF.Copy), reads=[("Gf", g)], writes=[("Gb", g)])
            for (dst, dkey, which) in ((QA[g], ("QA", g), 0), (QK[g], ("QK", g), 1)):
                for half in range(2):
                    ps, pk = banks.next()
                    for hq in range(4):
                        h = g * 8 + half * 4 + hq
                        p.op("pe", lambda e: e.matmul(ps[0:64, hq * 128:(hq + 1) * 128], lhsT=fbk[:, which, h, :],
                                                      rhs=far[:, h, :, :].rearrange("p a b -> p (a b)"), start=True, stop=True),
                             reads=[kfar, kfbk], writes=[pk])
                    p.op("dve", lambda e: e.tensor_tensor(
                        out=dst[:, half * 4:(half + 1) * 4, :, :],
                        in0=ps[0:64, :].rearrange("p (h a b) -> p h a b", a=2, b=64),
                        in1=mka[:].unsqueeze(1).broadcast_to([64, 4, 2, 64]), op=ALU.mult),
                        reads=[pk, "mka"], writes=[dkey])
            NT0 = QA[g][:, :, 0, :]
            for l in range(6):
                p.op("pool", lambda e: e.tensor_tensor(out=NM[g][l][:], in0=NT0, in1=bc8(mkl[:, l, :]), op=ALU.mult),
                     reads=[("QA", g), "mkl"], writes=[("NM", g, l)])
            p.op("pool", lambda e: e.tensor_tensor(out=Hf[g][:], in0=NM[g][0][:], in1=bc8(mkl[:, 6, :]), op=ALU.add),
                 reads=[("NM", g, 0), "mkl"], writes=[("Hf", g)])
            p.op("act", lambda e: e.activation(out=Hb[g][:], in_=Hf[g][:], func=AF.Copy), reads=[("Hf", g)], writes=[("Hb", g)])
        for g in range(NHG):
            ps, pk = banks.next()
            for hh in range(8):
                h = g * 8 + hh
                o = ps[0:64, hh * 64:(hh + 1) * 64]
                p.op("pe", lambda e: e.matmul(o, lhsT=far[:, h, 0, :], rhs=Sb[:, h, :], start=True, stop=False),
                     reads=[kfar, ("Sb", g)], writes=[pk])
                p.op("pe", lambda e: e.matmul(o, lhsT=far[:, h, 0, :], rhs=Sl[:, h, :], start=False, stop=False),
                     reads=[kfar, ("Sl", g)], writes=[pk])
                p.op("pe", lambda e: e.matmul(o, lhsT=QK[g][:, hh, 0, :], rhs=tm[:, 0, h * 64:(h + 1) * 64], start=False, stop=True),
                     reads=[("QK", g), ktm], writes=[pk])
            p.op("act", lambda e: e.activation(out=Rb[g][:], in_=v3(ps[0:64, :]), func=AF.Copy), reads=[pk], writes=[("Rb", g)])
        for l in range(1, 6):
            for g in range(NHG):
                ps, pk = banks.next()
                for hh in range(8):
                    p.op("pe", lambda e: e.matmul(ps[0:64, hh * 64:(hh + 1) * 64], lhsT=NM[g][l][:, hh, :], rhs=Gb[g][:, hh, :], start=True, stop=True),
                         reads=[("NM", g, l), ("Gb", g)], writes=[pk])
                p.op("act", lambda e: e.activation(out=Wb[g][:], in_=v3(ps[0:64, :]), func=AF.Copy), reads=[pk], writes=[("Wb", g)])
                if l < 5:
                    psz, pkz = banks.next()
                    for hh in range(8):
                        p.op("pe", lambda e: e.matmul(psz[0:64, hh * 64:(hh + 1) * 64], lhsT=Hb[g][:, hh, :], rhs=Wb[g][:, hh, :], start=True, stop=True),
                             reads=[("Hb", g), ("Wb", g)], writes=[pkz])
                ps2, pk2 = banks.next()
                for hh in range(8):
                    p.op("pe", lambda e: e.matmul(ps2[0:64, hh * 64:(hh + 1) * 64], lhsT=Wb[g][:, hh, :], rhs=Hb[g][:, hh, :], start=True, stop=True),
                         reads=[("Hb", g), ("Wb", g)], writes=[pk2])
                if l < 5:
                    p.op("dve", lambda e: e.tensor_tensor(out=Gf[g][:], in0=v3(psz[0:64, :]), in1=Gf[g][:], op=ALU.add),
                         reads=[pkz, ("Gf", g)], writes=[("Gf", g)])
                    p.op("act", lambda e: e.activation(out=Gb[g][:], in_=Gf[g][:], func=AF.Copy), reads=[("Gf", g)], writes=[("Gb", g)])
                p.op("dve", lambda e: e.tensor_tensor(out=Hf[g][:], in0=v3(ps2[0:64, :]), in1=Hf[g][:], op=ALU.add),
                     reads=[pk2, ("Hf", g)], writes=[("Hf", g)])
                p.op("act", lambda e: e.activation(out=Hb[g][:], in_=Hf[g][:], func=AF.Copy), reads=[("Hf", g)], writes=[("Hb", g)])
        for g in range(NHG):
            ps, pk = banks.next()
            for hh in range(8):
                p.op("pe", lambda e: e.matmul(ps[0:64, hh * 64:(hh + 1) * 64], lhsT=Hb[g][:, hh, :], rhs=Rb[g][:, hh, :], start=True, stop=True),
                     reads=[("Hb", g), ("Rb", g)], writes=[pk])
            p.op("act", lambda e: e.activation(out=Xb[g][:], in_=v3(ps[0:64, :]), func=AF.Copy), reads=[pk], writes=[("Xb", g)])
        ys = yst[s]
        for g in range(NHG):
            ps, pk = banks.next()
            for hh in range(8):
                h = g * 8 + hh
                o = ps[0:64, hh * 64:(hh + 1) * 64]
                p.op("pe", lambda e: e.matmul(o, lhsT=Sb[:, h, :], rhs=far[:, h, 1, :], start=True, stop=False),
                     reads=[("Sb", g), kfar], writes=[pk])
                p.op("pe", lambda e: e.matmul(o, lhsT=Sl[:, h, :], rhs=far[:, h, 1, :], start=False, stop=False),
                     reads=[("Sl", g), kfar], writes=[pk])
                p.op("pe", lambda e: e.matmul(o, lhsT=Xb[g][:, hh, :], rhs=QA[g][:, hh, 1, :], start=False, stop=False),
                     reads=[("Xb", g), ("QA", g)], writes=[pk])
                p.op("pe", lambda e: e.matmul(o, lhsT=tm[:, 0, h * 64:(h + 1) * 64], rhs=QK[g][:, hh, 1, :], start=False, stop=True),
                     reads=[ktm, ("QK", g)], writes=[pk])
            p.op("act", lambda e: e.activation(out=ys[:, g * 8:(g + 1) * 8, :], in_=v3(ps[0:64, :]), func=AF.Copy),
                 reads=[pk], writes=[("yst", s)])
        p.dma(y_d[ci], ys[:].rearrange("p a b -> p (a b)"), reads=[("yst", s)], q="act")
        for g in range(NHG):
            ps, pk = banks.next()
            for hh in range(8):
                h = g * 8 + hh
                o = ps[0:64, hh * 64:(hh + 1) * 64]
                p.op("pe", lambda e: e.matmul(o, lhsT=tm[:, 1, h * 64:(h + 1) * 64], rhs=Xb[g][:, hh, :], start=True, stop=False),
                     reads=[ktm, ("Xb", g)], writes=[pk])
                p.op("pe", lambda e: e.matmul(o, lhsT=tm[:, 2, h * 64:(h + 1) * 64], rhs=tm[:, 0, h * 64:(h + 1) * 64], start=False, stop=True),
                     reads=[ktm], writes=[pk])
            Sg = S[:, g * 8:(g + 1) * 8, :]
            Sdg = Sd[:, g * 8:(g + 1) * 8, :]
            p.op("dve", lambda e: e.tensor_tensor(out=Sg, in0=v3(ps[0:64, :]), in1=Sg, op=ALU.add), reads=[pk, ("S", g)], writes=[("S", g)])
            p.op("dve", lambda e: e.tensor_tensor(out=Sg, in0=Sg, in1=gam[:, g * 8:(g + 1) * 8, ci:ci + 1].broadcast_to([64, 8, 64]), op=ALU.mult),
                 reads=[("S", g), "gam"], writes=[("S", g)])
            p.op("act", lambda e: e.activation(out=Sb[:, g * 8:(g + 1) * 8, :], in_=Sg, func=AF.Copy), reads=[("S", g)], writes=[("Sb", g)])
            p.op("pool", lambda e: e.tensor_tensor(out=Sdg, in0=Sg, in1=Sb[:, g * 8:(g + 1) * 8, :], op=ALU.subtract),
                 reads=[("S", g), ("Sb", g)], writes=[("Sd", g)])
            p.op("act", lambda e: e.activation(out=Sl[:, g * 8:(g + 1) * 8, :], in_=Sdg, func=AF.Copy), reads=[("Sd", g)], writes=[("Sl", g)])
    return p.build()


def rw2_masks():
    r = np.arange(64)[:, None]
    f = np.arange(64)[None, :]
    mka = np.stack([(f > r), (f >= r)], 1).astype(np.float32).reshape(64, 128)

    def m_level(l, i, j):
        return ((i >> (l + 1)) == (j >> (l + 1))) & (((i >> l) & 1) == 1) & (((j >> l) & 1) == 0)
    eye = (r == f)
    mkp = np.stack([m_level(0, r, f), eye], 1).astype(np.float32).reshape(64, 128)
    mkl = np.stack([m_level(l, f, r) for l in range(6)] + [eye], 1).astype(np.float32).reshape(64, 7 * 64)
    return mka, mkp, mkl


RW_GN_EPS = 64e-5


def build_rw3():
    p = Prog()
    W = WF
    h_d = p.dram("h", [D, W])
    modv_d = p.dram("modv", [128, 6 * KC, 2])
    g1_d = p.dram("g1", [128, KC])
    mask_d = p.dram("mask", [1, W])
    coef_d = p.dram("coef", [128, 2, 6, KC])
    yin_d = p.dram("yin", [4, D, 1152])
    g1w_d = p.dram("g1w", [D, RW_GATE])
    g2w_d = p.dram("g2w", [RW_GATE, D])
    lnv_d = p.dram("lnv", [128, 2, KC])
    wo_d = p.dram("wo", [D, D])
    bones_d = p.dram("bones", [128, 128])
    y_d = p.dram("y", [D, 1152], kind="ExternalOutput")
    cm = Common(p)
    banks = Banks(p)
    a = p.sb("a", [128, KC, W], BF16)
    hbuf = p.sb("hbuf", [128, 2, W], F32)
    tmp = p.sb("tmp", [128, 2, W], F32)
    sq = p.sb("sq", [128, 2, 512], BF16)
    rstd = p.sb("rstd", [128, W], F32)
    modv = load_small(p, "modv", [128, 6 * KC, 2], modv_d)
    gvec = load_small(p, "gvec", [128, KC], g1_d)
    mask = load_small(p, "mask", [128, W], mask_d[0].partition_broadcast(128))
    coef = load_coef(p, coef_d)
    lnv = load_small(p, "lnv", [128, 2, KC], lnv_d)
    gne = p.sb("gne", [128, 1], F32)
    p.op("dve", lambda e: e.memset(gne[:], RW_GN_EPS), writes=["gne"])
    bones = p.sb("bones", [128, 128], BF16)
    p.dma(bones[:], bones_d, writes=["bones"], q="pool", max_dma_last_dim=512)
    g1w = p.sb("g1w", [128, KC, RW_GATE], BF16)
    g2w = p.sb("g2w", [128, 2, D], BF16)
    p.dma(g1w[:], g1w_d.rearrange("(c p) f -> p c f", p=128), writes=["g1w"], q="pool")
    for c2 in range(2):
        p.dma(g2w[:, c2, :], g2w_d[c2 * 128:(c2 + 1) * 128, :], writes=["g2w"], q="pool")
    A_l = make_AB(p, "rl", modv, gvec, 0, 1, 0)
    A_c = make_AB(p, "rc", modv, gvec, 0, 1, 1)
    segs = [(LAT0, LAT1, A_l, lambda c: modv[:, c, 0:1]), (CTX0, CTX1, A_c, lambda c: modv[:, c, 1:2])]
    norm_mod_stream(p, cm, banks, h_d.rearrange("(c p) w -> p c w", p=128), W, segs, a, "a", rstd, sq, hbuf, tmp,
                    ["rl_A", "rc_A"], mask=mask)
    xg = p.sb("xg", [128, KC, 512], BF16)
    ggb = p.sb("ggb", [128, 2, 512], BF16)
    ob = p.sb("ob", [128, KC, 512], BF16)
    yt = [p.sb("yt%d" % i, [128, 4, 512], F32) for i in range(2)]
    ft = [p.sb("ft%d" % i, [128, 512], F32) for i in range(4)]
    fk = [("ft", i) for i in range(4)]
    sqb = p.sb("sqb", [128, 2, 512], BF16)
    st = p.sb("st", [128, 2, 512], F32)
    slabs = [p.sb("slab%d" % i, [128, KC, 256], BF16) for i in range(2)]
    yinv = yin_d.rearrange("n (c p) w -> p n c w", p=128)
    yov = y_d.rearrange("(c p) w -> p c w", p=128)
    cblocks = [(1, 512, 0), (513, 512, 512), (1027, 128, 1024)]
    cnt = [0]
    for (j0, n, o0) in cblocks:
        shift_mix(p, a, lambda c: ("a", c), xg, "xg", coef, 5, j0, n)
        for c2 in range(2):
            ps, pk = banks.next()
            for c in range(KC):
                p.op("pe", lambda e: e.matmul(ps[:, 0:n], lhsT=g1w[:, c, c2 * 128:(c2 + 1) * 128], rhs=xg[:, c, 0:n],
                                              start=(c == 0), stop=(c == KC - 1)),
                     reads=["g1w", ("xg", c)], writes=[pk])
            p.op("act", lambda e: e.activation(out=ggb[:, c2, 0:n], in_=ps[:, 0:n], func=AF.Sigmoid), reads=[pk], writes=[("ggb", c2)])
        for m in range(KC):
            s2 = m % 2
            y4 = yt[s2]
            p.dma(y4[:, :, 0:n], yinv[:, :, m, o0:o0 + n], writes=[("yt", s2)])
            psg, pkg = banks.next()
            for c2 in range(2):
                p.op("pe", lambda e: e.matmul(psg[:, 0:n], lhsT=g2w[:, c2, m * 128:(m + 1) * 128], rhs=ggb[:, c2, 0:n],
                                              start=(c2 == 0), stop=(c2 == 1)),
                     reads=["g2w", ("ggb", c2)], writes=[pkg])
            T_ = lambda i: ft[i][:, 0:n]
            p.op("dve", lambda e: e.tensor_tensor(out=T_(0), in0=y4[:, 0, 0:n], in1=y4[:, 1, 0:n], op=ALU.add), reads=[("yt", s2)], writes=[fk[0]])
            p.op("act", lambda e: e.activation(out=sqb[:, 0, 0:n], in_=T_(0), func=AF.Copy), reads=[fk[0]], writes=[("sqb", 0)])
            psm, pkm = banks.next()
            p.op("pe", lambda e: e.matmul(psm[:, 0:n], lhsT=bones[:], rhs=sqb[:, 0, 0:n], start=True, stop=True),
                 reads=["bones", ("sqb", 0)], writes=[pkm])
            p.op("dve", lambda e: e.scalar_tensor_tensor(out=T_(1), in0=psm[:, 0:n], scalar=-1.0 / 64, in1=T_(0), op0=ALU.mult, op1=ALU.add),
                 reads=[pkm, fk[0]], writes=[fk[1]])
            p.op("act", lambda e: e.activation(out=sqb[:, 1, 0:n], in_=T_(1), func=AF.Square), reads=[fk[1]], writes=[("sqb", 1)])
            psv, pkv = banks.next()
            p.op("pe", lambda e: e.matmul(psv[:, 0:n], lhsT=bones[:], rhs=sqb[:, 1, 0:n], start=True, stop=True),
                 reads=["bones", ("sqb", 1)], writes=[pkv])
            p.op("act", lambda e: e.activation(out=T_(2), in_=psv[:, 0:n], func=AF.Sqrt, bias=gne[:, 0:1], scale=1.0 / 64),
                 reads=[pkv, "gne"], writes=[fk[2]])
            p.op("dve", lambda e: e.reciprocal(out=T_(2), in_=T_(2)), reads=[fk[2]], writes=[fk[2]])
            p.op("dve", lambda e: e.tensor_tensor(out=T_(1), in0=T_(1), in1=T_(2), op=ALU.mult), reads=[fk[1], fk[2]], writes=[fk[1]])
            p.op("dve", lambda e: e.tensor_scalar(out=T_(1), in0=T_(1), scalar1=lnv[:, 0, m:m + 1], scalar2=lnv[:, 1, m:m + 1],
                                                  op0=ALU.mult, op1=ALU.add),
                 reads=[fk[1], "lnv"], writes=[fk[1]])
            p.op("dve", lambda e: e.tensor_tensor(out=T_(3), in0=y4[:, 2, 0:n], in1=y4[:, 3, 0:n], op=ALU.add), reads=[("yt", s2)], writes=[fk[3]])
            p.op("dve", lambda e: e.tensor_tensor(out=T_(1), in0=T_(1), in1=T_(3), op=ALU.add), reads=[fk[1], fk[3]], writes=[fk[1]])
            p.op("dve", lambda e: e.tensor_tensor(out=ob[:, m, 0:n], in0=psg[:, 0:n], in1=T_(1), op=ALU.mult),
                 reads=[pkg, fk[1]], writes=[("ob", m)])

        def evac_y(m, ps, pk, b0, b1):
            s = cnt[0] % 2
            cnt[0] += 1
            p.op("act", lambda e: e.activation(out=st[:, s, 0:b1 - b0], in_=ps[:, 0:b1 - b0], func=AF.Copy), reads=[pk], writes=[("st", s)])
            p.dma(yov[:, m, o0 + b0:o0 + b1], st[:, s, 0:b1 - b0], reads=[("st", s)], q="act")
        linear_fm2(p, banks, slabs, "slab", ob, "ob", KC, wo_d.rearrange("(c p) f -> p c f", p=128), D, [(0, n)], evac_y)
    return p.build()


_PROGS = {}
_DEBUG = None


def _prog(name, fn, *args):
    key = (name,) + args
    if key not in _PROGS:
        _PROGS[key] = fn(*args)
    return _PROGS[key]


def _run(nc, in_maps):
    res = _bu.run_bass_kernel_spmd(nc, in_maps, core_ids=list(range(8)))
    return res.results


def _f32(x):
    return np.ascontiguousarray(np.asarray(x, dtype=np.float32))


def _slab(hl_b, hc_b, s, halo):
    Dn = hl_b.shape[0]
    wl, wc = 1024 + 2 * halo, 128 + 2 * halo
    out = np.zeros((Dn, wl + wc), hl_b.dtype)
    mask = np.zeros((1, wl + wc), np.float32)
    for (src, n, base, o0) in ((hl_b, 1024, s * 1024, 0), (hc_b, 128, s * 128, wl)):
        lo, hi = base - halo, base + n + halo
        a0, a1 = max(lo, 0), min(hi, src.shape[1])
        out[:, o0 + (a0 - lo):o0 + (a1 - lo)] = src[:, a0:a1]
        mask[0, o0 + (a0 - lo):o0 + (a1 - lo)] = 1.0
    return out, mask


def _core_cols(hl_b, hc_b, s):
    return np.ascontiguousarray(np.concatenate([hl_b[:, s * 1024:(s + 1) * 1024], hc_b[:, s * 128:(s + 1) * 128]], 1))


def _pool_icnt(s):
    out = np.ones((4, WP), np.float32)
    for g, win in enumerate((2, 4, 8, 16)):
        left, right = win // 2, win - 1 - win // 2
        for (n, base, o0, Tn) in ((1024, s * 1024, 8, T), (128, s * 128, 1040 + 8, L)):
            t = np.arange(base, base + n)
            cnt = np.minimum(t + right, Tn - 1) - np.maximum(t - left, 0) + 1
            out[g, o0:o0 + n] = (1.0 / cnt.astype(np.float64)).astype(np.float32)
    return out


def rwkv_mixer(hl, hc, modv3, W3, dbg=None):
    import ml_dtypes
    norm1_g = W3["norm1_g"]; rw_mu = W3["rw_mu"]; rw_w_rkv = W3["rw_w_rkv"]; rw_w0 = W3["rw_w0"]; rw_w1 = W3["rw_w1"]
    rw_w2 = W3["rw_w2"]; rw_a0 = W3["rw_a0"]; rw_a1 = W3["rw_a1"]; rw_a2 = W3["rw_a2"]; rw_g1 = W3["rw_g1"]; rw_g2 = W3["rw_g2"]
    rw_k_k = W3["rw_k_k"]; rw_k_a = W3["rw_k_a"]; rw_r_k = W3["rw_r_k"]; rw_ln_g = W3["rw_ln_g"]; rw_ln_b = W3["rw_ln_b"]
    rw_w_o = W3["rw_w_o"]
    cores = [(b, s) for b in range(NB) for s in range(2)]
    mu = _f32(rw_mu[0])
    bones = np.kron(np.eye(2), np.ones((64, 64))).astype(np.float32)
    rmask = np.ones((1, 512), np.float32)
    rmask[0, ::64] = 0

    def coef_of(mu_prev, mu_next):
        return np.ascontiguousarray(np.stack([
            np.stack([_chunked(mu_prev[n]) for n in range(6)], 1),
            np.stack([_chunked(mu_next[n]) for n in range(6)], 1)], 1))

    bf = ml_dtypes.bfloat16
    mka, mkp, mkl = rw2_masks()
    rw2_in = {}
    bv_nat = {}
    for d in range(2):
        if d == 0:
            hl_d, hc_d = hl, hc
            cf = (mu[0], mu[1])
        else:
            hl_d = [np.ascontiguousarray(hh_[:, ::-1]) for hh_ in hl]
            hc_d = [np.ascontiguousarray(hh_[:, ::-1]) for hh_ in hc]
            cf = (mu[1], mu[0])
        vecs = np.ascontiguousarray(np.stack([_chunked(rw_w0[0][d]), _chunked(rw_a0[0][d]), _chunked(rw_k_k[0]),
                                              _chunked(rw_k_a[0]), _chunked(_f32(rw_r_k[0]).reshape(-1))], 1))
        ims = []
        for (b, s) in cores:
            hs, mk = _slab(hl_d[b], hc_d[b], s, 1)
            ims.append(dict(h=hs, modv=modv3[b], g1=_chunked(norm1_g[3]), mask=mk, coef=coef_of(cf[0], cf[1]),
                            wrkv=_f32(rw_w_rkv[0]), w1=_f32(rw_w1[0][d]), w2=_f32(rw_w2[0][d]), a1=_f32(rw_a1[0][d]),
                            a2=_f32(rw_a2[0][d]), vecs=vecs, bones=bones, rmask=rmask))
        res = _run(_prog("rw1", build_rw1), ims)
        for b in range(NB):
            r0, r1 = res[2 * b], res[2 * b + 1]
            seq = lambda nm: np.concatenate([np.asarray(r0[nm])[:, 1024:], np.asarray(r1[nm])[:, 1024:],
                                             np.asarray(r0[nm])[:, :1024], np.asarray(r1[nm])[:, :1024]], 1)
            at, bt, kt, rt, vt = (seq(nm) for nm in ("at", "bt", "kt", "rt", "vt"))
            hm = lambda z: z.reshape(32, 64, NCH, 64).transpose(2, 1, 0, 3)
            far = np.ascontiguousarray(np.stack([hm(at), hm(rt)], 3).reshape(NCH, 64, -1))
            fbk = np.ascontiguousarray(np.stack([hm(bt), hm(kt)], 2).reshape(NCH, 64, -1))
            tk = lambda z: np.ascontiguousarray(z.T).reshape(NCH, 64, D)
            tm = np.ascontiguousarray(np.stack([tk(vt), tk(bt), tk(kt)], 2).reshape(NCH, 64, -1))
            g0, g1_ = np.asarray(r0["gam"]), np.asarray(r1["gam"])
            gam = np.concatenate([g0[:, 16:18], g1_[:, 16:18], g0[:, 0:16], g1_[:, 0:16]], 1)
            gam = np.ascontiguousarray(gam.reshape(32, 64, NCH).transpose(1, 0, 2))
            rw2_in[(b, d)] = dict(far=far.astype(bf, copy=False), fbk=fbk.astype(bf, copy=False), tm=tm.astype(bf, copy=False),
                                  gam=gam, mka=mka, mkp=mkp, mkl=mkl)
            bvs = np.concatenate([np.asarray(r0["bv"])[:, :1024], np.asarray(r1["bv"])[:, :1024]], 1)
            bvc = np.concatenate([np.asarray(r0["bv"])[:, 1024:], np.asarray(r1["bv"])[:, 1024:]], 1)
            if d == 1:
                bvs, bvc = bvs[:, ::-1], bvc[:, ::-1]
            bv_nat[(b, d)] = (np.ascontiguousarray(bvs), np.ascontiguousarray(bvc))
    res = _run(_prog("rw2", build_rw2), [rw2_in[(b, d)] for b in range(NB) for d in range(2)])
    y_nat = {}
    for b in range(NB):
        for d in range(2):
            y = res[2 * b + d]["y"].reshape(NCH, 64, 32, 64).transpose(2, 1, 0, 3).reshape(D, NTOK)
            yc_, yl_ = y[:, :L], y[:, L:]
            if d == 1:
                yc_, yl_ = yc_[:, ::-1], yl_[:, ::-1]
            y_nat[(b, d)] = (np.ascontiguousarray(yl_), np.ascontiguousarray(yc_))
    ims = []
    coef_n = coef_of(mu[0], mu[1])
    lnv = np.ascontiguousarray(np.stack([_chunked(rw_ln_g[0]), _chunked(rw_ln_b[0])], 1))
    for (b, s) in cores:
        hs, mk = _slab(hl[b], hc[b], s, 1)
        yin = np.stack([_core_cols(y_nat[(b, 0)][0], y_nat[(b, 0)][1], s), _core_cols(y_nat[(b, 1)][0], y_nat[(b, 1)][1], s),
                        _core_cols(bv_nat[(b, 0)][0], bv_nat[(b, 0)][1], s), _core_cols(bv_nat[(b, 1)][0], bv_nat[(b, 1)][1], s)], 0)
        ims.append(dict(h=hs, modv=modv3[b], g1=_chunked(norm1_g[3]), mask=mk, coef=coef_n, yin=np.ascontiguousarray(yin),
                        g1w=_f32(rw_g1[0]), g2w=_f32(rw_g2[0]), lnv=lnv, wo=_f32(rw_w_o[0]), bones=bones))
    res = _run(_prog("rw3", build_rw3), ims)
    if dbg is not None:
        dbg["rw2_in"] = rw2_in
        dbg["y_nat"] = y_nat
        dbg["bv_nat"] = bv_nat
    return res


def kernel(x, c, ctx, c_ctx, norm1_g, norm2_g, w_mod, b_mod, ffn_w_gate, ffn_w_up, ffn_conv_w, ffn_conv_b,
           ffn_w_down, final_norm_g, pool_w, pool_b, pool_scale, na_w_qkv, na_rpb, na_w_o, sg_w_in, sg_b_in,
           sg_norm_g, sg_w_s, sg_b_s, sg_w_o, rw_mu, rw_w_rkv, rw_w0, rw_w1, rw_w2, rw_a0, rw_a1, rw_a2, rw_g1,
           rw_g2, rw_k_k, rw_k_a, rw_r_k, rw_ln_g, rw_ln_b, rw_w_o):
    import ml_dtypes
    x, ctx = _f32(x), _f32(ctx)
    hl = [np.ascontiguousarray(x[b].T) for b in range(NB)]
    hc = [np.ascontiguousarray(ctx[b].T) for b in range(NB)]
    cores = [(b, s) for b in range(NB) for s in range(2)]

    cc = np.zeros((8, D), np.float32)
    cc[0:4] = _f32(c)
    cc[4] = _f32(c_ctx)
    ct = np.ascontiguousarray(cc.T.reshape(KC, 128, 8).transpose(1, 0, 2))
    w_mod = np.asarray(w_mod, np.float32)
    ims = []
    for core in range(8):
        i, half = core // 2, core % 2
        ims.append(dict(wm=np.ascontiguousarray(w_mod[i][:, half * 6144:(half + 1) * 6144]),
                        bm=_chunked(np.asarray(b_mod[i], np.float32)[half * 6144:(half + 1) * 6144]), ct=ct))
    res = _run(_prog("mod", build_mod), ims)
    modv = {}
    for i in range(4):
        mo = np.concatenate([res[2 * i]["mo"], res[2 * i + 1]["mo"]], 1)
        for b in range(NB):
            modv[(i, b)] = np.ascontiguousarray(mo[:, :, [b, 4]])

    def run_ffn(i, ys_l, ys_c, final):
        wg, wu, wd = _f32(ffn_w_gate[i]), _f32(ffn_w_up[i]), _f32(ffn_w_down[i])
        cw = np.ascontiguousarray(_f32(ffn_conv_w[i]).reshape(3, FC, 128).transpose(2, 1, 0))
        cb = _chunked(ffn_conv_b[i])
        g2 = _chunked(norm2_g[i])
        ims = []
        for (b, s) in cores:
            hs, mk = _slab(hl[b], hc[b], s, 1)
            y = np.zeros((2, D, WF), np.float32)
            for n in range(len(ys_l)):
                y[n] = _slab(ys_l[n][b], ys_c[n][b], s, 1)[0]
            im = dict(h=hs, y=y, modv=modv[(i, b)], g2=g2, mask=np.ascontiguousarray(np.repeat(mk, 128, 0)),
                      wg=wg, wu=wu, wd=wd, cw=cw, cb=cb)
            if final:
                im["gf"] = _chunked(final_norm_g)
            ims.append(im)
        res = _run(_prog("ffn", build_ffn, final), ims)
        out = None
        if final:
            out = np.empty((NB, T, D), np.float32)
        for ci, (b, s) in enumerate(cores):
            ho = res[ci]["ho"]
            hl[b][:, s * 1024:(s + 1) * 1024] = ho[:, 0:1024]
            hc[b][:, s * 128:(s + 1) * 128] = ho[:, 1024:1152]
            if final:
                out[b, s * 1024:(s + 1) * 1024, :] = res[ci]["of"].T
        return out

    def split_y(res_y):
        yl = [np.empty((D, T), np.float32) for _ in range(NB)]
        yc = [np.empty((D, L), np.float32) for _ in range(NB)]
        for ci, (b, s) in enumerate(cores):
            yl[b][:, s * 1024:(s + 1) * 1024] = res_y[ci][:, 0:1024]
            yc[b][:, s * 128:(s + 1) * 128] = res_y[ci][:, 1024:1152]
        return yl, yc

    i = 0
    ims = []
    for (b, s) in cores:
        hs, mk = _slab(hl[b], hc[b], s, 8)
        ims.append(dict(h=hs, modv=modv[(i, b)], g1=_chunked(norm1_g[i]), mask=mk, icnt=_pool_icnt(s),
                        pw=_f32(pool_w[0]), pb=_chunked(pool_b[0]), psc=_chunked(pool_scale[0])))
    res = _run(_prog("pool", build_pool), ims)
    yl, yc = split_y([r["y"] for r in res])
    run_ffn(i, [yl], [yc], False)
    if _DEBUG is not None:
        _DEBUG.append(([a.copy() for a in hl], [a.copy() for a in hc]))

    i = 1
    pm = rope_perm()
    ims = []
    for (b, s) in cores:
        rc, rs = rope_tables(s * 1024, 1024)
        ims.append(dict(h=_core_cols(hl[b], hc[b], s), modv=modv[(i, b)], g1=_chunked(norm1_g[i]), wqkv=_f32(na_w_qkv[0]),
                        rcos=rc, rsin=rs, pm=pm))
    res = _run(_prog("na1", build_na1), ims)
    Q, Kt, V = [], [], []
    for b in range(NB):
        r0, r1 = res[2 * b], res[2 * b + 1]
        Q.append(np.concatenate([r0["qT"][:, :1024], r1["qT"][:, :1024], r0["qT"][:, 1024:], r1["qT"][:, 1024:]], 1))
        Kt.append(np.concatenate([r0["kT"][:, :1024], r1["kT"][:, :1024], r0["kT"][:, 1024:], r1["kT"][:, 1024:]], 1))
        V.append(np.concatenate([r0["v"][:1024], r1["v"][:1024], r0["v"][1024:], r1["v"][1024:]], 0))
    rpb = _f32(na_rpb[0])
    wo = _f32(na_w_o[0])
    ims = []
    for b in range(NB):
        for hh in range(2):
            sl = slice(hh * 1024, (hh + 1) * 1024)
            ims.append(dict(qT=np.ascontiguousarray(Q[b][sl]), kT=np.ascontiguousarray(Kt[b][sl]),
                            v=np.ascontiguousarray(V[b][:, sl]), bt=na_bias_tables(rpb, list(range(hh * 16, hh * 16 + 16))),
                            wo=np.ascontiguousarray(wo[sl])))
    res = _run(_prog("na2", build_na2), ims)
    yls, ycs = [], []
    for hh in range(2):
        yls.append([np.ascontiguousarray(res[2 * b + hh]["y"][:, :T]) for b in range(NB)])
        ycs.append([np.ascontiguousarray(res[2 * b + hh]["y"][:, T:]) for b in range(NB)])
    run_ffn(i, yls, ycs, False)
    if _DEBUG is not None:
        _DEBUG.append(([a.copy() for a in hl], [a.copy() for a in hc]))

    i = 2
    b_in = _f32(sg_b_in[0])
    ims = []
    for (b, s) in cores:
        ims.append(dict(h=_core_cols(hl[b], hc[b], s), modv=modv[(i, b)], g1=_chunked(norm1_g[i]), win=_f32(sg_w_in[0]),
                        bzu=_chunked(b_in[:D]), bzv=np.ascontiguousarray(b_in[None, D:]), ng=_f32(sg_norm_g[0])[None].copy(),
                        wst=np.ascontiguousarray(_f32(sg_w_s[0]).transpose(2, 0, 1)), bs=_f32(sg_b_s[0]).reshape(1, -1).copy(),
                        wo=_f32(sg_w_o[0])))
    res = _run(_prog("gmlp", build_gmlp), ims)
    yl, yc = split_y([r["y"] for r in res])
    run_ffn(i, [yl], [yc], False)
    if _DEBUG is not None:
        _DEBUG.append(([a.copy() for a in hl], [a.copy() for a in hc]))

    i = 3
    W3 = dict(norm1_g=norm1_g, rw_mu=rw_mu, rw_w_rkv=rw_w_rkv, rw_w0=rw_w0, rw_w1=rw_w1, rw_w2=rw_w2, rw_a0=rw_a0, rw_a1=rw_a1,
              rw_a2=rw_a2, rw_g1=rw_g1, rw_g2=rw_g2, rw_k_k=rw_k_k, rw_k_a=rw_k_a, rw_r_k=rw_r_k, rw_ln_g=rw_ln_g,
              rw_ln_b=rw_ln_b, rw_w_o=rw_w_o)
    res = rwkv_mixer(hl, hc, {b: modv[(i, b)] for b in range(NB)}, W3)
    yl, yc = split_y([r["y"] for r in res])
    return run_ffn(i, [yl], [yc], True)
```

```python
import contextlib
import os
import numpy as np
import concourse.bass as bass
import concourse.mybir as mybir

F32 = mybir.dt.float32
BF16 = mybir.dt.bfloat16
AF = mybir.ActivationFunctionType
ALU = mybir.AluOpType
AX = mybir.AxisListType

ENGS = ("sync", "pe", "dve", "act", "pool")
SAME_ENGINE_SYNC = True


class Prog:
    EPOCH = 16000
    NDMA = 24

    def __init__(self):
        self.nc = bass.Bass("TRN2", target_bir_lowering=False)
        self.stack = contextlib.ExitStack()
        self.ops = {e: [] for e in ENGS}
        self.cnt = {e: 0 for e in ENGS}
        self.last_w = {}
        self.readers = {}
        self.waited = {e: {} for e in ENGS}
        self.sems = {}
        self.dma_n = 0
        self.dma_tot = [0] * self.NDMA
        self.n_uid = 0

    def dram(self, name, shape, dt=F32, kind="ExternalInput"):
        return self.nc.dram_tensor(name, list(shape), dt, kind=kind).ap()

    def sb(self, name, shape, dt=F32):
        return self.stack.enter_context(self.nc.sbuf_tensor("sb_" + name, list(shape), dt))

    def ps(self, name, shape, dt=F32):
        return self.stack.enter_context(self.nc.psum_tensor("pp_" + name, list(shape), dt))

    def _sem(self, key):
        if key not in self.sems:
            self.sems[key] = self.stack.enter_context(
                self.nc.semaphore("s_%s_%s" % (key[0], key[1])))
        return self.sems[key]

    def _deps(self, eng, reads, writes):
        toks = []
        for k in reads:
            w = self.last_w.get(k)
            if w is not None:
                toks.append(w)
        for k in writes:
            w = self.last_w.get(k)
            if w is not None:
                toks.append(w)
            toks.extend(self.readers.get(k, ()))
        need = {}
        for t in toks:
            if t[0] == "eng":
                _, e, seq = t
                if e == eng and (eng == "pe" or not SAME_ENGINE_SYNC):
                    continue
                sk = (e, seq // self.EPOCH)
                v = seq % self.EPOCH + 1
            else:
                _, s, v = t
                sk = ("dma", s)
            if need.get(sk, 0) < v:
                need[sk] = v
        out = []
        wd = self.waited[eng]
        for sk, v in need.items():
            if wd.get(sk, 0) >= v:
                continue
            wd[sk] = v
            out.append((sk, v))
        return out

    def _commit(self, tok, reads, writes):
        for k in reads:
            self.readers.setdefault(k, []).append(tok)
        for k in writes:
            self.last_w[k] = tok
            self.readers[k] = []

    @staticmethod
    def _psum_excl(reads, writes):
        r2, w2 = [], list(writes)
        for k in reads:
            if isinstance(k, tuple) and k and k[0] == "ps":
                w2.append(k)
            else:
                r2.append(k)
        return r2, w2

    def op(self, eng, fn, reads=(), writes=()):
        reads, writes = self._psum_excl(reads, writes)
        waits = self._deps(eng, reads, writes)
        seq = self.cnt[eng]
        self.cnt[eng] += 1
        tok = ("eng", eng, seq)
        self._emit(eng, waits, fn, ((eng, seq // self.EPOCH), 1))
        self._commit(tok, reads, writes)
        return tok

    def dma(self, out, in_, reads=(), writes=(), q="sync", **kw):
        if q == "pool" and "max_dma_last_dim" not in kw:
            kw["max_dma_last_dim"] = 2048
        s = self.dma_n % self.NDMA
        self.dma_n += 1
        waits = self._deps(q, reads, writes)
        prev = self.dma_tot[s]
        sk = ("dma", s)
        if prev and self.waited[q].get(sk, 0) < prev:
            self.waited[q][sk] = prev
            waits.append((sk, prev))
        self.dma_tot[s] = prev + 16
        tok = ("dma", s, prev + 16)
        fn = lambda e, out=out, in_=in_, kw=kw: e.dma_start(out=out, in_=in_, **kw)
        self._emit(q, waits, fn, (sk, 16))
        self._commit(tok, reads, writes)
        return tok

    def _emit(self, name, waits, fn, inc):
        nc = self.nc
        eng = {"sync": nc.sync, "pe": nc.tensor, "dve": nc.vector, "act": nc.scalar, "pool": nc.gpsimd}[name]
        for sk, v in waits:
            eng.wait_ge(self._sem(sk), v)
        fn(eng).then_inc(self._sem(inc[0]), inc[1])
        self.nops = getattr(self, "nops", 0) + 1 + len(waits)
        if not hasattr(self, "trace"):
            self.trace = {e: [] for e in ENGS}
        self.trace[name].append((list(waits), inc))

    def check_deadlock(self):
        tr = getattr(self, "trace", None)
        if tr is None:
            return
        val = {}
        pos = {e: 0 for e in ENGS}
        progress = True
        while progress:
            progress = False
            for e in ENGS:
                q = tr[e]
                while pos[e] < len(q):
                    waits, inc = q[pos[e]]
                    if all(val.get(sk, 0) >= v for sk, v in waits):
                        val[inc[0]] = val.get(inc[0], 0) + inc[1]
                        pos[e] += 1
                        progress = True
                    else:
                        break
        stuck = {e: (pos[e], len(tr[e])) for e in ENGS if pos[e] < len(tr[e])}
        if stuck:
            msg = []
            for e, (i, n) in stuck.items():
                waits, inc = tr[e][i]
                msg.append("%s stuck at %d/%d waiting %s (have %s)" % (
                    e, i, n, waits, [(sk, val.get(sk, 0)) for sk, v in waits]))
            raise RuntimeError("semaphore deadlock: " + "; ".join(msg))

    def build(self):
        nc = self.nc
        self.check_deadlock()
        for s in range(self.NDMA):
            if self.dma_tot[s]:
                nc.sync.wait_ge(self._sem(("dma", s)), self.dma_tot[s])
        for e in ENGS:
            if e != "sync" and self.cnt[e]:
                seq = self.cnt[e] - 1
                nc.sync.wait_ge(self._sem((e, seq // self.EPOCH)), seq % self.EPOCH + 1)
        self.stack.close()
        return nc


import concourse.bass_utils as _bu

D = 2048
KC = 16
T = 2048
L = 256
NB = 4
FF = 5504
FC = 43
EPS = 1e-6


class Banks:
    def __init__(self, p, n=8, prefix="psb"):
        self.t = [p.ps("%s%d" % (prefix, i), [128, 512], F32) for i in range(n)]
        self.keys = [("ps", prefix, i) for i in range(n)]
        self.i = 0

    def next(self):
        b = self.i % len(self.t)
        self.i += 1
        return self.t[b], self.keys[b]


def col_blocks(c0, c1, n=512):
    out = []
    while c0 < c1:
        out.append((c0, min(c1, c0 + n)))
        c0 += n
    return out


class Common:
    def __init__(self, p):
        self.ones = p.sb("c_ones", [128, 128], BF16)
        self.eps = p.sb("c_eps", [128, 1], F32)
        p.op("dve", lambda e: e.memset(self.ones[:], 1.0), writes=["c_ones"])
        p.op("dve", lambda e: e.memset(self.eps[:], EPS), writes=["c_eps"])


def load_small(p, name, shape, dram_ap, dt=F32):
    t = p.sb(name, shape, dt)
    p.dma(t[:], dram_ap, writes=[name])
    return t


def make_AB(p, name, modv, g, j_shift, j_scale, col):
    A = p.sb(name + "_A", [128, KC], F32)
    p.op("dve", lambda e: e.tensor_scalar(out=A[:], in0=modv[:, j_scale * KC:(j_scale + 1) * KC, col],
                                          scalar1=1.0, scalar2=None, op0=ALU.add),
         reads=["modv"], writes=[name + "_A"])
    p.op("dve", lambda e: e.tensor_tensor(out=A[:], in0=A[:], in1=g[:], op=ALU.mult),
         reads=[name + "_A", "gvec"], writes=[name + "_A"])
    return A


def rms_rstd(p, cm, banks, h, hkey, W, rstd, rkey, sq, nfeat_chunks=KC, dmodel=D, eps_ap=None):
    for (b0, b1) in col_blocks(0, W):
        ps, pk = banks.next()
        n = b1 - b0
        for c in range(nfeat_chunks):
            s = c % 2
            p.op("act", lambda e, c=c, s=s: e.activation(out=sq[:, s, 0:n], in_=h[:, c, b0:b1], func=AF.Square),
                 reads=[hkey], writes=[("sq", s)])
            p.op("pe", lambda e, c=c, s=s: e.matmul(ps[:, 0:n], lhsT=cm.ones[:], rhs=sq[:, s, 0:n],
                                                     start=(c == 0), stop=(c == nfeat_chunks - 1)),
                 reads=[("sq", s), "c_ones"], writes=[pk])
        p.op("act", lambda e: e.activation(out=rstd[:, b0:b1], in_=ps[:, 0:n], func=AF.Sqrt,
                                           bias=(eps_ap if eps_ap is not None else cm.eps)[:, 0:1], scale=1.0 / dmodel),
             reads=[pk, "c_eps"], writes=[rkey])
        p.op("dve", lambda e: e.reciprocal(out=rstd[:, b0:b1], in_=rstd[:, b0:b1]), reads=[rkey], writes=[rkey])


def norm_mod(p, cm, banks, h, hkey, W, segs, out_bf, okey, rstd, sq, tmp, mask=None):
    rms_rstd(p, cm, banks, h, hkey, W, rstd, "rstd", sq)
    for c in range(KC):
        s = c % 2
        p.op("dve", lambda e, c=c, s=s: e.tensor_tensor(out=tmp[:, s, 0:W], in0=h[:, c, 0:W], in1=rstd[:, 0:W], op=ALU.mult),
             reads=[hkey, "rstd"], writes=[("tmp", s)])
        for (c0, c1, A, Bfn) in segs:
            p.op("act", lambda e, c=c, s=s, c0=c0, c1=c1, A=A, Bfn=Bfn: e.activation(
                out=out_bf[:, c, c0:c1], in_=tmp[:, s, c0:c1], func=AF.Identity, scale=A[:, c:c + 1], bias=Bfn(c)),
                reads=[("tmp", s), "AB", "modv"], writes=[(okey, c)])
        if mask is not None:
            p.op("pool", lambda e, c=c: e.tensor_tensor(out=out_bf[:, c, 0:W], in0=out_bf[:, c, 0:W], in1=mask[:, 0:W], op=ALU.mult),
                 reads=[(okey, c), "mask"], writes=[(okey, c)])


WF = 1156
LAT0, LAT1 = 0, 1026
CTX0, CTX1 = 1026, 1156
FG = 4


def build_ffn(final):
    p = Prog()
    h_d = p.dram("h", [D, WF])
    y_d = p.dram("y", [2, D, WF])
    modv_d = p.dram("modv", [128, 6 * KC, 2])
    g2_d = p.dram("g2", [128, KC])
    mask_d = p.dram("mask", [128, WF])
    wg_d = p.dram("wg", [D, FF])
    wu_d = p.dram("wu", [D, FF])
    wd_d = p.dram("wd", [FF, D])
    cw_d = p.dram("cw", [128, FC, 3])
    cb_d = p.dram("cb", [128, FC])
    ho_d = p.dram("ho", [D, 1152], kind="ExternalOutput")
    if final:
        gf_d = p.dram("gf", [128, KC])
        of_d = p.dram("of", [D, 1024], kind="ExternalOutput")

    cm = Common(p)
    banks = Banks(p)
    h = p.sb("h", [128, KC, WF], F32)
    u = p.sb("u", [128, KC, WF], BF16)
    act = p.sb("actb", [128, FG, WF], BF16)
    gsb = p.sb("gsb", [128, 2, WF], F32)
    tsb = p.sb("tsb", [128, 2, WF], F32)
    sq = p.sb("sq", [128, 2, 512], BF16)
    rstd = p.sb("rstd", [128, WF], F32)
    modv = load_small(p, "modv", [128, 6 * KC, 2], modv_d)
    gvec = load_small(p, "gvec", [128, KC], g2_d)
    mask = load_small(p, "mask", [128, WF], mask_d)
    cw = load_small(p, "cw", [128, FC, 3], cw_d)
    cb = load_small(p, "cb", [128, FC], cb_d)
    gus = [p.sb("gus%d" % i, [128, KC, 256], BF16) for i in range(4)]
    evb = p.sb("evb", [128, 2, 512], F32)
    nev = [0]
    wds = p.sb("wds", [128, FG, D], BF16)

    hv = h_d.rearrange("(c p) w -> p c w", p=128)
    yv = y_d.rearrange("n (c p) w -> n p c w", p=128)
    for c4 in range(0, KC, 4):
        p.dma(h[:, c4:c4 + 4, :], hv[:, c4:c4 + 4, :], writes=[("h", c) for c in range(c4, c4 + 4)])
    for c in range(KC):
        for n in range(2):
            s = (c * 2 + n) % 2
            p.dma(gsb[:, s, :], yv[n, :, c, :], writes=[("gsb", s)])
            for (c0, c1, col) in ((LAT0, LAT1, 0), (CTX0, CTX1, 1)):
                p.op("dve", lambda e, c=c, s=s, c0=c0, c1=c1, col=col: e.scalar_tensor_tensor(
                    out=h[:, c, c0:c1], in0=gsb[:, s, c0:c1], scalar=modv[:, 2 * KC + c, col:col + 1],
                    in1=h[:, c, c0:c1], op0=ALU.mult, op1=ALU.add),
                    reads=[("gsb", s), "modv", ("h", c)], writes=[("h", c)])
    A_l = make_AB(p, "ffl", modv, gvec, 3, 4, 0)
    A_c = make_AB(p, "ffc", modv, gvec, 3, 4, 1)
    hkeys = [("h", c) for c in range(KC)]

    class HK:
        pass
    segs = [(LAT0, LAT1, A_l, lambda c: modv[:, 3 * KC + c, 0:1]), (CTX0, CTX1, A_c, lambda c: modv[:, 3 * KC + c, 1:2])]
    _norm_mod_chunked(p, cm, banks, h, W=WF, segs=segs, out_bf=u, okey="u", rstd=rstd, sq=sq, tmp=tsb, mask=mask,
                      keys_A=["ffl_A", "ffc_A"])

    wgv = wg_d.rearrange("(c p) f -> p c f", p=128)
    wuv = wu_d.rearrange("(c p) f -> p c f", p=128)
    wdv = wd_d.rearrange("(f p) d -> p f d", p=128)
    blocks = col_blocks(0, WF)
    dblocks = [(1, 513, 0), (513, 1025, 0), (1025, 1155, 1)]
    nslab = 0
    slab_of = {}
    ukeys = [("u", c) for c in range(KC)]

    def ensure_slab(fo):
        nonlocal nslab
        sidx = fo // 2
        if sidx in slab_of:
            return slab_of[sidx]
        i0 = (nslab % 2) * 2
        nslab += 1
        f0 = sidx * 256
        f1 = min(FF, f0 + 256)
        p.dma(gus[i0][:, :, 0:f1 - f0], wgv[:, :, f0:f1], writes=[("gus", i0)], q="pool")
        p.dma(gus[i0 + 1][:, :, 0:f1 - f0], wuv[:, :, f0:f1], writes=[("gus", i0 + 1)], q="pool")
        slab_of[sidx] = i0
        return i0

    ngroups = (FC + FG - 1) // FG
    for g in range(ngroups):
        fos = list(range(g * FG, min(FC, (g + 1) * FG)))
        for li_ in range(len(fos)):
            p.dma(wds[:, li_, :], wdv[:, fos[0] + li_, :], writes=["wds"], q="pool")
        for li, fo in enumerate(fos):
            i0 = ensure_slab(fo)
            off = (fo % 2) * 128
            s = fo % 2
            gps = []
            for (b0, b1) in blocks:
                ps, pk = banks.next()
                for c in range(KC):
                    p.op("pe", lambda e, ps=ps, c=c, b0=b0, b1=b1: e.matmul(
                        ps[:, 0:b1 - b0], lhsT=gus[i0][:, c, off:off + 128], rhs=u[:, c, b0:b1],
                        start=(c == 0), stop=(c == KC - 1)),
                        reads=[("gus", i0), ("u", c)], writes=[pk])
                p.op("act", lambda e, ps=ps, b0=b0, b1=b1: e.activation(out=gsb[:, s, b0:b1], in_=ps[:, 0:b1 - b0], func=AF.Copy),
                     reads=[pk], writes=[("gsb", s)])
            ups = []
            for (b0, b1) in blocks:
                ps, pk = banks.next()
                for c in range(KC):
                    p.op("pe", lambda e, ps=ps, c=c, b0=b0, b1=b1: e.matmul(
                        ps[:, 0:b1 - b0], lhsT=gus[i0 + 1][:, c, off:off + 128], rhs=u[:, c, b0:b1],
                        start=(c == 0), stop=(c == KC - 1)),
                        reads=[("gus", i0 + 1), ("u", c)], writes=[pk])
                ups.append((ps, pk, b0, b1))
            n = WF - 2
            p.op("dve", lambda e: e.tensor_scalar(out=tsb[:, s, 1:1 + n], in0=gsb[:, s, 0:n], scalar1=cw[:, fo, 0:1],
                                                  scalar2=None, op0=ALU.mult),
                 reads=[("gsb", s), "cw"], writes=[("tmp", s)])
            for k in (1, 2):
                p.op("dve", lambda e, k=k: e.scalar_tensor_tensor(out=tsb[:, s, 1:1 + n], in0=gsb[:, s, k:k + n],
                                                                  scalar=cw[:, fo, k:k + 1], in1=tsb[:, s, 1:1 + n],
                                                                  op0=ALU.mult, op1=ALU.add),
                     reads=[("gsb", s), "cw", ("tmp", s)], writes=[("tmp", s)])
            p.op("act", lambda e: e.activation(out=tsb[:, s, 1:1 + n], in_=tsb[:, s, 1:1 + n], func=AF.Silu,
                                               bias=cb[:, fo:fo + 1], scale=1.0),
                 reads=[("tmp", s), "cb"], writes=[("tmp", s)])
            for (ps, pk, b0, b1) in ups:
                a0 = max(b0, 1)
                a1 = min(b1, WF - 1)
                p.op("dve", lambda e, ps=ps, b0=b0, a0=a0, a1=a1: e.tensor_tensor(
                    out=act[:, li, a0:a1], in0=tsb[:, s, a0:a1], in1=ps[:, a0 - b0:a1 - b0], op=ALU.mult),
                    reads=[("tmp", s), pk], writes=[("act", li)])
        for m in range(KC):
            for (d0, d1, col) in dblocks:
                ps, pk = banks.next()
                for li in range(len(fos)):
                    p.op("pe", lambda e, ps=ps, li=li, d0=d0, d1=d1: e.matmul(
                        ps[:, 0:d1 - d0], lhsT=wds[:, li, m * 128:(m + 1) * 128], rhs=act[:, li, d0:d1],
                        start=(li == 0), stop=(li == len(fos) - 1)),
                        reads=["wds", ("act", li)], writes=[pk])
                if os.environ.get("FFN_EVAC", "dve") == "split":
                    es = nev[0] % 2
                    nev[0] += 1
                    p.op("act", lambda e: e.activation(out=evb[:, es, 0:d1 - d0], in_=ps[:, 0:d1 - d0], func=AF.Copy,
                                                       scale=modv[:, 5 * KC + m, col:col + 1]),
                         reads=[pk, "modv"], writes=[("evb", es)])
                    p.op("pool", lambda e: e.tensor_tensor(out=h[:, m, d0:d1], in0=h[:, m, d0:d1], in1=evb[:, es, 0:d1 - d0], op=ALU.add),
                         reads=[("evb", es), ("h", m)], writes=[("h", m)])
                else:
                    p.op("dve", lambda e, ps=ps, d0=d0, d1=d1, col=col: e.scalar_tensor_tensor(
                        out=h[:, m, d0:d1], in0=ps[:, 0:d1 - d0], scalar=modv[:, 5 * KC + m, col:col + 1],
                        in1=h[:, m, d0:d1], op0=ALU.mult, op1=ALU.add),
                        reads=[pk, "modv", ("h", m)], writes=[("h", m)])
    hov = ho_d.rearrange("(c p) w -> p c w", p=128)
    for c4 in range(0, KC, 4):
        ks = [("h", c) for c in range(c4, c4 + 4)]
        p.dma(hov[:, c4:c4 + 4, 0:1024], h[:, c4:c4 + 4, 1:1025], reads=ks)
        p.dma(hov[:, c4:c4 + 4, 1024:1152], h[:, c4:c4 + 4, 1027:1155], reads=ks)
    if final:
        gf = load_small(p, "gf", [128, KC], gf_d)
        rms_rstd_chunked(p, cm, banks, h, 1, 1025, rstd, sq)
        ofv = of_d.rearrange("(c p) w -> p c w", p=128)
        for c in range(KC):
            s = c % 2
            p.op("dve", lambda e, c=c, s=s: e.tensor_tensor(out=tsb[:, s, 1:1025], in0=h[:, c, 1:1025], in1=rstd[:, 1:1025], op=ALU.mult),
                 reads=[("h", c), "rstd"], writes=[("tmp", s)])
            p.op("act", lambda e, c=c, s=s: e.activation(out=tsb[:, s, 1:1025], in_=tsb[:, s, 1:1025], func=AF.Copy, scale=gf[:, c:c + 1]),
                 reads=[("tmp", s), "gf"], writes=[("tmp", s)])
            p.dma(ofv[:, c, :], tsb[:, s, 1:1025], reads=[("tmp", s)])
    return p.build()


def rms_rstd_chunked(p, cm, banks, h, w0, w1, rstd, sq, hname="h"):
    for (b0, b1) in col_blocks(w0, w1):
        ps, pk = banks.next()
        n = b1 - b0
        for c in range(KC):
            s = c % 2
            p.op("act", lambda e, c=c, s=s: e.activation(out=sq[:, s, 0:n], in_=h[:, c, b0:b1], func=AF.Square),
                 reads=[(hname, c)], writes=[("sq", s)])
            p.op("pe", lambda e, c=c, s=s: e.matmul(ps[:, 0:n], lhsT=cm.ones[:], rhs=sq[:, s, 0:n],
                                                     start=(c == 0), stop=(c == KC - 1)),
                 reads=[("sq", s), "c_ones"], writes=[pk])
        p.op("act", lambda e, ps=ps: e.activation(out=rstd[:, b0:b1], in_=ps[:, 0:n], func=AF.Sqrt,
                                                  bias=cm.eps[:, 0:1], scale=1.0 / D),
             reads=[pk, "c_eps"], writes=["rstd"])
        p.op("dve", lambda e: e.reciprocal(out=rstd[:, b0:b1], in_=rstd[:, b0:b1]), reads=["rstd"], writes=["rstd"])


def _norm_mod_chunked(p, cm, banks, h, W, segs, out_bf, okey, rstd, sq, tmp, mask, keys_A, hname="h", w0=0):
    rms_rstd_chunked(p, cm, banks, h, w0, W, rstd, sq, hname)
    for c in range(KC):
        s = c % 2
        p.op("dve", lambda e, c=c, s=s: e.tensor_tensor(out=tmp[:, s, w0:W], in0=h[:, c, w0:W], in1=rstd[:, w0:W], op=ALU.mult),
             reads=[(hname, c), "rstd"], writes=[("tmp", s)])
        for (c0, c1, A, Bfn) in segs:
            p.op("act", lambda e, c=c, s=s, c0=c0, c1=c1, A=A, Bfn=Bfn: e.activation(
                out=out_bf[:, c, c0:c1], in_=tmp[:, s, c0:c1], func=AF.Identity, scale=A[:, c:c + 1], bias=Bfn(c)),
                reads=[("tmp", s), "modv"] + keys_A, writes=[(okey, c)])
        if mask is not None:
            p.op("pool", lambda e, c=c: e.tensor_tensor(out=out_bf[:, c, w0:W], in0=out_bf[:, c, w0:W], in1=mask[:, w0:W], op=ALU.mult),
                 reads=[(okey, c), "mask"], writes=[(okey, c)])


def linear_fm(p, banks, name, x, xkey, kcin, wview, dout, blocks, evac, sw=512, q="pool", nbuf=2, wdt=BF16):
    slabs = [p.sb("%s_w%d" % (name, i), [128, kcin, sw], wdt) for i in range(nbuf)]
    ns = (dout + sw - 1) // sw
    for si in range(ns):
        sl = slabs[si % nbuf]
        sk = (name + "_w", si % nbuf)
        f0 = si * sw
        f1 = min(dout, f0 + sw)
        half = kcin // 2 if kcin >= 8 else kcin
        for k0 in range(0, kcin, half):
            p.dma(sl[:, k0:k0 + half, 0:f1 - f0], wview[:, k0:k0 + half, f0:f1], writes=[sk], q=q)
        for mi in range((f1 - f0) // 128):
            m = (f0 // 128) + mi
            for (b0, b1) in blocks:
                ps, pk = banks.next()
                for c in range(kcin):
                    p.op("pe", lambda e: e.matmul(ps[:, 0:b1 - b0], lhsT=sl[:, c, mi * 128:(mi + 1) * 128],
                                                  rhs=x[:, c, b0:b1], start=(c == 0), stop=(c == kcin - 1)),
                         reads=[sk, (xkey, c)], writes=[pk])
                evac(m, ps, pk, b0, b1)


def build_mod():
    p = Prog()
    wm_d = p.dram("wm", [D, 6144])
    bm_d = p.dram("bm", [128, 48])
    ct_d = p.dram("ct", [128, KC, 8])
    mo_d = p.dram("mo", [128, 48, 8], kind="ExternalOutput")
    banks = Banks(p)
    ct = load_small(p, "ct", [128, KC, 8], ct_d)
    bm = load_small(p, "bm", [128, 48], bm_d)
    cb16 = p.sb("cb16", [128, KC, 8], F32)
    mo = p.sb("mo", [128, 48, 8], F32)
    p.op("act", lambda e: e.activation(out=cb16[:], in_=ct[:], func=AF.Silu), reads=["ct"],
         writes=[("cb16", c) for c in range(KC)])

    def evac(m, ps, pk, b0, b1):
        p.op("dve", lambda e: e.tensor_scalar(out=mo[:, m, :], in0=ps[:, 0:8], scalar1=bm[:, m:m + 1], scalar2=None, op0=ALU.add),
             reads=[pk, "bm"], writes=["mo"])
    linear_fm(p, banks, "mod", cb16, "cb16", KC, wm_d.rearrange("(c p) f -> p c f", p=128), 6144, [(0, 8)], evac,
              q="sync", wdt=F32)
    p.dma(mo_d, mo[:], reads=["mo"])
    return p.build()


WP = 1184
P_L0, P_L1, P_C0, P_C1 = 0, 1040, 1040, 1184


def build_pool():
    p = Prog()
    h_d = p.dram("h", [D, WP])
    modv_d = p.dram("modv", [128, 6 * KC, 2])
    g1_d = p.dram("g1", [128, KC])
    mask_d = p.dram("mask", [1, WP])
    icnt_d = p.dram("icnt", [4, WP])
    pw_d = p.dram("pw", [4, 512, 512])
    pb_d = p.dram("pb", [128, KC])
    psc_d = p.dram("psc", [128, KC])
    y_d = p.dram("y", [D, 1152], kind="ExternalOutput")
    cm = Common(p)
    banks = Banks(p)
    h = p.sb("h", [128, KC, WP], F32)
    pbf = p.sb("pbf", [128, KC, WP], BF16)
    tsb = p.sb("tsb", [128, 2, WP], F32)
    t2 = p.sb("t2", [128, 2, WP], F32)
    sq = p.sb("sq", [128, 2, 512], BF16)
    rstd = p.sb("rstd", [128, WP], F32)
    yo = p.sb("yo", [128, 2, WP], F32)
    modv = load_small(p, "modv", [128, 6 * KC, 2], modv_d)
    gvec = load_small(p, "gvec", [128, KC], g1_d)
    pb = load_small(p, "pb", [128, KC], pb_d)
    psc = load_small(p, "psc", [128, KC], psc_d)
    mask = load_small(p, "mask", [128, WP], mask_d[0].partition_broadcast(128))
    icnt = p.sb("icnt", [128, 4, WP], F32)
    for g in range(4):
        p.dma(icnt[:, g, :], icnt_d[g].partition_broadcast(128), writes=["icnt"])
    hv = h_d.rearrange("(c p) w -> p c w", p=128)
    for c4 in range(0, KC, 4):
        p.dma(h[:, c4:c4 + 4, :], hv[:, c4:c4 + 4, :], writes=[("h", c) for c in range(c4, c4 + 4)])
    A_l = make_AB(p, "pl", modv, gvec, 0, 1, 0)
    A_c = make_AB(p, "pc", modv, gvec, 0, 1, 1)
    segs = [(P_L0, P_L1, A_l, lambda c: modv[:, c, 0:1]), (P_C0, P_C1, A_c, lambda c: modv[:, c, 1:2])]
    rms_rstd_chunked(p, cm, banks, h, 0, WP, rstd, sq)
    for c in range(KC):
        s = c % 2
        p.op("dve", lambda e: e.tensor_tensor(out=tsb[:, s, :], in0=h[:, c, :], in1=rstd[:], op=ALU.mult),
             reads=[("h", c), "rstd"], writes=[("tmp", s)])
        for (c0, c1, A, Bfn) in segs:
            p.op("act", lambda e: e.activation(out=h[:, c, c0:c1], in_=tsb[:, s, c0:c1], func=AF.Identity,
                                               scale=A[:, c:c + 1], bias=Bfn(c)),
                 reads=[("tmp", s), "modv", "pl_A", "pc_A"], writes=[("h", c)])
        p.op("pool", lambda e: e.tensor_tensor(out=h[:, c, :], in0=h[:, c, :], in1=mask[:], op=ALU.mult),
             reads=[("h", c), "mask"], writes=[("h", c)])
        g = c // 4
        win = (2, 4, 8, 16)[g]
        right = win - 1 - win // 2
        src, skey = h[:, c, :], ("h", c)
        sh = 1
        bufs = [tsb[:, s, :], t2[:, s, :]]
        bkeys = [("tmp", s), ("t2", s)]
        bi = 0
        while sh < win:
            dst, dkey = bufs[bi], bkeys[bi]
            p.op("dve", lambda e: e.tensor_tensor(out=dst[:, sh:WP], in0=src[:, sh:WP], in1=src[:, 0:WP - sh], op=ALU.add),
                 reads=[skey], writes=[dkey])
            p.op("dve", lambda e: e.tensor_copy(out=dst[:, 0:sh], in_=src[:, 0:sh]), reads=[skey], writes=[dkey])
            src, skey = dst, dkey
            sh *= 2
            bi ^= 1
        dst, dkey = bufs[bi], bkeys[bi]
        n = WP - 16
        p.op("dve", lambda e: e.tensor_tensor(out=dst[:, 8:8 + n], in0=src[:, 8 + right:8 + right + n],
                                              in1=icnt[:, g, 8:8 + n], op=ALU.mult),
             reads=[skey, "icnt"], writes=[dkey])
        p.op("dve", lambda e: e.tensor_tensor(out=pbf[:, c, 8:8 + n], in0=dst[:, 8:8 + n], in1=h[:, c, 8:8 + n], op=ALU.subtract),
             reads=[dkey, ("h", c)], writes=[("pbf", c)])
    yv = y_d.rearrange("(c p) w -> p c w", p=128)
    vblocks = [(8, 520), (520, 1032), (1048, 1176)]
    for g in range(4):
        wsl = p.sb("pw%d" % g, [128, 4, 512], BF16)
        p.dma(wsl[:], pw_d[g].rearrange("(c p) f -> p c f", p=128), writes=[("pw", g)], q="pool")
        for mo_ in range(4):
            m = g * 4 + mo_
            s = m % 2
            for (b0, b1) in vblocks:
                ps, pk = banks.next()
                for ci in range(4):
                    p.op("pe", lambda e: e.matmul(ps[:, 0:b1 - b0], lhsT=wsl[:, ci, mo_ * 128:(mo_ + 1) * 128],
                                                  rhs=pbf[:, g * 4 + ci, b0:b1], start=(ci == 0), stop=(ci == 3)),
                         reads=[("pw", g), ("pbf", g * 4 + ci)], writes=[pk])
                p.op("dve", lambda e: e.tensor_scalar(out=yo[:, s, b0:b1], in0=ps[:, 0:b1 - b0], scalar1=pb[:, m:m + 1],
                                                      scalar2=psc[:, m:m + 1], op0=ALU.add, op1=ALU.mult),
                     reads=[pk, "pb", "psc"], writes=[("yo", s)])
            p.dma(yv[:, m, 0:1024], yo[:, s, 8:1032], reads=[("yo", s)])
            p.dma(yv[:, m, 1024:1152], yo[:, s, 1048:1176], reads=[("yo", s)])
    return p.build()


def norm_mod_stream(p, cm, banks, hview, W, segs, out, okey, rstd, sq, hbuf, tmp, akeys, mask=None, out_chunks=None):
    blocks = col_blocks(0, W)
    pss = [banks.next() for _ in blocks]
    for c in range(KC):
        s = c % 2
        p.dma(hbuf[:, s, 0:W], hview[:, c, :], writes=[("hbuf", s)])
        for bi, (b0, b1) in enumerate(blocks):
            ps, pk = pss[bi]
            p.op("act", lambda e: e.activation(out=sq[:, s, 0:b1 - b0], in_=hbuf[:, s, b0:b1], func=AF.Square),
                 reads=[("hbuf", s)], writes=[("sq", s)])
            p.op("pe", lambda e: e.matmul(ps[:, 0:b1 - b0], lhsT=cm.ones[:], rhs=sq[:, s, 0:b1 - b0],
                                          start=(c == 0), stop=(c == KC - 1)),
                 reads=[("sq", s), "c_ones"], writes=[pk])
    for bi, (b0, b1) in enumerate(blocks):
        ps, pk = pss[bi]
        p.op("act", lambda e: e.activation(out=rstd[:, b0:b1], in_=ps[:, 0:b1 - b0], func=AF.Sqrt,
                                           bias=cm.eps[:, 0:1], scale=1.0 / D),
             reads=[pk, "c_eps"], writes=["rstd"])
        p.op("dve", lambda e: e.reciprocal(out=rstd[:, b0:b1], in_=rstd[:, b0:b1]), reads=["rstd"], writes=["rstd"])
    for c in range(KC):
        s = c % 2
        p.dma(hbuf[:, s, 0:W], hview[:, c, :], writes=[("hbuf", s)])
        p.op("dve", lambda e: e.tensor_tensor(out=tmp[:, s, 0:W], in0=hbuf[:, s, 0:W], in1=rstd[:, 0:W], op=ALU.mult),
             reads=[("hbuf", s), "rstd"], writes=[("tmp", s)])
        for (c0, c1, A, Bfn) in segs:
            p.op("act", lambda e: e.activation(out=out[:, c, c0:c1], in_=tmp[:, s, c0:c1], func=AF.Identity,
                                               scale=A[:, c:c + 1], bias=Bfn(c)),
                 reads=[("tmp", s), "modv"] + akeys, writes=[(okey, c)])
        if mask is not None:
            p.op("pool", lambda e: e.tensor_tensor(out=out[:, c, 0:W], in0=out[:, c, 0:W], in1=mask[:, 0:W], op=ALU.mult),
                 reads=[(okey, c), "mask"], writes=[(okey, c)])


def linear_fm2(p, banks, slabs, skey, x, xkey, kcin, wview, dout, blocks, evac, q="pool"):
    sw = slabs[0].shape[2]
    nbuf = len(slabs)
    ns = (dout + sw - 1) // sw
    for si in range(ns):
        sl = slabs[si % nbuf]
        sk = (skey, si % nbuf)
        f0 = si * sw
        f1 = min(dout, f0 + sw)
        half = kcin // 2 if kcin >= 8 else kcin
        for k0 in range(0, kcin, half):
            p.dma(sl[:, k0:k0 + half, 0:f1 - f0], wview[:, k0:k0 + half, f0:f1], writes=[sk], q=q)
        for mi in range((f1 - f0) // 128):
            m = (f0 // 128) + mi
            for (b0, b1) in blocks:
                ps, pk = banks.next()
                for c in range(kcin):
                    p.op("pe", lambda e: e.matmul(ps[:, 0:b1 - b0], lhsT=sl[:, c, mi * 128:(mi + 1) * 128],
                                                  rhs=x[:, c, b0:b1], start=(c == 0), stop=(c == kcin - 1)),
                         reads=[sk, (xkey, c)], writes=[pk])
                evac(m, ps, pk, b0, b1)


WG = 1152
NPC = 9


def build_gmlp():
    p = Prog()
    h_d = p.dram("h", [D, WG])
    modv_d = p.dram("modv", [128, 6 * KC, 2])
    g1_d = p.dram("g1", [128, KC])
    win_d = p.dram("win", [D, 2 * D])
    bzu_d = p.dram("bzu", [128, KC])
    bzv_d = p.dram("bzv", [1, D])
    ng_d = p.dram("ng", [1, D])
    wst_d = p.dram("wst", [128, 16, 128])
    bs_d = p.dram("bs", [1, 16 * 128])
    wo_d = p.dram("wo", [D, D])
    y_d = p.dram("y", [D, WG], kind="ExternalOutput")
    cm = Common(p)
    banks = Banks(p)
    aT = p.sb("aT", [128, KC, WG], BF16)
    zu = p.sb("zu", [128, KC, WG], BF16)
    zv = p.sb("zv", [128, NPC, D], BF16)
    hbuf = p.sb("hbuf", [128, 2, WG], F32)
    tmp = p.sb("tmp", [128, 2, WG], F32)
    sq = p.sb("sq", [128, 2, 512], BF16)
    rstd = p.sb("rstd", [128, WG], F32)
    slabs = [p.sb("slab%d" % i, [128, KC, 256], BF16) for i in range(2)]
    modv = load_small(p, "modv", [128, 6 * KC, 2], modv_d)
    gvec = load_small(p, "gvec", [128, KC], g1_d)
    bzu = load_small(p, "bzu", [128, KC], bzu_d)
    bzv = load_small(p, "bzv", [128, D], bzv_d[0].partition_broadcast(128))
    ng = load_small(p, "ng", [128, D], ng_d[0].partition_broadcast(128))
    bs = load_small(p, "bs", [128, 16 * 128], bs_d[0].partition_broadcast(128))
    wst = p.sb("wst", [128, 16, 128], BF16)
    p.dma(wst[:], wst_d, writes=["wst"], q="pool")
    ssq = p.sb("ssq", [128, NPC, 8], F32)
    rtok = p.sb("rtok", [128, NPC], F32)
    A_l = make_AB(p, "gl", modv, gvec, 0, 1, 0)
    A_c = make_AB(p, "gc", modv, gvec, 0, 1, 1)
    segs = [(0, 1024, A_l, lambda c: modv[:, c, 0:1]), (1024, WG, A_c, lambda c: modv[:, c, 1:2])]
    norm_mod_stream(p, cm, banks, h_d.rearrange("(c p) w -> p c w", p=128), WG, segs, aT, "aT", rstd, sq, hbuf, tmp,
                    ["gl_A", "gc_A"])
    blocks = col_blocks(0, WG)
    winv = win_d.rearrange("(c p) f -> p c f", p=128)
    import os
    stop = int(os.environ.get("GSTOP", "99"))
    if stop <= 0:
        return p.build()

    def evac_zu(m, ps, pk, b0, b1):
        p.op("act", lambda e: e.activation(out=zu[:, m, b0:b1], in_=ps[:, 0:b1 - b0], func=AF.Gelu_apprx_tanh,
                                           bias=bzu[:, m:m + 1], scale=1.0),
             reads=[pk, "bzu"], writes=[("zu", m)])
    linear_fm2(p, banks, slabs, "slab", aT, "aT", KC, winv[:, :, 0:D], D, blocks, evac_zu)

    if stop <= 1:
        return p.build()
    sw = 256
    for si in range(D // sw):
        sl = slabs[si % 2]
        sk = ("slab", si % 2)
        f0 = D + si * sw
        for k0 in (0, 8):
            p.dma(sl[:, k0:k0 + 8, :], winv[:, k0:k0 + 8, f0:f0 + sw], writes=[sk], q="pool")
        for n in range(NPC):
            ps, pk = banks.next()
            for c in range(KC):
                p.op("pe", lambda e: e.matmul(ps[:, 0:sw], lhsT=aT[:, c, n * 128:(n + 1) * 128], rhs=sl[:, c, :],
                                              start=(c == 0), stop=(c == KC - 1)),
                     reads=[sk, ("aT", c)], writes=[pk])
            s = (si * NPC + n) % 2
            p.op("dve", lambda e: e.tensor_tensor(out=tmp[:, s, 0:sw], in0=ps[:, 0:sw], in1=bzv[:, si * sw:(si + 1) * sw], op=ALU.add),
                 reads=[pk, "bzv"], writes=[("tmp", s)])
            p.op("act", lambda e: e.activation(out=tmp[:, s, 0:sw], in_=tmp[:, s, 0:sw], func=AF.Gelu_apprx_tanh),
                 reads=[("tmp", s)], writes=[("tmp", s)])
            p.op("act", lambda e: e.activation(out=tmp[:, s, 512:512 + sw], in_=tmp[:, s, 0:sw], func=AF.Square,
                                               accum_out=ssq[:, n, si:si + 1]),
                 reads=[("tmp", s)], writes=[("tmp", s), "ssq"])
            p.op("pool", lambda e: e.tensor_copy(out=zv[:, n, si * sw:(si + 1) * sw], in_=tmp[:, s, 0:sw]),
                 reads=[("tmp", s)], writes=[("zv", n)])
    if stop <= 2:
        return p.build()
    p.op("dve", lambda e: e.tensor_reduce(out=rtok[:], in_=ssq[:], axis=AX.X, op=ALU.add), reads=["ssq"], writes=["rtok"])
    p.op("act", lambda e: e.activation(out=rtok[:], in_=rtok[:], func=AF.Sqrt, bias=cm.eps[:, 0:1], scale=1.0 / D),
         reads=["rtok", "c_eps"], writes=["rtok"])
    p.op("dve", lambda e: e.reciprocal(out=rtok[:], in_=rtok[:]), reads=["rtok"], writes=["rtok"])
    for n in range(NPC):
        p.op("dve", lambda e: e.scalar_tensor_tensor(out=zv[:, n, :], in0=zv[:, n, :], scalar=rtok[:, n:n + 1], in1=ng[:],
                                                     op0=ALU.mult, op1=ALU.mult),
             reads=[("zv", n), "rtok", "ng"], writes=[("zv", n)])
    if stop <= 3:
        return p.build()
    bsv = bs[:].rearrange("p (g q) -> p g q", q=128)
    for n in range(NPC):
        for g4 in range(4):
            ps, pk = banks.next()
            for gi in range(4):
                g = g4 * 4 + gi
                p.op("pe", lambda e: e.matmul(ps[:, gi * 128:(gi + 1) * 128], lhsT=zv[:, n, g * 128:(g + 1) * 128],
                                              rhs=wst[:, g, :], start=True, stop=True),
                     reads=[("zv", n), "wst"], writes=[pk])
            s = (n * 4 + g4) % 2
            tv = tmp[:, s, 0:512].rearrange("p (g q) -> p g q", q=128)
            p.op("dve", lambda e: e.tensor_tensor(out=tv, in0=ps[:, 0:512].rearrange("p (g q) -> p g q", q=128),
                                                  in1=bsv[:, g4 * 4:(g4 + 1) * 4, :], op=ALU.add),
                 reads=[pk, "bs"], writes=[("tmp", s)])
            p.op("dve", lambda e: e.tensor_tensor(out=aT[:, g4 * 4:(g4 + 1) * 4, n * 128:(n + 1) * 128], in0=tv,
                                                  in1=zu[:, g4 * 4:(g4 + 1) * 4, n * 128:(n + 1) * 128], op=ALU.mult),
                 reads=[("tmp", s)] + [("zu", g4 * 4 + i) for i in range(4)],
                 writes=[("aT", g4 * 4 + i) for i in range(4)])
    if stop <= 4:
        return p.build()
    yv = y_d.rearrange("(c p) w -> p c w", p=128)
    cnt = [0]

    def evac_y(m, ps, pk, b0, b1):
        s = cnt[0] % 2
        cnt[0] += 1
        p.op("act", lambda e: e.activation(out=hbuf[:, s, b0:b1], in_=ps[:, 0:b1 - b0], func=AF.Copy),
             reads=[pk], writes=[("hbuf", s)])
        if os.environ.get("GNODMA") != "1":
            p.dma(yv[:, m, b0:b1], hbuf[:, s, b0:b1], reads=[("hbuf", s)], q="act")
    linear_fm2(p, banks, slabs, "slab", aT, "aT", KC, wo_d.rearrange("(c p) f -> p c f", p=128), D, blocks, evac_y)
    return p.build()


def build_na1():
    p = Prog()
    W = WG
    h_d = p.dram("h", [D, W])
    modv_d = p.dram("modv", [128, 6 * KC, 2])
    g1_d = p.dram("g1", [128, KC])
    w_d = p.dram("wqkv", [D, 3 * D])
    cos_d = p.dram("rcos", [128, 1024])
    sin_d = p.dram("rsin", [128, 1024])
    pm_d = p.dram("pm", [128, 128])
    q_d = p.dram("qT", [D, W], kind="ExternalOutput")
    k_d = p.dram("kT", [D, W], kind="ExternalOutput")
    v_d = p.dram("v", [W, D], kind="ExternalOutput")
    cm = Common(p)
    banks = Banks(p)
    aT = p.sb("aT", [128, KC, W], BF16)
    hbuf = p.sb("hbuf", [128, 2, W], F32)
    tmp = p.sb("tmp", [128, 2, W], F32)
    st = p.sb("st", [128, 2, W], F32)
    qb = p.sb("qb", [128, 2, 512], BF16)
    sq = p.sb("sq", [128, 2, 512], BF16)
    rstd = p.sb("rstd", [128, W], F32)
    slabs = [p.sb("slab%d" % i, [128, KC, 256], BF16) for i in range(2)]
    modv = load_small(p, "modv", [128, 6 * KC, 2], modv_d)
    gvec = load_small(p, "gvec", [128, KC], g1_d)
    rcos = load_small(p, "rcos", [128, 1024], cos_d)
    rsin = load_small(p, "rsin", [128, 1024], sin_d)
    pm = p.sb("pm", [128, 128], BF16)
    if os.environ.get("NOPM") != "1":
        p.dma(pm[:], pm_d, writes=["pm"], q="pool", max_dma_last_dim=int(os.environ.get("MDL", "512")))
    A_l = make_AB(p, "nl", modv, gvec, 0, 1, 0)
    A_c = make_AB(p, "ncx", modv, gvec, 0, 1, 1)
    segs = [(0, 1024, A_l, lambda c: modv[:, c, 0:1]), (1024, W, A_c, lambda c: modv[:, c, 1:2])]
    norm_mod_stream(p, cm, banks, h_d.rearrange("(c p) w -> p c w", p=128), W, segs, aT, "aT", rstd, sq, hbuf, tmp,
                    ["nl_A", "ncx_A"])
    wv = w_d.rearrange("(c p) f -> p c f", p=128)
    qv = q_d.rearrange("(c p) w -> p c w", p=128)
    kv = k_d.rearrange("(c p) w -> p c w", p=128)
    blocks = col_blocks(0, W)
    cnt = [0]
    stop = int(os.environ.get("GSTOP", "99"))
    if stop <= 0:
        return p.build()

    def evac_qk(m, ps, pk, b0, b1):
        isq = m < KC
        sc = 0.125 if isq else 1.0
        mm = m if isq else m - KC
        s = mm % 2
        n = b1 - b0
        ev = int(os.environ.get("EV", "9"))
        if b0 >= 1024 or ev == 0:
            p.op("act", lambda e: e.activation(out=st[:, s, b0:b1], in_=ps[:, 0:n], func=AF.Copy, scale=sc),
                 reads=[pk], writes=[("st", s, 2)])
        else:
            i = cnt[0] % 2
            cnt[0] += 1
            p.op("act", lambda e: e.activation(out=qb[:, i, 0:n], in_=ps[:, 0:n], func=AF.Copy, scale=sc),
                 reads=[pk], writes=[("qb", i)])
            ps2, pk2 = banks.next()
            p.op("pe", lambda e: e.matmul(ps2[:, 0:n], lhsT=pm[:], rhs=qb[:, i, 0:n], start=True, stop=True),
                 reads=["pm", ("qb", i)], writes=[pk2])
            if ev == 1:
                p.op("act", lambda e: e.activation(out=st[:, s, b0:b1], in_=ps2[:, 0:n], func=AF.Copy, scale=sc),
                     reads=[pk, pk2], writes=[("st", s, b0 // 512)])
                if b1 == W:
                    pass
                return
            p.op("dve", lambda e: e.scalar_tensor_tensor(out=st[:, s, b0:b1], in0=ps[:, 0:n], scalar=sc, in1=rcos[:, b0:b1],
                                                         op0=ALU.mult, op1=ALU.mult),
                 reads=[pk, "rcos"], writes=[("st", s, b0 // 512)])
            p.op("dve", lambda e: e.tensor_tensor(out=tmp[:, i, 0:n], in0=ps2[:, 0:n], in1=rsin[:, b0:b1], op=ALU.mult),
                 reads=[pk2, "rsin"], writes=[("tmp", i)])
            p.op("dve", lambda e: e.tensor_tensor(out=st[:, s, b0:b1], in0=st[:, s, b0:b1], in1=tmp[:, i, 0:n], op=ALU.add),
                 reads=[("tmp", i), ("st", s, b0 // 512)], writes=[("st", s, b0 // 512)])
        if b1 == W:
            dst = qv if isq else kv
            p.dma(dst[:, mm, :], st[:, s, :], reads=[("st", s, 0), ("st", s, 1), ("st", s, 2)], q="act")
    linear_fm2(p, banks, slabs, "slab", aT, "aT", KC, wv[:, :, 0:2 * D], 2 * D, blocks, evac_qk)
    if stop <= 1:
        return p.build()
    sw = 256
    for si in range(D // sw):
        sl = slabs[si % 2]
        sk = ("slab", si % 2)
        f0 = 2 * D + si * sw
        for k0 in (0, 8):
            p.dma(sl[:, k0:k0 + 8, :], wv[:, k0:k0 + 8, f0:f0 + sw], writes=[sk], q="pool")
        for n in range(NPC):
            ps, pk = banks.next()
            for c in range(KC):
                p.op("pe", lambda e: e.matmul(ps[:, 0:sw], lhsT=aT[:, c, n * 128:(n + 1) * 128], rhs=sl[:, c, :],
                                              start=(c == 0), stop=(c == KC - 1)),
                     reads=[sk, ("aT", c)], writes=[pk])
            s = (si * NPC + n) % 2
            p.op("act", lambda e: e.activation(out=tmp[:, s, 0:sw], in_=ps[:, 0:sw], func=AF.Copy),
                 reads=[pk], writes=[("tmp", s)])
            p.dma(v_d[n * 128:(n + 1) * 128, si * sw:(si + 1) * sw], tmp[:, s, 0:sw], reads=[("tmp", s)], q="act")
    return p.build()


NTOK = T + L


def na_rs(r):
    return min(max(r - 4, 0), 24)


def build_na2():
    p = Prog()
    q_d = p.dram("qT", [1024, NTOK])
    k_d = p.dram("kT", [1024, NTOK])
    v_d = p.dram("v", [NTOK, 1024])
    bt_d = p.dram("bt", [16, 128, 15 * 64])
    wo_d = p.dram("wo", [1024, D])
    y_d = p.dram("y", [D, NTOK], kind="ExternalOutput")
    sbanks = Banks(p, 4, "ps_s")
    abanks = Banks(p, 4, "ps_a")
    ones = p.sb("ones", [128, 64], BF16)
    p.op("dve", lambda e: e.memset(ones[:], 1.0), writes=["ones"])
    qT = p.sb("qT", [128, 8, NTOK], BF16)
    kT = p.sb("kT", [128, 8, NTOK], BF16)
    v = p.sb("v", [128, 18, 1024], BF16)
    at = p.sb("at", [128, 8, NTOK], BF16)
    bts = [p.sb("bt%d" % i, [128, 15 * 64], F32) for i in range(2)]
    pts = [p.sb("pt%d" % i, [128, 512], BF16) for i in range(4)]
    sbs = [p.sb("sbb%d" % i, [128, 512], F32) for i in range(2)]
    rec = p.sb("rec", [128, 2, 512], F32)
    st = p.sb("st", [128, 2, 512], F32)
    slabs = [p.sb("slab%d" % i, [128, 8, 512], BF16) for i in range(2)]
    qv = q_d.rearrange("(c p) w -> p c w", p=128)
    kvv = k_d.rearrange("(c p) w -> p c w", p=128)
    vv = v_d.rearrange("(n p) f -> p n f", p=128)
    for c in range(8):
        p.dma(qT[:, c, :], qv[:, c, :], writes=[("qT", c)], q="pool", max_dma_last_dim=4096)
        p.dma(kT[:, c, :], kvv[:, c, :], writes=[("kT", c)], q="pool", max_dma_last_dim=4096)
    for n in range(18):
        p.dma(v[:, n, :], vv[:, n, :], writes=[("v", n)], q="pool")
    npt = [0]
    nsb = [0]
    for h in range(16):
        c = h // 2
        base = (h % 2) * 64
        bt = bts[h % 2]
        bk = ("bt", h % 2)
        p.dma(bt[:], bt_d[h], writes=[bk])
        btv = bt[:].rearrange("p (i q) -> p i q", q=64)
        groups = [(g * 512, 512, g) for g in range(4)] + [(T, L, None)]
        for (q0, nq, g) in groups:
            O, ok = abanks.next()
            Dn, dk = abanks.next()
            items = []
            for j in range(2):
                items.append(("ctx", j))
            if g is not None:
                for kap in range(32):
                    rows = [r for r in range(8 * g, 8 * g + 8) if na_rs(r) <= kap < na_rs(r) + 8]
                    if rows:
                        items.append(("loc", kap, rows[0], rows[-1]))
            for ii, it in enumerate(items):
                first = ii == 0
                last = ii == len(items) - 1
                S, sk_ = sbanks.next()
                pt = pts[npt[0] % 4]
                ptk = ("pt", npt[0] % 4)
                npt[0] += 1
                if it[0] == "ctx":
                    j = it[1]
                    kc0 = T + j * 128
                    p.op("pe", lambda e: e.matmul(S[:, 0:nq], lhsT=kT[base:base + 64, c, kc0:kc0 + 128],
                                                  rhs=qT[base:base + 64, c, q0:q0 + nq], start=True, stop=True),
                         reads=[("kT", c), ("qT", c)], writes=[sk_])
                    p.op("act", lambda e: e.activation(out=pt[:, 0:nq], in_=S[:, 0:nq], func=AF.Exp),
                         reads=[sk_], writes=[ptk])
                    p.op("pe", lambda e: e.matmul(O[base:base + 64, 0:nq], lhsT=v[:, 16 + j, h * 64:(h + 1) * 64],
                                                  rhs=pt[:, 0:nq], start=first, stop=last),
                         reads=[("v", 16 + j), ptk], writes=[ok])
                    p.op("pe", lambda e: e.matmul(Dn[base:base + 64, 0:nq], lhsT=ones[:, :], rhs=pt[:, 0:nq],
                                                  start=first, stop=last),
                         reads=["ones", ptk], writes=[dk])
                else:
                    _, kap, ra, rb = it
                    kb = (kap % 2) * 64
                    nr = rb - ra + 1
                    nn = nr * 64
                    c0 = ra * 64 - q0
                    idx0 = ra - kap + 7
                    sb_ = sbs[nsb[0] % 2]
                    sbk = ("sbb", nsb[0] % 2)
                    nsb[0] += 1
                    p.op("pe", lambda e: e.matmul(S[kb:kb + 64, 0:nn], lhsT=kT[base:base + 64, c, kap * 64:(kap + 1) * 64],
                                                  rhs=qT[base:base + 64, c, ra * 64:(rb + 1) * 64], start=True, stop=True),
                         reads=[("kT", c), ("qT", c)], writes=[sk_])
                    p.op("dve", lambda e: e.tensor_tensor(out=sb_[kb:kb + 64, 0:nn].rearrange("p (i q) -> p i q", q=64),
                                                          in0=S[kb:kb + 64, 0:nn].rearrange("p (i q) -> p i q", q=64),
                                                          in1=btv[kb:kb + 64, idx0:idx0 + nr, :], op=ALU.add),
                         reads=[sk_, bk], writes=[sbk])
                    p.op("act", lambda e: e.activation(out=pt[kb:kb + 64, 0:nn], in_=sb_[kb:kb + 64, 0:nn], func=AF.Exp),
                         reads=[sbk], writes=[ptk])
                    p.op("pe", lambda e: e.matmul(O[base:base + 64, c0:c0 + nn], lhsT=v[kb:kb + 64, kap // 2, h * 64:(h + 1) * 64],
                                                  rhs=pt[kb:kb + 64, 0:nn], start=False, stop=last),
                         reads=[("v", kap // 2), ptk], writes=[ok])
                    p.op("pe", lambda e: e.matmul(Dn[base:base + 64, c0:c0 + nn], lhsT=ones[kb:kb + 64, :],
                                                  rhs=pt[kb:kb + 64, 0:nn], start=False, stop=last),
                         reads=["ones", ptk], writes=[dk])
            ri = (h * 5 + (g if g is not None else 4)) % 2
            p.op("dve", lambda e: e.reciprocal(out=rec[base:base + 64, ri, 0:nq], in_=Dn[base:base + 64, 0:nq]),
                 reads=[dk], writes=[("rec", ri)])
            p.op("dve", lambda e: e.tensor_tensor(out=at[base:base + 64, c, q0:q0 + nq], in0=O[base:base + 64, 0:nq],
                                                  in1=rec[base:base + 64, ri, 0:nq], op=ALU.mult),
                 reads=[ok, ("rec", ri)], writes=[("at", c)])
    yv = y_d.rearrange("(c p) w -> p c w", p=128)
    cnt = [0]

    def evac_y(m, ps, pk, b0, b1):
        s = cnt[0] % 2
        cnt[0] += 1
        p.op("act", lambda e: e.activation(out=st[:, s, 0:b1 - b0], in_=ps[:, 0:b1 - b0], func=AF.Copy),
             reads=[pk], writes=[("st", s)])
        p.dma(yv[:, m, b0:b1], st[:, s, 0:b1 - b0], reads=[("st", s)], q="act")
    linear_fm2(p, sbanks, slabs, "slab", at, "at", 8, wo_d.rearrange("(c p) f -> p c f", p=128), D, col_blocks(0, NTOK), evac_y)
    return p.build()


def _chunked(vv):
    vv = np.asarray(vv, np.float32)
    return np.ascontiguousarray(vv.reshape(-1, 128).T)


def rope_tables(t0, n):
    half = 32
    inv_freq = (10000.0 ** (-np.arange(0, half, 2, dtype=np.float32) / half)).astype(np.float32)
    pos = np.arange(t0, t0 + n)
    row = (pos // 64).astype(np.float32)
    col = (pos % 64).astype(np.float32)
    cos = np.zeros((64, n), np.float32)
    sin = np.zeros((64, n), np.float32)
    for d in range(64):
        pp = row if d < 32 else col
        ang = pp * inv_freq[d % 16]
        cos[d] = np.cos(ang)
        sgn = -1.0 if (d % 32) < 16 else 1.0
        sin[d] = sgn * np.sin(ang)
    return np.concatenate([cos, cos], 0), np.concatenate([sin, sin], 0)


def rope_perm():
    pm = np.zeros((128, 128), np.float32)
    for m in range(128):
        d = m % 32
        partner = m + 16 if d < 16 else m - 16
        pm[partner, m] = 1.0
    return pm


def na_bias_tables(rpb, heads):
    cq = np.arange(64)
    ck = np.arange(64)
    cs = np.clip(cq - 8, 0, 48)
    ok = (ck[:, None] >= cs[None, :]) & (ck[:, None] < cs[None, :] + 16)
    dc = np.clip(ck[:, None] - cq[None, :], -15, 15) + 15
    out = np.empty((len(heads), 128, 15, 64), np.float32)
    for i, h in enumerate(heads):
        for idx in range(15):
            tab = rpb[h, 14 - idx][dc]
            tab = np.where(ok, tab, np.float32(-30000.0))
            out[i, 0:64, idx] = tab
            out[i, 64:128, idx] = tab
    return out.reshape(len(heads), 128, 15 * 64)


RW_LORA = 96
RW_GATE = 256
C64 = 64


def shift_mix(p, a, akey_fn, xo, xkey, coef, n_idx, j0, n):
    for c in range(KC):
        p.op("dve", lambda e: e.tensor_scalar(out=xo[:, c, 0:n], in0=a[:, c, j0:j0 + n], scalar1=coef[:, 0, n_idx, c:c + 1],
                                              scalar2=None, op0=ALU.mult),
             reads=[akey_fn(c), "coef"], writes=[(xkey, c)])
        p.op("dve", lambda e: e.scalar_tensor_tensor(out=xo[:, c, 0:n], in0=a[:, c, j0 - 1:j0 - 1 + n], scalar=coef[:, 1, n_idx, c:c + 1],
                                                     in1=xo[:, c, 0:n], op0=ALU.mult, op1=ALU.add),
             reads=[akey_fn(c), "coef", (xkey, c)], writes=[(xkey, c)])
        p.op("dve", lambda e: e.scalar_tensor_tensor(out=xo[:, c, 0:n], in0=a[:, c, j0 + 1:j0 + 1 + n], scalar=coef[:, 2, n_idx, c:c + 1],
                                                     in1=xo[:, c, 0:n], op0=ALU.mult, op1=ALU.add),
             reads=[akey_fn(c), "coef", (xkey, c)], writes=[(xkey, c)])


def load_coef(p, coef_d):
    coef = p.sb("coef", [128, 3, 6, KC], F32)
    p.dma(coef[:, 1:3, :, :], coef_d, writes=["coef"])
    p.op("dve", lambda e: e.tensor_scalar(out=coef[:, 0, :, :], in0=coef[:, 1, :, :], scalar1=-1.0, scalar2=1.0, op0=ALU.mult, op1=ALU.add),
         reads=["coef"], writes=["coef"])
    p.op("dve", lambda e: e.tensor_tensor(out=coef[:, 0, :, :], in0=coef[:, 0, :, :], in1=coef[:, 2, :, :], op=ALU.subtract),
         reads=["coef"], writes=["coef"])
    return coef


def build_rw1():
    p = Prog()
    W = WF
    h_d = p.dram("h", [D, W])
    modv_d = p.dram("modv", [128, 6 * KC, 2])
    g1_d = p.dram("g1", [128, KC])
    mask_d = p.dram("mask", [1, W])
    coef_d = p.dram("coef", [128, 2, 6, KC])
    wrkv_d = p.dram("wrkv", [3, D, D])
    w1_d = p.dram("w1", [2, D, RW_LORA])
    w2_d = p.dram("w2", [2, RW_LORA, D])
    a1_d = p.dram("a1", [2, D, RW_LORA])
    a2_d = p.dram("a2", [2, RW_LORA, D])
    vecs_d = p.dram("vecs", [128, 7, KC])
    bones_d = p.dram("bones", [128, 128])
    rmask_d = p.dram("rmask", [1, 512])
    outs = {}
    outs["vt"] = p.dram("vt", [D, 1152], BF16, kind="ExternalOutput")
    for d in range(2):
        for nm in ("at", "bt", "kt", "rt"):
            outs[(nm, d)] = p.dram("%s%d" % (nm, d), [D, 1152], BF16, kind="ExternalOutput")
    bv_d = [p.dram("bv%d" % d, [D, 1152], kind="ExternalOutput") for d in range(2)]
    gam_d = [p.dram("gam%d" % d, [D, 18], kind="ExternalOutput") for d in range(2)]
    cm = Common(p)
    banks = Banks(p)
    a = p.sb("a", [128, KC, W], BF16)
    hbuf = p.sb("hbuf", [128, 2, W], F32)
    tmp = p.sb("tmp", [128, 2, W], F32)
    sq = p.sb("sq", [128, 2, 512], BF16)
    rstd = p.sb("rstd", [128, W], F32)
    modv = load_small(p, "modv", [128, 6 * KC, 2], modv_d)
    gvec = load_small(p, "gvec", [128, KC], g1_d)
    mask = load_small(p, "mask", [128, W], mask_d[0].partition_broadcast(128))
    coef = load_coef(p, coef_d)
    vecs = load_small(p, "vecs", [128, 7, KC], vecs_d)
    rmask = load_small(p, "rmask", [128, 512], rmask_d[0].partition_broadcast(128))
    bones = p.sb("bones", [128, 128], BF16)
    p.dma(bones[:], bones_d, writes=["bones"], q="pool", max_dma_last_dim=512)
    w1 = [p.sb("w1_%d" % d, [128, KC, RW_LORA], BF16) for d in range(2)]
    a1 = [p.sb("a1_%d" % d, [128, KC, RW_LORA], BF16) for d in range(2)]
    w2 = [p.sb("w2_%d" % d, [RW_LORA, D], BF16) for d in range(2)]
    a2 = [p.sb("a2_%d" % d, [RW_LORA, D], BF16) for d in range(2)]
    for d in range(2):
        p.dma(w1[d][:], w1_d[d].rearrange("(c p) f -> p c f", p=128), writes=[("w1", d)], q="pool")
        p.dma(a1[d][:], a1_d[d].rearrange("(c p) f -> p c f", p=128), writes=[("a1", d)], q="pool")
        p.dma(w2[d][:], w2_d[d], writes=[("w2", d)], q="pool")
        p.dma(a2[d][:], a2_d[d], writes=[("a2", d)], q="pool")
    A_l = make_AB(p, "rl", modv, gvec, 0, 1, 0)
    A_c = make_AB(p, "rc", modv, gvec, 0, 1, 1)
    segs = [(LAT0, LAT1, A_l, lambda c: modv[:, c, 0:1]), (CTX0, CTX1, A_c, lambda c: modv[:, c, 1:2])]
    norm_mod_stream(p, cm, banks, h_d.rearrange("(c p) w -> p c w", p=128), W, segs, a, "a", rstd, sq, hbuf, tmp,
                    ["rl_A", "rc_A"], mask=mask)
    akey = lambda c: ("a", c)
    xs = {nm: p.sb("x_" + nm, [128, KC, 512], BF16) for nm in ("k", "v", "r")}
    tw = [p.sb("tw%d" % d, [RW_LORA, 512], BF16) for d in range(2)]
    ta = [p.sb("ta%d" % d, [RW_LORA, 512], BF16) for d in range(2)]
    NT_ = 14
    ft = [hbuf[:, 0, 0:512], hbuf[:, 0, 512:1024], hbuf[:, 1, 0:512], hbuf[:, 1, 512:1024],
          tmp[:, 0, 0:512], tmp[:, 0, 512:1024], tmp[:, 1, 0:512], tmp[:, 1, 512:1024]]
    ft += [p.sb("ft%d" % i, [128, 512], F32)[:] for i in range(8, NT_)]
    fk = [("ft", i) for i in range(NT_)]
    fence = p.sb("fence", [128, 1], F32)
    p.op("dve", lambda e: e.memset(fence[:], 0.0), reads=[("hbuf", 0), ("hbuf", 1), ("tmp", 0), ("tmp", 1)], writes=fk[0:8])
    ob = {nm: p.sb("ob_" + nm, [128, 2, 512], BF16) for nm in ("at", "bt", "kt", "rt", "vt")}
    sqb = p.sb("sqb", [128, 2, 512], BF16)
    gam = [p.sb("gamt%d" % d, [128, KC, 18], F32) for d in range(2)]
    slabs = {nm: [p.sb("sl_%s%d" % (nm, i), [128, KC, 128], BF16) for i in range(2)] for nm in ("r", "k", "v")}
    widx = {"r": 0, "k": 1, "v": 2}
    wv = wrkv_d.rearrange("n (c p) f -> n p c f", p=128)
    cblocks = [(1, 512, 0), (513, 512, 512), (1027, 128, 1024)]
    nslab = [0]
    for (j0, n, o0) in cblocks:
        nch = n // C64
        for (kind, n_idx, fn) in (("w", 1, AF.Tanh), ("a", 4, AF.Copy)):
            shift_mix(p, a, akey, xs["r"], "x_r", coef, n_idx, j0, n)
            for d in range(2):
                lw, lkey = (w1[d], ("w1", d)) if kind == "w" else (a1[d], ("a1", d))
                lt, ltkey = (tw[d], ("tw", d)) if kind == "w" else (ta[d], ("ta", d))
                ps, pk = banks.next()
                for c in range(KC):
                    p.op("pe", lambda e: e.matmul(ps[0:RW_LORA, 0:n], lhsT=lw[:, c, :], rhs=xs["r"][:, c, 0:n],
                                                  start=(c == 0), stop=(c == KC - 1)),
                         reads=[lkey, ("x_r", c)], writes=[pk])
                p.op("act", lambda e: e.activation(out=lt[:, 0:n], in_=ps[0:RW_LORA, 0:n], func=fn), reads=[pk], writes=[ltkey])
        shift_mix(p, a, akey, xs["k"], "x_k", coef, 2, j0, n)
        shift_mix(p, a, akey, xs["v"], "x_v", coef, 3, j0, n)
        shift_mix(p, a, akey, xs["r"], "x_r", coef, 0, j0, n)
        for m in range(KC):
            cur = {}
            for nm in ("r", "k", "v"):
                i = nslab[0] % 2
                sl = slabs[nm][i]
                sk = ("sl_" + nm, i)
                for k0 in (0, 8):
                    p.dma(sl[:, k0:k0 + 8, :], wv[widx[nm], :, k0:k0 + 8, m * 128:(m + 1) * 128], writes=[sk], q="pool")
                cur[nm] = (sl, sk)
            nslab[0] += 1
            s2 = m % 2
            T_ = lambda i: ft[i][:, 0:n]
            pss = {}
            for nm in ("k", "v", "r"):
                ps, pk = banks.next()
                sl, sk = cur[nm]
                for c in range(KC):
                    p.op("pe", lambda e: e.matmul(ps[:, 0:n], lhsT=sl[:, c, :], rhs=xs[nm][:, c, 0:n],
                                                  start=(c == 0), stop=(c == KC - 1)),
                         reads=[sk, ("x_" + nm, c)], writes=[pk])
                pss[nm] = (ps, pk)
            p.op("act", lambda e: e.activation(out=T_(0), in_=pss["k"][0][:, 0:n], func=AF.Copy), reads=[pss["k"][1]], writes=[fk[0]])
            p.op("act", lambda e: e.activation(out=T_(1), in_=pss["v"][0][:, 0:n], func=AF.Copy), reads=[pss["v"][1]], writes=[fk[1]])
            p.op("act", lambda e: e.activation(out=T_(2), in_=pss["r"][0][:, 0:n], func=AF.Copy), reads=[pss["r"][1]], writes=[fk[2]])
            p.op("pool", lambda e: e.tensor_copy(out=ob["vt"][:, s2, 0:n], in_=T_(1)), reads=[fk[1]], writes=[("ob_vt", s2)])
            ovv = outs["vt"].rearrange("(c p) w -> p c w", p=128)
            p.dma(ovv[:, m, o0:o0 + n], ob["vt"][:, s2, 0:n], reads=[("ob_vt", s2)])
            p.op("dve", lambda e: e.tensor_scalar(out=T_(5), in0=T_(0), scalar1=vecs[:, 4, m:m + 1], scalar2=None, op0=ALU.mult),
                 reads=[fk[0], "vecs"], writes=[fk[5]])
            p.op("act", lambda e: e.activation(out=sqb[:, 0, 0:n], in_=T_(5), func=AF.Square), reads=[fk[5]], writes=[("sqb", 0)])
            ps_n, pk_n = banks.next()
            p.op("pe", lambda e: e.matmul(ps_n[:, 0:n], lhsT=bones[:], rhs=sqb[:, 0, 0:n], start=True, stop=True),
                 reads=["bones", ("sqb", 0)], writes=[pk_n])
            p.op("act", lambda e: e.activation(out=T_(6), in_=ps_n[:, 0:n], func=AF.Sqrt), reads=[pk_n], writes=[fk[6]])
            p.op("dve", lambda e: e.tensor_scalar(out=T_(6), in0=T_(6), scalar1=1e-6, scalar2=None, op0=ALU.max),
                 reads=[fk[6]], writes=[fk[6]])
            p.op("dve", lambda e: e.reciprocal(out=T_(6), in_=T_(6)), reads=[fk[6]], writes=[fk[6]])
            p.op("dve", lambda e: e.tensor_tensor(out=T_(5), in0=T_(5), in1=T_(6), op=ALU.mult), reads=[fk[5], fk[6]], writes=[fk[5]])
            for d in range(2):
                ps_s, pk_s = banks.next()
                p.op("pe", lambda e: e.matmul(ps_s[:, 0:n], lhsT=w2[d][:, m * 128:(m + 1) * 128], rhs=tw[d][:, 0:n], start=True, stop=True),
                     reads=[("w2", d), ("tw", d)], writes=[pk_s])
                ps_a, pk_a = banks.next()
                p.op("pe", lambda e: e.matmul(ps_a[:, 0:n], lhsT=a2[d][:, m * 128:(m + 1) * 128], rhs=ta[d][:, 0:n], start=True, stop=True),
                     reads=[("a2", d), ("ta", d)], writes=[pk_a])
                p.op("act", lambda e: e.activation(out=T_(3), in_=ps_s[:, 0:n], func=AF.Sigmoid, bias=vecs[:, 0 + d, m:m + 1], scale=1.0),
                     reads=[pk_s, "vecs"], writes=[fk[3]])
                p.op("act", lambda e: e.activation(out=T_(4), in_=ps_a[:, 0:n], func=AF.Sigmoid, bias=vecs[:, 2 + d, m:m + 1], scale=1.0),
                     reads=[pk_a, "vecs"], writes=[fk[4]])
                p.op("dve", lambda e: e.tensor_scalar(out=T_(3), in0=T_(3), scalar1=-0.6065306597126334, scalar2=None, op0=ALU.mult),
                     reads=[fk[3]], writes=[fk[3]])
                p.op("dve", lambda e: e.tensor_tensor_scan(out=T_(7), data0=rmask[:, 0:n], data1=T_(3), initial=0.0,
                                                           op0=ALU.mult, op1=ALU.add),
                     reads=["rmask", fk[3]], writes=[fk[7]])
                if d == 0:
                    p.op("dve", lambda e: e.tensor_tensor(out=T_(8), in0=T_(7), in1=T_(3), op=ALU.subtract), reads=[fk[7], fk[3]], writes=[fk[8]])
                    p.op("pool", lambda e: e.tensor_copy(out=gam[d][:, m, o0 // C64:o0 // C64 + nch],
                                                         in_=ft[7][:, 0:n].rearrange("p (a b) -> p a b", b=C64)[:, :, C64 - 1]),
                         reads=[fk[7]], writes=[("gam", d)])
                else:
                    P3 = ft[7][:, 0:n].rearrange("p (a b) -> p a b", b=C64)
                    p.op("pool", lambda e: e.tensor_copy(out=gam[d][:, m, o0 // C64:o0 // C64 + nch], in_=P3[:, :, C64 - 1]),
                         reads=[fk[7]], writes=[("gam", d)])
                    tot = gam[d][:, m, o0 // C64:o0 // C64 + nch].unsqueeze(2).broadcast_to([128, nch, C64])
                    p.op("dve", lambda e: e.tensor_tensor(out=ft[8][:, 0:n].rearrange("p (a b) -> p a b", b=C64), in0=tot, in1=P3, op=ALU.subtract),
                         reads=[fk[7], ("gam", d)], writes=[fk[8]])
                    p.op("dve", lambda e: e.tensor_tensor(out=T_(7), in0=T_(8), in1=T_(3), op=ALU.add), reads=[fk[8], fk[3]], writes=[fk[7]])
                p.op("act", lambda e: e.activation(out=T_(8), in_=T_(8), func=AF.Exp), reads=[fk[8]], writes=[fk[8]])
                p.op("act", lambda e: e.activation(out=T_(9), in_=T_(7), func=AF.Exp, scale=-1.0), reads=[fk[7]], writes=[fk[9]])
                p.op("act", lambda e: e.activation(out=T_(7), in_=T_(7), func=AF.Exp), reads=[fk[7]], writes=[fk[7]])
                p.op("dve", lambda e: e.tensor_scalar(out=T_(10), in0=T_(4), scalar1=-1.0, scalar2=vecs[:, 5, m:m + 1], op0=ALU.add, op1=ALU.mult),
                     reads=[fk[4], "vecs"], writes=[fk[10]])
                p.op("dve", lambda e: e.scalar_tensor_tensor(out=T_(10), in0=T_(10), scalar=1.0, in1=T_(0), op0=ALU.add, op1=ALU.mult),
                     reads=[fk[10], fk[0]], writes=[fk[10]])
                p.op("dve", lambda e: e.scalar_tensor_tensor(out=ob["at"][:, d, 0:n], in0=T_(5), scalar=-1.0, in1=T_(8), op0=ALU.mult, op1=ALU.mult),
                     reads=[fk[5], fk[8]], writes=[("ob_at", d)])
                p.op("dve", lambda e: e.tensor_tensor(out=T_(11), in0=T_(5), in1=T_(4), op=ALU.mult), reads=[fk[5], fk[4]], writes=[fk[11]])
                p.op("dve", lambda e: e.tensor_tensor(out=ob["bt"][:, d, 0:n], in0=T_(11), in1=T_(9), op=ALU.mult),
                     reads=[fk[11], fk[9]], writes=[("ob_bt", d)])
                p.op("dve", lambda e: e.tensor_tensor(out=ob["kt"][:, d, 0:n], in0=T_(10), in1=T_(9), op=ALU.mult),
                     reads=[fk[10], fk[9]], writes=[("ob_kt", d)])
                p.op("dve", lambda e: e.tensor_tensor(out=ob["rt"][:, d, 0:n], in0=T_(2), in1=T_(7), op=ALU.mult),
                     reads=[fk[2], fk[7]], writes=[("ob_rt", d)])
                p.op("dve", lambda e: e.scalar_tensor_tensor(out=sqb[:, 1, 0:n], in0=T_(2), scalar=vecs[:, 6, m:m + 1], in1=T_(10), op0=ALU.mult, op1=ALU.mult),
                     reads=[fk[2], fk[10], "vecs"], writes=[("sqb", 1)])
                ps_b, pk_b = banks.next()
                p.op("pe", lambda e: e.matmul(ps_b[:, 0:n], lhsT=bones[:], rhs=sqb[:, 1, 0:n], start=True, stop=True),
                     reads=["bones", ("sqb", 1)], writes=[pk_b])
                i12 = 12 + d
                p.op("dve", lambda e: e.tensor_tensor(out=T_(i12), in0=ps_b[:, 0:n], in1=T_(1), op=ALU.mult), reads=[pk_b, fk[1]], writes=[fk[i12]])
                for nm in ("at", "bt", "kt", "rt"):
                    ov = outs[(nm, d)].rearrange("(c p) w -> p c w", p=128)
                    p.dma(ov[:, m, o0:o0 + n], ob[nm][:, d, 0:n], reads=[("ob_" + nm, d)])
                bvv = bv_d[d].rearrange("(c p) w -> p c w", p=128)
                p.dma(bvv[:, m, o0:o0 + n], T_(i12), reads=[fk[i12]])
    for d in range(2):
        p.op("act", lambda e: e.activation(out=gam[d][:], in_=gam[d][:], func=AF.Exp), reads=[("gam", d)], writes=[("gam", d)])
        p.dma(gam_d[d].rearrange("(c p) w -> p c w", p=128), gam[d][:], reads=[("gam", d)])
    return p.build()


NCH = NTOK // C64
RW2_MASK_ENG = os.environ.get("RW2_MASK_ENG", "dve")
NHG = 4


def build_rw2():
    p = Prog()
    far_d = p.dram("far", [NCH, 64, 32 * 2 * 64], BF16)
    fbk_d = p.dram("fbk", [NCH, 64, 2 * 32 * 64], BF16)
    tm_d = p.dram("tm", [NCH, 64, 3 * D], BF16)
    gam_d = p.dram("gam", [64, 32, NCH])
    mka_d = p.dram("mka", [64, 2 * 64])
    mkp_d = p.dram("mkp", [64, 2 * 64])
    mkl_d = p.dram("mkl", [64, 7 * 64])
    y_d = p.dram("y", [NCH, 64, 32 * 64], kind="ExternalOutput")
    banks = Banks(p)
    FAR = [p.sb("far%d" % i, [64, 32, 2, 64], BF16) for i in range(2)]
    FBK = [p.sb("fbk%d" % i, [64, 2, 32, 64], BF16) for i in range(2)]
    TM = [p.sb("tm%d" % i, [64, 3, D], BF16) for i in range(2)]
    gam = load_small(p, "gam", [64, 32, NCH], gam_d)
    mka = load_small(p, "mka", [64, 2, 64], mka_d.rearrange("p (a b) -> p a b", b=64))
    mkp = load_small(p, "mkp", [64, 2, 64], mkp_d.rearrange("p (a b) -> p a b", b=64))
    mkl = load_small(p, "mkl", [64, 7, 64], mkl_d.rearrange("p (a b) -> p a b", b=64))
    mklb = p.sb("mklb", [64, 7, 64], BF16)
    p.op("dve", lambda e: e.tensor_copy(out=mklb[:], in_=mkl[:]), reads=["mkl"], writes=["mklb"])
    S = p.sb("S", [64, 32, 64], F32)
    Sb = p.sb("Sb", [64, 32, 64], BF16)
    Sl = p.sb("Sl", [64, 32, 64], BF16)
    Sd = p.sb("Sd", [64, 32, 64], F32)
    yst = [p.sb("yst%d" % i, [64, 32, 64], F32) for i in range(2)]
    QA = [p.sb("QA%d" % g, [64, 8, 2, 64], BF16) for g in range(NHG)]
    QK = [p.sb("QK%d" % g, [64, 8, 2, 64], BF16) for g in range(NHG)]
    NM = [[p.sb("NM%d_%d" % (g, l), [64, 8, 64], BF16) for l in range(6)] for g in range(NHG)]
    Gf = [p.sb("Gf%d" % g, [64, 8, 64], F32) for g in range(NHG)]
    Hf = [p.sb("Hf%d" % g, [64, 8, 64], F32) for g in range(NHG)]
    Gb = [p.sb("Gb%d" % g, [64, 8, 64], BF16) for g in range(NHG)]
    Hb = [p.sb("Hb%d" % g, [64, 8, 64], BF16) for g in range(NHG)]
    Wb = [p.sb("Wb%d" % g, [64, 8, 64], BF16) for g in range(NHG)]
    Rb = [p.sb("Rb%d" % g, [64, 8, 64], BF16) for g in range(NHG)]
    Xb = [p.sb("Xb%d" % g, [64, 8, 64], BF16) for g in range(NHG)]
    p.op("dve", lambda e: e.memset(S[:], 0.0), writes=[("S", g) for g in range(NHG)])
    p.op("dve", lambda e: e.memset(Sb[:], 0.0), writes=[("Sb", g) for g in range(NHG)])
    p.op("dve", lambda e: e.memset(Sl[:], 0.0), writes=[("Sl", g) for g in range(NHG)])
    v3 = lambda ap: ap.rearrange("p (a b) -> p a b", b=64)
    bc8 = lambda ap2: ap2.unsqueeze(1).broadcast_to([64, 8, 64])
    for ci in range(NCH):
        s = ci % 2
        far, fbk, tm = FAR[s], FBK[s], TM[s]
        p.dma(far[:].rearrange("p a b c -> p (a b c)"), far_d[ci], writes=[("far", s)])
        p.dma(fbk[:].rearrange("p a b c -> p (a b c)"), fbk_d[ci], writes=[("fbk", s)])
        p.dma(tm[:].rearrange("p a b -> p (a b)"), tm_d[ci], writes=[("tm", s)])
        kfar, kfbk, ktm = ("far", s), ("fbk", s), ("tm", s)
        for g in range(NHG):
            ps, pk = banks.next()
            for hh in range(8):
                h = g * 8 + hh
                p.op("pe", lambda e: e.matmul(ps[0:64, hh * 64:(hh + 1) * 64], lhsT=far[:, h, 0, :], rhs=fbk[:, 0, h, :], start=True, stop=True),
                     reads=[kfar, kfbk], writes=[pk])
            p.op("dve", lambda e: e.tensor_tensor(out=Gf[g][:], in0=v3(ps[0:64, :]), in1=bc8(mkp[:, 0, :]), op=ALU.mult),
                 reads=[pk, "mkp"], writes=[("Gf", g)])
            p.op(RW2_MASK_ENG, lambda e: e.tensor_tensor(out=Gf[g][:], in0=Gf[g][:], in1=bc8(mkp[:, 1, :]), op=ALU.add),
                 reads=[("Gf", g), "mkp"], writes=[("Gf", g)])
            p.op("act", lambda e: e.activation(out=Gb[g][:], in_=Gf[g][:], func=AF.Copy), reads=[("Gf", g)], writes=[("Gb", g)])
            for (dst, dkey, which) in ((QA[g], ("QA", g), 0), (QK[g], ("QK", g), 1)):
                for half in range(2):
                    ps, pk = banks.next()
                    for hq in range(4):
                        h = g * 8 + half * 4 + hq
                        p.op("pe", lambda e: e.matmul(ps[0:64, hq * 128:(hq + 1) * 128], lhsT=fbk[:, which, h, :],
                                                      rhs=far[:, h, :, :].rearrange("p a b -> p (a b)"), start=True, stop=True),
                             reads=[kfar, kfbk], writes=[pk])
                    p.op("dve", lambda e: e.tensor_tensor(
                        out=dst[:, half * 4:(half + 1) * 4, :, :],
                        in0=ps[0:64, :].rearrange("p (h a b) -> p h a b", a=2, b=64),
                        in1=mka[:].unsqueeze(1).broadcast_to([64, 4, 2, 64]), op=ALU.mult),
                        reads=[pk, "mka"], writes=[dkey])
            NT0 = QA[g][:, :, 0, :]
            for l in range(6):
                p.op(RW2_MASK_ENG, lambda e: e.tensor_tensor(out=NM[g][l][:], in0=NT0, in1=bc8(mklb[:, l, :]), op=ALU.mult),
                     reads=[("QA", g), "mklb"], writes=[("NM", g, l)])
            p.op(RW2_MASK_ENG, lambda e: e.tensor_tensor(out=Hf[g][:], in0=NM[g][0][:], in1=bc8(mkl[:, 6, :]), op=ALU.add),
                 reads=[("NM", g, 0), "mkl"], writes=[("Hf", g)])
            p.op("act", lambda e: e.activation(out=Hb[g][:], in_=Hf[g][:], func=AF.Copy), reads=[("Hf", g)], writes=[("Hb", g)])
        for g in range(NHG):
            ps, pk = banks.next()
            for hh in range(8):
                h = g * 8 + hh
                o = ps[0:64, hh * 64:(hh + 1) * 64]
                p.op("pe", lambda e: e.matmul(o, lhsT=far[:, h, 0, :], rhs=Sb[:, h, :], start=True, stop=False),
                     reads=[kfar, ("Sb", g)], writes=[pk])
                p.op("pe", lambda e: e.matmul(o, lhsT=far[:, h, 0, :], rhs=Sl[:, h, :], start=False, stop=False),
                     reads=[kfar, ("Sl", g)], writes=[pk])
                p.op("pe", lambda e: e.matmul(o, lhsT=QK[g][:, hh, 0, :], rhs=tm[:, 0, h * 64:(h + 1) * 64], start=False, stop=True),
                     reads=[("QK", g), ktm], writes=[pk])
            p.op("act", lambda e: e.activation(out=Rb[g][:], in_=v3(ps[0:64, :]), func=AF.Copy), reads=[pk], writes=[("Rb", g)])
        for l in range(1, 6):
            for g in range(NHG):
                ps, pk = banks.next()
                for hh in range(8):
                    p.op("pe", lambda e: e.matmul(ps[0:64, hh * 64:(hh + 1) * 64], lhsT=NM[g][l][:, hh, :], rhs=Gb[g][:, hh, :], start=True, stop=True),
                         reads=[("NM", g, l), ("Gb", g)], writes=[pk])
                p.op("act", lambda e: e.activation(out=Wb[g][:], in_=v3(ps[0:64, :]), func=AF.Copy), reads=[pk], writes=[("Wb", g)])
            zz = []
            for g in range(NHG):
                psz = pkz = None
                if l < 5:
                    psz, pkz = banks.next()
                    for hh in range(8):
                        p.op("pe", lambda e: e.matmul(psz[0:64, hh * 64:(hh + 1) * 64], lhsT=Hb[g][:, hh, :], rhs=Wb[g][:, hh, :], start=True, stop=True),
                             reads=[("Hb", g), ("Wb", g)], writes=[pkz])
                ps2, pk2 = banks.next()
                for hh in range(8):
                    p.op("pe", lambda e: e.matmul(ps2[0:64, hh * 64:(hh + 1) * 64], lhsT=Wb[g][:, hh, :], rhs=Hb[g][:, hh, :], start=True, stop=True),
                         reads=[("Hb", g), ("Wb", g)], writes=[pk2])
                if l < 5:
                    p.op("dve", lambda e: e.tensor_tensor(out=Gf[g][:], in0=v3(psz[0:64, :]), in1=Gf[g][:], op=ALU.add),
                         reads=[pkz, ("Gf", g)], writes=[("Gf", g)])
                    p.op("act", lambda e: e.activation(out=Gb[g][:], in_=Gf[g][:], func=AF.Copy), reads=[("Gf", g)], writes=[("Gb", g)])
                p.op("dve", lambda e: e.tensor_tensor(out=Hf[g][:], in0=v3(ps2[0:64, :]), in1=Hf[g][:], op=ALU.add),
                     reads=[pk2, ("Hf", g)], writes=[("Hf", g)])
                p.op("act", lambda e: e.activation(out=Hb[g][:], in_=Hf[g][:], func=AF.Copy), reads=[("Hf", g)], writes=[("Hb", g)])
        for g in range(NHG):
            ps, pk = banks.next()
            for hh in range(8):
                p.op("pe", lambda e: e.matmul(ps[0:64, hh * 64:(hh + 1) * 64], lhsT=Hb[g][:, hh, :], rhs=Rb[g][:, hh, :], start=True, stop=True),
                     reads=[("Hb", g), ("Rb", g)], writes=[pk])
            p.op("act", lambda e: e.activation(out=Xb[g][:], in_=v3(ps[0:64, :]), func=AF.Copy), reads=[pk], writes=[("Xb", g)])
        ys = yst[s]
        for g in range(NHG):
            ps, pk = banks.next()
            for hh in range(8):
                h = g * 8 + hh
                o = ps[0:64, hh * 64:(hh + 1) * 64]
                p.op("pe", lambda e: e.matmul(o, lhsT=Sb[:, h, :], rhs=far[:, h, 1, :], start=True, stop=False),
                     reads=[("Sb", g), kfar], writes=[pk])
                p.op("pe", lambda e: e.matmul(o, lhsT=Sl[:, h, :], rhs=far[:, h, 1, :], start=False, stop=False),
                     reads=[("Sl", g), kfar], writes=[pk])
                p.op("pe", lambda e: e.matmul(o, lhsT=Xb[g][:, hh, :], rhs=QA[g][:, hh, 1, :], start=False, stop=False),
                     reads=[("Xb", g), ("QA", g)], writes=[pk])
                p.op("pe", lambda e: e.matmul(o, lhsT=tm[:, 0, h * 64:(h + 1) * 64], rhs=QK[g][:, hh, 1, :], start=False, stop=True),
                     reads=[ktm, ("QK", g)], writes=[pk])
            p.op("act", lambda e: e.activation(out=ys[:, g * 8:(g + 1) * 8, :], in_=v3(ps[0:64, :]), func=AF.Copy),
                 reads=[pk], writes=[("yst", s)])
        p.dma(y_d[ci], ys[:].rearrange("p a b -> p (a b)"), reads=[("yst", s)], q="act")
        for g in range(NHG):
            ps, pk = banks.next()
            for hh in range(8):
                h = g * 8 + hh
                o = ps[0:64, hh * 64:(hh + 1) * 64]
                p.op("pe", lambda e: e.matmul(o, lhsT=tm[:, 1, h * 64:(h + 1) * 64], rhs=Xb[g][:, hh, :], start=True, stop=False),
                     reads=[ktm, ("Xb", g)], writes=[pk])
                p.op("pe", lambda e: e.matmul(o, lhsT=tm[:, 2, h * 64:(h + 1) * 64], rhs=tm[:, 0, h * 64:(h + 1) * 64], start=False, stop=True),
                     reads=[ktm], writes=[pk])
            Sg = S[:, g * 8:(g + 1) * 8, :]
            Sdg = Sd[:, g * 8:(g + 1) * 8, :]
            p.op("dve", lambda e: e.tensor_tensor(out=Sg, in0=v3(ps[0:64, :]), in1=Sg, op=ALU.add), reads=[pk, ("S", g)], writes=[("S", g)])
            p.op("dve", lambda e: e.tensor_tensor(out=Sg, in0=Sg, in1=gam[:, g * 8:(g + 1) * 8, ci:ci + 1].broadcast_to([64, 8, 64]), op=ALU.mult),
                 reads=[("S", g), "gam"], writes=[("S", g)])
            p.op("act", lambda e: e.activation(out=Sb[:, g * 8:(g + 1) * 8, :], in_=Sg, func=AF.Copy), reads=[("S", g)], writes=[("Sb", g)])
            p.op(RW2_MASK_ENG, lambda e: e.tensor_tensor(out=Sdg, in0=Sg, in1=Sb[:, g * 8:(g + 1) * 8, :], op=ALU.subtract),
                 reads=[("S", g), ("Sb", g)], writes=[("Sd", g)])
            p.op("act", lambda e: e.activation(out=Sl[:, g * 8:(g + 1) * 8, :], in_=Sdg, func=AF.Copy), reads=[("Sd", g)], writes=[("Sl", g)])
    return p.build()


def rw2_masks():
    r = np.arange(64)[:, None]
    f = np.arange(64)[None, :]
    mka = np.stack([(f > r), (f >= r)], 1).astype(np.float32).reshape(64, 128)

    def m_level(l, i, j):
        return ((i >> (l + 1)) == (j >> (l + 1))) & (((i >> l) & 1) == 1) & (((j >> l) & 1) == 0)
    eye = (r == f)
    mkp = np.stack([m_level(0, r, f), eye], 1).astype(np.float32).reshape(64, 128)
    mkl = np.stack([m_level(l, f, r) for l in range(6)] + [eye], 1).astype(np.float32).reshape(64, 7 * 64)
    return mka, mkp, mkl


RW_GN_EPS = 64e-5


def build_rw3():
    p = Prog()
    W = WF
    h_d = p.dram("h", [D, W])
    modv_d = p.dram("modv", [128, 6 * KC, 2])
    g1_d = p.dram("g1", [128, KC])
    mask_d = p.dram("mask", [1, W])
    coef_d = p.dram("coef", [128, 2, 6, KC])
    yin_d = p.dram("yin", [4, D, 1152])
    g1w_d = p.dram("g1w", [D, RW_GATE])
    g2w_d = p.dram("g2w", [RW_GATE, D])
    lnv_d = p.dram("lnv", [128, 2, KC])
    wo_d = p.dram("wo", [D, D])
    bones_d = p.dram("bones", [128, 128])
    y_d = p.dram("y", [D, 1152], kind="ExternalOutput")
    cm = Common(p)
    banks = Banks(p)
    a = p.sb("a", [128, KC, W], BF16)
    hbuf = p.sb("hbuf", [128, 2, W], F32)
    tmp = p.sb("tmp", [128, 2, W], F32)
    sq = p.sb("sq", [128, 2, 512], BF16)
    rstd = p.sb("rstd", [128, W], F32)
    modv = load_small(p, "modv", [128, 6 * KC, 2], modv_d)
    gvec = load_small(p, "gvec", [128, KC], g1_d)
    mask = load_small(p, "mask", [128, W], mask_d[0].partition_broadcast(128))
    coef = load_coef(p, coef_d)
    lnv = load_small(p, "lnv", [128, 2, KC], lnv_d)
    gne = p.sb("gne", [128, 1], F32)
    p.op("dve", lambda e: e.memset(gne[:], RW_GN_EPS), writes=["gne"])
    bones = p.sb("bones", [128, 128], BF16)
    p.dma(bones[:], bones_d, writes=["bones"], q="pool", max_dma_last_dim=512)
    g1w = p.sb("g1w", [128, KC, RW_GATE], BF16)
    g2w = p.sb("g2w", [128, 2, D], BF16)
    p.dma(g1w[:], g1w_d.rearrange("(c p) f -> p c f", p=128), writes=["g1w"], q="pool")
    for c2 in range(2):
        p.dma(g2w[:, c2, :], g2w_d[c2 * 128:(c2 + 1) * 128, :], writes=["g2w"], q="pool")
    A_l = make_AB(p, "rl", modv, gvec, 0, 1, 0)
    A_c = make_AB(p, "rc", modv, gvec, 0, 1, 1)
    segs = [(LAT0, LAT1, A_l, lambda c: modv[:, c, 0:1]), (CTX0, CTX1, A_c, lambda c: modv[:, c, 1:2])]
    norm_mod_stream(p, cm, banks, h_d.rearrange("(c p) w -> p c w", p=128), W, segs, a, "a", rstd, sq, hbuf, tmp,
                    ["rl_A", "rc_A"], mask=mask)
    xg = p.sb("xg", [128, KC, 512], BF16)
    ggb = p.sb("ggb", [128, 2, 512], BF16)
    ob = p.sb("ob", [128, KC, 512], BF16)
    yt = [p.sb("yt%d" % i, [128, 4, 512], F32) for i in range(2)]
    ft = [p.sb("ft%d" % i, [128, 512], F32) for i in range(4)]
    fk = [("ft", i) for i in range(4)]
    sqb = p.sb("sqb", [128, 2, 512], BF16)
    st = p.sb("st", [128, 2, 512], F32)
    slabs = [p.sb("slab%d" % i, [128, KC, 256], BF16) for i in range(2)]
    yinv = yin_d.rearrange("n (c p) w -> p n c w", p=128)
    yov = y_d.rearrange("(c p) w -> p c w", p=128)
    cblocks = [(1, 512, 0), (513, 512, 512), (1027, 128, 1024)]
    cnt = [0]
    for (j0, n, o0) in cblocks:
        shift_mix(p, a, lambda c: ("a", c), xg, "xg", coef, 5, j0, n)
        for c2 in range(2):
            ps, pk = banks.next()
            for c in range(KC):
                p.op("pe", lambda e: e.matmul(ps[:, 0:n], lhsT=g1w[:, c, c2 * 128:(c2 + 1) * 128], rhs=xg[:, c, 0:n],
                                              start=(c == 0), stop=(c == KC - 1)),
                     reads=["g1w", ("xg", c)], writes=[pk])
            p.op("act", lambda e: e.activation(out=ggb[:, c2, 0:n], in_=ps[:, 0:n], func=AF.Sigmoid), reads=[pk], writes=[("ggb", c2)])
        for m in range(KC):
            s2 = m % 2
            y4 = yt[s2]
            p.dma(y4[:, :, 0:n], yinv[:, :, m, o0:o0 + n], writes=[("yt", s2)])
            psg, pkg = banks.next()
            for c2 in range(2):
                p.op("pe", lambda e: e.matmul(psg[:, 0:n], lhsT=g2w[:, c2, m * 128:(m + 1) * 128], rhs=ggb[:, c2, 0:n],
                                              start=(c2 == 0), stop=(c2 == 1)),
                     reads=["g2w", ("ggb", c2)], writes=[pkg])
            T_ = lambda i: ft[i][:, 0:n]
            p.op("dve", lambda e: e.tensor_tensor(out=T_(0), in0=y4[:, 0, 0:n], in1=y4[:, 1, 0:n], op=ALU.add), reads=[("yt", s2)], writes=[fk[0]])
            p.op("act", lambda e: e.activation(out=sqb[:, 0, 0:n], in_=T_(0), func=AF.Copy), reads=[fk[0]], writes=[("sqb", 0)])
            psm, pkm = banks.next()
            p.op("pe", lambda e: e.matmul(psm[:, 0:n], lhsT=bones[:], rhs=sqb[:, 0, 0:n], start=True, stop=True),
                 reads=["bones", ("sqb", 0)], writes=[pkm])
            p.op("dve", lambda e: e.scalar_tensor_tensor(out=T_(1), in0=psm[:, 0:n], scalar=-1.0 / 64, in1=T_(0), op0=ALU.mult, op1=ALU.add),
                 reads=[pkm, fk[0]], writes=[fk[1]])
            p.op("act", lambda e: e.activation(out=sqb[:, 1, 0:n], in_=T_(1), func=AF.Square), reads=[fk[1]], writes=[("sqb", 1)])
            psv, pkv = banks.next()
            p.op("pe", lambda e: e.matmul(psv[:, 0:n], lhsT=bones[:], rhs=sqb[:, 1, 0:n], start=True, stop=True),
                 reads=["bones", ("sqb", 1)], writes=[pkv])
            p.op("act", lambda e: e.activation(out=T_(2), in_=psv[:, 0:n], func=AF.Sqrt, bias=gne[:, 0:1], scale=1.0 / 64),
                 reads=[pkv, "gne"], writes=[fk[2]])
            p.op("dve", lambda e: e.reciprocal(out=T_(2), in_=T_(2)), reads=[fk[2]], writes=[fk[2]])
            p.op("dve", lambda e: e.tensor_tensor(out=T_(1), in0=T_(1), in1=T_(2), op=ALU.mult), reads=[fk[1], fk[2]], writes=[fk[1]])
            p.op("dve", lambda e: e.tensor_scalar(out=T_(1), in0=T_(1), scalar1=lnv[:, 0, m:m + 1], scalar2=lnv[:, 1, m:m + 1],
                                                  op0=ALU.mult, op1=ALU.add),
                 reads=[fk[1], "lnv"], writes=[fk[1]])
            p.op("dve", lambda e: e.tensor_tensor(out=T_(3), in0=y4[:, 2, 0:n], in1=y4[:, 3, 0:n], op=ALU.add), reads=[("yt", s2)], writes=[fk[3]])
            p.op("dve", lambda e: e.tensor_tensor(out=T_(1), in0=T_(1), in1=T_(3), op=ALU.add), reads=[fk[1], fk[3]], writes=[fk[1]])
            p.op("dve", lambda e: e.tensor_tensor(out=ob[:, m, 0:n], in0=psg[:, 0:n], in1=T_(1), op=ALU.mult),
                 reads=[pkg, fk[1]], writes=[("ob", m)])

        def evac_y(m, ps, pk, b0, b1):
            s = cnt[0] % 2
            cnt[0] += 1
            p.op("act", lambda e: e.activation(out=st[:, s, 0:b1 - b0], in_=ps[:, 0:b1 - b0], func=AF.Copy), reads=[pk], writes=[("st", s)])
            p.dma(yov[:, m, o0 + b0:o0 + b1], st[:, s, 0:b1 - b0], reads=[("st", s)], q="act")
        linear_fm2(p, banks, slabs, "slab", ob, "ob", KC, wo_d.rearrange("(c p) f -> p c f", p=128), D, [(0, n)], evac_y)
    return p.build()


_PROGS = {}
_DEBUG = None


def _prog(name, fn, *args):
    key = (name,) + args
    if key not in _PROGS:
        _PROGS[key] = fn(*args)
    return _PROGS[key]


def _run(nc, in_maps):
    res = _bu.run_bass_kernel_spmd(nc, in_maps, core_ids=list(range(8)))
    return res.results


def _f32(x):
    return np.ascontiguousarray(np.asarray(x, dtype=np.float32))


def _slab(hl_b, hc_b, s, halo):
    Dn = hl_b.shape[0]
    wl, wc = 1024 + 2 * halo, 128 + 2 * halo
    out = np.zeros((Dn, wl + wc), hl_b.dtype)
    mask = np.zeros((1, wl + wc), np.float32)
    for (src, n, base, o0) in ((hl_b, 1024, s * 1024, 0), (hc_b, 128, s * 128, wl)):
        lo, hi = base - halo, base + n + halo
        a0, a1 = max(lo, 0), min(hi, src.shape[1])
        out[:, o0 + (a0 - lo):o0 + (a1 - lo)] = src[:, a0:a1]
        mask[0, o0 + (a0 - lo):o0 + (a1 - lo)] = 1.0
    return out, mask


def _core_cols(hl_b, hc_b, s):
    return np.ascontiguousarray(np.concatenate([hl_b[:, s * 1024:(s + 1) * 1024], hc_b[:, s * 128:(s + 1) * 128]], 1))


def _pool_icnt(s):
    out = np.ones((4, WP), np.float32)
    for g, win in enumerate((2, 4, 8, 16)):
        left, right = win // 2, win - 1 - win // 2
        for (n, base, o0, Tn) in ((1024, s * 1024, 8, T), (128, s * 128, 1040 + 8, L)):
            t = np.arange(base, base + n)
            cnt = np.minimum(t + right, Tn - 1) - np.maximum(t - left, 0) + 1
            out[g, o0:o0 + n] = (1.0 / cnt.astype(np.float64)).astype(np.float32)
    return out


def rwkv_mixer(hl, hc, modv3, W3, dbg=None):
    import ml_dtypes
    norm1_g = W3["norm1_g"]; rw_mu = W3["rw_mu"]; rw_w_rkv = W3["rw_w_rkv"]; rw_w0 = W3["rw_w0"]; rw_w1 = W3["rw_w1"]
    rw_w2 = W3["rw_w2"]; rw_a0 = W3["rw_a0"]; rw_a1 = W3["rw_a1"]; rw_a2 = W3["rw_a2"]; rw_g1 = W3["rw_g1"]; rw_g2 = W3["rw_g2"]
    rw_k_k = W3["rw_k_k"]; rw_k_a = W3["rw_k_a"]; rw_r_k = W3["rw_r_k"]; rw_ln_g = W3["rw_ln_g"]; rw_ln_b = W3["rw_ln_b"]
    rw_w_o = W3["rw_w_o"]
    cores = [(b, s) for b in range(NB) for s in range(2)]
    mu = _f32(rw_mu[0])
    bones = np.kron(np.eye(2), np.ones((64, 64))).astype(np.float32)
    rmask = np.ones((1, 512), np.float32)
    rmask[0, ::64] = 0

    def coef_of(mu_prev, mu_next):
        return np.ascontiguousarray(np.stack([
            np.stack([_chunked(mu_prev[n]) for n in range(6)], 1),
            np.stack([_chunked(mu_next[n]) for n in range(6)], 1)], 1))

    bf = ml_dtypes.bfloat16
    mka, mkp, mkl = rw2_masks()
    rw2_in = {}
    bv_nat = {}
    vecs = np.ascontiguousarray(np.stack([_chunked(rw_w0[0][0]), _chunked(rw_w0[0][1]), _chunked(rw_a0[0][0]), _chunked(rw_a0[0][1]),
                                          _chunked(rw_k_k[0]), _chunked(rw_k_a[0]), _chunked(_f32(rw_r_k[0]).reshape(-1))], 1))
    ims = []
    for (b, s) in cores:
        hs, mk = _slab(hl[b], hc[b], s, 1)
        ims.append(dict(h=hs, modv=modv3[b], g1=_chunked(norm1_g[3]), mask=mk, coef=coef_of(mu[0], mu[1]),
                        wrkv=_f32(rw_w_rkv[0]), w1=_f32(rw_w1[0]), w2=_f32(rw_w2[0]), a1=_f32(rw_a1[0]),
                        a2=_f32(rw_a2[0]), vecs=vecs, bones=bones, rmask=rmask))
    res = _run(_prog("rw1", build_rw1), ims)
    for b in range(NB):
        r0, r1 = res[2 * b], res[2 * b + 1]
        for d in range(2):
            def seq(nm):
                lat = np.concatenate([np.asarray(r0[nm])[:, :1024], np.asarray(r1[nm])[:, :1024]], 1)
                cx = np.concatenate([np.asarray(r0[nm])[:, 1024:], np.asarray(r1[nm])[:, 1024:]], 1)
                if d == 1:
                    lat, cx = lat[:, ::-1], cx[:, ::-1]
                return np.concatenate([cx, lat], 1)
            at, bt, kt, rt = (seq("%s%d" % (nm, d)) for nm in ("at", "bt", "kt", "rt"))
            vt = seq("vt")
            hm = lambda z: z.reshape(32, 64, NCH, 64).transpose(2, 1, 0, 3)
            far = np.ascontiguousarray(np.stack([hm(at), hm(rt)], 3).reshape(NCH, 64, -1))
            fbk = np.ascontiguousarray(np.stack([hm(bt), hm(kt)], 2).reshape(NCH, 64, -1))
            tk = lambda z: np.ascontiguousarray(z.T).reshape(NCH, 64, D)
            tm = np.ascontiguousarray(np.stack([tk(vt), tk(bt), tk(kt)], 2).reshape(NCH, 64, -1))
            g0, g1_ = np.asarray(r0["gam%d" % d]), np.asarray(r1["gam%d" % d])
            glat = np.concatenate([g0[:, 0:16], g1_[:, 0:16]], 1)
            gcx = np.concatenate([g0[:, 16:18], g1_[:, 16:18]], 1)
            if d == 1:
                glat, gcx = glat[:, ::-1], gcx[:, ::-1]
            gam = np.concatenate([gcx, glat], 1)
            gam = np.ascontiguousarray(gam.reshape(32, 64, NCH).transpose(1, 0, 2))
            rw2_in[(b, d)] = dict(far=far.astype(bf, copy=False), fbk=fbk.astype(bf, copy=False), tm=tm.astype(bf, copy=False),
                                  gam=gam, mka=mka, mkp=mkp, mkl=mkl)
            bvs = np.concatenate([np.asarray(r0["bv%d" % d])[:, :1024], np.asarray(r1["bv%d" % d])[:, :1024]], 1)
            bvc = np.concatenate([np.asarray(r0["bv%d" % d])[:, 1024:], np.asarray(r1["bv%d" % d])[:, 1024:]], 1)
            bv_nat[(b, d)] = (np.ascontiguousarray(bvs), np.ascontiguousarray(bvc))
    res = _run(_prog("rw2", build_rw2), [rw2_in[(b, d)] for b in range(NB) for d in range(2)])
    y_nat = {}
    for b in range(NB):
        for d in range(2):
            y = res[2 * b + d]["y"].reshape(NCH, 64, 32, 64).transpose(2, 1, 0, 3).reshape(D, NTOK)
            yc_, yl_ = y[:, :L], y[:, L:]
            if d == 1:
                yc_, yl_ = yc_[:, ::-1], yl_[:, ::-1]
            y_nat[(b, d)] = (np.ascontiguousarray(yl_), np.ascontiguousarray(yc_))
    ims = []
    coef_n = coef_of(mu[0], mu[1])
    lnv = np.ascontiguousarray(np.stack([_chunked(rw_ln_g[0]), _chunked(rw_ln_b[0])], 1))
    for (b, s) in cores:
        hs, mk = _slab(hl[b], hc[b], s, 1)
        yin = np.stack([_core_cols(y_nat[(b, 0)][0], y_nat[(b, 0)][1], s), _core_cols(y_nat[(b, 1)][0], y_nat[(b, 1)][1], s),
                        _core_cols(bv_nat[(b, 0)][0], bv_nat[(b, 0)][1], s), _core_cols(bv_nat[(b, 1)][0], bv_nat[(b, 1)][1], s)], 0)
        ims.append(dict(h=hs, modv=modv3[b], g1=_chunked(norm1_g[3]), mask=mk, coef=coef_n, yin=np.ascontiguousarray(yin),
                        g1w=_f32(rw_g1[0]), g2w=_f32(rw_g2[0]), lnv=lnv, wo=_f32(rw_w_o[0]), bones=bones))
    res = _run(_prog("rw3", build_rw3), ims)
    if dbg is not None:
        dbg["rw2_in"] = rw2_in
        dbg["y_nat"] = y_nat
        dbg["bv_nat"] = bv_nat
    return res


def kernel(x, c, ctx, c_ctx, norm1_g, norm2_g, w_mod, b_mod, ffn_w_gate, ffn_w_up, ffn_conv_w, ffn_conv_b,
           ffn_w_down, final_norm_g, pool_w, pool_b, pool_scale, na_w_qkv, na_rpb, na_w_o, sg_w_in, sg_b_in,
           sg_norm_g, sg_w_s, sg_b_s, sg_w_o, rw_mu, rw_w_rkv, rw_w0, rw_w1, rw_w2, rw_a0, rw_a1, rw_a2, rw_g1,
           rw_g2, rw_k_k, rw_k_a, rw_r_k, rw_ln_g, rw_ln_b, rw_w_o):
    import ml_dtypes
    x, ctx = _f32(x), _f32(ctx)
    hl = [np.ascontiguousarray(x[b].T) for b in range(NB)]
    hc = [np.ascontiguousarray(ctx[b].T) for b in range(NB)]
    cores = [(b, s) for b in range(NB) for s in range(2)]

    cc = np.zeros((8, D), np.float32)
    cc[0:4] = _f32(c)
    cc[4] = _f32(c_ctx)
    ct = np.ascontiguousarray(cc.T.reshape(KC, 128, 8).transpose(1, 0, 2))
    w_mod = np.asarray(w_mod, np.float32)
    ims = []
    for core in range(8):
        i, half = core // 2, core % 2
        ims.append(dict(wm=np.ascontiguousarray(w_mod[i][:, half * 6144:(half + 1) * 6144]),
                        bm=_chunked(np.asarray(b_mod[i], np.float32)[half * 6144:(half + 1) * 6144]), ct=ct))
    res = _run(_prog("mod", build_mod), ims)
    modv = {}
    for i in range(4):
        mo = np.concatenate([res[2 * i]["mo"], res[2 * i + 1]["mo"]], 1)
        for b in range(NB):
            modv[(i, b)] = np.ascontiguousarray(mo[:, :, [b, 4]])

    def run_ffn(i, ys_l, ys_c, final):
        wg, wu, wd = _f32(ffn_w_gate[i]), _f32(ffn_w_up[i]), _f32(ffn_w_down[i])
        cw = np.ascontiguousarray(_f32(ffn_conv_w[i]).reshape(3, FC, 128).transpose(2, 1, 0))
        cb = _chunked(ffn_conv_b[i])
        g2 = _chunked(norm2_g[i])
        ims = []
        for (b, s) in cores:
            hs, mk = _slab(hl[b], hc[b], s, 1)
            y = np.zeros((2, D, WF), np.float32)
            for n in range(len(ys_l)):
                y[n] = _slab(ys_l[n][b], ys_c[n][b], s, 1)[0]
            im = dict(h=hs, y=y, modv=modv[(i, b)], g2=g2, mask=np.ascontiguousarray(np.repeat(mk, 128, 0)),
                      wg=wg, wu=wu, wd=wd, cw=cw, cb=cb)
            if final:
                im["gf"] = _chunked(final_norm_g)
            ims.append(im)
        res = _run(_prog("ffn", build_ffn, final), ims)
        out = None
        if final:
            out = np.empty((NB, T, D), np.float32)
        for ci, (b, s) in enumerate(cores):
            ho = res[ci]["ho"]
            hl[b][:, s * 1024:(s + 1) * 1024] = ho[:, 0:1024]
            hc[b][:, s * 128:(s + 1) * 128] = ho[:, 1024:1152]
            if final:
                out[b, s * 1024:(s + 1) * 1024, :] = res[ci]["of"].T
        return out

    def split_y(res_y):
        yl = [np.empty((D, T), np.float32) for _ in range(NB)]
        yc = [np.empty((D, L), np.float32) for _ in range(NB)]
        for ci, (b, s) in enumerate(cores):
            yl[b][:, s * 1024:(s + 1) * 1024] = res_y[ci][:, 0:1024]
            yc[b][:, s * 128:(s + 1) * 128] = res_y[ci][:, 1024:1152]
        return yl, yc

    i = 0
    ims = []
    for (b, s) in cores:
        hs, mk = _slab(hl[b], hc[b], s, 8)
        ims.append(dict(h=hs, modv=modv[(i, b)], g1=_chunked(norm1_g[i]), mask=mk, icnt=_pool_icnt(s),
                        pw=_f32(pool_w[0]), pb=_chunked(pool_b[0]), psc=_chunked(pool_scale[0])))
    res = _run(_prog("pool", build_pool), ims)
    yl, yc = split_y([r["y"] for r in res])
    run_ffn(i, [yl], [yc], False)
    if _DEBUG is not None:
        _DEBUG.append(([a.copy() for a in hl], [a.copy() for a in hc]))

    i = 1
    pm = rope_perm()
    ims = []
    for (b, s) in cores:
        rc, rs = rope_tables(s * 1024, 1024)
        ims.append(dict(h=_core_cols(hl[b], hc[b], s), modv=modv[(i, b)], g1=_chunked(norm1_g[i]), wqkv=_f32(na_w_qkv[0]),
                        rcos=rc, rsin=rs, pm=pm))
    res = _run(_prog("na1", build_na1), ims)
    Q, Kt, V = [], [], []
    for b in range(NB):
        r0, r1 = res[2 * b], res[2 * b + 1]
        Q.append(np.concatenate([r0["qT"][:, :1024], r1["qT"][:, :1024], r0["qT"][:, 1024:], r1["qT"][:, 1024:]], 1))
        Kt.append(np.concatenate([r0["kT"][:, :1024], r1["kT"][:, :1024], r0["kT"][:, 1024:], r1["kT"][:, 1024:]], 1))
        V.append(np.concatenate([r0["v"][:1024], r1["v"][:1024], r0["v"][1024:], r1["v"][1024:]], 0))
    rpb = _f32(na_rpb[0])
    wo = _f32(na_w_o[0])
    ims = []
    for b in range(NB):
        for hh in range(2):
            sl = slice(hh * 1024, (hh + 1) * 1024)
            ims.append(dict(qT=np.ascontiguousarray(Q[b][sl]), kT=np.ascontiguousarray(Kt[b][sl]),
                            v=np.ascontiguousarray(V[b][:, sl]), bt=na_bias_tables(rpb, list(range(hh * 16, hh * 16 + 16))),
                            wo=np.ascontiguousarray(wo[sl])))
    res = _run(_prog("na2", build_na2), ims)
    yls, ycs = [], []
    for hh in range(2):
        yls.append([np.ascontiguousarray(res[2 * b + hh]["y"][:, :T]) for b in range(NB)])
        ycs.append([np.ascontiguousarray(res[2 * b + hh]["y"][:, T:]) for b in range(NB)])
    run_ffn(i, yls, ycs, False)
    if _DEBUG is not None:
        _DEBUG.append(([a.copy() for a in hl], [a.copy() for a in hc]))

    i = 2
    b_in = _f32(sg_b_in[0])
    ims = []
    for (b, s) in cores:
        ims.append(dict(h=_core_cols(hl[b], hc[b], s), modv=modv[(i, b)], g1=_chunked(norm1_g[i]), win=_f32(sg_w_in[0]),
                        bzu=_chunked(b_in[:D]), bzv=np.ascontiguousarray(b_in[None, D:]), ng=_f32(sg_norm_g[0])[None].copy(),
                        wst=np.ascontiguousarray(_f32(sg_w_s[0]).transpose(2, 0, 1)), bs=_f32(sg_b_s[0]).reshape(1, -1).copy(),
                        wo=_f32(sg_w_o[0])))
    res = _run(_prog("gmlp", build_gmlp), ims)
    yl, yc = split_y([r["y"] for r in res])
    run_ffn(i, [yl], [yc], False)
    if _DEBUG is not None:
        _DEBUG.append(([a.copy() for a in hl], [a.copy() for a in hc]))

    i = 3
    W3 = dict(norm1_g=norm1_g, rw_mu=rw_mu, rw_w_rkv=rw_w_rkv, rw_w0=rw_w0, rw_w1=rw_w1, rw_w2=rw_w2, rw_a0=rw_a0, rw_a1=rw_a1,
              rw_a2=rw_a2, rw_g1=rw_g1, rw_g2=rw_g2, rw_k_k=rw_k_k, rw_k_a=rw_k_a, rw_r_k=rw_r_k, rw_ln_g=rw_ln_g,
              rw_ln_b=rw_ln_b, rw_w_o=rw_w_o)
    res = rwkv_mixer(hl, hc, {b: modv[(i, b)] for b in range(NB)}, W3)
    yl, yc = split_y([r["y"] for r in res])
    return run_ffn(i, [yl], [yc], True)
```

```python
import contextlib
import os
import numpy as np
import concourse.bass as bass
import concourse.mybir as mybir

F32 = mybir.dt.float32
BF16 = mybir.dt.bfloat16
AF = mybir.ActivationFunctionType
ALU = mybir.AluOpType
AX = mybir.AxisListType

ENGS = ("sync", "pe", "dve", "act", "pool")
SAME_ENGINE_SYNC = True


class Prog:
    EPOCH = 16000
    NDMA = 24

    def __init__(self):
        self.nc = bass.Bass("TRN2", target_bir_lowering=False)
        self.stack = contextlib.ExitStack()
        self.ops = {e: [] for e in ENGS}
        self.cnt = {e: 0 for e in ENGS}
        self.last_w = {}
        self.readers = {}
        self.waited = {e: {} for e in ENGS}
        self.sems = {}
        self.dma_n = 0
        self.dma_tot = [0] * self.NDMA
        self.n_uid = 0

    def dram(self, name, shape, dt=F32, kind="ExternalInput"):
        return self.nc.dram_tensor(name, list(shape), dt, kind=kind).ap()

    def sb(self, name, shape, dt=F32):
        return self.stack.enter_context(self.nc.sbuf_tensor("sb_" + name, list(shape), dt))

    def ps(self, name, shape, dt=F32):
        return self.stack.enter_context(self.nc.psum_tensor("pp_" + name, list(shape), dt))

    def _sem(self, key):
        if key not in self.sems:
            self.sems[key] = self.stack.enter_context(
                self.nc.semaphore("s_%s_%s" % (key[0], key[1])))
        return self.sems[key]

    def _deps(self, eng, reads, writes, pe_sync=False):
        toks = []
        for k in reads:
            w = self.last_w.get(k)
            if w is not None:
                toks.append(w)
        for k in writes:
            w = self.last_w.get(k)
            if w is not None:
                toks.append(w)
            toks.extend(self.readers.get(k, ()))
        need = {}
        for t in toks:
            if t[0] == "eng":
                _, e, seq = t
                if e == eng and ((eng == "pe" and not pe_sync) or not SAME_ENGINE_SYNC):
                    continue
                sk = (e, seq // self.EPOCH)
                v = seq % self.EPOCH + 1
            else:
                _, s, v = t
                sk = ("dma", s)
            if need.get(sk, 0) < v:
                need[sk] = v
        out = []
        wd = self.waited[eng]
        for sk, v in need.items():
            if wd.get(sk, 0) >= v:
                continue
            wd[sk] = v
            out.append((sk, v))
        return out

    def _commit(self, tok, reads, writes):
        for k in reads:
            self.readers.setdefault(k, []).append(tok)
        for k in writes:
            self.last_w[k] = tok
            self.readers[k] = []

    @staticmethod
    def _psum_excl(reads, writes):
        r2, w2 = [], list(writes)
        for k in reads:
            if isinstance(k, tuple) and k and k[0] == "ps":
                w2.append(k)
            else:
                r2.append(k)
        return r2, w2

    def op(self, eng, fn, reads=(), writes=(), pe_sync=False):
        reads, writes = self._psum_excl(reads, writes)
        waits = self._deps(eng, reads, writes, pe_sync)
        seq = self.cnt[eng]
        self.cnt[eng] += 1
        tok = ("eng", eng, seq)
        self._emit(eng, waits, fn, ((eng, seq // self.EPOCH), 1))
        self._commit(tok, reads, writes)
        return tok

    def dma(self, out, in_, reads=(), writes=(), q="sync", **kw):
        if q == "pool" and "max_dma_last_dim" not in kw:
            kw["max_dma_last_dim"] = 2048
        s = self.dma_n % self.NDMA
        self.dma_n += 1
        waits = self._deps(q, reads, writes)
        prev = self.dma_tot[s]
        sk = ("dma", s)
        if prev and self.waited[q].get(sk, 0) < prev:
            self.waited[q][sk] = prev
            waits.append((sk, prev))
        self.dma_tot[s] = prev + 16
        tok = ("dma", s, prev + 16)
        fn = lambda e, out=out, in_=in_, kw=kw: e.dma_start(out=out, in_=in_, **kw)
        self._emit(q, waits, fn, (sk, 16))
        self._commit(tok, reads, writes)
        return tok

    def _emit(self, name, waits, fn, inc):
        nc = self.nc
        eng = {"sync": nc.sync, "pe": nc.tensor, "dve": nc.vector, "act": nc.scalar, "pool": nc.gpsimd}[name]
        for sk, v in waits:
            eng.wait_ge(self._sem(sk), v)
        fn(eng).then_inc(self._sem(inc[0]), inc[1])
        self.nops = getattr(self, "nops", 0) + 1 + len(waits)
        if not hasattr(self, "trace"):
            self.trace = {e: [] for e in ENGS}
        self.trace[name].append((list(waits), inc))

    def check_deadlock(self):
        tr = getattr(self, "trace", None)
        if tr is None:
            return
        val = {}
        pos = {e: 0 for e in ENGS}
        progress = True
        while progress:
            progress = False
            for e in ENGS:
                q = tr[e]
                while pos[e] < len(q):
                    waits, inc = q[pos[e]]
                    if all(val.get(sk, 0) >= v for sk, v in waits):
                        val[inc[0]] = val.get(inc[0], 0) + inc[1]
                        pos[e] += 1
                        progress = True
                    else:
                        break
        stuck = {e: (pos[e], len(tr[e])) for e in ENGS if pos[e] < len(tr[e])}
        if stuck:
            msg = []
            for e, (i, n) in stuck.items():
                waits, inc = tr[e][i]
                msg.append("%s stuck at %d/%d waiting %s (have %s)" % (
                    e, i, n, waits, [(sk, val.get(sk, 0)) for sk, v in waits]))
            raise RuntimeError("semaphore deadlock: " + "; ".join(msg))

    def build(self):
        nc = self.nc
        self.check_deadlock()
        for s in range(self.NDMA):
            if self.dma_tot[s]:
                nc.sync.wait_ge(self._sem(("dma", s)), self.dma_tot[s])
        for e in ENGS:
            if e != "sync" and self.cnt[e]:
                seq = self.cnt[e] - 1
                nc.sync.wait_ge(self._sem((e, seq // self.EPOCH)), seq % self.EPOCH + 1)
        self.stack.close()
        return nc


import concourse.bass_utils as _bu

D = 2048
KC = 16
T = 2048
L = 256
NB = 4
FF = 5504
FC = 43
EPS = 1e-6


class Banks:
    def __init__(self, p, n=8, prefix="psb"):
        self.t = [p.ps("%s%d" % (prefix, i), [128, 512], F32) for i in range(n)]
        self.keys = [("ps", prefix, i) for i in range(n)]
        self.i = 0

    def next(self):
        b = self.i % len(self.t)
        self.i += 1
        return self.t[b], self.keys[b]


def col_blocks(c0, c1, n=512):
    out = []
    while c0 < c1:
        out.append((c0, min(c1, c0 + n)))
        c0 += n
    return out


class Common:
    def __init__(self, p):
        self.ones = p.sb("c_ones", [128, 128], BF16)
        self.eps = p.sb("c_eps", [128, 1], F32)
        p.op("dve", lambda e: e.memset(self.ones[:], 1.0), writes=["c_ones"])
        p.op("dve", lambda e: e.memset(self.eps[:], EPS), writes=["c_eps"])


def load_small(p, name, shape, dram_ap, dt=F32):
    t = p.sb(name, shape, dt)
    p.dma(t[:], dram_ap, writes=[name])
    return t


def make_AB(p, name, modv, g, j_shift, j_scale, col):
    A = p.sb(name + "_A", [128, KC], F32)
    p.op("dve", lambda e: e.tensor_scalar(out=A[:], in0=modv[:, j_scale * KC:(j_scale + 1) * KC, col],
                                          scalar1=1.0, scalar2=None, op0=ALU.add),
         reads=["modv"], writes=[name + "_A"])
    p.op("dve", lambda e: e.tensor_tensor(out=A[:], in0=A[:], in1=g[:], op=ALU.mult),
         reads=[name + "_A", "gvec"], writes=[name + "_A"])
    return A


def rms_rstd(p, cm, banks, h, hkey, W, rstd, rkey, sq, nfeat_chunks=KC, dmodel=D, eps_ap=None):
    for (b0, b1) in col_blocks(0, W):
        ps, pk = banks.next()
        n = b1 - b0
        for c in range(nfeat_chunks):
            s = c % 2
            p.op("act", lambda e, c=c, s=s: e.activation(out=sq[:, s, 0:n], in_=h[:, c, b0:b1], func=AF.Square),
                 reads=[hkey], writes=[("sq", s)])
            p.op("pe", lambda e, c=c, s=s: e.matmul(ps[:, 0:n], lhsT=cm.ones[:], rhs=sq[:, s, 0:n],
                                                     start=(c == 0), stop=(c == nfeat_chunks - 1)),
                 reads=[("sq", s), "c_ones"], writes=[pk])
        p.op("act", lambda e: e.activation(out=rstd[:, b0:b1], in_=ps[:, 0:n], func=AF.Sqrt,
                                           bias=(eps_ap if eps_ap is not None else cm.eps)[:, 0:1], scale=1.0 / dmodel),
             reads=[pk, "c_eps"], writes=[rkey])
        p.op("dve", lambda e: e.reciprocal(out=rstd[:, b0:b1], in_=rstd[:, b0:b1]), reads=[rkey], writes=[rkey])


def norm_mod(p, cm, banks, h, hkey, W, segs, out_bf, okey, rstd, sq, tmp, mask=None):
    rms_rstd(p, cm, banks, h, hkey, W, rstd, "rstd", sq)
    for c in range(KC):
        s = c % 2
        p.op("dve", lambda e, c=c, s=s: e.tensor_tensor(out=tmp[:, s, 0:W], in0=h[:, c, 0:W], in1=rstd[:, 0:W], op=ALU.mult),
             reads=[hkey, "rstd"], writes=[("tmp", s)])
        for (c0, c1, A, Bfn) in segs:
            p.op("act", lambda e, c=c, s=s, c0=c0, c1=c1, A=A, Bfn=Bfn: e.activation(
                out=out_bf[:, c, c0:c1], in_=tmp[:, s, c0:c1], func=AF.Identity, scale=A[:, c:c + 1], bias=Bfn(c)),
                reads=[("tmp", s), "AB", "modv"], writes=[(okey, c)])
        if mask is not None:
            p.op("pool", lambda e, c=c: e.tensor_tensor(out=out_bf[:, c, 0:W], in0=out_bf[:, c, 0:W], in1=mask[:, 0:W], op=ALU.mult),
                 reads=[(okey, c), "mask"], writes=[(okey, c)])


WF = 1156
LAT0, LAT1 = 0, 1026
CTX0, CTX1 = 1026, 1156
FG = 4


def build_ffn(final):
    p = Prog()
    h_d = p.dram("h", [D, WF])
    y_d = p.dram("y", [2, D, WF])
    modv_d = p.dram("modv", [128, 6 * KC, 2])
    g2_d = p.dram("g2", [128, KC])
    mask_d = p.dram("mask", [128, WF])
    wg_d = p.dram("wg", [D, FF])
    wu_d = p.dram("wu", [D, FF])
    wd_d = p.dram("wd", [FF, D])
    cw_d = p.dram("cw", [128, FC, 3])
    cb_d = p.dram("cb", [128, FC])
    ho_d = p.dram("ho", [D, 1152], kind="ExternalOutput")
    if final:
        gf_d = p.dram("gf", [128, KC])
        of_d = p.dram("of", [D, 1024], kind="ExternalOutput")

    cm = Common(p)
    banks = Banks(p)
    h = p.sb("h", [128, KC, WF], F32)
    u = p.sb("u", [128, KC, WF], BF16)
    act = p.sb("actb", [128, FG, WF], BF16)
    gsb = p.sb("gsb", [128, 2, WF], F32)
    tsb = p.sb("tsb", [128, 2, WF], F32)
    sq = p.sb("sq", [128, 2, 512], BF16)
    rstd = p.sb("rstd", [128, WF], F32)
    modv = load_small(p, "modv", [128, 6 * KC, 2], modv_d)
    gvec = load_small(p, "gvec", [128, KC], g2_d)
    mask = load_small(p, "mask", [128, WF], mask_d)
    cw = load_small(p, "cw", [128, FC, 3], cw_d)
    cb = load_small(p, "cb", [128, FC], cb_d)
    gus = [p.sb("gus%d" % i, [128, KC, 256], BF16) for i in range(4)]
    evb = p.sb("evb", [128, 2, 512], F32)
    nev = [0]
    wds = p.sb("wds", [128, FG, D], BF16)

    hv = h_d.rearrange("(c p) w -> p c w", p=128)
    yv = y_d.rearrange("n (c p) w -> n p c w", p=128)
    for c4 in range(0, KC, 4):
        p.dma(h[:, c4:c4 + 4, :], hv[:, c4:c4 + 4, :], writes=[("h", c) for c in range(c4, c4 + 4)])
    for c in range(KC):
        for n in range(2):
            s = (c * 2 + n) % 2
            p.dma(gsb[:, s, :], yv[n, :, c, :], writes=[("gsb", s)])
            for (c0, c1, col) in ((LAT0, LAT1, 0), (CTX0, CTX1, 1)):
                p.op("dve", lambda e, c=c, s=s, c0=c0, c1=c1, col=col: e.scalar_tensor_tensor(
                    out=h[:, c, c0:c1], in0=gsb[:, s, c0:c1], scalar=modv[:, 2 * KC + c, col:col + 1],
                    in1=h[:, c, c0:c1], op0=ALU.mult, op1=ALU.add),
                    reads=[("gsb", s), "modv", ("h", c)], writes=[("h", c)])
    A_l = make_AB(p, "ffl", modv, gvec, 3, 4, 0)
    A_c = make_AB(p, "ffc", modv, gvec, 3, 4, 1)
    hkeys = [("h", c) for c in range(KC)]

    class HK:
        pass
    segs = [(LAT0, LAT1, A_l, lambda c: modv[:, 3 * KC + c, 0:1]), (CTX0, CTX1, A_c, lambda c: modv[:, 3 * KC + c, 1:2])]
    _norm_mod_chunked(p, cm, banks, h, W=WF, segs=segs, out_bf=u, okey="u", rstd=rstd, sq=sq, tmp=tsb, mask=mask,
                      keys_A=["ffl_A", "ffc_A"])

    wgv = wg_d.rearrange("(c p) f -> p c f", p=128)
    wuv = wu_d.rearrange("(c p) f -> p c f", p=128)
    wdv = wd_d.rearrange("(f p) d -> p f d", p=128)
    blocks = col_blocks(0, WF)
    dblocks = [(1, 513, 0), (513, 1025, 0), (1025, 1155, 1)]
    nslab = 0
    slab_of = {}
    ukeys = [("u", c) for c in range(KC)]

    def ensure_slab(fo):
        nonlocal nslab
        sidx = fo // 2
        if sidx in slab_of:
            return slab_of[sidx]
        i0 = (nslab % 2) * 2
        nslab += 1
        f0 = sidx * 256
        f1 = min(FF, f0 + 256)
        p.dma(gus[i0][:, :, 0:f1 - f0], wgv[:, :, f0:f1], writes=[("gus", i0)], q="pool")
        p.dma(gus[i0 + 1][:, :, 0:f1 - f0], wuv[:, :, f0:f1], writes=[("gus", i0 + 1)], q="pool")
        slab_of[sidx] = i0
        return i0

    ngroups = (FC + FG - 1) // FG
    for g in range(ngroups):
        fos = list(range(g * FG, min(FC, (g + 1) * FG)))
        for li_ in range(len(fos)):
            p.dma(wds[:, li_, :], wdv[:, fos[0] + li_, :], writes=["wds"], q="pool")
        for li, fo in enumerate(fos):
            i0 = ensure_slab(fo)
            off = (fo % 2) * 128
            s = fo % 2
            gps = []
            for (b0, b1) in blocks:
                ps, pk = banks.next()
                for c in range(KC):
                    p.op("pe", lambda e, ps=ps, c=c, b0=b0, b1=b1: e.matmul(
                        ps[:, 0:b1 - b0], lhsT=gus[i0][:, c, off:off + 128], rhs=u[:, c, b0:b1],
                        start=(c == 0), stop=(c == KC - 1)),
                        reads=[("gus", i0), ("u", c)], writes=[pk])
                p.op("act", lambda e, ps=ps, b0=b0, b1=b1: e.activation(out=gsb[:, s, b0:b1], in_=ps[:, 0:b1 - b0], func=AF.Copy),
                     reads=[pk], writes=[("gsb", s)])
            ups = []
            for (b0, b1) in blocks:
                ps, pk = banks.next()
                for c in range(KC):
                    p.op("pe", lambda e, ps=ps, c=c, b0=b0, b1=b1: e.matmul(
                        ps[:, 0:b1 - b0], lhsT=gus[i0 + 1][:, c, off:off + 128], rhs=u[:, c, b0:b1],
                        start=(c == 0), stop=(c == KC - 1)),
                        reads=[("gus", i0 + 1), ("u", c)], writes=[pk])
                ups.append((ps, pk, b0, b1))
            n = WF - 2
            p.op("dve", lambda e: e.tensor_scalar(out=tsb[:, s, 1:1 + n], in0=gsb[:, s, 0:n], scalar1=cw[:, fo, 0:1],
                                                  scalar2=None, op0=ALU.mult),
                 reads=[("gsb", s), "cw"], writes=[("tmp", s)])
            for k in (1, 2):
                p.op("dve", lambda e, k=k: e.scalar_tensor_tensor(out=tsb[:, s, 1:1 + n], in0=gsb[:, s, k:k + n],
                                                                  scalar=cw[:, fo, k:k + 1], in1=tsb[:, s, 1:1 + n],
                                                                  op0=ALU.mult, op1=ALU.add),
                     reads=[("gsb", s), "cw", ("tmp", s)], writes=[("tmp", s)])
            p.op("act", lambda e: e.activation(out=tsb[:, s, 1:1 + n], in_=tsb[:, s, 1:1 + n], func=AF.Silu,
                                               bias=cb[:, fo:fo + 1], scale=1.0),
                 reads=[("tmp", s), "cb"], writes=[("tmp", s)])
            for (ps, pk, b0, b1) in ups:
                a0 = max(b0, 1)
                a1 = min(b1, WF - 1)
                p.op("dve", lambda e, ps=ps, b0=b0, a0=a0, a1=a1: e.tensor_tensor(
                    out=act[:, li, a0:a1], in0=tsb[:, s, a0:a1], in1=ps[:, a0 - b0:a1 - b0], op=ALU.mult),
                    reads=[("tmp", s), pk], writes=[("act", li)])
        for m in range(KC):
            for (d0, d1, col) in dblocks:
                ps, pk = banks.next()
                for li in range(len(fos)):
                    p.op("pe", lambda e, ps=ps, li=li, d0=d0, d1=d1: e.matmul(
                        ps[:, 0:d1 - d0], lhsT=wds[:, li, m * 128:(m + 1) * 128], rhs=act[:, li, d0:d1],
                        start=(li == 0), stop=(li == len(fos) - 1)),
                        reads=["wds", ("act", li)], writes=[pk])
                if os.environ.get("FFN_EVAC", "dve") == "split":
                    es = nev[0] % 2
                    nev[0] += 1
                    p.op("act", lambda e: e.activation(out=evb[:, es, 0:d1 - d0], in_=ps[:, 0:d1 - d0], func=AF.Copy,
                                                       scale=modv[:, 5 * KC + m, col:col + 1]),
                         reads=[pk, "modv"], writes=[("evb", es)])
                    p.op("pool", lambda e: e.tensor_tensor(out=h[:, m, d0:d1], in0=h[:, m, d0:d1], in1=evb[:, es, 0:d1 - d0], op=ALU.add),
                         reads=[("evb", es), ("h", m)], writes=[("h", m)])
                else:
                    p.op("dve", lambda e, ps=ps, d0=d0, d1=d1, col=col: e.scalar_tensor_tensor(
                        out=h[:, m, d0:d1], in0=ps[:, 0:d1 - d0], scalar=modv[:, 5 * KC + m, col:col + 1],
                        in1=h[:, m, d0:d1], op0=ALU.mult, op1=ALU.add),
                        reads=[pk, "modv", ("h", m)], writes=[("h", m)])
    hov = ho_d.rearrange("(c p) w -> p c w", p=128)
    for c4 in range(0, KC, 4):
        ks = [("h", c) for c in range(c4, c4 + 4)]
        p.dma(hov[:, c4:c4 + 4, 0:1024], h[:, c4:c4 + 4, 1:1025], reads=ks)
        p.dma(hov[:, c4:c4 + 4, 1024:1152], h[:, c4:c4 + 4, 1027:1155], reads=ks)
    if final:
        gf = load_small(p, "gf", [128, KC], gf_d)
        rms_rstd_chunked(p, cm, banks, h, 1, 1025, rstd, sq)
        ofv = of_d.rearrange("(c p) w -> p c w", p=128)
        for c in range(KC):
            s = c % 2
            p.op("dve", lambda e, c=c, s=s: e.tensor_tensor(out=tsb[:, s, 1:1025], in0=h[:, c, 1:1025], in1=rstd[:, 1:1025], op=ALU.mult),
                 reads=[("h", c), "rstd"], writes=[("tmp", s)])
            p.op("act", lambda e, c=c, s=s: e.activation(out=tsb[:, s, 1:1025], in_=tsb[:, s, 1:1025], func=AF.Copy, scale=gf[:, c:c + 1]),
                 reads=[("tmp", s), "gf"], writes=[("tmp", s)])
            p.dma(ofv[:, c, :], tsb[:, s, 1:1025], reads=[("tmp", s)])
    return p.build()


def rms_rstd_chunked(p, cm, banks, h, w0, w1, rstd, sq, hname="h"):
    for (b0, b1) in col_blocks(w0, w1):
        ps, pk = banks.next()
        n = b1 - b0
        for c in range(KC):
            s = c % 2
            p.op("act", lambda e, c=c, s=s: e.activation(out=sq[:, s, 0:n], in_=h[:, c, b0:b1], func=AF.Square),
                 reads=[(hname, c)], writes=[("sq", s)])
            p.op("pe", lambda e, c=c, s=s: e.matmul(ps[:, 0:n], lhsT=cm.ones[:], rhs=sq[:, s, 0:n],
                                                     start=(c == 0), stop=(c == KC - 1)),
                 reads=[("sq", s), "c_ones"], writes=[pk])
        p.op("act", lambda e, ps=ps: e.activation(out=rstd[:, b0:b1], in_=ps[:, 0:n], func=AF.Sqrt,
                                                  bias=cm.eps[:, 0:1], scale=1.0 / D),
             reads=[pk, "c_eps"], writes=["rstd"])
        p.op("dve", lambda e: e.reciprocal(out=rstd[:, b0:b1], in_=rstd[:, b0:b1]), reads=["rstd"], writes=["rstd"])


def _norm_mod_chunked(p, cm, banks, h, W, segs, out_bf, okey, rstd, sq, tmp, mask, keys_A, hname="h", w0=0):
    rms_rstd_chunked(p, cm, banks, h, w0, W, rstd, sq, hname)
    for c in range(KC):
        s = c % 2
        p.op("dve", lambda e, c=c, s=s: e.tensor_tensor(out=tmp[:, s, w0:W], in0=h[:, c, w0:W], in1=rstd[:, w0:W], op=ALU.mult),
             reads=[(hname, c), "rstd"], writes=[("tmp", s)])
        for (c0, c1, A, Bfn) in segs:
            p.op("act", lambda e, c=c, s=s, c0=c0, c1=c1, A=A, Bfn=Bfn: e.activation(
                out=out_bf[:, c, c0:c1], in_=tmp[:, s, c0:c1], func=AF.Identity, scale=A[:, c:c + 1], bias=Bfn(c)),
                reads=[("tmp", s), "modv"] + keys_A, writes=[(okey, c)])
        if mask is not None:
            p.op("pool", lambda e, c=c: e.tensor_tensor(out=out_bf[:, c, w0:W], in0=out_bf[:, c, w0:W], in1=mask[:, w0:W], op=ALU.mult),
                 reads=[(okey, c), "mask"], writes=[(okey, c)])


def linear_fm(p, banks, name, x, xkey, kcin, wview, dout, blocks, evac, sw=512, q="pool", nbuf=2, wdt=BF16):
    slabs = [p.sb("%s_w%d" % (name, i), [128, kcin, sw], wdt) for i in range(nbuf)]
    ns = (dout + sw - 1) // sw
    for si in range(ns):
        sl = slabs[si % nbuf]
        sk = (name + "_w", si % nbuf)
        f0 = si * sw
        f1 = min(dout, f0 + sw)
        half = kcin // 2 if kcin >= 8 else kcin
        for k0 in range(0, kcin, half):
            p.dma(sl[:, k0:k0 + half, 0:f1 - f0], wview[:, k0:k0 + half, f0:f1], writes=[sk], q=q)
        for mi in range((f1 - f0) // 128):
            m = (f0 // 128) + mi
            for (b0, b1) in blocks:
                ps, pk = banks.next()
                for c in range(kcin):
                    p.op("pe", lambda e: e.matmul(ps[:, 0:b1 - b0], lhsT=sl[:, c, mi * 128:(mi + 1) * 128],
                                                  rhs=x[:, c, b0:b1], start=(c == 0), stop=(c == kcin - 1)),
                         reads=[sk, (xkey, c)], writes=[pk])
                evac(m, ps, pk, b0, b1)


def build_mod():
    p = Prog()
    wm_d = p.dram("wm", [D, 6144])
    bm_d = p.dram("bm", [128, 48])
    ct_d = p.dram("ct", [128, KC, 8])
    mo_d = p.dram("mo", [128, 48, 8], kind="ExternalOutput")
    banks = Banks(p)
    ct = load_small(p, "ct", [128, KC, 8], ct_d)
    bm = load_small(p, "bm", [128, 48], bm_d)
    cb16 = p.sb("cb16", [128, KC, 8], F32)
    mo = p.sb("mo", [128, 48, 8], F32)
    p.op("act", lambda e: e.activation(out=cb16[:], in_=ct[:], func=AF.Silu), reads=["ct"],
         writes=[("cb16", c) for c in range(KC)])

    def evac(m, ps, pk, b0, b1):
        p.op("dve", lambda e: e.tensor_scalar(out=mo[:, m, :], in0=ps[:, 0:8], scalar1=bm[:, m:m + 1], scalar2=None, op0=ALU.add),
             reads=[pk, "bm"], writes=["mo"])
    linear_fm(p, banks, "mod", cb16, "cb16", KC, wm_d.rearrange("(c p) f -> p c f", p=128), 6144, [(0, 8)], evac,
              q="sync", wdt=F32)
    p.dma(mo_d, mo[:], reads=["mo"])
    return p.build()


WP = 1184
P_L0, P_L1, P_C0, P_C1 = 0, 1040, 1040, 1184


def build_pool():
    p = Prog()
    h_d = p.dram("h", [D, WP])
    modv_d = p.dram("modv", [128, 6 * KC, 2])
    g1_d = p.dram("g1", [128, KC])
    mask_d = p.dram("mask", [1, WP])
    icnt_d = p.dram("icnt", [4, WP])
    pw_d = p.dram("pw", [4, 512, 512])
    pb_d = p.dram("pb", [128, KC])
    psc_d = p.dram("psc", [128, KC])
    y_d = p.dram("y", [D, 1152], kind="ExternalOutput")
    cm = Common(p)
    banks = Banks(p)
    h = p.sb("h", [128, KC, WP], F32)
    pbf = p.sb("pbf", [128, KC, WP], BF16)
    tsb = p.sb("tsb", [128, 2, WP], F32)
    t2 = p.sb("t2", [128, 2, WP], F32)
    sq = p.sb("sq", [128, 2, 512], BF16)
    rstd = p.sb("rstd", [128, WP], F32)
    yo = p.sb("yo", [128, 2, WP], F32)
    modv = load_small(p, "modv", [128, 6 * KC, 2], modv_d)
    gvec = load_small(p, "gvec", [128, KC], g1_d)
    pb = load_small(p, "pb", [128, KC], pb_d)
    psc = load_small(p, "psc", [128, KC], psc_d)
    mask = load_small(p, "mask", [128, WP], mask_d[0].partition_broadcast(128))
    icnt = p.sb("icnt", [128, 4, WP], F32)
    for g in range(4):
        p.dma(icnt[:, g, :], icnt_d[g].partition_broadcast(128), writes=["icnt"])
    hv = h_d.rearrange("(c p) w -> p c w", p=128)
    for c4 in range(0, KC, 4):
        p.dma(h[:, c4:c4 + 4, :], hv[:, c4:c4 + 4, :], writes=[("h", c) for c in range(c4, c4 + 4)])
    A_l = make_AB(p, "pl", modv, gvec, 0, 1, 0)
    A_c = make_AB(p, "pc", modv, gvec, 0, 1, 1)
    segs = [(P_L0, P_L1, A_l, lambda c: modv[:, c, 0:1]), (P_C0, P_C1, A_c, lambda c: modv[:, c, 1:2])]
    rms_rstd_chunked(p, cm, banks, h, 0, WP, rstd, sq)
    for c in range(KC):
        s = c % 2
        p.op("dve", lambda e: e.tensor_tensor(out=tsb[:, s, :], in0=h[:, c, :], in1=rstd[:], op=ALU.mult),
             reads=[("h", c), "rstd"], writes=[("tmp", s)])
        for (c0, c1, A, Bfn) in segs:
            p.op("act", lambda e: e.activation(out=h[:, c, c0:c1], in_=tsb[:, s, c0:c1], func=AF.Identity,
                                               scale=A[:, c:c + 1], bias=Bfn(c)),
                 reads=[("tmp", s), "modv", "pl_A", "pc_A"], writes=[("h", c)])
        p.op("pool", lambda e: e.tensor_tensor(out=h[:, c, :], in0=h[:, c, :], in1=mask[:], op=ALU.mult),
             reads=[("h", c), "mask"], writes=[("h", c)])
        g = c // 4
        win = (2, 4, 8, 16)[g]
        right = win - 1 - win // 2
        src, skey = h[:, c, :], ("h", c)
        sh = 1
        bufs = [tsb[:, s, :], t2[:, s, :]]
        bkeys = [("tmp", s), ("t2", s)]
        bi = 0
        while sh < win:
            dst, dkey = bufs[bi], bkeys[bi]
            p.op("dve", lambda e: e.tensor_tensor(out=dst[:, sh:WP], in0=src[:, sh:WP], in1=src[:, 0:WP - sh], op=ALU.add),
                 reads=[skey], writes=[dkey])
            p.op("dve", lambda e: e.tensor_copy(out=dst[:, 0:sh], in_=src[:, 0:sh]), reads=[skey], writes=[dkey])
            src, skey = dst, dkey
            sh *= 2
            bi ^= 1
        dst, dkey = bufs[bi], bkeys[bi]
        n = WP - 16
        p.op("dve", lambda e: e.tensor_tensor(out=dst[:, 8:8 + n], in0=src[:, 8 + right:8 + right + n],
                                              in1=icnt[:, g, 8:8 + n], op=ALU.mult),
             reads=[skey, "icnt"], writes=[dkey])
        p.op("dve", lambda e: e.tensor_tensor(out=pbf[:, c, 8:8 + n], in0=dst[:, 8:8 + n], in1=h[:, c, 8:8 + n], op=ALU.subtract),
             reads=[dkey, ("h", c)], writes=[("pbf", c)])
    yv = y_d.rearrange("(c p) w -> p c w", p=128)
    vblocks = [(8, 520), (520, 1032), (1048, 1176)]
    for g in range(4):
        wsl = p.sb("pw%d" % g, [128, 4, 512], BF16)
        p.dma(wsl[:], pw_d[g].rearrange("(c p) f -> p c f", p=128), writes=[("pw", g)], q="pool")
        for mo_ in range(4):
            m = g * 4 + mo_
            s = m % 2
            for (b0, b1) in vblocks:
                ps, pk = banks.next()
                for ci in range(4):
                    p.op("pe", lambda e: e.matmul(ps[:, 0:b1 - b0], lhsT=wsl[:, ci, mo_ * 128:(mo_ + 1) * 128],
                                                  rhs=pbf[:, g * 4 + ci, b0:b1], start=(ci == 0), stop=(ci == 3)),
                         reads=[("pw", g), ("pbf", g * 4 + ci)], writes=[pk])
                p.op("dve", lambda e: e.tensor_scalar(out=yo[:, s, b0:b1], in0=ps[:, 0:b1 - b0], scalar1=pb[:, m:m + 1],
                                                      scalar2=psc[:, m:m + 1], op0=ALU.add, op1=ALU.mult),
                     reads=[pk, "pb", "psc"], writes=[("yo", s)])
            p.dma(yv[:, m, 0:1024], yo[:, s, 8:1032], reads=[("yo", s)])
            p.dma(yv[:, m, 1024:1152], yo[:, s, 1048:1176], reads=[("yo", s)])
    return p.build()


def norm_mod_stream(p, cm, banks, hview, W, segs, out, okey, rstd, sq, hbuf, tmp, akeys, mask=None, out_chunks=None):
    blocks = col_blocks(0, W)
    pss = [banks.next() for _ in blocks]
    for c in range(KC):
        s = c % 2
        p.dma(hbuf[:, s, 0:W], hview[:, c, :], writes=[("hbuf", s)])
        for bi, (b0, b1) in enumerate(blocks):
            ps, pk = pss[bi]
            p.op("act", lambda e: e.activation(out=sq[:, s, 0:b1 - b0], in_=hbuf[:, s, b0:b1], func=AF.Square),
                 reads=[("hbuf", s)], writes=[("sq", s)])
            p.op("pe", lambda e: e.matmul(ps[:, 0:b1 - b0], lhsT=cm.ones[:], rhs=sq[:, s, 0:b1 - b0],
                                          start=(c == 0), stop=(c == KC - 1)),
                 reads=[("sq", s), "c_ones"], writes=[pk])
    for bi, (b0, b1) in enumerate(blocks):
        ps, pk = pss[bi]
        p.op("act", lambda e: e.activation(out=rstd[:, b0:b1], in_=ps[:, 0:b1 - b0], func=AF.Sqrt,
                                           bias=cm.eps[:, 0:1], scale=1.0 / D),
             reads=[pk, "c_eps"], writes=["rstd"])
        p.op("dve", lambda e: e.reciprocal(out=rstd[:, b0:b1], in_=rstd[:, b0:b1]), reads=["rstd"], writes=["rstd"])
    for c in range(KC):
        s = c % 2
        p.dma(hbuf[:, s, 0:W], hview[:, c, :], writes=[("hbuf", s)])
        p.op("dve", lambda e: e.tensor_tensor(out=tmp[:, s, 0:W], in0=hbuf[:, s, 0:W], in1=rstd[:, 0:W], op=ALU.mult),
             reads=[("hbuf", s), "rstd"], writes=[("tmp", s)])
        for (c0, c1, A, Bfn) in segs:
            p.op("act", lambda e: e.activation(out=out[:, c, c0:c1], in_=tmp[:, s, c0:c1], func=AF.Identity,
                                               scale=A[:, c:c + 1], bias=Bfn(c)),
                 reads=[("tmp", s), "modv"] + akeys, writes=[(okey, c)])
        if mask is not None:
            p.op("pool", lambda e: e.tensor_tensor(out=out[:, c, 0:W], in0=out[:, c, 0:W], in1=mask[:, 0:W], op=ALU.mult),
                 reads=[(okey, c), "mask"], writes=[(okey, c)])


def linear_fm2(p, banks, slabs, skey, x, xkey, kcin, wview, dout, blocks, evac, q="pool"):
    sw = slabs[0].shape[2]
    nbuf = len(slabs)
    ns = (dout + sw - 1) // sw
    for si in range(ns):
        sl = slabs[si % nbuf]
        sk = (skey, si % nbuf)
        f0 = si * sw
        f1 = min(dout, f0 + sw)
        half = kcin // 2 if kcin >= 8 else kcin
        for k0 in range(0, kcin, half):
            p.dma(sl[:, k0:k0 + half, 0:f1 - f0], wview[:, k0:k0 + half, f0:f1], writes=[sk], q=q)
        for mi in range((f1 - f0) // 128):
            m = (f0 // 128) + mi
            for (b0, b1) in blocks:
                ps, pk = banks.next()
                for c in range(kcin):
                    p.op("pe", lambda e: e.matmul(ps[:, 0:b1 - b0], lhsT=sl[:, c, mi * 128:(mi + 1) * 128],
                                                  rhs=x[:, c, b0:b1], start=(c == 0), stop=(c == kcin - 1)),
                         reads=[sk, (xkey, c)], writes=[pk])
                evac(m, ps, pk, b0, b1)


WG = 1152
NPC = 9


def build_gmlp():
    p = Prog()
    h_d = p.dram("h", [D, WG])
    modv_d = p.dram("modv", [128, 6 * KC, 2])
    g1_d = p.dram("g1", [128, KC])
    win_d = p.dram("win", [D, 2 * D])
    bzu_d = p.dram("bzu", [128, KC])
    bzv_d = p.dram("bzv", [1, D])
    ng_d = p.dram("ng", [1, D])
    wst_d = p.dram("wst", [128, 16, 128])
    bs_d = p.dram("bs", [1, 16 * 128])
    wo_d = p.dram("wo", [D, D])
    y_d = p.dram("y", [D, WG], kind="ExternalOutput")
    cm = Common(p)
    banks = Banks(p)
    aT = p.sb("aT", [128, KC, WG], BF16)
    zu = p.sb("zu", [128, KC, WG], BF16)
    zv = p.sb("zv", [128, NPC, D], BF16)
    hbuf = p.sb("hbuf", [128, 2, WG], F32)
    tmp = p.sb("tmp", [128, 2, WG], F32)
    sq = p.sb("sq", [128, 2, 512], BF16)
    rstd = p.sb("rstd", [128, WG], F32)
    slabs = [p.sb("slab%d" % i, [128, KC, 256], BF16) for i in range(2)]
    modv = load_small(p, "modv", [128, 6 * KC, 2], modv_d)
    gvec = load_small(p, "gvec", [128, KC], g1_d)
    bzu = load_small(p, "bzu", [128, KC], bzu_d)
    bzv = load_small(p, "bzv", [128, D], bzv_d[0].partition_broadcast(128))
    ng = load_small(p, "ng", [128, D], ng_d[0].partition_broadcast(128))
    bs = load_small(p, "bs", [128, 16 * 128], bs_d[0].partition_broadcast(128))
    wst = p.sb("wst", [128, 16, 128], BF16)
    p.dma(wst[:], wst_d, writes=["wst"], q="pool")
    ssq = p.sb("ssq", [128, NPC, 8], F32)
    rtok = p.sb("rtok", [128, NPC], F32)
    A_l = make_AB(p, "gl", modv, gvec, 0, 1, 0)
    A_c = make_AB(p, "gc", modv, gvec, 0, 1, 1)
    segs = [(0, 1024, A_l, lambda c: modv[:, c, 0:1]), (1024, WG, A_c, lambda c: modv[:, c, 1:2])]
    norm_mod_stream(p, cm, banks, h_d.rearrange("(c p) w -> p c w", p=128), WG, segs, aT, "aT", rstd, sq, hbuf, tmp,
                    ["gl_A", "gc_A"])
    blocks = col_blocks(0, WG)
    winv = win_d.rearrange("(c p) f -> p c f", p=128)
    import os
    stop = int(os.environ.get("GSTOP", "99"))
    if stop <= 0:
        return p.build()

    def evac_zu(m, ps, pk, b0, b1):
        p.op("act", lambda e: e.activation(out=zu[:, m, b0:b1], in_=ps[:, 0:b1 - b0], func=AF.Gelu_apprx_tanh,
                                           bias=bzu[:, m:m + 1], scale=1.0),
             reads=[pk, "bzu"], writes=[("zu", m)])
    linear_fm2(p, banks, slabs, "slab", aT, "aT", KC, winv[:, :, 0:D], D, blocks, evac_zu)

    if stop <= 1:
        return p.build()
    sw = 256
    for si in range(D // sw):
        sl = slabs[si % 2]
        sk = ("slab", si % 2)
        f0 = D + si * sw
        for k0 in (0, 8):
            p.dma(sl[:, k0:k0 + 8, :], winv[:, k0:k0 + 8, f0:f0 + sw], writes=[sk], q="pool")
        for n in range(NPC):
            ps, pk = banks.next()
            for c in range(KC):
                p.op("pe", lambda e: e.matmul(ps[:, 0:sw], lhsT=aT[:, c, n * 128:(n + 1) * 128], rhs=sl[:, c, :],
                                              start=(c == 0), stop=(c == KC - 1)),
                     reads=[sk, ("aT", c)], writes=[pk])
            s = (si * NPC + n) % 2
            p.op("dve", lambda e: e.tensor_tensor(out=tmp[:, s, 0:sw], in0=ps[:, 0:sw], in1=bzv[:, si * sw:(si + 1) * sw], op=ALU.add),
                 reads=[pk, "bzv"], writes=[("tmp", s)])
            p.op("act", lambda e: e.activation(out=tmp[:, s, 0:sw], in_=tmp[:, s, 0:sw], func=AF.Gelu_apprx_tanh),
                 reads=[("tmp", s)], writes=[("tmp", s)])
            p.op("act", lambda e: e.activation(out=tmp[:, s, 512:512 + sw], in_=tmp[:, s, 0:sw], func=AF.Square,
                                               accum_out=ssq[:, n, si:si + 1]),
                 reads=[("tmp", s)], writes=[("tmp", s), "ssq"])
            p.op("pool", lambda e: e.tensor_copy(out=zv[:, n, si * sw:(si + 1) * sw], in_=tmp[:, s, 0:sw]),
                 reads=[("tmp", s)], writes=[("zv", n)])
    if stop <= 2:
        return p.build()
    p.op("dve", lambda e: e.tensor_reduce(out=rtok[:], in_=ssq[:], axis=AX.X, op=ALU.add), reads=["ssq"], writes=["rtok"])
    p.op("act", lambda e: e.activation(out=rtok[:], in_=rtok[:], func=AF.Sqrt, bias=cm.eps[:, 0:1], scale=1.0 / D),
         reads=["rtok", "c_eps"], writes=["rtok"])
    p.op("dve", lambda e: e.reciprocal(out=rtok[:], in_=rtok[:]), reads=["rtok"], writes=["rtok"])
    for n in range(NPC):
        p.op("dve", lambda e: e.scalar_tensor_tensor(out=zv[:, n, :], in0=zv[:, n, :], scalar=rtok[:, n:n + 1], in1=ng[:],
                                                     op0=ALU.mult, op1=ALU.mult),
             reads=[("zv", n), "rtok", "ng"], writes=[("zv", n)])
    if stop <= 3:
        return p.build()
    bsv = bs[:].rearrange("p (g q) -> p g q", q=128)
    for n in range(NPC):
        for g4 in range(4):
            ps, pk = banks.next()
            for gi in range(4):
                g = g4 * 4 + gi
                p.op("pe", lambda e: e.matmul(ps[:, gi * 128:(gi + 1) * 128], lhsT=zv[:, n, g * 128:(g + 1) * 128],
                                              rhs=wst[:, g, :], start=True, stop=True),
                     reads=[("zv", n), "wst"], writes=[pk])
            s = (n * 4 + g4) % 2
            tv = tmp[:, s, 0:512].rearrange("p (g q) -> p g q", q=128)
            p.op("dve", lambda e: e.tensor_tensor(out=tv, in0=ps[:, 0:512].rearrange("p (g q) -> p g q", q=128),
                                                  in1=bsv[:, g4 * 4:(g4 + 1) * 4, :], op=ALU.add),
                 reads=[pk, "bs"], writes=[("tmp", s)])
            p.op("dve", lambda e: e.tensor_tensor(out=aT[:, g4 * 4:(g4 + 1) * 4, n * 128:(n + 1) * 128], in0=tv,
                                                  in1=zu[:, g4 * 4:(g4 + 1) * 4, n * 128:(n + 1) * 128], op=ALU.mult),
                 reads=[("tmp", s)] + [("zu", g4 * 4 + i) for i in range(4)],
                 writes=[("aT", g4 * 4 + i) for i in range(4)])
    if stop <= 4:
        return p.build()
    yv = y_d.rearrange("(c p) w -> p c w", p=128)
    cnt = [0]

    def evac_y(m, ps, pk, b0, b1):
        s = cnt[0] % 2
        cnt[0] += 1
        p.op("act", lambda e: e.activation(out=hbuf[:, s, b0:b1], in_=ps[:, 0:b1 - b0], func=AF.Copy),
             reads=[pk], writes=[("hbuf", s)])
        if os.environ.get("GNODMA") != "1":
            p.dma(yv[:, m, b0:b1], hbuf[:, s, b0:b1], reads=[("hbuf", s)], q="act")
    linear_fm2(p, banks, slabs, "slab", aT, "aT", KC, wo_d.rearrange("(c p) f -> p c f", p=128), D, blocks, evac_y)
    return p.build()


def build_na1():
    p = Prog()
    W = WG
    h_d = p.dram("h", [D, W])
    modv_d = p.dram("modv", [128, 6 * KC, 2])
    g1_d = p.dram("g1", [128, KC])
    w_d = p.dram("wqkv", [D, 3 * D])
    cos_d = p.dram("rcos", [128, 1024])
    sin_d = p.dram("rsin", [128, 1024])
    pm_d = p.dram("pm", [128, 128])
    q_d = p.dram("qT", [D, W], kind="ExternalOutput")
    k_d = p.dram("kT", [D, W], kind="ExternalOutput")
    v_d = p.dram("v", [W, D], kind="ExternalOutput")
    cm = Common(p)
    banks = Banks(p)
    aT = p.sb("aT", [128, KC, W], BF16)
    hbuf = p.sb("hbuf", [128, 2, W], F32)
    tmp = p.sb("tmp", [128, 2, W], F32)
    st = p.sb("st", [128, 2, W], F32)
    qb = p.sb("qb", [128, 2, 512], BF16)
    sq = p.sb("sq", [128, 2, 512], BF16)
    rstd = p.sb("rstd", [128, W], F32)
    slabs = [p.sb("slab%d" % i, [128, KC, 256], BF16) for i in range(2)]
    modv = load_small(p, "modv", [128, 6 * KC, 2], modv_d)
    gvec = load_small(p, "gvec", [128, KC], g1_d)
    rcos = load_small(p, "rcos", [128, 1024], cos_d)
    rsin = load_small(p, "rsin", [128, 1024], sin_d)
    pm = p.sb("pm", [128, 128], BF16)
    if os.environ.get("NOPM") != "1":
        p.dma(pm[:], pm_d, writes=["pm"], q="pool", max_dma_last_dim=int(os.environ.get("MDL", "512")))
    A_l = make_AB(p, "nl", modv, gvec, 0, 1, 0)
    A_c = make_AB(p, "ncx", modv, gvec, 0, 1, 1)
    segs = [(0, 1024, A_l, lambda c: modv[:, c, 0:1]), (1024, W, A_c, lambda c: modv[:, c, 1:2])]
    norm_mod_stream(p, cm, banks, h_d.rearrange("(c p) w -> p c w", p=128), W, segs, aT, "aT", rstd, sq, hbuf, tmp,
                    ["nl_A", "ncx_A"])
    wv = w_d.rearrange("(c p) f -> p c f", p=128)
    qv = q_d.rearrange("(c p) w -> p c w", p=128)
    kv = k_d.rearrange("(c p) w -> p c w", p=128)
    blocks = col_blocks(0, W)
    cnt = [0]
    stop = int(os.environ.get("GSTOP", "99"))
    if stop <= 0:
        return p.build()

    def evac_qk(m, ps, pk, b0, b1):
        isq = m < KC
        sc = 0.125 if isq else 1.0
        mm = m if isq else m - KC
        s = mm % 2
        n = b1 - b0
        ev = int(os.environ.get("EV", "9"))
        if b0 >= 1024 or ev == 0:
            p.op("act", lambda e: e.activation(out=st[:, s, b0:b1], in_=ps[:, 0:n], func=AF.Copy, scale=sc),
                 reads=[pk], writes=[("st", s, 2)])
        else:
            i = cnt[0] % 2
            cnt[0] += 1
            p.op("act", lambda e: e.activation(out=qb[:, i, 0:n], in_=ps[:, 0:n], func=AF.Copy, scale=sc),
                 reads=[pk], writes=[("qb", i)])
            ps2, pk2 = banks.next()
            p.op("pe", lambda e: e.matmul(ps2[:, 0:n], lhsT=pm[:], rhs=qb[:, i, 0:n], start=True, stop=True),
                 reads=["pm", ("qb", i)], writes=[pk2])
            if ev == 1:
                p.op("act", lambda e: e.activation(out=st[:, s, b0:b1], in_=ps2[:, 0:n], func=AF.Copy, scale=sc),
                     reads=[pk, pk2], writes=[("st", s, b0 // 512)])
                if b1 == W:
                    pass
                return
            p.op("dve", lambda e: e.scalar_tensor_tensor(out=st[:, s, b0:b1], in0=ps[:, 0:n], scalar=sc, in1=rcos[:, b0:b1],
                                                         op0=ALU.mult, op1=ALU.mult),
                 reads=[pk, "rcos"], writes=[("st", s, b0 // 512)])
            p.op("dve", lambda e: e.tensor_tensor(out=tmp[:, i, 0:n], in0=ps2[:, 0:n], in1=rsin[:, b0:b1], op=ALU.mult),
                 reads=[pk2, "rsin"], writes=[("tmp", i)])
            p.op("dve", lambda e: e.tensor_tensor(out=st[:, s, b0:b1], in0=st[:, s, b0:b1], in1=tmp[:, i, 0:n], op=ALU.add),
                 reads=[("tmp", i), ("st", s, b0 // 512)], writes=[("st", s, b0 // 512)])
        if b1 == W:
            dst = qv if isq else kv
            p.dma(dst[:, mm, :], st[:, s, :], reads=[("st", s, 0), ("st", s, 1), ("st", s, 2)], q="act")
    linear_fm2(p, banks, slabs, "slab", aT, "aT", KC, wv[:, :, 0:2 * D], 2 * D, blocks, evac_qk)
    if stop <= 1:
        return p.build()
    sw = 256
    for si in range(D // sw):
        sl = slabs[si % 2]
        sk = ("slab", si % 2)
        f0 = 2 * D + si * sw
        for k0 in (0, 8):
            p.dma(sl[:, k0:k0 + 8, :], wv[:, k0:k0 + 8, f0:f0 + sw], writes=[sk], q="pool")
        for n in range(NPC):
            ps, pk = banks.next()
            for c in range(KC):
                p.op("pe", lambda e: e.matmul(ps[:, 0:sw], lhsT=aT[:, c, n * 128:(n + 1) * 128], rhs=sl[:, c, :],
                                              start=(c == 0), stop=(c == KC - 1)),
                     reads=[sk, ("aT", c)], writes=[pk])
            s = (si * NPC + n) % 2
            p.op("act", lambda e: e.activation(out=tmp[:, s, 0:sw], in_=ps[:, 0:sw], func=AF.Copy),
                 reads=[pk], writes=[("tmp", s)])
            p.dma(v_d[n * 128:(n + 1) * 128, si * sw:(si + 1) * sw], tmp[:, s, 0:sw], reads=[("tmp", s)], q="act")
    return p.build()


NTOK = T + L


def na_rs(r):
    return min(max(r - 4, 0), 24)


def build_na2():
    p = Prog()
    q_d = p.dram("qT", [1024, NTOK])
    k_d = p.dram("kT", [1024, NTOK])
    v_d = p.dram("v", [NTOK, 1024])
    bt_d = p.dram("bt", [16, 128, 15 * 64])
    wo_d = p.dram("wo", [1024, D])
    y_d = p.dram("y", [D, NTOK], kind="ExternalOutput")
    sbanks = Banks(p, 4, "ps_s")
    abanks = Banks(p, 4, "ps_a")
    ones = p.sb("ones", [128, 64], BF16)
    p.op("dve", lambda e: e.memset(ones[:], 1.0), writes=["ones"])
    qT = p.sb("qT", [128, 8, NTOK], BF16)
    kT = p.sb("kT", [128, 8, NTOK], BF16)
    v = p.sb("v", [128, 18, 1024], BF16)
    at = p.sb("at", [128, 8, NTOK], BF16)
    bts = [p.sb("bt%d" % i, [128, 15 * 64], F32) for i in range(2)]
    pts = [p.sb("pt%d" % i, [128, 512], BF16) for i in range(4)]
    sbs = [p.sb("sbb%d" % i, [128, 512], F32) for i in range(2)]
    rec = p.sb("rec", [128, 2, 512], F32)
    st = p.sb("st", [128, 2, 512], F32)
    slabs = [p.sb("slab%d" % i, [128, 8, 512], BF16) for i in range(2)]
    qv = q_d.rearrange("(c p) w -> p c w", p=128)
    kvv = k_d.rearrange("(c p) w -> p c w", p=128)
    vv = v_d.rearrange("(n p) f -> p n f", p=128)
    for c in range(8):
        p.dma(qT[:, c, :], qv[:, c, :], writes=[("qT", c)], q="pool", max_dma_last_dim=4096)
        p.dma(kT[:, c, :], kvv[:, c, :], writes=[("kT", c)], q="pool", max_dma_last_dim=4096)
    for n in range(18):
        p.dma(v[:, n, :], vv[:, n, :], writes=[("v", n)], q="pool")
    npt = [0]
    nsb = [0]
    for h in range(16):
        c = h // 2
        base = (h % 2) * 64
        bt = bts[h % 2]
        bk = ("bt", h % 2)
        p.dma(bt[:], bt_d[h], writes=[bk])
        btv = bt[:].rearrange("p (i q) -> p i q", q=64)
        groups = [(g * 512, 512, g) for g in range(4)] + [(T, L, None)]
        for (q0, nq, g) in groups:
            O, ok = abanks.next()
            Dn, dk = abanks.next()
            items = []
            for j in range(2):
                items.append(("ctx", j))
            if g is not None:
                for kap in range(32):
                    rows = [r for r in range(8 * g, 8 * g + 8) if na_rs(r) <= kap < na_rs(r) + 8]
                    if rows:
                        items.append(("loc", kap, rows[0], rows[-1]))
            pend = None

            def stage2(st2):
                (kind, first, last, pt, ptk, args) = st2
                if kind == "ctx":
                    (j,) = args
                    p.op("pe", lambda e: e.matmul(O[base:base + 64, 0:nq], lhsT=v[:, 16 + j, h * 64:(h + 1) * 64],
                                                  rhs=pt[:, 0:nq], start=first, stop=last),
                         reads=[("v", 16 + j), ptk], writes=[ok], pe_sync=True)
                    p.op("pe", lambda e: e.matmul(Dn[base:base + 64, 0:nq], lhsT=ones[:, :], rhs=pt[:, 0:nq],
                                                  start=first, stop=last),
                         reads=["ones", ptk], writes=[dk], pe_sync=True)
                else:
                    (kap, kb, c0, nn) = args
                    p.op("pe", lambda e: e.matmul(O[base:base + 64, c0:c0 + nn], lhsT=v[kb:kb + 64, kap // 2, h * 64:(h + 1) * 64],
                                                  rhs=pt[kb:kb + 64, 0:nn], start=False, stop=last),
                         reads=[("v", kap // 2), ptk], writes=[ok], pe_sync=True)
                    p.op("pe", lambda e: e.matmul(Dn[base:base + 64, c0:c0 + nn], lhsT=ones[kb:kb + 64, :],
                                                  rhs=pt[kb:kb + 64, 0:nn], start=False, stop=last),
                         reads=["ones", ptk], writes=[dk], pe_sync=True)

            for ii, it in enumerate(items):
                first = ii == 0
                last = ii == len(items) - 1
                S, sk_ = sbanks.next()
                pt = pts[npt[0] % 4]
                ptk = ("pt", npt[0] % 4)
                npt[0] += 1
                if it[0] == "ctx":
                    j = it[1]
                    kc0 = T + j * 128
                    p.op("pe", lambda e: e.matmul(S[:, 0:nq], lhsT=kT[base:base + 64, c, kc0:kc0 + 128],
                                                  rhs=qT[base:base + 64, c, q0:q0 + nq], start=True, stop=True),
                         reads=[("kT", c), ("qT", c)], writes=[sk_])
                    p.op("act", lambda e: e.activation(out=pt[:, 0:nq], in_=S[:, 0:nq], func=AF.Exp),
                         reads=[sk_], writes=[ptk])
                    cur = ("ctx", first, last, pt, ptk, (j,))
                else:
                    _, kap, ra, rb = it
                    kb = (kap % 2) * 64
                    nr = rb - ra + 1
                    nn = nr * 64
                    c0 = ra * 64 - q0
                    idx0 = ra - kap + 7
                    sb_ = sbs[nsb[0] % 2]
                    sbk = ("sbb", nsb[0] % 2)
                    nsb[0] += 1
                    p.op("pe", lambda e: e.matmul(S[kb:kb + 64, 0:nn], lhsT=kT[base:base + 64, c, kap * 64:(kap + 1) * 64],
                                                  rhs=qT[base:base + 64, c, ra * 64:(rb + 1) * 64], start=True, stop=True),
                         reads=[("kT", c), ("qT", c)], writes=[sk_])
                    p.op("dve", lambda e: e.tensor_tensor(out=sb_[kb:kb + 64, 0:nn].rearrange("p (i q) -> p i q", q=64),
                                                          in0=S[kb:kb + 64, 0:nn].rearrange("p (i q) -> p i q", q=64),
                                                          in1=btv[kb:kb + 64, idx0:idx0 + nr, :], op=ALU.add),
                         reads=[sk_, bk], writes=[sbk])
                    p.op("act", lambda e: e.activation(out=pt[kb:kb + 64, 0:nn], in_=sb_[kb:kb + 64, 0:nn], func=AF.Exp),
                         reads=[sbk], writes=[ptk])
                    cur = ("loc", first, last, pt, ptk, (kap, kb, c0, nn))
                if pend is not None:
                    stage2(pend)
                pend = cur
            stage2(pend)
            ri = (h * 5 + (g if g is not None else 4)) % 2
            p.op("dve", lambda e: e.reciprocal(out=rec[base:base + 64, ri, 0:nq], in_=Dn[base:base + 64, 0:nq]),
                 reads=[dk], writes=[("rec", ri)])
            p.op("dve", lambda e: e.tensor_tensor(out=at[base:base + 64, c, q0:q0 + nq], in0=O[base:base + 64, 0:nq],
                                                  in1=rec[base:base + 64, ri, 0:nq], op=ALU.mult),
                 reads=[ok, ("rec", ri)], writes=[("at", c)])
    yv = y_d.rearrange("(c p) w -> p c w", p=128)
    cnt = [0]

    def evac_y(m, ps, pk, b0, b1):
        s = cnt[0] % 2
        cnt[0] += 1
        p.op("act", lambda e: e.activation(out=st[:, s, 0:b1 - b0], in_=ps[:, 0:b1 - b0], func=AF.Copy),
             reads=[pk], writes=[("st", s)])
        p.dma(yv[:, m, b0:b1], st[:, s, 0:b1 - b0], reads=[("st", s)], q="act")
    linear_fm2(p, sbanks, slabs, "slab", at, "at", 8, wo_d.rearrange("(c p) f -> p c f", p=128), D, col_blocks(0, NTOK), evac_y)
    return p.build()


def _chunked(vv):
    vv = np.asarray(vv, np.float32)
    return np.ascontiguousarray(vv.reshape(-1, 128).T)


def rope_tables(t0, n):
    half = 32
    inv_freq = (10000.0 ** (-np.arange(0, half, 2, dtype=np.float32) / half)).astype(np.float32)
    pos = np.arange(t0, t0 + n)
    row = (pos // 64).astype(np.float32)
    col = (pos % 64).astype(np.float32)
    cos = np.zeros((64, n), np.float32)
    sin = np.zeros((64, n), np.float32)
    for d in range(64):
        pp = row if d < 32 else col
        ang = pp * inv_freq[d % 16]
        cos[d] = np.cos(ang)
        sgn = -1.0 if (d % 32) < 16 else 1.0
        sin[d] = sgn * np.sin(ang)
    return np.concatenate([cos, cos], 0), np.concatenate([sin, sin], 0)


def rope_perm():
    pm = np.zeros((128, 128), np.float32)
    for m in range(128):
        d = m % 32
        partner = m + 16 if d < 16 else m - 16
        pm[partner, m] = 1.0
    return pm


def na_bias_tables(rpb, heads):
    cq = np.arange(64)
    ck = np.arange(64)
    cs = np.clip(cq - 8, 0, 48)
    ok = (ck[:, None] >= cs[None, :]) & (ck[:, None] < cs[None, :] + 16)
    dc = np.clip(ck[:, None] - cq[None, :], -15, 15) + 15
    out = np.empty((len(heads), 128, 15, 64), np.float32)
    for i, h in enumerate(heads):
        for idx in range(15):
            tab = rpb[h, 14 - idx][dc]
            tab = np.where(ok, tab, np.float32(-30000.0))
            out[i, 0:64, idx] = tab
            out[i, 64:128, idx] = tab
    return out.reshape(len(heads), 128, 15 * 64)


RW_LORA = 96
RW_GATE = 256
C64 = 64


def shift_mix(p, a, akey_fn, xo, xkey, coef, n_idx, j0, n):
    for c in range(KC):
        p.op("dve", lambda e: e.tensor_scalar(out=xo[:, c, 0:n], in0=a[:, c, j0:j0 + n], scalar1=coef[:, 0, n_idx, c:c + 1],
                                              scalar2=None, op0=ALU.mult),
             reads=[akey_fn(c), "coef"], writes=[(xkey, c)])
        p.op("dve", lambda e: e.scalar_tensor_tensor(out=xo[:, c, 0:n], in0=a[:, c, j0 - 1:j0 - 1 + n], scalar=coef[:, 1, n_idx, c:c + 1],
                                                     in1=xo[:, c, 0:n], op0=ALU.mult, op1=ALU.add),
             reads=[akey_fn(c), "coef", (xkey, c)], writes=[(xkey, c)])
        p.op("dve", lambda e: e.scalar_tensor_tensor(out=xo[:, c, 0:n], in0=a[:, c, j0 + 1:j0 + 1 + n], scalar=coef[:, 2, n_idx, c:c + 1],
                                                     in1=xo[:, c, 0:n], op0=ALU.mult, op1=ALU.add),
             reads=[akey_fn(c), "coef", (xkey, c)], writes=[(xkey, c)])


def load_coef(p, coef_d):
    coef = p.sb("coef", [128, 3, 6, KC], F32)
    p.dma(coef[:, 1:3, :, :], coef_d, writes=["coef"])
    p.op("dve", lambda e: e.tensor_scalar(out=coef[:, 0, :, :], in0=coef[:, 1, :, :], scalar1=-1.0, scalar2=1.0, op0=ALU.mult, op1=ALU.add),
         reads=["coef"], writes=["coef"])
    p.op("dve", lambda e: e.tensor_tensor(out=coef[:, 0, :, :], in0=coef[:, 0, :, :], in1=coef[:, 2, :, :], op=ALU.subtract),
         reads=["coef"], writes=["coef"])
    return coef


def build_rw1():
    p = Prog()
    W = WF
    h_d = p.dram("h", [D, W])
    modv_d = p.dram("modv", [128, 6 * KC, 2])
    g1_d = p.dram("g1", [128, KC])
    mask_d = p.dram("mask", [1, W])
    coef_d = p.dram("coef", [128, 2, 6, KC])
    wrkv_d = p.dram("wrkv", [3, D, D])
    w1_d = p.dram("w1", [2, D, RW_LORA])
    w2_d = p.dram("w2", [2, RW_LORA, D])
    a1_d = p.dram("a1", [2, D, RW_LORA])
    a2_d = p.dram("a2", [2, RW_LORA, D])
    vecs_d = p.dram("vecs", [128, 7, KC])
    bones_d = p.dram("bones", [128, 128])
    rmask_d = p.dram("rmask", [1, 512])
    outs = {}
    outs["vt"] = p.dram("vt", [D, 1152], BF16, kind="ExternalOutput")
    for d in range(2):
        for nm in ("at", "bt", "kt", "rt"):
            outs[(nm, d)] = p.dram("%s%d" % (nm, d), [D, 1152], BF16, kind="ExternalOutput")
    bv_d = [p.dram("bv%d" % d, [D, 1152], kind="ExternalOutput") for d in range(2)]
    gam_d = [p.dram("gam%d" % d, [D, 18], kind="ExternalOutput") for d in range(2)]
    cm = Common(p)
    banks = Banks(p)
    a = p.sb("a", [128, KC, W], BF16)
    hbuf = p.sb("hbuf", [128, 2, W], F32)
    tmp = p.sb("tmp", [128, 2, W], F32)
    sq = p.sb("sq", [128, 2, 512], BF16)
    rstd = p.sb("rstd", [128, W], F32)
    modv = load_small(p, "modv", [128, 6 * KC, 2], modv_d)
    gvec = load_small(p, "gvec", [128, KC], g1_d)
    mask = load_small(p, "mask", [128, W], mask_d[0].partition_broadcast(128))
    coef = load_coef(p, coef_d)
    vecs = load_small(p, "vecs", [128, 7, KC], vecs_d)
    rmask = load_small(p, "rmask", [128, 512], rmask_d[0].partition_broadcast(128))
    bones = p.sb("bones", [128, 128], BF16)
    p.dma(bones[:], bones_d, writes=["bones"], q="pool", max_dma_last_dim=512)
    w1 = [p.sb("w1_%d" % d, [128, KC, RW_LORA], BF16) for d in range(2)]
    a1 = [p.sb("a1_%d" % d, [128, KC, RW_LORA], BF16) for d in range(2)]
    w2 = [p.sb("w2_%d" % d, [RW_LORA, D], BF16) for d in range(2)]
    a2 = [p.sb("a2_%d" % d, [RW_LORA, D], BF16) for d in range(2)]
    for d in range(2):
        p.dma(w1[d][:], w1_d[d].rearrange("(c p) f -> p c f", p=128), writes=[("w1", d)], q="pool")
        p.dma(a1[d][:], a1_d[d].rearrange("(c p) f -> p c f", p=128), writes=[("a1", d)], q="pool")
        p.dma(w2[d][:], w2_d[d], writes=[("w2", d)], q="pool")
        p.dma(a2[d][:], a2_d[d], writes=[("a2", d)], q="pool")
    A_l = make_AB(p, "rl", modv, gvec, 0, 1, 0)
    A_c = make_AB(p, "rc", modv, gvec, 0, 1, 1)
    segs = [(LAT0, LAT1, A_l, lambda c: modv[:, c, 0:1]), (CTX0, CTX1, A_c, lambda c: modv[:, c, 1:2])]
    norm_mod_stream(p, cm, banks, h_d.rearrange("(c p) w -> p c w", p=128), W, segs, a, "a", rstd, sq, hbuf, tmp,
                    ["rl_A", "rc_A"], mask=mask)
    akey = lambda c: ("a", c)
    xs = {nm: p.sb("x_" + nm, [128, KC, 512], BF16) for nm in ("k", "v", "r")}
    tw = [p.sb("tw%d" % d, [RW_LORA, 512], BF16) for d in range(2)]
    ta = [p.sb("ta%d" % d, [RW_LORA, 512], BF16) for d in range(2)]
    NT_ = 14
    ft = [hbuf[:, 0, 0:512], hbuf[:, 0, 512:1024], hbuf[:, 1, 0:512], hbuf[:, 1, 512:1024],
          tmp[:, 0, 0:512], tmp[:, 0, 512:1024], tmp[:, 1, 0:512], tmp[:, 1, 512:1024]]
    ft += [p.sb("ft%d" % i, [128, 512], F32)[:] for i in range(8, NT_)]
    fk = [("ft", i) for i in range(NT_)]
    fence = p.sb("fence", [128, 1], F32)
    p.op("dve", lambda e: e.memset(fence[:], 0.0), reads=[("hbuf", 0), ("hbuf", 1), ("tmp", 0), ("tmp", 1)], writes=fk[0:8])
    ob = {nm: p.sb("ob_" + nm, [128, 2, 512], BF16) for nm in ("at", "bt", "kt", "rt", "vt")}
    sqb = p.sb("sqb", [128, 2, 512], BF16)
    gam = [p.sb("gamt%d" % d, [128, KC, 18], F32) for d in range(2)]
    slabs = {nm: [p.sb("sl_%s%d" % (nm, i), [128, KC, 128], BF16) for i in range(2)] for nm in ("r", "k", "v")}
    widx = {"r": 0, "k": 1, "v": 2}
    wv = wrkv_d.rearrange("n (c p) f -> n p c f", p=128)
    cblocks = [(1, 512, 0), (513, 512, 512), (1027, 128, 1024)]
    nslab = [0]
    for (j0, n, o0) in cblocks:
        nch = n // C64
        for (kind, n_idx, fn) in (("w", 1, AF.Tanh), ("a", 4, AF.Copy)):
            shift_mix(p, a, akey, xs["r"], "x_r", coef, n_idx, j0, n)
            for d in range(2):
                lw, lkey = (w1[d], ("w1", d)) if kind == "w" else (a1[d], ("a1", d))
                lt, ltkey = (tw[d], ("tw", d)) if kind == "w" else (ta[d], ("ta", d))
                ps, pk = banks.next()
                for c in range(KC):
                    p.op("pe", lambda e: e.matmul(ps[0:RW_LORA, 0:n], lhsT=lw[:, c, :], rhs=xs["r"][:, c, 0:n],
                                                  start=(c == 0), stop=(c == KC - 1)),
                         reads=[lkey, ("x_r", c)], writes=[pk])
                p.op("act", lambda e: e.activation(out=lt[:, 0:n], in_=ps[0:RW_LORA, 0:n], func=fn), reads=[pk], writes=[ltkey])
        shift_mix(p, a, akey, xs["k"], "x_k", coef, 2, j0, n)
        shift_mix(p, a, akey, xs["v"], "x_v", coef, 3, j0, n)
        shift_mix(p, a, akey, xs["r"], "x_r", coef, 0, j0, n)
        for m in range(KC):
            cur = {}
            for nm in ("r", "k", "v"):
                i = nslab[0] % 2
                sl = slabs[nm][i]
                sk = ("sl_" + nm, i)
                for k0 in (0, 8):
                    p.dma(sl[:, k0:k0 + 8, :], wv[widx[nm], :, k0:k0 + 8, m * 128:(m + 1) * 128], writes=[sk], q="pool")
                cur[nm] = (sl, sk)
            nslab[0] += 1
            s2 = m % 2
            T_ = lambda i: ft[i][:, 0:n]
            pss = {}
            for nm in ("k", "v", "r"):
                ps, pk = banks.next()
                sl, sk = cur[nm]
                for c in range(KC):
                    p.op("pe", lambda e: e.matmul(ps[:, 0:n], lhsT=sl[:, c, :], rhs=xs[nm][:, c, 0:n],
                                                  start=(c == 0), stop=(c == KC - 1)),
                         reads=[sk, ("x_" + nm, c)], writes=[pk])
                pss[nm] = (ps, pk)
            p.op("act", lambda e: e.activation(out=T_(0), in_=pss["k"][0][:, 0:n], func=AF.Copy), reads=[pss["k"][1]], writes=[fk[0]])
            p.op("act", lambda e: e.activation(out=T_(1), in_=pss["v"][0][:, 0:n], func=AF.Copy), reads=[pss["v"][1]], writes=[fk[1]])
            p.op("act", lambda e: e.activation(out=T_(2), in_=pss["r"][0][:, 0:n], func=AF.Copy), reads=[pss["r"][1]], writes=[fk[2]])
            p.op("pool", lambda e: e.tensor_copy(out=ob["vt"][:, s2, 0:n], in_=T_(1)), reads=[fk[1]], writes=[("ob_vt", s2)])
            ovv = outs["vt"].rearrange("(c p) w -> p c w", p=128)
            p.dma(ovv[:, m, o0:o0 + n], ob["vt"][:, s2, 0:n], reads=[("ob_vt", s2)])
            p.op("dve", lambda e: e.tensor_scalar(out=T_(5), in0=T_(0), scalar1=vecs[:, 4, m:m + 1], scalar2=None, op0=ALU.mult),
                 reads=[fk[0], "vecs"], writes=[fk[5]])
            p.op("act", lambda e: e.activation(out=sqb[:, 0, 0:n], in_=T_(5), func=AF.Square), reads=[fk[5]], writes=[("sqb", 0)])
            ps_n, pk_n = banks.next()
            p.op("pe", lambda e: e.matmul(ps_n[:, 0:n], lhsT=bones[:], rhs=sqb[:, 0, 0:n], start=True, stop=True),
                 reads=["bones", ("sqb", 0)], writes=[pk_n])
            p.op("act", lambda e: e.activation(out=T_(6), in_=ps_n[:, 0:n], func=AF.Sqrt), reads=[pk_n], writes=[fk[6]])
            p.op("dve", lambda e: e.tensor_scalar(out=T_(6), in0=T_(6), scalar1=1e-6, scalar2=None, op0=ALU.max),
                 reads=[fk[6]], writes=[fk[6]])
            p.op("dve", lambda e: e.reciprocal(out=T_(6), in_=T_(6)), reads=[fk[6]], writes=[fk[6]])
            p.op("dve", lambda e: e.tensor_tensor(out=T_(5), in0=T_(5), in1=T_(6), op=ALU.mult), reads=[fk[5], fk[6]], writes=[fk[5]])
            for d in range(2):
                ps_s, pk_s = banks.next()
                p.op("pe", lambda e: e.matmul(ps_s[:, 0:n], lhsT=w2[d][:, m * 128:(m + 1) * 128], rhs=tw[d][:, 0:n], start=True, stop=True),
                     reads=[("w2", d), ("tw", d)], writes=[pk_s])
                ps_a, pk_a = banks.next()
                p.op("pe", lambda e: e.matmul(ps_a[:, 0:n], lhsT=a2[d][:, m * 128:(m + 1) * 128], rhs=ta[d][:, 0:n], start=True, stop=True),
                     reads=[("a2", d), ("ta", d)], writes=[pk_a])
                p.op("act", lambda e: e.activation(out=T_(3), in_=ps_s[:, 0:n], func=AF.Sigmoid, bias=vecs[:, 0 + d, m:m + 1], scale=1.0),
                     reads=[pk_s, "vecs"], writes=[fk[3]])
                p.op("act", lambda e: e.activation(out=T_(4), in_=ps_a[:, 0:n], func=AF.Sigmoid, bias=vecs[:, 2 + d, m:m + 1], scale=1.0),
                     reads=[pk_a, "vecs"], writes=[fk[4]])
                p.op("dve", lambda e: e.tensor_scalar(out=T_(3), in0=T_(3), scalar1=-0.6065306597126334, scalar2=None, op0=ALU.mult),
                     reads=[fk[3]], writes=[fk[3]])
                p.op("dve", lambda e: e.tensor_tensor_scan(out=T_(7), data0=rmask[:, 0:n], data1=T_(3), initial=0.0,
                                                           op0=ALU.mult, op1=ALU.add),
                     reads=["rmask", fk[3]], writes=[fk[7]])
                if d == 0:
                    p.op("dve", lambda e: e.tensor_tensor(out=T_(8), in0=T_(7), in1=T_(3), op=ALU.subtract), reads=[fk[7], fk[3]], writes=[fk[8]])
                    p.op("pool", lambda e: e.tensor_copy(out=gam[d][:, m, o0 // C64:o0 // C64 + nch],
                                                         in_=ft[7][:, 0:n].rearrange("p (a b) -> p a b", b=C64)[:, :, C64 - 1]),
                         reads=[fk[7]], writes=[("gam", d)])
                else:
                    P3 = ft[7][:, 0:n].rearrange("p (a b) -> p a b", b=C64)
                    p.op("pool", lambda e: e.tensor_copy(out=gam[d][:, m, o0 // C64:o0 // C64 + nch], in_=P3[:, :, C64 - 1]),
                         reads=[fk[7]], writes=[("gam", d)])
                    tot = gam[d][:, m, o0 // C64:o0 // C64 + nch].unsqueeze(2).broadcast_to([128, nch, C64])
                    p.op("dve", lambda e: e.tensor_tensor(out=ft[8][:, 0:n].rearrange("p (a b) -> p a b", b=C64), in0=tot, in1=P3, op=ALU.subtract),
                         reads=[fk[7], ("gam", d)], writes=[fk[8]])
                    p.op("dve", lambda e: e.tensor_tensor(out=T_(7), in0=T_(8), in1=T_(3), op=ALU.add), reads=[fk[8], fk[3]], writes=[fk[7]])
                p.op("act", lambda e: e.activation(out=T_(8), in_=T_(8), func=AF.Exp), reads=[fk[8]], writes=[fk[8]])
                p.op("act", lambda e: e.activation(out=T_(9), in_=T_(7), func=AF.Exp, scale=-1.0), reads=[fk[7]], writes=[fk[9]])
                p.op("act", lambda e: e.activation(out=T_(7), in_=T_(7), func=AF.Exp), reads=[fk[7]], writes=[fk[7]])
                p.op("dve", lambda e: e.tensor_scalar(out=T_(10), in0=T_(4), scalar1=-1.0, scalar2=vecs[:, 5, m:m + 1], op0=ALU.add, op1=ALU.mult),
                     reads=[fk[4], "vecs"], writes=[fk[10]])
                p.op("dve", lambda e: e.scalar_tensor_tensor(out=T_(10), in0=T_(10), scalar=1.0, in1=T_(0), op0=ALU.add, op1=ALU.mult),
                     reads=[fk[10], fk[0]], writes=[fk[10]])
                p.op("dve", lambda e: e.scalar_tensor_tensor(out=ob["at"][:, d, 0:n], in0=T_(5), scalar=-1.0, in1=T_(8), op0=ALU.mult, op1=ALU.mult),
                     reads=[fk[5], fk[8]], writes=[("ob_at", d)])
                p.op("dve", lambda e: e.tensor_tensor(out=T_(11), in0=T_(5), in1=T_(4), op=ALU.mult), reads=[fk[5], fk[4]], writes=[fk[11]])
                p.op("dve", lambda e: e.tensor_tensor(out=ob["bt"][:, d, 0:n], in0=T_(11), in1=T_(9), op=ALU.mult),
                     reads=[fk[11], fk[9]], writes=[("ob_bt", d)])
                p.op("dve", lambda e: e.tensor_tensor(out=ob["kt"][:, d, 0:n], in0=T_(10), in1=T_(9), op=ALU.mult),
                     reads=[fk[10], fk[9]], writes=[("ob_kt", d)])
                p.op("dve", lambda e: e.tensor_tensor(out=ob["rt"][:, d, 0:n], in0=T_(2), in1=T_(7), op=ALU.mult),
                     reads=[fk[2], fk[7]], writes=[("ob_rt", d)])
                p.op("dve", lambda e: e.scalar_tensor_tensor(out=sqb[:, 1, 0:n], in0=T_(2), scalar=vecs[:, 6, m:m + 1], in1=T_(10), op0=ALU.mult, op1=ALU.mult),
                     reads=[fk[2], fk[10], "vecs"], writes=[("sqb", 1)])
                ps_b, pk_b = banks.next()
                p.op("pe", lambda e: e.matmul(ps_b[:, 0:n], lhsT=bones[:], rhs=sqb[:, 1, 0:n], start=True, stop=True),
                     reads=["bones", ("sqb", 1)], writes=[pk_b])
                i12 = 12 + d
                p.op("dve", lambda e: e.tensor_tensor(out=T_(i12), in0=ps_b[:, 0:n], in1=T_(1), op=ALU.mult), reads=[pk_b, fk[1]], writes=[fk[i12]])
                for nm in ("at", "bt", "kt", "rt"):
                    ov = outs[(nm, d)].rearrange("(c p) w -> p c w", p=128)
                    p.dma(ov[:, m, o0:o0 + n], ob[nm][:, d, 0:n], reads=[("ob_" + nm, d)])
                bvv = bv_d[d].rearrange("(c p) w -> p c w", p=128)
                p.dma(bvv[:, m, o0:o0 + n], T_(i12), reads=[fk[i12]])
    for d in range(2):
        p.op("act", lambda e: e.activation(out=gam[d][:], in_=gam[d][:], func=AF.Exp), reads=[("gam", d)], writes=[("gam", d)])
        p.dma(gam_d[d].rearrange("(c p) w -> p c w", p=128), gam[d][:], reads=[("gam", d)])
    return p.build()


NCH = NTOK // C64
RW2_MASK_ENG = os.environ.get("RW2_MASK_ENG", "dve")
NHG = 4


def build_rw2():
    p = Prog()
    far_d = p.dram("far", [NCH, 64, 32 * 2 * 64], BF16)
    fbk_d = p.dram("fbk", [NCH, 64, 2 * 32 * 64], BF16)
    tm_d = p.dram("tm", [NCH, 64, 3 * D], BF16)
    gam_d = p.dram("gam", [64, 32, NCH])
    mka_d = p.dram("mka", [64, 2 * 64])
    mkp_d = p.dram("mkp", [64, 2 * 64])
    mkl_d = p.dram("mkl", [64, 7 * 64])
    y_d = p.dram("y", [NCH, 64, 32 * 64], kind="ExternalOutput")
    banks = Banks(p)
    FAR = [p.sb("far%d" % i, [64, 32, 2, 64], BF16) for i in range(2)]
    FBK = [p.sb("fbk%d" % i, [64, 2, 32, 64], BF16) for i in range(2)]
    TM = [p.sb("tm%d" % i, [64, 3, D], BF16) for i in range(2)]
    gam = load_small(p, "gam", [64, 32, NCH], gam_d)
    mka = load_small(p, "mka", [64, 2, 64], mka_d.rearrange("p (a b) -> p a b", b=64))
    mkp = load_small(p, "mkp", [64, 2, 64], mkp_d.rearrange("p (a b) -> p a b", b=64))
    mkl = load_small(p, "mkl", [64, 7, 64], mkl_d.rearrange("p (a b) -> p a b", b=64))
    mklb = p.sb("mklb", [64, 7, 64], BF16)
    p.op("dve", lambda e: e.tensor_copy(out=mklb[:], in_=mkl[:]), reads=["mkl"], writes=["mklb"])
    S = p.sb("S", [64, 32, 64], F32)
    Sb = p.sb("Sb", [64, 32, 64], BF16)
    Sl = p.sb("Sl", [64, 32, 64], BF16)
    Sd = p.sb("Sd", [64, 32, 64], F32)
    yst = [p.sb("yst%d" % i, [64, 32, 64], F32) for i in range(2)]
    QA = [p.sb("QA%d" % g, [64, 8, 2, 64], BF16) for g in range(NHG)]
    QK = [p.sb("QK%d" % g, [64, 8, 2, 64], BF16) for g in range(NHG)]
    NM = [[p.sb("NM%d_%d" % (g, l), [64, 8, 64], BF16) for l in range(6)] for g in range(NHG)]
    Gf = [p.sb("Gf%d" % g, [64, 8, 64], F32) for g in range(NHG)]
    Hf = [p.sb("Hf%d" % g, [64, 8, 64], F32) for g in range(NHG)]
    Gb = [p.sb("Gb%d" % g, [64, 8, 64], BF16) for g in range(NHG)]
    Hb = [p.sb("Hb%d" % g, [64, 8, 64], BF16) for g in range(NHG)]
    Wb = [p.sb("Wb%d" % g, [64, 8, 64], BF16) for g in range(NHG)]
    Rb = [p.sb("Rb%d" % g, [64, 8, 64], BF16) for g in range(NHG)]
    Xb = [p.sb("Xb%d" % g, [64, 8, 64], BF16) for g in range(NHG)]
    p.op("dve", lambda e: e.memset(S[:], 0.0), writes=[("S", g) for g in range(NHG)])
    p.op("dve", lambda e: e.memset(Sb[:], 0.0), writes=[("Sb", g) for g in range(NHG)])
    p.op("dve", lambda e: e.memset(Sl[:], 0.0), writes=[("Sl", g) for g in range(NHG)])
    v3 = lambda ap: ap.rearrange("p (a b) -> p a b", b=64)
    bc8 = lambda ap2: ap2.unsqueeze(1).broadcast_to([64, 8, 64])
    for ci in range(NCH):
        s = ci % 2
        far, fbk, tm = FAR[s], FBK[s], TM[s]
        p.dma(far[:].rearrange("p a b c -> p (a b c)"), far_d[ci], writes=[("far", s)])
        p.dma(fbk[:].rearrange("p a b c -> p (a b c)"), fbk_d[ci], writes=[("fbk", s)])
        p.dma(tm[:].rearrange("p a b -> p (a b)"), tm_d[ci], writes=[("tm", s)])
        kfar, kfbk, ktm = ("far", s), ("fbk", s), ("tm", s)
        for g in range(NHG):
            ps, pk = banks.next()
            for hh in range(8):
                h = g * 8 + hh
                p.op("pe", lambda e: e.matmul(ps[0:64, hh * 64:(hh + 1) * 64], lhsT=far[:, h, 0, :], rhs=fbk[:, 0, h, :], start=True, stop=True),
                     reads=[kfar, kfbk], writes=[pk])
            p.op("dve", lambda e: e.tensor_tensor(out=Gf[g][:], in0=v3(ps[0:64, :]), in1=bc8(mkp[:, 0, :]), op=ALU.mult),
                 reads=[pk, "mkp"], writes=[("Gf", g)])
            p.op(RW2_MASK_ENG, lambda e: e.tensor_tensor(out=Gf[g][:], in0=Gf[g][:], in1=bc8(mkp[:, 1, :]), op=ALU.add),
                 reads=[("Gf", g), "mkp"], writes=[("Gf", g)])
            p.op("act", lambda e: e.activation(out=Gb[g][:], in_=Gf[g][:], func=AF.Copy), reads=[("Gf", g)], writes=[("Gb", g)])
            for (dst, dkey, which) in ((QA[g], ("QA", g), 0), (QK[g], ("QK", g), 1)):
                for half in range(2):
                    ps, pk = banks.next()
                    for hq in range(4):
                        h = g * 8 + half * 4 + hq
                        p.op("pe", lambda e: e.matmul(ps[0:64, hq * 128:(hq + 1) * 128], lhsT=fbk[:, which, h, :],
                                                      rhs=far[:, h, :, :].rearrange("p a b -> p (a b)"), start=True, stop=True),
                             reads=[kfar, kfbk], writes=[pk])
                    p.op("dve", lambda e: e.tensor_tensor(
                        out=dst[:, half * 4:(half + 1) * 4, :, :],
                        in0=ps[0:64, :].rearrange("p (h a b) -> p h a b", a=2, b=64),
                        in1=mka[:].unsqueeze(1).broadcast_to([64, 4, 2, 64]), op=ALU.mult),
                        reads=[pk, "mka"], writes=[dkey])
            NT0 = QA[g][:, :, 0, :]
            for l in range(6):
                p.op(RW2_MASK_ENG, lambda e: e.tensor_tensor(out=NM[g][l][:], in0=NT0, in1=bc8(mklb[:, l, :]), op=ALU.mult),
                     reads=[("QA", g), "mklb"], writes=[("NM", g, l)])
            p.op(RW2_MASK_ENG, lambda e: e.tensor_tensor(out=Hf[g][:], in0=NM[g][0][:], in1=bc8(mkl[:, 6, :]), op=ALU.add),
                 reads=[("NM", g, 0), "mkl"], writes=[("Hf", g)])
            p.op("act", lambda e: e.activation(out=Hb[g][:], in_=Hf[g][:], func=AF.Copy), reads=[("Hf", g)], writes=[("Hb", g)])
        for g in range(NHG):
            ps, pk = banks.next()
            for hh in range(8):
                h = g * 8 + hh
                o = ps[0:64, hh * 64:(hh + 1) * 64]
                p.op("pe", lambda e: e.matmul(o, lhsT=far[:, h, 0, :], rhs=Sb[:, h, :], start=True, stop=False),
                     reads=[kfar, ("Sb", g)], writes=[pk])
                p.op("pe", lambda e: e.matmul(o, lhsT=far[:, h, 0, :], rhs=Sl[:, h, :], start=False, stop=False),
                     reads=[kfar, ("Sl", g)], writes=[pk])
                p.op("pe", lambda e: e.matmul(o, lhsT=QK[g][:, hh, 0, :], rhs=tm[:, 0, h * 64:(h + 1) * 64], start=False, stop=True),
                     reads=[("QK", g), ktm], writes=[pk])
            p.op("act", lambda e: e.activation(out=Rb[g][:], in_=v3(ps[0:64, :]), func=AF.Copy), reads=[pk], writes=[("Rb", g)])
        for l in range(1, 6):
            for g in range(NHG):
                ps, pk = banks.next()
                for hh in range(8):
                    p.op("pe", lambda e: e.matmul(ps[0:64, hh * 64:(hh + 1) * 64], lhsT=NM[g][l][:, hh, :], rhs=Gb[g][:, hh, :], start=True, stop=True),
                         reads=[("NM", g, l), ("Gb", g)], writes=[pk])
                p.op("act", lambda e: e.activation(out=Wb[g][:], in_=v3(ps[0:64, :]), func=AF.Copy), reads=[pk], writes=[("Wb", g)])
            zz = []
            for g in range(NHG):
                psz = pkz = None
                if l < 5:
                    psz, pkz = banks.next()
                    for hh in range(8):
                        p.op("pe", lambda e: e.matmul(psz[0:64, hh * 64:(hh + 1) * 64], lhsT=Hb[g][:, hh, :], rhs=Wb[g][:, hh, :], start=True, stop=True),
                             reads=[("Hb", g), ("Wb", g)], writes=[pkz])
                ps2, pk2 = banks.next()
                for hh in range(8):
                    p.op("pe", lambda e: e.matmul(ps2[0:64, hh * 64:(hh + 1) * 64], lhsT=Wb[g][:, hh, :], rhs=Hb[g][:, hh, :], start=True, stop=True),
                         reads=[("Hb", g), ("Wb", g)], writes=[pk2])
                if l < 5:
                    p.op("dve", lambda e: e.tensor_tensor(out=Gf[g][:], in0=v3(psz[0:64, :]), in1=Gf[g][:], op=ALU.add),
                         reads=[pkz, ("Gf", g)], writes=[("Gf", g)])
                    p.op("act", lambda e: e.activation(out=Gb[g][:], in_=Gf[g][:], func=AF.Copy), reads=[("Gf", g)], writes=[("Gb", g)])
                p.op("dve", lambda e: e.tensor_tensor(out=Hf[g][:], in0=v3(ps2[0:64, :]), in1=Hf[g][:], op=ALU.add),
                     reads=[pk2, ("Hf", g)], writes=[("Hf", g)])
                p.op("act", lambda e: e.activation(out=Hb[g][:], in_=Hf[g][:], func=AF.Copy), reads=[("Hf", g)], writes=[("Hb", g)])
        for g in range(NHG):
            ps, pk = banks.next()
            for hh in range(8):
                p.op("pe", lambda e: e.matmul(ps[0:64, hh * 64:(hh + 1) * 64], lhsT=Hb[g][:, hh, :], rhs=Rb[g][:, hh, :], start=True, stop=True),
                     reads=[("Hb", g), ("Rb", g)], writes=[pk])
            p.op("act", lambda e: e.activation(out=Xb[g][:], in_=v3(ps[0:64, :]), func=AF.Copy), reads=[pk], writes=[("Xb", g)])
        ys = yst[s]
        for g in range(NHG):
            ps, pk = banks.next()
            for hh in range(8):
                h = g * 8 + hh
                o = ps[0:64, hh * 64:(hh + 1) * 64]
                p.op("pe", lambda e: e.matmul(o, lhsT=Sb[:, h, :], rhs=far[:, h, 1, :], start=True, stop=False),
                     reads=[("Sb", g), kfar], writes=[pk])
                p.op("pe", lambda e: e.matmul(o, lhsT=Sl[:, h, :], rhs=far[:, h, 1, :], start=False, stop=False),
                     reads=[("Sl", g), kfar], writes=[pk])
                p.op("pe", lambda e: e.matmul(o, lhsT=Xb[g][:, hh, :], rhs=QA[g][:, hh, 1, :], start=False, stop=False),
                     reads=[("Xb", g), ("QA", g)], writes=[pk])
                p.op("pe", lambda e: e.matmul(o, lhsT=tm[:, 0, h * 64:(h + 1) * 64], rhs=QK[g][:, hh, 1, :], start=False, stop=True),
                     reads=[ktm, ("QK", g)], writes=[pk])
            p.op("act", lambda e: e.activation(out=ys[:, g * 8:(g + 1) * 8, :], in_=v3(ps[0:64, :]), func=AF.Copy),
                 reads=[pk], writes=[("yst", s)])
        p.dma(y_d[ci], ys[:].rearrange("p a b -> p (a b)"), reads=[("yst", s)], q="act")
        for g in range(NHG):
            ps, pk = banks.next()
            for hh in range(8):
                h = g * 8 + hh
                o = ps[0:64, hh * 64:(hh + 1) * 64]
                p.op("pe", lambda e: e.matmul(o, lhsT=tm[:, 1, h * 64:(h + 1) * 64], rhs=Xb[g][:, hh, :], start=True, stop=False),
                     reads=[ktm, ("Xb", g)], writes=[pk])
                p.op("pe", lambda e: e.matmul(o, lhsT=tm[:, 2, h * 64:(h + 1) * 64], rhs=tm[:, 0, h * 64:(h + 1) * 64], start=False, stop=True),
                     reads=[ktm], writes=[pk])
            Sg = S[:, g * 8:(g + 1) * 8, :]
            Sdg = Sd[:, g * 8:(g + 1) * 8, :]
            p.op("dve", lambda e: e.tensor_tensor(out=Sg, in0=v3(ps[0:64, :]), in1=Sg, op=ALU.add), reads=[pk, ("S", g)], writes=[("S", g)])
            p.op("dve", lambda e: e.tensor_tensor(out=Sg, in0=Sg, in1=gam[:, g * 8:(g + 1) * 8, ci:ci + 1].broadcast_to([64, 8, 64]), op=ALU.mult),
                 reads=[("S", g), "gam"], writes=[("S", g)])
            p.op("act", lambda e: e.activation(out=Sb[:, g * 8:(g + 1) * 8, :], in_=Sg, func=AF.Copy), reads=[("S", g)], writes=[("Sb", g)])
            p.op(RW2_MASK_ENG, lambda e: e.tensor_tensor(out=Sdg, in0=Sg, in1=Sb[:, g * 8:(g + 1) * 8, :], op=ALU.subtract),
                 reads=[("S", g), ("Sb", g)], writes=[("Sd", g)])
            p.op("act", lambda e: e.activation(out=Sl[:, g * 8:(g + 1) * 8, :], in_=Sdg, func=AF.Copy), reads=[("Sd", g)], writes=[("Sl", g)])
    return p.build()


def rw2_masks():
    r = np.arange(64)[:, None]
    f = np.arange(64)[None, :]
    mka = np.stack([(f > r), (f >= r)], 1).astype(np.float32).reshape(64, 128)

    def m_level(l, i, j):
        return ((i >> (l + 1)) == (j >> (l + 1))) & (((i >> l) & 1) == 1) & (((j >> l) & 1) == 0)
    eye = (r == f)
    mkp = np.stack([m_level(0, r, f), eye], 1).astype(np.float32).reshape(64, 128)
    mkl = np.stack([m_level(l, f, r) for l in range(6)] + [eye], 1).astype(np.float32).reshape(64, 7 * 64)
    return mka, mkp, mkl


RW_GN_EPS = 64e-5


def build_rw3():
    p = Prog()
    W = WF
    h_d = p.dram("h", [D, W])
    modv_d = p.dram("modv", [128, 6 * KC, 2])
    g1_d = p.dram("g1", [128, KC])
    mask_d = p.dram("mask", [1, W])
    coef_d = p.dram("coef", [128, 2, 6, KC])
    yin_d = p.dram("yin", [4, D, 1152])
    g1w_d = p.dram("g1w", [D, RW_GATE])
    g2w_d = p.dram("g2w", [RW_GATE, D])
    lnv_d = p.dram("lnv", [128, 2, KC])
    wo_d = p.dram("wo", [D, D])
    bones_d = p.dram("bones", [128, 128])
    y_d = p.dram("y", [D, 1152], kind="ExternalOutput")
    cm = Common(p)
    banks = Banks(p)
    a = p.sb("a", [128, KC, W], BF16)
    hbuf = p.sb("hbuf", [128, 2, W], F32)
    tmp = p.sb("tmp", [128, 2, W], F32)
    sq = p.sb("sq", [128, 2, 512], BF16)
    rstd = p.sb("rstd", [128, W], F32)
    modv = load_small(p, "modv", [128, 6 * KC, 2], modv_d)
    gvec = load_small(p, "gvec", [128, KC], g1_d)
    mask = load_small(p, "mask", [128, W], mask_d[0].partition_broadcast(128))
    coef = load_coef(p, coef_d)
    lnv = load_small(p, "lnv", [128, 2, KC], lnv_d)
    gne = p.sb("gne", [128, 1], F32)
    p.op("dve", lambda e: e.memset(gne[:], RW_GN_EPS), writes=["gne"])
    bones = p.sb("bones", [128, 128], BF16)
    p.dma(bones[:], bones_d, writes=["bones"], q="pool", max_dma_last_dim=512)
    g1w = p.sb("g1w", [128, KC, RW_GATE], BF16)
    g2w = p.sb("g2w", [128, 2, D], BF16)
    p.dma(g1w[:], g1w_d.rearrange("(c p) f -> p c f", p=128), writes=["g1w"], q="pool")
    for c2 in range(2):
        p.dma(g2w[:, c2, :], g2w_d[c2 * 128:(c2 + 1) * 128, :], writes=["g2w"], q="pool")
    A_l = make_AB(p, "rl", modv, gvec, 0, 1, 0)
    A_c = make_AB(p, "rc", modv, gvec, 0, 1, 1)
    segs = [(LAT0, LAT1, A_l, lambda c: modv[:, c, 0:1]), (CTX0, CTX1, A_c, lambda c: modv[:, c, 1:2])]
    norm_mod_stream(p, cm, banks, h_d.rearrange("(c p) w -> p c w", p=128), W, segs, a, "a", rstd, sq, hbuf, tmp,
                    ["rl_A", "rc_A"], mask=mask)
    xg = p.sb("xg", [128, KC, 512], BF16)
    ggb = p.sb("ggb", [128, 2, 512], BF16)
    ob = p.sb("ob", [128, KC, 512], BF16)
    yt = [p.sb("yt%d" % i, [128, 4, 512], F32) for i in range(2)]
    ft = [p.sb("ft%d" % i, [128, 512], F32) for i in range(4)]
    fk = [("ft", i) for i in range(4)]
    sqb = p.sb("sqb", [128, 2, 512], BF16)
    st = p.sb("st", [128, 2, 512], F32)
    slabs = [p.sb("slab%d" % i, [128, KC, 256], BF16) for i in range(2)]
    yinv = yin_d.rearrange("n (c p) w -> p n c w", p=128)
    yov = y_d.rearrange("(c p) w -> p c w", p=128)
    cblocks = [(1, 512, 0), (513, 512, 512), (1027, 128, 1024)]
    cnt = [0]
    for (j0, n, o0) in cblocks:
        shift_mix(p, a, lambda c: ("a", c), xg, "xg", coef, 5, j0, n)
        for c2 in range(2):
            ps, pk = banks.next()
            for c in range(KC):
                p.op("pe", lambda e: e.matmul(ps[:, 0:n], lhsT=g1w[:, c, c2 * 128:(c2 + 1) * 128], rhs=xg[:, c, 0:n],
                                              start=(c == 0), stop=(c == KC - 1)),
                     reads=["g1w", ("xg", c)], writes=[pk])
            p.op("act", lambda e: e.activation(out=ggb[:, c2, 0:n], in_=ps[:, 0:n], func=AF.Sigmoid), reads=[pk], writes=[("ggb", c2)])
        for m in range(KC):
            s2 = m % 2
            y4 = yt[s2]
            p.dma(y4[:, :, 0:n], yinv[:, :, m, o0:o0 + n], writes=[("yt", s2)])
            psg, pkg = banks.next()
            for c2 in range(2):
                p.op("pe", lambda e: e.matmul(psg[:, 0:n], lhsT=g2w[:, c2, m * 128:(m + 1) * 128], rhs=ggb[:, c2, 0:n],
                                              start=(c2 == 0), stop=(c2 == 1)),
                     reads=["g2w", ("ggb", c2)], writes=[pkg])
            T_ = lambda i: ft[i][:, 0:n]
            p.op("dve", lambda e: e.tensor_tensor(out=T_(0), in0=y4[:, 0, 0:n], in1=y4[:, 1, 0:n], op=ALU.add), reads=[("yt", s2)], writes=[fk[0]])
            p.op("act", lambda e: e.activation(out=sqb[:, 0, 0:n], in_=T_(0), func=AF.Copy), reads=[fk[0]], writes=[("sqb", 0)])
            psm, pkm = banks.next()
            p.op("pe", lambda e: e.matmul(psm[:, 0:n], lhsT=bones[:], rhs=sqb[:, 0, 0:n], start=True, stop=True),
                 reads=["bones", ("sqb", 0)], writes=[pkm])
            p.op("dve", lambda e: e.scalar_tensor_tensor(out=T_(1), in0=psm[:, 0:n], scalar=-1.0 / 64, in1=T_(0), op0=ALU.mult, op1=ALU.add),
                 reads=[pkm, fk[0]], writes=[fk[1]])
            p.op("act", lambda e: e.activation(out=sqb[:, 1, 0:n], in_=T_(1), func=AF.Square), reads=[fk[1]], writes=[("sqb", 1)])
            psv, pkv = banks.next()
            p.op("pe", lambda e: e.matmul(psv[:, 0:n], lhsT=bones[:], rhs=sqb[:, 1, 0:n], start=True, stop=True),
                 reads=["bones", ("sqb", 1)], writes=[pkv])
            p.op("act", lambda e: e.activation(out=T_(2), in_=psv[:, 0:n], func=AF.Sqrt, bias=gne[:, 0:1], scale=1.0 / 64),
                 reads=[pkv, "gne"], writes=[fk[2]])
            p.op("dve", lambda e: e.reciprocal(out=T_(2), in_=T_(2)), reads=[fk[2]], writes=[fk[2]])
            p.op("dve", lambda e: e.tensor_tensor(out=T_(1), in0=T_(1), in1=T_(2), op=ALU.mult), reads=[fk[1], fk[2]], writes=[fk[1]])
            p.op("dve", lambda e: e.tensor_scalar(out=T_(1), in0=T_(1), scalar1=lnv[:, 0, m:m + 1], scalar2=lnv[:, 1, m:m + 1],
                                                  op0=ALU.mult, op1=ALU.add),
                 reads=[fk[1], "lnv"], writes=[fk[1]])
            p.op("dve", lambda e: e.tensor_tensor(out=T_(3), in0=y4[:, 2, 0:n], in1=y4[:, 3, 0:n], op=ALU.add), reads=[("yt", s2)], writes=[fk[3]])
            p.op("dve", lambda e: e.tensor_tensor(out=T_(1), in0=T_(1), in1=T_(3), op=ALU.add), reads=[fk[1], fk[3]], writes=[fk[1]])
            p.op("dve", lambda e: e.tensor_tensor(out=ob[:, m, 0:n], in0=psg[:, 0:n], in1=T_(1), op=ALU.mult),
                 reads=[pkg, fk[1]], writes=[("ob", m)])

        def evac_y(m, ps, pk, b0, b1):
            s = cnt[0] % 2
            cnt[0] += 1
            p.op("act", lambda e: e.activation(out=st[:, s, 0:b1 - b0], in_=ps[:, 0:b1 - b0], func=AF.Copy), reads=[pk], writes=[("st", s)])
            p.dma(yov[:, m, o0 + b0:o0 + b1], st[:, s, 0:b1 - b0], reads=[("st", s)], q="act")
        linear_fm2(p, banks, slabs, "slab", ob, "ob", KC, wo_d.rearrange("(c p) f -> p c f", p=128), D, [(0, n)], evac_y)
    return p.build()


_PROGS = {}
_DEBUG = None


def _prog(name, fn, *args):
    key = (name,) + args
    if key not in _PROGS:
        _PROGS[key] = fn(*args)
    return _PROGS[key]


def _run(nc, in_maps):
    res = _bu.run_bass_kernel_spmd(nc, in_maps, core_ids=list(range(8)))
    return res.results


def _f32(x):
    return np.ascontiguousarray(np.asarray(x, dtype=np.float32))


def _slab(hl_b, hc_b, s, halo):
    Dn = hl_b.shape[0]
    wl, wc = 1024 + 2 * halo, 128 + 2 * halo
    out = np.zeros((Dn, wl + wc), hl_b.dtype)
    mask = np.zeros((1, wl + wc), np.float32)
    for (src, n, base, o0) in ((hl_b, 1024, s * 1024, 0), (hc_b, 128, s * 128, wl)):
        lo, hi = base - halo, base + n + halo
        a0, a1 = max(lo, 0), min(hi, src.shape[1])
        out[:, o0 + (a0 - lo):o0 + (a1 - lo)] = src[:, a0:a1]
        mask[0, o0 + (a0 - lo):o0 + (a1 - lo)] = 1.0
    return out, mask


def _core_cols(hl_b, hc_b, s):
    return np.ascontiguousarray(np.concatenate([hl_b[:, s * 1024:(s + 1) * 1024], hc_b[:, s * 128:(s + 1) * 128]], 1))


def _pool_icnt(s):
    out = np.ones((4, WP), np.float32)
    for g, win in enumerate((2, 4, 8, 16)):
        left, right = win // 2, win - 1 - win // 2
        for (n, base, o0, Tn) in ((1024, s * 1024, 8, T), (128, s * 128, 1040 + 8, L)):
            t = np.arange(base, base + n)
            cnt = np.minimum(t + right, Tn - 1) - np.maximum(t - left, 0) + 1
            out[g, o0:o0 + n] = (1.0 / cnt.astype(np.float64)).astype(np.float32)
    return out


def rwkv_mixer(hl, hc, modv3, W3, dbg=None):
    import ml_dtypes
    norm1_g = W3["norm1_g"]; rw_mu = W3["rw_mu"]; rw_w_rkv = W3["rw_w_rkv"]; rw_w0 = W3["rw_w0"]; rw_w1 = W3["rw_w1"]
    rw_w2 = W3["rw_w2"]; rw_a0 = W3["rw_a0"]; rw_a1 = W3["rw_a1"]; rw_a2 = W3["rw_a2"]; rw_g1 = W3["rw_g1"]; rw_g2 = W3["rw_g2"]
    rw_k_k = W3["rw_k_k"]; rw_k_a = W3["rw_k_a"]; rw_r_k = W3["rw_r_k"]; rw_ln_g = W3["rw_ln_g"]; rw_ln_b = W3["rw_ln_b"]
    rw_w_o = W3["rw_w_o"]
    cores = [(b, s) for b in range(NB) for s in range(2)]
    mu = _f32(rw_mu[0])
    bones = np.kron(np.eye(2), np.ones((64, 64))).astype(np.float32)
    rmask = np.ones((1, 512), np.float32)
    rmask[0, ::64] = 0

    def coef_of(mu_prev, mu_next):
        return np.ascontiguousarray(np.stack([
            np.stack([_chunked(mu_prev[n]) for n in range(6)], 1),
            np.stack([_chunked(mu_next[n]) for n in range(6)], 1)], 1))

    bf = ml_dtypes.bfloat16
    mka, mkp, mkl = rw2_masks()
    rw2_in = {}
    bv_nat = {}
    vecs = np.ascontiguousarray(np.stack([_chunked(rw_w0[0][0]), _chunked(rw_w0[0][1]), _chunked(rw_a0[0][0]), _chunked(rw_a0[0][1]),
                                          _chunked(rw_k_k[0]), _chunked(rw_k_a[0]), _chunked(_f32(rw_r_k[0]).reshape(-1))], 1))
    ims = []
    for (b, s) in cores:
        hs, mk = _slab(hl[b], hc[b], s, 1)
        ims.append(dict(h=hs, modv=modv3[b], g1=_chunked(norm1_g[3]), mask=mk, coef=coef_of(mu[0], mu[1]),
                        wrkv=_f32(rw_w_rkv[0]), w1=_f32(rw_w1[0]), w2=_f32(rw_w2[0]), a1=_f32(rw_a1[0]),
                        a2=_f32(rw_a2[0]), vecs=vecs, bones=bones, rmask=rmask))
    res = _run(_prog("rw1", build_rw1), ims)
    for b in range(NB):
        r0, r1 = res[2 * b], res[2 * b + 1]
        for d in range(2):
            def seq(nm):
                lat = np.concatenate([np.asarray(r0[nm])[:, :1024], np.asarray(r1[nm])[:, :1024]], 1)
                cx = np.concatenate([np.asarray(r0[nm])[:, 1024:], np.asarray(r1[nm])[:, 1024:]], 1)
                if d == 1:
                    lat, cx = lat[:, ::-1], cx[:, ::-1]
                return np.concatenate([cx, lat], 1)
            at, bt, kt, rt = (seq("%s%d" % (nm, d)) for nm in ("at", "bt", "kt", "rt"))
            vt = seq("vt")
            hm = lambda z: z.reshape(32, 64, NCH, 64).transpose(2, 1, 0, 3)
            far = np.ascontiguousarray(np.stack([hm(at), hm(rt)], 3).reshape(NCH, 64, -1))
            fbk = np.ascontiguousarray(np.stack([hm(bt), hm(kt)], 2).reshape(NCH, 64, -1))
            tk = lambda z: np.ascontiguousarray(z.T).reshape(NCH, 64, D)
            tm = np.ascontiguousarray(np.stack([tk(vt), tk(bt), tk(kt)], 2).reshape(NCH, 64, -1))
            g0, g1_ = np.asarray(r0["gam%d" % d]), np.asarray(r1["gam%d" % d])
            glat = np.concatenate([g0[:, 0:16], g1_[:, 0:16]], 1)
            gcx = np.concatenate([g0[:, 16:18], g1_[:, 16:18]], 1)
            if d == 1:
                glat, gcx = glat[:, ::-1], gcx[:, ::-1]
            gam = np.concatenate([gcx, glat], 1)
            gam = np.ascontiguousarray(gam.reshape(32, 64, NCH).transpose(1, 0, 2))
            rw2_in[(b, d)] = dict(far=far.astype(bf, copy=False), fbk=fbk.astype(bf, copy=False), tm=tm.astype(bf, copy=False),
                                  gam=gam, mka=mka, mkp=mkp, mkl=mkl)
            bvs = np.concatenate([np.asarray(r0["bv%d" % d])[:, :1024], np.asarray(r1["bv%d" % d])[:, :1024]], 1)
            bvc = np.concatenate([np.asarray(r0["bv%d" % d])[:, 1024:], np.asarray(r1["bv%d" % d])[:, 1024:]], 1)
            bv_nat[(b, d)] = (np.ascontiguousarray(bvs), np.ascontiguousarray(bvc))
    res = _run(_prog("rw2", build_rw2), [rw2_in[(b, d)] for b in range(NB) for d in range(2)])
    y_nat = {}
    for b in range(NB):
        for d in range(2):
            y = res[2 * b + d]["y"].reshape(NCH, 64, 32, 64).transpose(2, 1, 0, 3).reshape(D, NTOK)
            yc_, yl_ = y[:, :L], y[:, L:]
            if d == 1:
                yc_, yl_ = yc_[:, ::-1], yl_[:, ::-1]
            y_nat[(b, d)] = (np.ascontiguousarray(yl_), np.ascontiguousarray(yc_))
    ims = []
    coef_n = coef_of(mu[0], mu[1])
    lnv = np.ascontiguousarray(np.stack([_chunked(rw_ln_g[0]), _chunked(rw_ln_b[0])], 1))
    for (b, s) in cores:
        hs, mk = _slab(hl[b], hc[b], s, 1)
        yin = np.stack([_core_cols(y_nat[(b, 0)][0], y_nat[(b, 0)][1], s), _core_cols(y_nat[(b, 1)][0], y_nat[(b, 1)][1], s),
                        _core_cols(bv_nat[(b, 0)][0], bv_nat[(b, 0)][1], s), _core_cols(bv_nat[(b, 1)][0], bv_nat[(b, 1)][1], s)], 0)
        ims.append(dict(h=hs, modv=modv3[b], g1=_chunked(norm1_g[3]), mask=mk, coef=coef_n, yin=np.ascontiguousarray(yin),
                        g1w=_f32(rw_g1[0]), g2w=_f32(rw_g2[0]), lnv=lnv, wo=_f32(rw_w_o[0]), bones=bones))
    res = _run(_prog("rw3", build_rw3), ims)
    if dbg is not None:
        dbg["rw2_in"] = rw2_in
        dbg["y_nat"] = y_nat
        dbg["bv_nat"] = bv_nat
    return res


def kernel(x, c, ctx, c_ctx, norm1_g, norm2_g, w_mod, b_mod, ffn_w_gate, ffn_w_up, ffn_conv_w, ffn_conv_b,
           ffn_w_down, final_norm_g, pool_w, pool_b, pool_scale, na_w_qkv, na_rpb, na_w_o, sg_w_in, sg_b_in,
           sg_norm_g, sg_w_s, sg_b_s, sg_w_o, rw_mu, rw_w_rkv, rw_w0, rw_w1, rw_w2, rw_a0, rw_a1, rw_a2, rw_g1,
           rw_g2, rw_k_k, rw_k_a, rw_r_k, rw_ln_g, rw_ln_b, rw_w_o):
    import ml_dtypes
    x, ctx = _f32(x), _f32(ctx)
    hl = [np.ascontiguousarray(x[b].T) for b in range(NB)]
    hc = [np.ascontiguousarray(ctx[b].T) for b in range(NB)]
    cores = [(b, s) for b in range(NB) for s in range(2)]

    cc = np.zeros((8, D), np.float32)
    cc[0:4] = _f32(c)
    cc[4] = _f32(c_ctx)
    ct = np.ascontiguousarray(cc.T.reshape(KC, 128, 8).transpose(1, 0, 2))
    w_mod = np.asarray(w_mod, np.float32)
    ims = []
    for core in range(8):
        i, half = core // 2, core % 2
        ims.append(dict(wm=np.ascontiguousarray(w_mod[i][:, half * 6144:(half + 1) * 6144]),
                        bm=_chunked(np.asarray(b_mod[i], np.float32)[half * 6144:(half + 1) * 6144]), ct=ct))
    res = _run(_prog("mod", build_mod), ims)
    modv = {}
    for i in range(4):
        mo = np.concatenate([res[2 * i]["mo"], res[2 * i + 1]["mo"]], 1)
        for b in range(NB):
            modv[(i, b)] = np.ascontiguousarray(mo[:, :, [b, 4]])

    def run_ffn(i, ys_l, ys_c, final):
        wg, wu, wd = _f32(ffn_w_gate[i]), _f32(ffn_w_up[i]), _f32(ffn_w_down[i])
        cw = np.ascontiguousarray(_f32(ffn_conv_w[i]).reshape(3, FC, 128).transpose(2, 1, 0))
        cb = _chunked(ffn_conv_b[i])
        g2 = _chunked(norm2_g[i])
        ims = []
        for (b, s) in cores:
            hs, mk = _slab(hl[b], hc[b], s, 1)
            y = np.zeros((2, D, WF), np.float32)
            for n in range(len(ys_l)):
                y[n] = _slab(ys_l[n][b], ys_c[n][b], s, 1)[0]
            im = dict(h=hs, y=y, modv=modv[(i, b)], g2=g2, mask=np.ascontiguousarray(np.repeat(mk, 128, 0)),
                      wg=wg, wu=wu, wd=wd, cw=cw, cb=cb)
            if final:
                im["gf"] = _chunked(final_norm_g)
            ims.append(im)
        res = _run(_prog("ffn", build_ffn, final), ims)
        out = None
        if final:
            out = np.empty((NB, T, D), np.float32)
        for ci, (b, s) in enumerate(cores):
            ho = res[ci]["ho"]
            hl[b][:, s * 1024:(s + 1) * 1024] = ho[:, 0:1024]
            hc[b][:, s * 128:(s + 1) * 128] = ho[:, 1024:1152]
            if final:
                out[b, s * 1024:(s + 1) * 1024, :] = res[ci]["of"].T
        return out

    def split_y(res_y):
        yl = [np.empty((D, T), np.float32) for _ in range(NB)]
        yc = [np.empty((D, L), np.float32) for _ in range(NB)]
        for ci, (b, s) in enumerate(cores):
            yl[b][:, s * 1024:(s + 1) * 1024] = res_y[ci][:, 0:1024]
            yc[b][:, s * 128:(s + 1) * 128] = res_y[ci][:, 1024:1152]
        return yl, yc

    i = 0
    ims = []
    for (b, s) in cores:
        hs, mk = _slab(hl[b], hc[b], s, 8)
        ims.append(dict(h=hs, modv=modv[(i, b)], g1=_chunked(norm1_g[i]), mask=mk, icnt=_pool_icnt(s),
                        pw=_f32(pool_w[0]), pb=_chunked(pool_b[0]), psc=_chunked(pool_scale[0])))
    res = _run(_prog("pool", build_pool), ims)
    yl, yc = split_y([r["y"] for r in res])
    run_ffn(i, [yl], [yc], False)
    if _DEBUG is not None:
        _DEBUG.append(([a.copy() for a in hl], [a.copy() for a in hc]))

    i = 1
    pm = rope_perm()
    ims = []
    for (b, s) in cores:
        rc, rs = rope_tables(s * 1024, 1024)
        ims.append(dict(h=_core_cols(hl[b], hc[b], s), modv=modv[(i, b)], g1=_chunked(norm1_g[i]), wqkv=_f32(na_w_qkv[0]),
                        rcos=rc, rsin=rs, pm=pm))
    res = _run(_prog("na1", build_na1), ims)
    Q, Kt, V = [], [], []
    for b in range(NB):
        r0, r1 = res[2 * b], res[2 * b + 1]
        Q.append(np.concatenate([r0["qT"][:, :1024], r1["qT"][:, :1024], r0["qT"][:, 1024:], r1["qT"][:, 1024:]], 1))
        Kt.append(np.concatenate([r0["kT"][:, :1024], r1["kT"][:, :1024], r0["kT"][:, 1024:], r1["kT"][:, 1024:]], 1))
        V.append(np.concatenate([r0["v"][:1024], r1["v"][:1024], r0["v"][1024:], r1["v"][1024:]], 0))
    rpb = _f32(na_rpb[0])
    wo = _f32(na_w_o[0])
    ims = []
    for b in range(NB):
        for hh in range(2):
            sl = slice(hh * 1024, (hh + 1) * 1024)
            ims.append(dict(qT=np.ascontiguousarray(Q[b][sl]), kT=np.ascontiguousarray(Kt[b][sl]),
                            v=np.ascontiguousarray(V[b][:, sl]), bt=na_bias_tables(rpb, list(range(hh * 16, hh * 16 + 16))),
                            wo=np.ascontiguousarray(wo[sl])))
    res = _run(_prog("na2", build_na2), ims)
    yls, ycs = [], []
    for hh in range(2):
        yls.append([np.ascontiguousarray(res[2 * b + hh]["y"][:, :T]) for b in range(NB)])
        ycs.append([np.ascontiguousarray(res[2 * b + hh]["y"][:, T:]) for b in range(NB)])
    run_ffn(i, yls, ycs, False)
    if _DEBUG is not None:
        _DEBUG.append(([a.copy() for a in hl], [a.copy() for a in hc]))

    i = 2
    b_in = _f32(sg_b_in[0])
    ims = []
    for (b, s) in cores:
        ims.append(dict(h=_core_cols(hl[b], hc[b], s), modv=modv[(i, b)], g1=_chunked(norm1_g[i]), win=_f32(sg_w_in[0]),
                        bzu=_chunked(b_in[:D]), bzv=np.ascontiguousarray(b_in[None, D:]), ng=_f32(sg_norm_g[0])[None].copy(),
                        wst=np.ascontiguousarray(_f32(sg_w_s[0]).transpose(2, 0, 1)), bs=_f32(sg_b_s[0]).reshape(1, -1).copy(),
                        wo=_f32(sg_w_o[0])))
    res = _run(_prog("gmlp", build_gmlp), ims)
    yl, yc = split_y([r["y"] for r in res])
    run_ffn(i, [yl], [yc], False)
    if _DEBUG is not None:
        _DEBUG.append(([a.copy() for a in hl], [a.copy() for a in hc]))

    i = 3
    W3 = dict(norm1_g=norm1_g, rw_mu=rw_mu, rw_w_rkv=rw_w_rkv, rw_w0=rw_w0, rw_w1=rw_w1, rw_w2=rw_w2, rw_a0=rw_a0, rw_a1=rw_a1,
              rw_a2=rw_a2, rw_g1=rw_g1, rw_g2=rw_g2, rw_k_k=rw_k_k, rw_k_a=rw_k_a, rw_r_k=rw_r_k, rw_ln_g=rw_ln_g,
              rw_ln_b=rw_ln_b, rw_w_o=rw_w_o)
    res = rwkv_mixer(hl, hc, {b: modv[(i, b)] for b in range(NB)}, W3)
    yl, yc = split_y([r["y"] for r in res])
    return run_ffn(i, [yl], [yc], True)
```

```python
import contextlib
import os
import numpy as np
import concourse.bass as bass
import concourse.mybir as mybir

F32 = mybir.dt.float32
BF16 = mybir.dt.bfloat16
AF = mybir.ActivationFunctionType
ALU = mybir.AluOpType
AX = mybir.AxisListType

ENGS = ("sync", "pe", "dve", "act", "pool")
SAME_ENGINE_SYNC = True


class Prog:
    EPOCH = 16000
    NDMA = 24

    def __init__(self):
        self.nc = bass.Bass("TRN2", target_bir_lowering=False)
        self.stack = contextlib.ExitStack()
        self.ops = {e: [] for e in ENGS}
        self.cnt = {e: 0 for e in ENGS}
        self.last_w = {}
        self.readers = {}
        self.waited = {e: {} for e in ENGS}
        self.sems = {}
        self.dma_n = 0
        self.dma_tot = [0] * self.NDMA
        self.n_uid = 0

    def dram(self, name, shape, dt=F32, kind="ExternalInput"):
        return self.nc.dram_tensor(name, list(shape), dt, kind=kind).ap()

    def sb(self, name, shape, dt=F32):
        return self.stack.enter_context(self.nc.sbuf_tensor("sb_" + name, list(shape), dt))

    def ps(self, name, shape, dt=F32):
        return self.stack.enter_context(self.nc.psum_tensor("pp_" + name, list(shape), dt))

    def _sem(self, key):
        if key not in self.sems:
            self.sems[key] = self.stack.enter_context(
                self.nc.semaphore("s_%s_%s" % (key[0], key[1])))
        return self.sems[key]

    def _deps(self, eng, reads, writes, pe_sync=False):
        toks = []
        for k in reads:
            w = self.last_w.get(k)
            if w is not None:
                toks.append(w)
        for k in writes:
            w = self.last_w.get(k)
            if w is not None:
                toks.append(w)
            toks.extend(self.readers.get(k, ()))
        need = {}
        for t in toks:
            if t[0] == "eng":
                _, e, seq = t
                if e == eng and ((eng == "pe" and not pe_sync) or not SAME_ENGINE_SYNC):
                    continue
                sk = (e, seq // self.EPOCH)
                v = seq % self.EPOCH + 1
            else:
                _, s, v = t
                sk = ("dma", s)
            if need.get(sk, 0) < v:
                need[sk] = v
        out = []
        wd = self.waited[eng]
        for sk, v in need.items():
            if wd.get(sk, 0) >= v:
                continue
            wd[sk] = v
            out.append((sk, v))
        return out

    def _commit(self, tok, reads, writes):
        for k in reads:
            self.readers.setdefault(k, []).append(tok)
        for k in writes:
            self.last_w[k] = tok
            self.readers[k] = []

    @staticmethod
    def _psum_excl(reads, writes):
        r2, w2 = [], list(writes)
        for k in reads:
            if isinstance(k, tuple) and k and k[0] == "ps":
                w2.append(k)
            else:
                r2.append(k)
        return r2, w2

    def op(self, eng, fn, reads=(), writes=(), pe_sync=False):
        reads, writes = self._psum_excl(reads, writes)
        waits = self._deps(eng, reads, writes, pe_sync)
        seq = self.cnt[eng]
        self.cnt[eng] += 1
        tok = ("eng", eng, seq)
        self._emit(eng, waits, fn, ((eng, seq // self.EPOCH), 1))
        self._commit(tok, reads, writes)
        return tok

    def dma(self, out, in_, reads=(), writes=(), q="sync", **kw):
        if q == "pool" and "max_dma_last_dim" not in kw:
            kw["max_dma_last_dim"] = 2048
        s = self.dma_n % self.NDMA
        self.dma_n += 1
        waits = self._deps(q, reads, writes)
        prev = self.dma_tot[s]
        sk = ("dma", s)
        if prev and self.waited[q].get(sk, 0) < prev:
            self.waited[q][sk] = prev
            waits.append((sk, prev))
        self.dma_tot[s] = prev + 16
        tok = ("dma", s, prev + 16)
        fn = lambda e, out=out, in_=in_, kw=kw: e.dma_start(out=out, in_=in_, **kw)
        self._emit(q, waits, fn, (sk, 16))
        self._commit(tok, reads, writes)
        return tok

    def _emit(self, name, waits, fn, inc):
        nc = self.nc
        eng = {"sync": nc.sync, "pe": nc.tensor, "dve": nc.vector, "act": nc.scalar, "pool": nc.gpsimd}[name]
        for sk, v in waits:
            eng.wait_ge(self._sem(sk), v)
        fn(eng).then_inc(self._sem(inc[0]), inc[1])
        self.nops = getattr(self, "nops", 0) + 1 + len(waits)
        if not hasattr(self, "trace"):
            self.trace = {e: [] for e in ENGS}
        self.trace[name].append((list(waits), inc))

    def check_deadlock(self):
        tr = getattr(self, "trace", None)
        if tr is None:
            return
        val = {}
        pos = {e: 0 for e in ENGS}
        progress = True
        while progress:
            progress = False
            for e in ENGS:
                q = tr[e]
                while pos[e] < len(q):
                    waits, inc = q[pos[e]]
                    if all(val.get(sk, 0) >= v for sk, v in waits):
                        val[inc[0]] = val.get(inc[0], 0) + inc[1]
                        pos[e] += 1
                        progress = True
                    else:
                        break
        stuck = {e: (pos[e], len(tr[e])) for e in ENGS if pos[e] < len(tr[e])}
        if stuck:
            msg = []
            for e, (i, n) in stuck.items():
                waits, inc = tr[e][i]
                msg.append("%s stuck at %d/%d waiting %s (have %s)" % (
                    e, i, n, waits, [(sk, val.get(sk, 0)) for sk, v in waits]))
            raise RuntimeError("semaphore deadlock: " + "; ".join(msg))

    def build(self):
        nc = self.nc
        self.check_deadlock()
        for s in range(self.NDMA):
            if self.dma_tot[s]:
                nc.sync.wait_ge(self._sem(("dma", s)), self.dma_tot[s])
        for e in ENGS:
            if e != "sync" and self.cnt[e]:
                seq = self.cnt[e] - 1
                nc.sync.wait_ge(self._sem((e, seq // self.EPOCH)), seq % self.EPOCH + 1)
        self.stack.close()
        return nc


import concourse.bass_utils as _bu

D = 2048
KC = 16
T = 2048
L = 256
NB = 4
FF = 5504
FC = 43
EPS = 1e-6


class Banks:
    def __init__(self, p, n=8, prefix="psb"):
        self.t = [p.ps("%s%d" % (prefix, i), [128, 512], F32) for i in range(n)]
        self.keys = [("ps", prefix, i) for i in range(n)]
        self.i = 0

    def next(self):
        b = self.i % len(self.t)
        self.i += 1
        return self.t[b], self.keys[b]


def col_blocks(c0, c1, n=512):
    out = []
    while c0 < c1:
        out.append((c0, min(c1, c0 + n)))
        c0 += n
    return out


class Common:
    def __init__(self, p):
        self.ones = p.sb("c_ones", [128, 128], BF16)
        self.eps = p.sb("c_eps", [128, 1], F32)
        p.op("dve", lambda e: e.memset(self.ones[:], 1.0), writes=["c_ones"])
        p.op("dve", lambda e: e.memset(self.eps[:], EPS), writes=["c_eps"])


def load_small(p, name, shape, dram_ap, dt=F32):
    t = p.sb(name, shape, dt)
    p.dma(t[:], dram_ap, writes=[name])
    return t


def make_AB(p, name, modv, g, j_shift, j_scale, col):
    A = p.sb(name + "_A", [128, KC], F32)
    p.op("dve", lambda e: e.tensor_scalar(out=A[:], in0=modv[:, j_scale * KC:(j_scale + 1) * KC, col],
                                          scalar1=1.0, scalar2=None, op0=ALU.add),
         reads=["modv"], writes=[name + "_A"])
    p.op("dve", lambda e: e.tensor_tensor(out=A[:], in0=A[:], in1=g[:], op=ALU.mult),
         reads=[name + "_A", "gvec"], writes=[name + "_A"])
    return A


def rms_rstd(p, cm, banks, h, hkey, W, rstd, rkey, sq, nfeat_chunks=KC, dmodel=D, eps_ap=None):
    for (b0, b1) in col_blocks(0, W):
        ps, pk = banks.next()
        n = b1 - b0
        for c in range(nfeat_chunks):
            s = c % 2
            p.op("act", lambda e, c=c, s=s: e.activation(out=sq[:, s, 0:n], in_=h[:, c, b0:b1], func=AF.Square),
                 reads=[hkey], writes=[("sq", s)])
            p.op("pe", lambda e, c=c, s=s: e.matmul(ps[:, 0:n], lhsT=cm.ones[:], rhs=sq[:, s, 0:n],
                                                     start=(c == 0), stop=(c == nfeat_chunks - 1)),
                 reads=[("sq", s), "c_ones"], writes=[pk])
        p.op("act", lambda e: e.activation(out=rstd[:, b0:b1], in_=ps[:, 0:n], func=AF.Sqrt,
                                           bias=(eps_ap if eps_ap is not None else cm.eps)[:, 0:1], scale=1.0 / dmodel),
             reads=[pk, "c_eps"], writes=[rkey])
        p.op("dve", lambda e: e.reciprocal(out=rstd[:, b0:b1], in_=rstd[:, b0:b1]), reads=[rkey], writes=[rkey])


def norm_mod(p, cm, banks, h, hkey, W, segs, out_bf, okey, rstd, sq, tmp, mask=None):
    rms_rstd(p, cm, banks, h, hkey, W, rstd, "rstd", sq)
    for c in range(KC):
        s = c % 2
        p.op("dve", lambda e, c=c, s=s: e.tensor_tensor(out=tmp[:, s, 0:W], in0=h[:, c, 0:W], in1=rstd[:, 0:W], op=ALU.mult),
             reads=[hkey, "rstd"], writes=[("tmp", s)])
        for (c0, c1, A, Bfn) in segs:
            p.op("act", lambda e, c=c, s=s, c0=c0, c1=c1, A=A, Bfn=Bfn: e.activation(
                out=out_bf[:, c, c0:c1], in_=tmp[:, s, c0:c1], func=AF.Identity, scale=A[:, c:c + 1], bias=Bfn(c)),
                reads=[("tmp", s), "AB", "modv"], writes=[(okey, c)])
        if mask is not None:
            p.op("pool", lambda e, c=c: e.tensor_tensor(out=out_bf[:, c, 0:W], in0=out_bf[:, c, 0:W], in1=mask[:, 0:W], op=ALU.mult),
                 reads=[(okey, c), "mask"], writes=[(okey, c)])


WF = 1156
LAT0, LAT1 = 0, 1026
CTX0, CTX1 = 1026, 1156
FG = 6


def build_ffn(final, nparts=2):
    p = Prog()
    h_d = p.dram("h", [D, WF])
    y_d = p.dram("y", [nparts, D, WF])
    modv_d = p.dram("modv", [128, 6 * KC, 2])
    g2_d = p.dram("g2", [128, KC])
    mask_d = p.dram("mask", [128, WF])
    wg_d = p.dram("wg", [D, FF])
    wu_d = p.dram("wu", [D, FF])
    wd_d = p.dram("wd", [FF, D])
    cw_d = p.dram("cw", [128, FC, 3])
    cb_d = p.dram("cb", [128, FC])
    ho_d = p.dram("ho", [D, 1152], kind="ExternalOutput")
    if final:
        gf_d = p.dram("gf", [128, KC])
        of_d = p.dram("of", [D, 1024], kind="ExternalOutput")

    cm = Common(p)
    banks = Banks(p)
    h = p.sb("h", [128, KC, WF], F32)
    u = p.sb("u", [128, KC, WF], BF16)
    act = p.sb("actb", [128, FG, WF], BF16)
    gsb = p.sb("gsb", [128, 2, WF], F32)
    tsb = p.sb("tsb", [128, 2, WF], F32)
    sq = p.sb("sq", [128, 2, 512], BF16)
    rstd = p.sb("rstd", [128, WF], F32)
    modv = load_small(p, "modv", [128, 6 * KC, 2], modv_d)
    gvec = load_small(p, "gvec", [128, KC], g2_d)
    mask = p.sb("mask", [128, WF], BF16)
    p.dma(mask[:], mask_d, writes=["mask"], q="pool")
    cw = load_small(p, "cw", [128, FC, 3], cw_d)
    cb = load_small(p, "cb", [128, FC], cb_d)
    gus = [p.sb("gus%d" % i, [128, KC, 256], BF16) for i in range(4)]
    nev = [0]
    wds = p.sb("wds", [128, FG, D], BF16)

    hv = h_d.rearrange("(c p) w -> p c w", p=128)
    yv = y_d.rearrange("n (c p) w -> n p c w", p=128)
    for c4 in range(0, KC, 4):
        p.dma(h[:, c4:c4 + 4, :], hv[:, c4:c4 + 4, :], writes=[("h", c) for c in range(c4, c4 + 4)])
    for c in range(KC):
        for n in range(nparts):
            s = (c * 2 + n) % 2
            p.dma(gsb[:, s, :], yv[n, :, c, :], writes=[("gsb", s)])
            for (c0, c1, col) in ((LAT0, LAT1, 0), (CTX0, CTX1, 1)):
                p.op("dve", lambda e, c=c, s=s, c0=c0, c1=c1, col=col: e.scalar_tensor_tensor(
                    out=h[:, c, c0:c1], in0=gsb[:, s, c0:c1], scalar=modv[:, 2 * KC + c, col:col + 1],
                    in1=h[:, c, c0:c1], op0=ALU.mult, op1=ALU.add),
                    reads=[("gsb", s), "modv", ("h", c)], writes=[("h", c)])
    A_l = make_AB(p, "ffl", modv, gvec, 3, 4, 0)
    A_c = make_AB(p, "ffc", modv, gvec, 3, 4, 1)
    hkeys = [("h", c) for c in range(KC)]

    class HK:
        pass
    segs = [(LAT0, LAT1, A_l, lambda c: modv[:, 3 * KC + c, 0:1]), (CTX0, CTX1, A_c, lambda c: modv[:, 3 * KC + c, 1:2])]
    _norm_mod_chunked(p, cm, banks, h, W=WF, segs=segs, out_bf=u, okey="u", rstd=rstd, sq=sq, tmp=tsb, mask=mask,
                      keys_A=["ffl_A", "ffc_A"])

    wgv = wg_d.rearrange("(c p) f -> p c f", p=128)
    wuv = wu_d.rearrange("(c p) f -> p c f", p=128)
    wdv = wd_d.rearrange("(f p) d -> p f d", p=128)
    blocks = col_blocks(0, WF)
    dblocks = [(1, 513, 0), (513, 1025, 0), (1025, 1155, 1)]
    nslab = 0
    slab_of = {}
    ukeys = [("u", c) for c in range(KC)]

    def ensure_slab(fo):
        nonlocal nslab
        sidx = fo // 2
        if sidx in slab_of:
            return slab_of[sidx]
        i0 = (nslab % 2) * 2
        nslab += 1
        f0 = sidx * 256
        f1 = min(FF, f0 + 256)
        p.dma(gus[i0][:, :, 0:f1 - f0], wgv[:, :, f0:f1], writes=[("gus", i0)], q="pool")
        p.dma(gus[i0 + 1][:, :, 0:f1 - f0], wuv[:, :, f0:f1], writes=[("gus", i0 + 1)], q="pool")
        slab_of[sidx] = i0
        return i0

    ngroups = (FC + FG - 1) // FG
    for g in range(ngroups):
        fos = list(range(g * FG, min(FC, (g + 1) * FG)))
        for li_ in range(len(fos)):
            p.dma(wds[:, li_, :], wdv[:, fos[0] + li_, :], writes=["wds"], q="pool")
        for li, fo in enumerate(fos):
            i0 = ensure_slab(fo)
            off = (fo % 2) * 128
            s = fo % 2
            gps = []
            for (b0, b1) in blocks:
                ps, pk = banks.next()
                for c in range(KC):
                    p.op("pe", lambda e, ps=ps, c=c, b0=b0, b1=b1: e.matmul(
                        ps[:, 0:b1 - b0], lhsT=gus[i0][:, c, off:off + 128], rhs=u[:, c, b0:b1],
                        start=(c == 0), stop=(c == KC - 1)),
                        reads=[("gus", i0), ("u", c)], writes=[pk])
                p.op("act", lambda e, ps=ps, b0=b0, b1=b1: e.activation(out=gsb[:, s, b0:b1], in_=ps[:, 0:b1 - b0], func=AF.Copy),
                     reads=[pk], writes=[("gsb", s)])
            ups = []
            for (b0, b1) in blocks:
                ps, pk = banks.next()
                for c in range(KC):
                    p.op("pe", lambda e, ps=ps, c=c, b0=b0, b1=b1: e.matmul(
                        ps[:, 0:b1 - b0], lhsT=gus[i0 + 1][:, c, off:off + 128], rhs=u[:, c, b0:b1],
                        start=(c == 0), stop=(c == KC - 1)),
                        reads=[("gus", i0 + 1), ("u", c)], writes=[pk])
                ups.append((ps, pk, b0, b1))
            n = WF - 2
            p.op("dve", lambda e: e.tensor_scalar(out=tsb[:, s, 1:1 + n], in0=gsb[:, s, 0:n], scalar1=cw[:, fo, 0:1],
                                                  scalar2=None, op0=ALU.mult),
                 reads=[("gsb", s), "cw"], writes=[("tmp", s)])
            for k in (1, 2):
                p.op("dve", lambda e, k=k: e.scalar_tensor_tensor(out=tsb[:, s, 1:1 + n], in0=gsb[:, s, k:k + n],
                                                                  scalar=cw[:, fo, k:k + 1], in1=tsb[:, s, 1:1 + n],
                                                                  op0=ALU.mult, op1=ALU.add),
                     reads=[("gsb", s), "cw", ("tmp", s)], writes=[("tmp", s)])
            p.op("act", lambda e: e.activation(out=tsb[:, s, 1:1 + n], in_=tsb[:, s, 1:1 + n], func=AF.Silu,
                                               bias=cb[:, fo:fo + 1], scale=1.0),
                 reads=[("tmp", s), "cb"], writes=[("tmp", s)])
            for (ps, pk, b0, b1) in ups:
                a0 = max(b0, 1)
                a1 = min(b1, WF - 1)
                p.op("dve", lambda e, ps=ps, b0=b0, a0=a0, a1=a1: e.tensor_tensor(
                    out=act[:, li, a0:a1], in0=tsb[:, s, a0:a1], in1=ps[:, a0 - b0:a1 - b0], op=ALU.mult),
                    reads=[("tmp", s), pk], writes=[("act", li)])
        if g + 1 < ngroups:
            ensure_slab((g + 1) * FG)
        for m in range(KC):
            for (d0, d1, col) in dblocks:
                ps, pk = banks.next()
                for li in range(len(fos)):
                    p.op("pe", lambda e, ps=ps, li=li, d0=d0, d1=d1: e.matmul(
                        ps[:, 0:d1 - d0], lhsT=wds[:, li, m * 128:(m + 1) * 128], rhs=act[:, li, d0:d1],
                        start=(li == 0), stop=(li == len(fos) - 1)),
                        reads=["wds", ("act", li)], writes=[pk])
                if os.environ.get("FFN_EVAC", "dve") == "split":
                    es = nev[0] % 2
                    nev[0] += 1
                    p.op("act", lambda e: e.activation(out=evb[:, es, 0:d1 - d0], in_=ps[:, 0:d1 - d0], func=AF.Copy,
                                                       scale=modv[:, 5 * KC + m, col:col + 1]),
                         reads=[pk, "modv"], writes=[("evb", es)])
                    p.op("pool", lambda e: e.tensor_tensor(out=h[:, m, d0:d1], in0=h[:, m, d0:d1], in1=evb[:, es, 0:d1 - d0], op=ALU.add),
                         reads=[("evb", es), ("h", m)], writes=[("h", m)])
                else:
                    p.op("dve", lambda e, ps=ps, d0=d0, d1=d1, col=col: e.scalar_tensor_tensor(
                        out=h[:, m, d0:d1], in0=ps[:, 0:d1 - d0], scalar=modv[:, 5 * KC + m, col:col + 1],
                        in1=h[:, m, d0:d1], op0=ALU.mult, op1=ALU.add),
                        reads=[pk, "modv", ("h", m)], writes=[("h", m)])
    hov = ho_d.rearrange("(c p) w -> p c w", p=128)
    for c4 in range(0, KC, 4):
        ks = [("h", c) for c in range(c4, c4 + 4)]
        p.dma(hov[:, c4:c4 + 4, 0:1024], h[:, c4:c4 + 4, 1:1025], reads=ks)
        p.dma(hov[:, c4:c4 + 4, 1024:1152], h[:, c4:c4 + 4, 1027:1155], reads=ks)
    if final:
        gf = load_small(p, "gf", [128, KC], gf_d)
        rms_rstd_chunked(p, cm, banks, h, 1, 1025, rstd, sq)
        ofv = of_d.rearrange("(c p) w -> p c w", p=128)
        for c in range(KC):
            s = c % 2
            p.op("dve", lambda e, c=c, s=s: e.tensor_tensor(out=tsb[:, s, 1:1025], in0=h[:, c, 1:1025], in1=rstd[:, 1:1025], op=ALU.mult),
                 reads=[("h", c), "rstd"], writes=[("tmp", s)])
            p.op("act", lambda e, c=c, s=s: e.activation(out=tsb[:, s, 1:1025], in_=tsb[:, s, 1:1025], func=AF.Copy, scale=gf[:, c:c + 1]),
                 reads=[("tmp", s), "gf"], writes=[("tmp", s)])
            p.dma(ofv[:, c, :], tsb[:, s, 1:1025], reads=[("tmp", s)])
    return p.build()


def rms_rstd_chunked(p, cm, banks, h, w0, w1, rstd, sq, hname="h"):
    for (b0, b1) in col_blocks(w0, w1):
        ps, pk = banks.next()
        n = b1 - b0
        for c in range(KC):
            s = c % 2
            p.op("act", lambda e, c=c, s=s: e.activation(out=sq[:, s, 0:n], in_=h[:, c, b0:b1], func=AF.Square),
                 reads=[(hname, c)], writes=[("sq", s)])
            p.op("pe", lambda e, c=c, s=s: e.matmul(ps[:, 0:n], lhsT=cm.ones[:], rhs=sq[:, s, 0:n],
                                                     start=(c == 0), stop=(c == KC - 1)),
                 reads=[("sq", s), "c_ones"], writes=[pk])
        p.op("act", lambda e, ps=ps: e.activation(out=rstd[:, b0:b1], in_=ps[:, 0:n], func=AF.Sqrt,
                                                  bias=cm.eps[:, 0:1], scale=1.0 / D),
             reads=[pk, "c_eps"], writes=["rstd"])
        p.op("dve", lambda e: e.reciprocal(out=rstd[:, b0:b1], in_=rstd[:, b0:b1]), reads=["rstd"], writes=["rstd"])


def _norm_mod_chunked(p, cm, banks, h, W, segs, out_bf, okey, rstd, sq, tmp, mask, keys_A, hname="h", w0=0):
    rms_rstd_chunked(p, cm, banks, h, w0, W, rstd, sq, hname)
    for c in range(KC):
        s = c % 2
        p.op("dve", lambda e, c=c, s=s: e.tensor_tensor(out=tmp[:, s, w0:W], in0=h[:, c, w0:W], in1=rstd[:, w0:W], op=ALU.mult),
             reads=[(hname, c), "rstd"], writes=[("tmp", s)])
        for (c0, c1, A, Bfn) in segs:
            p.op("act", lambda e, c=c, s=s, c0=c0, c1=c1, A=A, Bfn=Bfn: e.activation(
                out=out_bf[:, c, c0:c1], in_=tmp[:, s, c0:c1], func=AF.Identity, scale=A[:, c:c + 1], bias=Bfn(c)),
                reads=[("tmp", s), "modv"] + keys_A, writes=[(okey, c)])
        if mask is not None:
            p.op("pool", lambda e, c=c: e.tensor_tensor(out=out_bf[:, c, w0:W], in0=out_bf[:, c, w0:W], in1=mask[:, w0:W], op=ALU.mult),
                 reads=[(okey, c), "mask"], writes=[(okey, c)])


def linear_fm(p, banks, name, x, xkey, kcin, wview, dout, blocks, evac, sw=512, q="pool", nbuf=2, wdt=BF16):
    slabs = [p.sb("%s_w%d" % (name, i), [128, kcin, sw], wdt) for i in range(nbuf)]
    ns = (dout + sw - 1) // sw
    for si in range(ns):
        sl = slabs[si % nbuf]
        sk = (name + "_w", si % nbuf)
        f0 = si * sw
        f1 = min(dout, f0 + sw)
        half = kcin // 2 if kcin >= 8 else kcin
        for k0 in range(0, kcin, half):
            p.dma(sl[:, k0:k0 + half, 0:f1 - f0], wview[:, k0:k0 + half, f0:f1], writes=[sk], q=q)
        for mi in range((f1 - f0) // 128):
            m = (f0 // 128) + mi
            for (b0, b1) in blocks:
                ps, pk = banks.next()
                for c in range(kcin):
                    p.op("pe", lambda e: e.matmul(ps[:, 0:b1 - b0], lhsT=sl[:, c, mi * 128:(mi + 1) * 128],
                                                  rhs=x[:, c, b0:b1], start=(c == 0), stop=(c == kcin - 1)),
                         reads=[sk, (xkey, c)], writes=[pk])
                evac(m, ps, pk, b0, b1)


def build_mod():
    p = Prog()
    wm_d = p.dram("wm", [D, 6144])
    bm_d = p.dram("bm", [128, 48])
    ct_d = p.dram("ct", [128, KC, 8])
    mo_d = p.dram("mo", [128, 48, 8], kind="ExternalOutput")
    banks = Banks(p)
    ct = load_small(p, "ct", [128, KC, 8], ct_d)
    bm = load_small(p, "bm", [128, 48], bm_d)
    cb16 = p.sb("cb16", [128, KC, 8], F32)
    mo = p.sb("mo", [128, 48, 8], F32)
    p.op("act", lambda e: e.activation(out=cb16[:], in_=ct[:], func=AF.Silu), reads=["ct"],
         writes=[("cb16", c) for c in range(KC)])

    def evac(m, ps, pk, b0, b1):
        p.op("dve", lambda e: e.tensor_scalar(out=mo[:, m, :], in0=ps[:, 0:8], scalar1=bm[:, m:m + 1], scalar2=None, op0=ALU.add),
             reads=[pk, "bm"], writes=["mo"])
    linear_fm(p, banks, "mod", cb16, "cb16", KC, wm_d.rearrange("(c p) f -> p c f", p=128), 6144, [(0, 8)], evac,
              q="sync", wdt=F32)
    p.dma(mo_d, mo[:], reads=["mo"])
    return p.build()


WP = 1184
P_L0, P_L1, P_C0, P_C1 = 0, 1040, 1040, 1184


def build_pool():
    p = Prog()
    h_d = p.dram("h", [D, WP])
    modv_d = p.dram("modv", [128, 6 * KC, 2])
    g1_d = p.dram("g1", [128, KC])
    mask_d = p.dram("mask", [1, WP])
    icnt_d = p.dram("icnt", [4, WP])
    pw_d = p.dram("pw", [4, 512, 512])
    pb_d = p.dram("pb", [128, KC])
    psc_d = p.dram("psc", [128, KC])
    y_d = p.dram("y", [D, 1152], kind="ExternalOutput")
    cm = Common(p)
    banks = Banks(p)
    h = p.sb("h", [128, KC, WP], F32)
    pbf = p.sb("pbf", [128, KC, WP], BF16)
    tsb = p.sb("tsb", [128, 2, WP], F32)
    t2 = p.sb("t2", [128, 2, WP], F32)
    sq = p.sb("sq", [128, 2, 512], BF16)
    rstd = p.sb("rstd", [128, WP], F32)
    yo = p.sb("yo", [128, 2, WP], F32)
    modv = load_small(p, "modv", [128, 6 * KC, 2], modv_d)
    gvec = load_small(p, "gvec", [128, KC], g1_d)
    pb = load_small(p, "pb", [128, KC], pb_d)
    psc = load_small(p, "psc", [128, KC], psc_d)
    mask = load_small(p, "mask", [128, WP], mask_d[0].partition_broadcast(128))
    icnt = p.sb("icnt", [128, 4, WP], F32)
    for g in range(4):
        p.dma(icnt[:, g, :], icnt_d[g].partition_broadcast(128), writes=["icnt"])
    hv = h_d.rearrange("(c p) w -> p c w", p=128)
    for c4 in range(0, KC, 4):
        p.dma(h[:, c4:c4 + 4, :], hv[:, c4:c4 + 4, :], writes=[("h", c) for c in range(c4, c4 + 4)])
    A_l = make_AB(p, "pl", modv, gvec, 0, 1, 0)
    A_c = make_AB(p, "pc", modv, gvec, 0, 1, 1)
    segs = [(P_L0, P_L1, A_l, lambda c: modv[:, c, 0:1]), (P_C0, P_C1, A_c, lambda c: modv[:, c, 1:2])]
    rms_rstd_chunked(p, cm, banks, h, 0, WP, rstd, sq)
    for c in range(KC):
        s = c % 2
        p.op("dve", lambda e: e.tensor_tensor(out=tsb[:, s, :], in0=h[:, c, :], in1=rstd[:], op=ALU.mult),
             reads=[("h", c), "rstd"], writes=[("tmp", s)])
        for (c0, c1, A, Bfn) in segs:
            p.op("act", lambda e: e.activation(out=h[:, c, c0:c1], in_=tsb[:, s, c0:c1], func=AF.Identity,
                                               scale=A[:, c:c + 1], bias=Bfn(c)),
                 reads=[("tmp", s), "modv", "pl_A", "pc_A"], writes=[("h", c)])
        p.op("pool", lambda e: e.tensor_tensor(out=h[:, c, :], in0=h[:, c, :], in1=mask[:], op=ALU.mult),
             reads=[("h", c), "mask"], writes=[("h", c)])
        g = c // 4
        win = (2, 4, 8, 16)[g]
        right = win - 1 - win // 2
        src, skey = h[:, c, :], ("h", c)
        sh = 1
        bufs = [tsb[:, s, :], t2[:, s, :]]
        bkeys = [("tmp", s), ("t2", s)]
        bi = 0
        while sh < win:
            dst, dkey = bufs[bi], bkeys[bi]
            p.op("dve", lambda e: e.tensor_tensor(out=dst[:, sh:WP], in0=src[:, sh:WP], in1=src[:, 0:WP - sh], op=ALU.add),
                 reads=[skey], writes=[dkey])
            p.op("dve", lambda e: e.tensor_copy(out=dst[:, 0:sh], in_=src[:, 0:sh]), reads=[skey], writes=[dkey])
            src, skey = dst, dkey
            sh *= 2
            bi ^= 1
        dst, dkey = bufs[bi], bkeys[bi]
        n = WP - 16
        p.op("dve", lambda e: e.tensor_tensor(out=dst[:, 8:8 + n], in0=src[:, 8 + right:8 + right + n],
                                              in1=icnt[:, g, 8:8 + n], op=ALU.mult),
             reads=[skey, "icnt"], writes=[dkey])
        p.op("dve", lambda e: e.tensor_tensor(out=pbf[:, c, 8:8 + n], in0=dst[:, 8:8 + n], in1=h[:, c, 8:8 + n], op=ALU.subtract),
             reads=[dkey, ("h", c)], writes=[("pbf", c)])
    yv = y_d.rearrange("(c p) w -> p c w", p=128)
    vblocks = [(8, 520), (520, 1032), (1048, 1176)]
    for g in range(4):
        wsl = p.sb("pw%d" % g, [128, 4, 512], BF16)
        p.dma(wsl[:], pw_d[g].rearrange("(c p) f -> p c f", p=128), writes=[("pw", g)], q="pool")
        for mo_ in range(4):
            m = g * 4 + mo_
            s = m % 2
            for (b0, b1) in vblocks:
                ps, pk = banks.next()
                for ci in range(4):
                    p.op("pe", lambda e: e.matmul(ps[:, 0:b1 - b0], lhsT=wsl[:, ci, mo_ * 128:(mo_ + 1) * 128],
                                                  rhs=pbf[:, g * 4 + ci, b0:b1], start=(ci == 0), stop=(ci == 3)),
                         reads=[("pw", g), ("pbf", g * 4 + ci)], writes=[pk])
                p.op("dve", lambda e: e.tensor_scalar(out=yo[:, s, b0:b1], in0=ps[:, 0:b1 - b0], scalar1=pb[:, m:m + 1],
                                                      scalar2=psc[:, m:m + 1], op0=ALU.add, op1=ALU.mult),
                     reads=[pk, "pb", "psc"], writes=[("yo", s)])
            p.dma(yv[:, m, 0:1024], yo[:, s, 8:1032], reads=[("yo", s)])
            p.dma(yv[:, m, 1024:1152], yo[:, s, 1048:1176], reads=[("yo", s)])
    return p.build()


def norm_mod_stream(p, cm, banks, hview, W, segs, out, okey, rstd, sq, hbuf, tmp, akeys, mask=None, out_chunks=None):
    blocks = col_blocks(0, W)
    pss = [banks.next() for _ in blocks]
    for c in range(KC):
        s = c % 2
        p.dma(hbuf[:, s, 0:W], hview[:, c, :], writes=[("hbuf", s)])
        for bi, (b0, b1) in enumerate(blocks):
            ps, pk = pss[bi]
            p.op("act", lambda e: e.activation(out=sq[:, s, 0:b1 - b0], in_=hbuf[:, s, b0:b1], func=AF.Square),
                 reads=[("hbuf", s)], writes=[("sq", s)])
            p.op("pe", lambda e: e.matmul(ps[:, 0:b1 - b0], lhsT=cm.ones[:], rhs=sq[:, s, 0:b1 - b0],
                                          start=(c == 0), stop=(c == KC - 1)),
                 reads=[("sq", s), "c_ones"], writes=[pk])
    for bi, (b0, b1) in enumerate(blocks):
        ps, pk = pss[bi]
        p.op("act", lambda e: e.activation(out=rstd[:, b0:b1], in_=ps[:, 0:b1 - b0], func=AF.Sqrt,
                                           bias=cm.eps[:, 0:1], scale=1.0 / D),
             reads=[pk, "c_eps"], writes=["rstd"])
        p.op("dve", lambda e: e.reciprocal(out=rstd[:, b0:b1], in_=rstd[:, b0:b1]), reads=["rstd"], writes=["rstd"])
    for c in range(KC):
        s = c % 2
        p.dma(hbuf[:, s, 0:W], hview[:, c, :], writes=[("hbuf", s)])
        p.op("dve", lambda e: e.tensor_tensor(out=tmp[:, s, 0:W], in0=hbuf[:, s, 0:W], in1=rstd[:, 0:W], op=ALU.mult),
             reads=[("hbuf", s), "rstd"], writes=[("tmp", s)])
        for (c0, c1, A, Bfn) in segs:
            p.op("act", lambda e: e.activation(out=out[:, c, c0:c1], in_=tmp[:, s, c0:c1], func=AF.Identity,
                                               scale=A[:, c:c + 1], bias=Bfn(c)),
                 reads=[("tmp", s), "modv"] + akeys, writes=[(okey, c)])
        if mask is not None:
            p.op("pool", lambda e: e.tensor_tensor(out=out[:, c, 0:W], in0=out[:, c, 0:W], in1=mask[:, 0:W], op=ALU.mult),
                 reads=[(okey, c), "mask"], writes=[(okey, c)])


def linear_fm2(p, banks, slabs, skey, x, xkey, kcin, wview, dout, blocks, evac, q="pool"):
    sw = slabs[0].shape[2]
    nbuf = len(slabs)
    ns = (dout + sw - 1) // sw
    for si in range(ns):
        sl = slabs[si % nbuf]
        sk = (skey, si % nbuf)
        f0 = si * sw
        f1 = min(dout, f0 + sw)
        half = kcin // 2 if kcin >= 8 else kcin
        for k0 in range(0, kcin, half):
            p.dma(sl[:, k0:k0 + half, 0:f1 - f0], wview[:, k0:k0 + half, f0:f1], writes=[sk], q=q)
        for mi in range((f1 - f0) // 128):
            m = (f0 // 128) + mi
            for (b0, b1) in blocks:
                ps, pk = banks.next()
                for c in range(kcin):
                    p.op("pe", lambda e: e.matmul(ps[:, 0:b1 - b0], lhsT=sl[:, c, mi * 128:(mi + 1) * 128],
                                                  rhs=x[:, c, b0:b1], start=(c == 0), stop=(c == kcin - 1)),
                         reads=[sk, (xkey, c)], writes=[pk])
                evac(m, ps, pk, b0, b1)


WG = 1152
NPC = 9


def build_gmlp():
    p = Prog()
    h_d = p.dram("h", [D, WG])
    modv_d = p.dram("modv", [128, 6 * KC, 2])
    g1_d = p.dram("g1", [128, KC])
    win_d = p.dram("win", [D, 2 * D])
    bzu_d = p.dram("bzu", [128, KC])
    bzv_d = p.dram("bzv", [1, D])
    ng_d = p.dram("ng", [1, D])
    wst_d = p.dram("wst", [128, 16, 128])
    bs_d = p.dram("bs", [1, 16 * 128])
    wo_d = p.dram("wo", [D, D])
    y_d = p.dram("y", [D, WG], kind="ExternalOutput")
    cm = Common(p)
    banks = Banks(p)
    aT = p.sb("aT", [128, KC, WG], BF16)
    zu = p.sb("zu", [128, KC, WG], BF16)
    zv = p.sb("zv", [128, NPC, D], BF16)
    hbuf = p.sb("hbuf", [128, 2, WG], F32)
    tmp = p.sb("tmp", [128, 2, WG], F32)
    sq = p.sb("sq", [128, 2, 512], BF16)
    rstd = p.sb("rstd", [128, WG], F32)
    slabs = [p.sb("slab%d" % i, [128, KC, 256], BF16) for i in range(2)]
    modv = load_small(p, "modv", [128, 6 * KC, 2], modv_d)
    gvec = load_small(p, "gvec", [128, KC], g1_d)
    bzu = load_small(p, "bzu", [128, KC], bzu_d)
    bzv = load_small(p, "bzv", [128, D], bzv_d[0].partition_broadcast(128))
    ng = load_small(p, "ng", [128, D], ng_d[0].partition_broadcast(128))
    bs = load_small(p, "bs", [128, 16 * 128], bs_d[0].partition_broadcast(128))
    wst = p.sb("wst", [128, 16, 128], BF16)
    p.dma(wst[:], wst_d, writes=["wst"], q="pool")
    ssq = p.sb("ssq", [128, NPC, 8], F32)
    rtok = p.sb("rtok", [128, NPC], F32)
    A_l = make_AB(p, "gl", modv, gvec, 0, 1, 0)
    A_c = make_AB(p, "gc", modv, gvec, 0, 1, 1)
    segs = [(0, 1024, A_l, lambda c: modv[:, c, 0:1]), (1024, WG, A_c, lambda c: modv[:, c, 1:2])]
    norm_mod_stream(p, cm, banks, h_d.rearrange("(c p) w -> p c w", p=128), WG, segs, aT, "aT", rstd, sq, hbuf, tmp,
                    ["gl_A", "gc_A"])
    blocks = col_blocks(0, WG)
    winv = win_d.rearrange("(c p) f -> p c f", p=128)
    import os
    stop = int(os.environ.get("GSTOP", "99"))
    if stop <= 0:
        return p.build()

    def evac_zu(m, ps, pk, b0, b1):
        p.op("act", lambda e: e.activation(out=zu[:, m, b0:b1], in_=ps[:, 0:b1 - b0], func=AF.Gelu_apprx_tanh,
                                           bias=bzu[:, m:m + 1], scale=1.0),
             reads=[pk, "bzu"], writes=[("zu", m)])
    linear_fm2(p, banks, slabs, "slab", aT, "aT", KC, winv[:, :, 0:D], D, blocks, evac_zu)

    if stop <= 1:
        return p.build()
    sw = 256
    for si in range(D // sw):
        sl = slabs[si % 2]
        sk = ("slab", si % 2)
        f0 = D + si * sw
        for k0 in (0, 8):
            p.dma(sl[:, k0:k0 + 8, :], winv[:, k0:k0 + 8, f0:f0 + sw], writes=[sk], q="pool")
        for n in range(NPC):
            ps, pk = banks.next()
            for c in range(KC):
                p.op("pe", lambda e: e.matmul(ps[:, 0:sw], lhsT=aT[:, c, n * 128:(n + 1) * 128], rhs=sl[:, c, :],
                                              start=(c == 0), stop=(c == KC - 1)),
                     reads=[sk, ("aT", c)], writes=[pk])
            s = (si * NPC + n) % 2
            p.op("dve", lambda e: e.tensor_tensor(out=tmp[:, s, 0:sw], in0=ps[:, 0:sw], in1=bzv[:, si * sw:(si + 1) * sw], op=ALU.add),
                 reads=[pk, "bzv"], writes=[("tmp", s)])
            p.op("act", lambda e: e.activation(out=tmp[:, s, 0:sw], in_=tmp[:, s, 0:sw], func=AF.Gelu_apprx_tanh),
                 reads=[("tmp", s)], writes=[("tmp", s)])
            p.op("act", lambda e: e.activation(out=tmp[:, s, 512:512 + sw], in_=tmp[:, s, 0:sw], func=AF.Square,
                                               accum_out=ssq[:, n, si:si + 1]),
                 reads=[("tmp", s)], writes=[("tmp", s), "ssq"])
            p.op("pool", lambda e: e.tensor_copy(out=zv[:, n, si * sw:(si + 1) * sw], in_=tmp[:, s, 0:sw]),
                 reads=[("tmp", s)], writes=[("zv", n)])
    if stop <= 2:
        return p.build()
    p.op("dve", lambda e: e.tensor_reduce(out=rtok[:], in_=ssq[:], axis=AX.X, op=ALU.add), reads=["ssq"], writes=["rtok"])
    p.op("act", lambda e: e.activation(out=rtok[:], in_=rtok[:], func=AF.Sqrt, bias=cm.eps[:, 0:1], scale=1.0 / D),
         reads=["rtok", "c_eps"], writes=["rtok"])
    p.op("dve", lambda e: e.reciprocal(out=rtok[:], in_=rtok[:]), reads=["rtok"], writes=["rtok"])
    for n in range(NPC):
        p.op("dve", lambda e: e.scalar_tensor_tensor(out=zv[:, n, :], in0=zv[:, n, :], scalar=rtok[:, n:n + 1], in1=ng[:],
                                                     op0=ALU.mult, op1=ALU.mult),
             reads=[("zv", n), "rtok", "ng"], writes=[("zv", n)])
    if stop <= 3:
        return p.build()
    bsv = bs[:].rearrange("p (g q) -> p g q", q=128)
    for n in range(NPC):
        for g4 in range(4):
            ps, pk = banks.next()
            for gi in range(4):
                g = g4 * 4 + gi
                p.op("pe", lambda e: e.matmul(ps[:, gi * 128:(gi + 1) * 128], lhsT=zv[:, n, g * 128:(g + 1) * 128],
                                              rhs=wst[:, g, :], start=True, stop=True),
                     reads=[("zv", n), "wst"], writes=[pk])
            s = (n * 4 + g4) % 2
            tv = tmp[:, s, 0:512].rearrange("p (g q) -> p g q", q=128)
            p.op("dve", lambda e: e.tensor_tensor(out=tv, in0=ps[:, 0:512].rearrange("p (g q) -> p g q", q=128),
                                                  in1=bsv[:, g4 * 4:(g4 + 1) * 4, :], op=ALU.add),
                 reads=[pk, "bs"], writes=[("tmp", s)])
            p.op("dve", lambda e: e.tensor_tensor(out=aT[:, g4 * 4:(g4 + 1) * 4, n * 128:(n + 1) * 128], in0=tv,
                                                  in1=zu[:, g4 * 4:(g4 + 1) * 4, n * 128:(n + 1) * 128], op=ALU.mult),
                 reads=[("tmp", s)] + [("zu", g4 * 4 + i) for i in range(4)],
                 writes=[("aT", g4 * 4 + i) for i in range(4)])
    if stop <= 4:
        return p.build()
    yv = y_d.rearrange("(c p) w -> p c w", p=128)
    cnt = [0]

    def evac_y(m, ps, pk, b0, b1):
        s = cnt[0] % 2
        cnt[0] += 1
        p.op("act", lambda e: e.activation(out=hbuf[:, s, b0:b1], in_=ps[:, 0:b1 - b0], func=AF.Copy),
             reads=[pk], writes=[("hbuf", s)])
        if os.environ.get("GNODMA") != "1":
            p.dma(yv[:, m, b0:b1], hbuf[:, s, b0:b1], reads=[("hbuf", s)], q="act")
    linear_fm2(p, banks, slabs, "slab", aT, "aT", KC, wo_d.rearrange("(c p) f -> p c f", p=128), D, blocks, evac_y)
    return p.build()


def build_na1():
    p = Prog()
    W = WG
    h_d = p.dram("h", [D, W])
    modv_d = p.dram("modv", [128, 6 * KC, 2])
    g1_d = p.dram("g1", [128, KC])
    w_d = p.dram("wqkv", [D, 3 * D])
    cos_d = p.dram("rcos", [128, 1024])
    sin_d = p.dram("rsin", [128, 1024])
    pm_d = p.dram("pm", [128, 128])
    q_d = p.dram("qT", [D, W], kind="ExternalOutput")
    k_d = p.dram("kT", [D, W], kind="ExternalOutput")
    v_d = p.dram("v", [W, D], kind="ExternalOutput")
    cm = Common(p)
    banks = Banks(p)
    aT = p.sb("aT", [128, KC, W], BF16)
    hbuf = p.sb("hbuf", [128, 2, W], F32)
    tmp = p.sb("tmp", [128, 2, W], F32)
    st = p.sb("st", [128, 2, W], F32)
    qb = p.sb("qb", [128, 2, 512], BF16)
    sq = p.sb("sq", [128, 2, 512], BF16)
    rstd = p.sb("rstd", [128, W], F32)
    slabs = [p.sb("slab%d" % i, [128, KC, 256], BF16) for i in range(2)]
    modv = load_small(p, "modv", [128, 6 * KC, 2], modv_d)
    gvec = load_small(p, "gvec", [128, KC], g1_d)
    rcos = load_small(p, "rcos", [128, 1024], cos_d)
    rsin = load_small(p, "rsin", [128, 1024], sin_d)
    pm = p.sb("pm", [128, 128], BF16)
    if os.environ.get("NOPM") != "1":
        p.dma(pm[:], pm_d, writes=["pm"], q="pool", max_dma_last_dim=int(os.environ.get("MDL", "512")))
    A_l = make_AB(p, "nl", modv, gvec, 0, 1, 0)
    A_c = make_AB(p, "ncx", modv, gvec, 0, 1, 1)
    segs = [(0, 1024, A_l, lambda c: modv[:, c, 0:1]), (1024, W, A_c, lambda c: modv[:, c, 1:2])]
    norm_mod_stream(p, cm, banks, h_d.rearrange("(c p) w -> p c w", p=128), W, segs, aT, "aT", rstd, sq, hbuf, tmp,
                    ["nl_A", "ncx_A"])
    wv = w_d.rearrange("(c p) f -> p c f", p=128)
    qv = q_d.rearrange("(c p) w -> p c w", p=128)
    kv = k_d.rearrange("(c p) w -> p c w", p=128)
    blocks = col_blocks(0, W)
    cnt = [0]
    stop = int(os.environ.get("GSTOP", "99"))
    if stop <= 0:
        return p.build()

    def evac_qk(m, ps, pk, b0, b1):
        isq = m < KC
        sc = 0.125 if isq else 1.0
        mm = m if isq else m - KC
        s = mm % 2
        n = b1 - b0
        ev = int(os.environ.get("EV", "9"))
        if b0 >= 1024 or ev == 0:
            p.op("act", lambda e: e.activation(out=st[:, s, b0:b1], in_=ps[:, 0:n], func=AF.Copy, scale=sc),
                 reads=[pk], writes=[("st", s, 2)])
        else:
            i = cnt[0] % 2
            cnt[0] += 1
            p.op("act", lambda e: e.activation(out=qb[:, i, 0:n], in_=ps[:, 0:n], func=AF.Copy, scale=sc),
                 reads=[pk], writes=[("qb", i)])
            ps2, pk2 = banks.next()
            p.op("pe", lambda e: e.matmul(ps2[:, 0:n], lhsT=pm[:], rhs=qb[:, i, 0:n], start=True, stop=True),
                 reads=["pm", ("qb", i)], writes=[pk2])
            if ev == 1:
                p.op("act", lambda e: e.activation(out=st[:, s, b0:b1], in_=ps2[:, 0:n], func=AF.Copy, scale=sc),
                     reads=[pk, pk2], writes=[("st", s, b0 // 512)])
                if b1 == W:
                    pass
                return
            p.op("dve", lambda e: e.scalar_tensor_tensor(out=st[:, s, b0:b1], in0=ps[:, 0:n], scalar=sc, in1=rcos[:, b0:b1],
                                                         op0=ALU.mult, op1=ALU.mult),
                 reads=[pk, "rcos"], writes=[("st", s, b0 // 512)])
            p.op("dve", lambda e: e.tensor_tensor(out=tmp[:, i, 0:n], in0=ps2[:, 0:n], in1=rsin[:, b0:b1], op=ALU.mult),
                 reads=[pk2, "rsin"], writes=[("tmp", i)])
            p.op("dve", lambda e: e.tensor_tensor(out=st[:, s, b0:b1], in0=st[:, s, b0:b1], in1=tmp[:, i, 0:n], op=ALU.add),
                 reads=[("tmp", i), ("st", s, b0 // 512)], writes=[("st", s, b0 // 512)])
        if b1 == W:
            dst = qv if isq else kv
            p.dma(dst[:, mm, :], st[:, s, :], reads=[("st", s, 0), ("st", s, 1), ("st", s, 2)], q="act")
    linear_fm2(p, banks, slabs, "slab", aT, "aT", KC, wv[:, :, 0:2 * D], 2 * D, blocks, evac_qk)
    if stop <= 1:
        return p.build()
    sw = 256
    for si in range(D // sw):
        sl = slabs[si % 2]
        sk = ("slab", si % 2)
        f0 = 2 * D + si * sw
        for k0 in (0, 8):
            p.dma(sl[:, k0:k0 + 8, :], wv[:, k0:k0 + 8, f0:f0 + sw], writes=[sk], q="pool")
        for n in range(NPC):
            ps, pk = banks.next()
            for c in range(KC):
                p.op("pe", lambda e: e.matmul(ps[:, 0:sw], lhsT=aT[:, c, n * 128:(n + 1) * 128], rhs=sl[:, c, :],
                                              start=(c == 0), stop=(c == KC - 1)),
                     reads=[sk, ("aT", c)], writes=[pk])
            s = (si * NPC + n) % 2
            p.op("act", lambda e: e.activation(out=tmp[:, s, 0:sw], in_=ps[:, 0:sw], func=AF.Copy),
                 reads=[pk], writes=[("tmp", s)])
            p.dma(v_d[n * 128:(n + 1) * 128, si * sw:(si + 1) * sw], tmp[:, s, 0:sw], reads=[("tmp", s)], q="act")
    return p.build()


NTOK = T + L


def na_rs(r):
    return min(max(r - 4, 0), 24)


def build_na2():
    p = Prog()
    q_d = p.dram("qT", [1024, NTOK])
    k_d = p.dram("kT", [1024, NTOK])
    v_d = p.dram("v", [NTOK, 1024])
    bt_d = p.dram("bt", [16, 128, 15 * 64])
    wo_d = p.dram("wo", [1024, D])
    y_d = p.dram("y", [D, NTOK], kind="ExternalOutput")
    sbanks = Banks(p, 4, "ps_s")
    abanks = Banks(p, 4, "ps_a")
    ones = p.sb("ones", [128, 64], BF16)
    p.op("dve", lambda e: e.memset(ones[:], 1.0), writes=["ones"])
    qT = p.sb("qT", [128, 8, NTOK], BF16)
    kT = p.sb("kT", [128, 8, NTOK], BF16)
    v = p.sb("v", [128, 18, 1024], BF16)
    at = p.sb("at", [128, 8, NTOK], BF16)
    bts = [p.sb("bt%d" % i, [128, 15 * 64], F32) for i in range(2)]
    pts = [p.sb("pt%d" % i, [128, 512], BF16) for i in range(4)]
    sbs = [p.sb("sbb%d" % i, [128, 512], F32) for i in range(2)]
    rec = p.sb("rec", [128, 2, 512], F32)
    st = p.sb("st", [128, 2, 512], F32)
    slabs = [p.sb("slab%d" % i, [128, 8, 512], BF16) for i in range(2)]
    qv = q_d.rearrange("(c p) w -> p c w", p=128)
    kvv = k_d.rearrange("(c p) w -> p c w", p=128)
    vv = v_d.rearrange("(n p) f -> p n f", p=128)
    for c in range(8):
        p.dma(qT[:, c, :], qv[:, c, :], writes=[("qT", c)], q="pool", max_dma_last_dim=4096)
        p.dma(kT[:, c, :], kvv[:, c, :], writes=[("kT", c)], q="pool", max_dma_last_dim=4096)
    for n in range(18):
        p.dma(v[:, n, :], vv[:, n, :], writes=[("v", n)], q="pool")
    npt = [0]
    nsb = [0]
    for h in range(16):
        c = h // 2
        base = (h % 2) * 64
        bt = bts[h % 2]
        bk = ("bt", h % 2)
        p.dma(bt[:], bt_d[h], writes=[bk])
        btv = bt[:].rearrange("p (i q) -> p i q", q=64)
        groups = [(g * 512, 512, g) for g in range(4)] + [(T, L, None)]
        for (q0, nq, g) in groups:
            O, ok = abanks.next()
            Dn, dk = abanks.next()
            items = []
            for j in range(2):
                items.append(("ctx", j))
            if g is not None:
                for kap in range(32):
                    rows = [r for r in range(8 * g, 8 * g + 8) if na_rs(r) <= kap < na_rs(r) + 8]
                    if rows:
                        items.append(("loc", kap, rows[0], rows[-1]))
            pend = None

            def stage2(st2):
                (kind, first, last, pt, ptk, args) = st2
                if kind == "ctx":
                    (j,) = args
                    p.op("pe", lambda e: e.matmul(O[base:base + 64, 0:nq], lhsT=v[:, 16 + j, h * 64:(h + 1) * 64],
                                                  rhs=pt[:, 0:nq], start=first, stop=last),
                         reads=[("v", 16 + j), ptk], writes=[ok], pe_sync=True)
                    p.op("pe", lambda e: e.matmul(Dn[base:base + 64, 0:nq], lhsT=ones[:, :], rhs=pt[:, 0:nq],
                                                  start=first, stop=last),
                         reads=["ones", ptk], writes=[dk], pe_sync=True)
                else:
                    (kap, kb, c0, nn) = args
                    p.op("pe", lambda e: e.matmul(O[base:base + 64, c0:c0 + nn], lhsT=v[kb:kb + 64, kap // 2, h * 64:(h + 1) * 64],
                                                  rhs=pt[kb:kb + 64, 0:nn], start=False, stop=last),
                         reads=[("v", kap // 2), ptk], writes=[ok], pe_sync=True)
                    p.op("pe", lambda e: e.matmul(Dn[base:base + 64, c0:c0 + nn], lhsT=ones[kb:kb + 64, :],
                                                  rhs=pt[kb:kb + 64, 0:nn], start=False, stop=last),
                         reads=["ones", ptk], writes=[dk], pe_sync=True)

            for ii, it in enumerate(items):
                first = ii == 0
                last = ii == len(items) - 1
                S, sk_ = sbanks.next()
                pt = pts[npt[0] % 4]
                ptk = ("pt", npt[0] % 4)
                npt[0] += 1
                if it[0] == "ctx":
                    j = it[1]
                    kc0 = T + j * 128
                    p.op("pe", lambda e: e.matmul(S[:, 0:nq], lhsT=kT[base:base + 64, c, kc0:kc0 + 128],
                                                  rhs=qT[base:base + 64, c, q0:q0 + nq], start=True, stop=True),
                         reads=[("kT", c), ("qT", c)], writes=[sk_])
                    p.op("act", lambda e: e.activation(out=pt[:, 0:nq], in_=S[:, 0:nq], func=AF.Exp),
                         reads=[sk_], writes=[ptk])
                    cur = ("ctx", first, last, pt, ptk, (j,))
                else:
                    _, kap, ra, rb = it
                    kb = (kap % 2) * 64
                    nr = rb - ra + 1
                    nn = nr * 64
                    c0 = ra * 64 - q0
                    idx0 = ra - kap + 7
                    sb_ = sbs[nsb[0] % 2]
                    sbk = ("sbb", nsb[0] % 2)
                    nsb[0] += 1
                    p.op("pe", lambda e: e.matmul(S[kb:kb + 64, 0:nn], lhsT=kT[base:base + 64, c, kap * 64:(kap + 1) * 64],
                                                  rhs=qT[base:base + 64, c, ra * 64:(rb + 1) * 64], start=True, stop=True),
                         reads=[("kT", c), ("qT", c)], writes=[sk_])
                    p.op("dve", lambda e: e.tensor_tensor(out=sb_[kb:kb + 64, 0:nn].rearrange("p (i q) -> p i q", q=64),
                                                          in0=S[kb:kb + 64, 0:nn].rearrange("p (i q) -> p i q", q=64),
                                                          in1=btv[kb:kb + 64, idx0:idx0 + nr, :], op=ALU.add),
                         reads=[sk_, bk], writes=[sbk])
                    p.op("act", lambda e: e.activation(out=pt[kb:kb + 64, 0:nn], in_=sb_[kb:kb + 64, 0:nn], func=AF.Exp),
                         reads=[sbk], writes=[ptk])
                    cur = ("loc", first, last, pt, ptk, (kap, kb, c0, nn))
                if pend is not None:
                    stage2(pend)
                pend = cur
            stage2(pend)
            ri = (h * 5 + (g if g is not None else 4)) % 2
            p.op("dve", lambda e: e.reciprocal(out=rec[base:base + 64, ri, 0:nq], in_=Dn[base:base + 64, 0:nq]),
                 reads=[dk], writes=[("rec", ri)])
            p.op("dve", lambda e: e.tensor_tensor(out=at[base:base + 64, c, q0:q0 + nq], in0=O[base:base + 64, 0:nq],
                                                  in1=rec[base:base + 64, ri, 0:nq], op=ALU.mult),
                 reads=[ok, ("rec", ri)], writes=[("at", c)])
    yv = y_d.rearrange("(c p) w -> p c w", p=128)
    cnt = [0]

    def evac_y(m, ps, pk, b0, b1):
        s = cnt[0] % 2
        cnt[0] += 1
        p.op("act", lambda e: e.activation(out=st[:, s, 0:b1 - b0], in_=ps[:, 0:b1 - b0], func=AF.Copy),
             reads=[pk], writes=[("st", s)])
        p.dma(yv[:, m, b0:b1], st[:, s, 0:b1 - b0], reads=[("st", s)], q="act")
    linear_fm2(p, sbanks, slabs, "slab", at, "at", 8, wo_d.rearrange("(c p) f -> p c f", p=128), D, col_blocks(0, NTOK), evac_y)
    return p.build()


def _chunked(vv):
    vv = np.asarray(vv, np.float32)
    return np.ascontiguousarray(vv.reshape(-1, 128).T)


def rope_tables(t0, n):
    half = 32
    inv_freq = (10000.0 ** (-np.arange(0, half, 2, dtype=np.float32) / half)).astype(np.float32)
    pos = np.arange(t0, t0 + n)
    row = (pos // 64).astype(np.float32)
    col = (pos % 64).astype(np.float32)
    cos = np.zeros((64, n), np.float32)
    sin = np.zeros((64, n), np.float32)
    for d in range(64):
        pp = row if d < 32 else col
        ang = pp * inv_freq[d % 16]
        cos[d] = np.cos(ang)
        sgn = -1.0 if (d % 32) < 16 else 1.0
        sin[d] = sgn * np.sin(ang)
    return np.concatenate([cos, cos], 0), np.concatenate([sin, sin], 0)


def rope_perm():
    pm = np.zeros((128, 128), np.float32)
    for m in range(128):
        d = m % 32
        partner = m + 16 if d < 16 else m - 16
        pm[partner, m] = 1.0
    return pm


def na_bias_tables(rpb, heads):
    cq = np.arange(64)
    ck = np.arange(64)
    cs = np.clip(cq - 8, 0, 48)
    ok = (ck[:, None] >= cs[None, :]) & (ck[:, None] < cs[None, :] + 16)
    dc = np.clip(ck[:, None] - cq[None, :], -15, 15) + 15
    out = np.empty((len(heads), 128, 15, 64), np.float32)
    for i, h in enumerate(heads):
        for idx in range(15):
            tab = rpb[h, 14 - idx][dc]
            tab = np.where(ok, tab, np.float32(-30000.0))
            out[i, 0:64, idx] = tab
            out[i, 64:128, idx] = tab
    return out.reshape(len(heads), 128, 15 * 64)


RW_LORA = 96
RW_GATE = 256
C64 = 64


def shift_mix(p, a, akey_fn, xo, xkey, coef, n_idx, j0, n):
    for c in range(KC):
        p.op("dve", lambda e: e.tensor_scalar(out=xo[:, c, 0:n], in0=a[:, c, j0:j0 + n], scalar1=coef[:, 0, n_idx, c:c + 1],
                                              scalar2=None, op0=ALU.mult),
             reads=[akey_fn(c), "coef"], writes=[(xkey, c)])
        p.op("dve", lambda e: e.scalar_tensor_tensor(out=xo[:, c, 0:n], in0=a[:, c, j0 - 1:j0 - 1 + n], scalar=coef[:, 1, n_idx, c:c + 1],
                                                     in1=xo[:, c, 0:n], op0=ALU.mult, op1=ALU.add),
             reads=[akey_fn(c), "coef", (xkey, c)], writes=[(xkey, c)])
        p.op("dve", lambda e: e.scalar_tensor_tensor(out=xo[:, c, 0:n], in0=a[:, c, j0 + 1:j0 + 1 + n], scalar=coef[:, 2, n_idx, c:c + 1],
                                                     in1=xo[:, c, 0:n], op0=ALU.mult, op1=ALU.add),
             reads=[akey_fn(c), "coef", (xkey, c)], writes=[(xkey, c)])


def load_coef(p, coef_d):
    coef = p.sb("coef", [128, 3, 6, KC], F32)
    p.dma(coef[:, 1:3, :, :], coef_d, writes=["coef"])
    p.op("dve", lambda e: e.tensor_scalar(out=coef[:, 0, :, :], in0=coef[:, 1, :, :], scalar1=-1.0, scalar2=1.0, op0=ALU.mult, op1=ALU.add),
         reads=["coef"], writes=["coef"])
    p.op("dve", lambda e: e.tensor_tensor(out=coef[:, 0, :, :], in0=coef[:, 0, :, :], in1=coef[:, 2, :, :], op=ALU.subtract),
         reads=["coef"], writes=["coef"])
    return coef


def build_rw1():
    p = Prog()
    W = WF
    h_d = p.dram("h", [D, W])
    modv_d = p.dram("modv", [128, 6 * KC, 2])
    g1_d = p.dram("g1", [128, KC])
    mask_d = p.dram("mask", [1, W])
    coef_d = p.dram("coef", [128, 2, 6, KC])
    wrkv_d = p.dram("wrkv", [3, D, D])
    w1_d = p.dram("w1", [2, D, RW_LORA])
    w2_d = p.dram("w2", [2, RW_LORA, D])
    a1_d = p.dram("a1", [2, D, RW_LORA])
    a2_d = p.dram("a2", [2, RW_LORA, D])
    vecs_d = p.dram("vecs", [128, 7, KC])
    bones_d = p.dram("bones", [128, 128])
    rmask_d = p.dram("rmask", [1, 512])
    outs = {}
    outs["vt"] = p.dram("vt", [D, 1152], BF16, kind="ExternalOutput")
    for d in range(2):
        for nm in ("at", "bt", "kt", "rt"):
            outs[(nm, d)] = p.dram("%s%d" % (nm, d), [D, 1152], BF16, kind="ExternalOutput")
    bv_d = [p.dram("bv%d" % d, [D, 1152], kind="ExternalOutput") for d in range(2)]
    gam_d = [p.dram("gam%d" % d, [D, 18], kind="ExternalOutput") for d in range(2)]
    cm = Common(p)
    banks = Banks(p)
    a = p.sb("a", [128, KC, W], BF16)
    hbuf = p.sb("hbuf", [128, 2, W], F32)
    tmp = p.sb("tmp", [128, 2, W], F32)
    sq = p.sb("sq", [128, 2, 512], BF16)
    rstd = p.sb("rstd", [128, W], F32)
    modv = load_small(p, "modv", [128, 6 * KC, 2], modv_d)
    gvec = load_small(p, "gvec", [128, KC], g1_d)
    mask = load_small(p, "mask", [128, W], mask_d[0].partition_broadcast(128))
    coef = load_coef(p, coef_d)
    vecs = load_small(p, "vecs", [128, 7, KC], vecs_d)
    rmask = load_small(p, "rmask", [128, 512], rmask_d[0].partition_broadcast(128))
    bones = p.sb("bones", [128, 128], BF16)
    p.dma(bones[:], bones_d, writes=["bones"], q="pool", max_dma_last_dim=512)
    w1 = [p.sb("w1_%d" % d, [128, KC, RW_LORA], BF16) for d in range(2)]
    a1 = [p.sb("a1_%d" % d, [128, KC, RW_LORA], BF16) for d in range(2)]
    w2 = [p.sb("w2_%d" % d, [RW_LORA, D], BF16) for d in range(2)]
    a2 = [p.sb("a2_%d" % d, [RW_LORA, D], BF16) for d in range(2)]
    for d in range(2):
        p.dma(w1[d][:], w1_d[d].rearrange("(c p) f -> p c f", p=128), writes=[("w1", d)], q="pool")
        p.dma(a1[d][:], a1_d[d].rearrange("(c p) f -> p c f", p=128), writes=[("a1", d)], q="pool")
        p.dma(w2[d][:], w2_d[d], writes=[("w2", d)], q="pool")
        p.dma(a2[d][:], a2_d[d], writes=[("a2", d)], q="pool")
    A_l = make_AB(p, "rl", modv, gvec, 0, 1, 0)
    A_c = make_AB(p, "rc", modv, gvec, 0, 1, 1)
    segs = [(LAT0, LAT1, A_l, lambda c: modv[:, c, 0:1]), (CTX0, CTX1, A_c, lambda c: modv[:, c, 1:2])]
    norm_mod_stream(p, cm, banks, h_d.rearrange("(c p) w -> p c w", p=128), W, segs, a, "a", rstd, sq, hbuf, tmp,
                    ["rl_A", "rc_A"], mask=mask)
    akey = lambda c: ("a", c)
    xs = {nm: p.sb("x_" + nm, [128, KC, 512], BF16) for nm in ("k", "v", "r")}
    tw = [p.sb("tw%d" % d, [RW_LORA, 512], BF16) for d in range(2)]
    ta = [p.sb("ta%d" % d, [RW_LORA, 512], BF16) for d in range(2)]
    NT_ = 14
    ft = [hbuf[:, 0, 0:512], hbuf[:, 0, 512:1024], hbuf[:, 1, 0:512], hbuf[:, 1, 512:1024],
          tmp[:, 0, 0:512], tmp[:, 0, 512:1024], tmp[:, 1, 0:512], tmp[:, 1, 512:1024]]
    ft += [p.sb("ft%d" % i, [128, 512], F32)[:] for i in range(8, NT_)]
    fk = [("ft", i) for i in range(NT_)]
    fence = p.sb("fence", [128, 1], F32)
    p.op("dve", lambda e: e.memset(fence[:], 0.0), reads=[("hbuf", 0), ("hbuf", 1), ("tmp", 0), ("tmp", 1)], writes=fk[0:8])
    ob = {nm: p.sb("ob_" + nm, [128, 2, 512], BF16) for nm in ("at", "bt", "kt", "rt", "vt")}
    sqb = p.sb("sqb", [128, 2, 512], BF16)
    gam = [p.sb("gamt%d" % d, [128, KC, 18], F32) for d in range(2)]
    slabs = {nm: [p.sb("sl_%s%d" % (nm, i), [128, KC, 128], BF16) for i in range(2)] for nm in ("r", "k", "v")}
    widx = {"r": 0, "k": 1, "v": 2}
    wv = wrkv_d.rearrange("n (c p) f -> n p c f", p=128)
    cblocks = [(1, 512, 0), (513, 512, 512), (1027, 128, 1024)]
    nslab = [0]
    for (j0, n, o0) in cblocks:
        nch = n // C64
        for (kind, n_idx, fn) in (("w", 1, AF.Tanh), ("a", 4, AF.Copy)):
            shift_mix(p, a, akey, xs["r"], "x_r", coef, n_idx, j0, n)
            for d in range(2):
                lw, lkey = (w1[d], ("w1", d)) if kind == "w" else (a1[d], ("a1", d))
                lt, ltkey = (tw[d], ("tw", d)) if kind == "w" else (ta[d], ("ta", d))
                ps, pk = banks.next()
                for c in range(KC):
                    p.op("pe", lambda e: e.matmul(ps[0:RW_LORA, 0:n], lhsT=lw[:, c, :], rhs=xs["r"][:, c, 0:n],
                                                  start=(c == 0), stop=(c == KC - 1)),
                         reads=[lkey, ("x_r", c)], writes=[pk])
                p.op("act", lambda e: e.activation(out=lt[:, 0:n], in_=ps[0:RW_LORA, 0:n], func=fn), reads=[pk], writes=[ltkey])
        shift_mix(p, a, akey, xs["k"], "x_k", coef, 2, j0, n)
        shift_mix(p, a, akey, xs["v"], "x_v", coef, 3, j0, n)
        shift_mix(p, a, akey, xs["r"], "x_r", coef, 0, j0, n)
        for m in range(KC):
            cur = {}
            for nm in ("r", "k", "v"):
                i = nslab[0] % 2
                sl = slabs[nm][i]
                sk = ("sl_" + nm, i)
                for k0 in (0, 8):
                    p.dma(sl[:, k0:k0 + 8, :], wv[widx[nm], :, k0:k0 + 8, m * 128:(m + 1) * 128], writes=[sk], q="pool")
                cur[nm] = (sl, sk)
            nslab[0] += 1
            s2 = m % 2
            T_ = lambda i: ft[i][:, 0:n]
            pss = {}
            for nm in ("k", "v", "r"):
                ps, pk = banks.next()
                sl, sk = cur[nm]
                for c in range(KC):
                    p.op("pe", lambda e: e.matmul(ps[:, 0:n], lhsT=sl[:, c, :], rhs=xs[nm][:, c, 0:n],
                                                  start=(c == 0), stop=(c == KC - 1)),
                         reads=[sk, ("x_" + nm, c)], writes=[pk])
                pss[nm] = (ps, pk)
            p.op("act", lambda e: e.activation(out=T_(0), in_=pss["k"][0][:, 0:n], func=AF.Copy), reads=[pss["k"][1]], writes=[fk[0]])
            p.op("act", lambda e: e.activation(out=T_(1), in_=pss["v"][0][:, 0:n], func=AF.Copy), reads=[pss["v"][1]], writes=[fk[1]])
            p.op("act", lambda e: e.activation(out=T_(2), in_=pss["r"][0][:, 0:n], func=AF.Copy), reads=[pss["r"][1]], writes=[fk[2]])
            p.op("pool", lambda e: e.tensor_copy(out=ob["vt"][:, s2, 0:n], in_=T_(1)), reads=[fk[1]], writes=[("ob_vt", s2)])
            ovv = outs["vt"].rearrange("(c p) w -> p c w", p=128)
            p.dma(ovv[:, m, o0:o0 + n], ob["vt"][:, s2, 0:n], reads=[("ob_vt", s2)])
            p.op("dve", lambda e: e.tensor_scalar(out=T_(5), in0=T_(0), scalar1=vecs[:, 4, m:m + 1], scalar2=None, op0=ALU.mult),
                 reads=[fk[0], "vecs"], writes=[fk[5]])
            p.op("act", lambda e: e.activation(out=sqb[:, 0, 0:n], in_=T_(5), func=AF.Square), reads=[fk[5]], writes=[("sqb", 0)])
            ps_n, pk_n = banks.next()
            p.op("pe", lambda e: e.matmul(ps_n[:, 0:n], lhsT=bones[:], rhs=sqb[:, 0, 0:n], start=True, stop=True),
                 reads=["bones", ("sqb", 0)], writes=[pk_n])
            p.op("act", lambda e: e.activation(out=T_(6), in_=ps_n[:, 0:n], func=AF.Sqrt), reads=[pk_n], writes=[fk[6]])
            p.op("dve", lambda e: e.tensor_scalar(out=T_(6), in0=T_(6), scalar1=1e-6, scalar2=None, op0=ALU.max),
                 reads=[fk[6]], writes=[fk[6]])
            p.op("dve", lambda e: e.reciprocal(out=T_(6), in_=T_(6)), reads=[fk[6]], writes=[fk[6]])
            p.op("dve", lambda e: e.tensor_tensor(out=T_(5), in0=T_(5), in1=T_(6), op=ALU.mult), reads=[fk[5], fk[6]], writes=[fk[5]])
            for d in range(2):
                ps_s, pk_s = banks.next()
                p.op("pe", lambda e: e.matmul(ps_s[:, 0:n], lhsT=w2[d][:, m * 128:(m + 1) * 128], rhs=tw[d][:, 0:n], start=True, stop=True),
                     reads=[("w2", d), ("tw", d)], writes=[pk_s])
                ps_a, pk_a = banks.next()
                p.op("pe", lambda e: e.matmul(ps_a[:, 0:n], lhsT=a2[d][:, m * 128:(m + 1) * 128], rhs=ta[d][:, 0:n], start=True, stop=True),
                     reads=[("a2", d), ("ta", d)], writes=[pk_a])
                p.op("act", lambda e: e.activation(out=T_(3), in_=ps_s[:, 0:n], func=AF.Sigmoid, bias=vecs[:, 0 + d, m:m + 1], scale=1.0),
                     reads=[pk_s, "vecs"], writes=[fk[3]])
                p.op("act", lambda e: e.activation(out=T_(4), in_=ps_a[:, 0:n], func=AF.Sigmoid, bias=vecs[:, 2 + d, m:m + 1], scale=1.0),
                     reads=[pk_a, "vecs"], writes=[fk[4]])
                p.op("dve", lambda e: e.tensor_scalar(out=T_(3), in0=T_(3), scalar1=-0.6065306597126334, scalar2=None, op0=ALU.mult),
                     reads=[fk[3]], writes=[fk[3]])
                p.op("dve", lambda e: e.tensor_tensor_scan(out=T_(7), data0=rmask[:, 0:n], data1=T_(3), initial=0.0,
                                                           op0=ALU.mult, op1=ALU.add),
                     reads=["rmask", fk[3]], writes=[fk[7]])
                if d == 0:
                    p.op("dve", lambda e: e.tensor_tensor(out=T_(8), in0=T_(7), in1=T_(3), op=ALU.subtract), reads=[fk[7], fk[3]], writes=[fk[8]])
                    p.op("pool", lambda e: e.tensor_copy(out=gam[d][:, m, o0 // C64:o0 // C64 + nch],
                                                         in_=ft[7][:, 0:n].rearrange("p (a b) -> p a b", b=C64)[:, :, C64 - 1]),
                         reads=[fk[7]], writes=[("gam", d)])
                else:
                    P3 = ft[7][:, 0:n].rearrange("p (a b) -> p a b", b=C64)
                    p.op("pool", lambda e: e.tensor_copy(out=gam[d][:, m, o0 // C64:o0 // C64 + nch], in_=P3[:, :, C64 - 1]),
                         reads=[fk[7]], writes=[("gam", d)])
                    tot = gam[d][:, m, o0 // C64:o0 // C64 + nch].unsqueeze(2).broadcast_to([128, nch, C64])
                    p.op("dve", lambda e: e.tensor_tensor(out=ft[8][:, 0:n].rearrange("p (a b) -> p a b", b=C64), in0=tot, in1=P3, op=ALU.subtract),
                         reads=[fk[7], ("gam", d)], writes=[fk[8]])
                    p.op("dve", lambda e: e.tensor_tensor(out=T_(7), in0=T_(8), in1=T_(3), op=ALU.add), reads=[fk[8], fk[3]], writes=[fk[7]])
                p.op("act", lambda e: e.activation(out=T_(8), in_=T_(8), func=AF.Exp), reads=[fk[8]], writes=[fk[8]])
                p.op("act", lambda e: e.activation(out=T_(9), in_=T_(7), func=AF.Exp, scale=-1.0), reads=[fk[7]], writes=[fk[9]])
                p.op("act", lambda e: e.activation(out=T_(7), in_=T_(7), func=AF.Exp), reads=[fk[7]], writes=[fk[7]])
                p.op("dve", lambda e: e.tensor_scalar(out=T_(10), in0=T_(4), scalar1=-1.0, scalar2=vecs[:, 5, m:m + 1], op0=ALU.add, op1=ALU.mult),
                     reads=[fk[4], "vecs"], writes=[fk[10]])
                p.op("dve", lambda e: e.scalar_tensor_tensor(out=T_(10), in0=T_(10), scalar=1.0, in1=T_(0), op0=ALU.add, op1=ALU.mult),
                     reads=[fk[10], fk[0]], writes=[fk[10]])
                p.op("dve", lambda e: e.scalar_tensor_tensor(out=ob["at"][:, d, 0:n], in0=T_(5), scalar=-1.0, in1=T_(8), op0=ALU.mult, op1=ALU.mult),
                     reads=[fk[5], fk[8]], writes=[("ob_at", d)])
                p.op("dve", lambda e: e.tensor_tensor(out=T_(11), in0=T_(5), in1=T_(4), op=ALU.mult), reads=[fk[5], fk[4]], writes=[fk[11]])
                p.op("dve", lambda e: e.tensor_tensor(out=ob["bt"][:, d, 0:n], in0=T_(11), in1=T_(9), op=ALU.mult),
                     reads=[fk[11], fk[9]], writes=[("ob_bt", d)])
                p.op("dve", lambda e: e.tensor_tensor(out=ob["kt"][:, d, 0:n], in0=T_(10), in1=T_(9), op=ALU.mult),
                     reads=[fk[10], fk[9]], writes=[("ob_kt", d)])
                p.op("dve", lambda e: e.tensor_tensor(out=ob["rt"][:, d, 0:n], in0=T_(2), in1=T_(7), op=ALU.mult),
                     reads=[fk[2], fk[7]], writes=[("ob_rt", d)])
                p.op("dve", lambda e: e.scalar_tensor_tensor(out=sqb[:, 1, 0:n], in0=T_(2), scalar=vecs[:, 6, m:m + 1], in1=T_(10), op0=ALU.mult, op1=ALU.mult),
                     reads=[fk[2], fk[10], "vecs"], writes=[("sqb", 1)])
                ps_b, pk_b = banks.next()
                p.op("pe", lambda e: e.matmul(ps_b[:, 0:n], lhsT=bones[:], rhs=sqb[:, 1, 0:n], start=True, stop=True),
                     reads=["bones", ("sqb", 1)], writes=[pk_b])
                i12 = 12 + d
                p.op("dve", lambda e: e.tensor_tensor(out=T_(i12), in0=ps_b[:, 0:n], in1=T_(1), op=ALU.mult), reads=[pk_b, fk[1]], writes=[fk[i12]])
                for nm in ("at", "bt", "kt", "rt"):
                    ov = outs[(nm, d)].rearrange("(c p) w -> p c w", p=128)
                    p.dma(ov[:, m, o0:o0 + n], ob[nm][:, d, 0:n], reads=[("ob_" + nm, d)])
                bvv = bv_d[d].rearrange("(c p) w -> p c w", p=128)
                p.dma(bvv[:, m, o0:o0 + n], T_(i12), reads=[fk[i12]])
    for d in range(2):
        p.op("act", lambda e: e.activation(out=gam[d][:], in_=gam[d][:], func=AF.Exp), reads=[("gam", d)], writes=[("gam", d)])
        p.dma(gam_d[d].rearrange("(c p) w -> p c w", p=128), gam[d][:], reads=[("gam", d)])
    return p.build()


NCH = NTOK // C64
RW2_MASK_ENG = os.environ.get("RW2_MASK_ENG", "dve")
NHG = 4


def build_rw2():
    p = Prog()
    far_d = p.dram("far", [NCH, 64, 32 * 2 * 64], BF16)
    fbk_d = p.dram("fbk", [NCH, 64, 2 * 32 * 64], BF16)
    tm_d = p.dram("tm", [NCH, 64, 3 * D], BF16)
    gam_d = p.dram("gam", [64, 32, NCH])
    mka_d = p.dram("mka", [64, 2 * 64])
    mkp_d = p.dram("mkp", [64, 2 * 64])
    mkl_d = p.dram("mkl", [64, 7 * 64])
    y_d = p.dram("y", [NCH, 64, 32 * 64], kind="ExternalOutput")
    banks = Banks(p)
    FAR = [p.sb("far%d" % i, [64, 32, 2, 64], BF16) for i in range(2)]
    FBK = [p.sb("fbk%d" % i, [64, 2, 32, 64], BF16) for i in range(2)]
    TM = [p.sb("tm%d" % i, [64, 3, D], BF16) for i in range(2)]
    gam = load_small(p, "gam", [64, 32, NCH], gam_d)
    mka = load_small(p, "mka", [64, 2, 64], mka_d.rearrange("p (a b) -> p a b", b=64))
    mkp = load_small(p, "mkp", [64, 2, 64], mkp_d.rearrange("p (a b) -> p a b", b=64))
    mkl = load_small(p, "mkl", [64, 7, 64], mkl_d.rearrange("p (a b) -> p a b", b=64))
    mklb = p.sb("mklb", [64, 7, 64], BF16)
    p.op("dve", lambda e: e.tensor_copy(out=mklb[:], in_=mkl[:]), reads=["mkl"], writes=["mklb"])
    S = p.sb("S", [64, 32, 64], F32)
    Sb = p.sb("Sb", [64, 32, 64], BF16)
    Sl = p.sb("Sl", [64, 32, 64], BF16)
    Sd = p.sb("Sd", [64, 32, 64], F32)
    yst = [p.sb("yst%d" % i, [64, 32, 64], F32) for i in range(2)]
    QA = [p.sb("QA%d" % g, [64, 8, 2, 64], BF16) for g in range(NHG)]
    QK = [p.sb("QK%d" % g, [64, 8, 2, 64], BF16) for g in range(NHG)]
    NM = [[p.sb("NM%d_%d" % (g, l), [64, 8, 64], BF16) for l in range(6)] for g in range(NHG)]
    Gf = [p.sb("Gf%d" % g, [64, 8, 64], F32) for g in range(NHG)]
    Hf = [p.sb("Hf%d" % g, [64, 8, 64], F32) for g in range(NHG)]
    Gb = [p.sb("Gb%d" % g, [64, 8, 64], BF16) for g in range(NHG)]
    Hb = [p.sb("Hb%d" % g, [64, 8, 64], BF16) for g in range(NHG)]
    Wb = [p.sb("Wb%d" % g, [64, 8, 64], BF16) for g in range(NHG)]
    Rb = [p.sb("Rb%d" % g, [64, 8, 64], BF16) for g in range(NHG)]
    Xb = [p.sb("Xb%d" % g, [64, 8, 64], BF16) for g in range(NHG)]
    p.op("dve", lambda e: e.memset(S[:], 0.0), writes=[("S", g) for g in range(NHG)])
    p.op("dve", lambda e: e.memset(Sb[:], 0.0), writes=[("Sb", g) for g in range(NHG)])
    p.op("dve", lambda e: e.memset(Sl[:], 0.0), writes=[("Sl", g) for g in range(NHG)])
    v3 = lambda ap: ap.rearrange("p (a b) -> p a b", b=64)
    bc8 = lambda ap2: ap2.unsqueeze(1).broadcast_to([64, 8, 64])
    for ci in range(NCH):
        s = ci % 2
        far, fbk, tm = FAR[s], FBK[s], TM[s]
        p.dma(far[:].rearrange("p a b c -> p (a b c)"), far_d[ci], writes=[("far", s)])
        p.dma(fbk[:].rearrange("p a b c -> p (a b c)"), fbk_d[ci], writes=[("fbk", s)])
        p.dma(tm[:].rearrange("p a b -> p (a b)"), tm_d[ci], writes=[("tm", s)])
        kfar, kfbk, ktm = ("far", s), ("fbk", s), ("tm", s)
        for g in range(NHG):
            ps, pk = banks.next()
            for hh in range(8):
                h = g * 8 + hh
                p.op("pe", lambda e: e.matmul(ps[0:64, hh * 64:(hh + 1) * 64], lhsT=far[:, h, 0, :], rhs=fbk[:, 0, h, :], start=True, stop=True),
                     reads=[kfar, kfbk], writes=[pk])
            p.op("dve", lambda e: e.tensor_tensor(out=Gf[g][:], in0=v3(ps[0:64, :]), in1=bc8(mkp[:, 0, :]), op=ALU.mult),
                 reads=[pk, "mkp"], writes=[("Gf", g)])
            p.op(RW2_MASK_ENG, lambda e: e.tensor_tensor(out=Gf[g][:], in0=Gf[g][:], in1=bc8(mkp[:, 1, :]), op=ALU.add),
                 reads=[("Gf", g), "mkp"], writes=[("Gf", g)])
            p.op("act", lambda e: e.activation(out=Gb[g][:], in_=Gf[g][:], func=AF.Copy), reads=[("Gf", g)], writes=[("Gb", g)])
            for (dst, dkey, which) in ((QA[g], ("QA", g), 0), (QK[g], ("QK", g), 1)):
                for half in range(2):
                    ps, pk = banks.next()
                    for hq in range(4):
                        h = g * 8 + half * 4 + hq
                        p.op("pe", lambda e: e.matmul(ps[0:64, hq * 128:(hq + 1) * 128], lhsT=fbk[:, which, h, :],
                                                      rhs=far[:, h, :, :].rearrange("p a b -> p (a b)"), start=True, stop=True),
                             reads=[kfar, kfbk], writes=[pk])
                    p.op("dve", lambda e: e.tensor_tensor(
                        out=dst[:, half * 4:(half + 1) * 4, :, :],
                        in0=ps[0:64, :].rearrange("p (h a b) -> p h a b", a=2, b=64),
                        in1=mka[:].unsqueeze(1).broadcast_to([64, 4, 2, 64]), op=ALU.mult),
                        reads=[pk, "mka"], writes=[dkey])
            NT0 = QA[g][:, :, 0, :]
            for l in range(6):
                p.op(RW2_MASK_ENG, lambda e: e.tensor_tensor(out=NM[g][l][:], in0=NT0, in1=bc8(mklb[:, l, :]), op=ALU.mult),
                     reads=[("QA", g), "mklb"], writes=[("NM", g, l)])
            p.op(RW2_MASK_ENG, lambda e: e.tensor_tensor(out=Hf[g][:], in0=NM[g][0][:], in1=bc8(mkl[:, 6, :]), op=ALU.add),
                 reads=[("NM", g, 0), "mkl"], writes=[("Hf", g)])
            p.op("act", lambda e: e.activation(out=Hb[g][:], in_=Hf[g][:], func=AF.Copy), reads=[("Hf", g)], writes=[("Hb", g)])
        for g in range(NHG):
            ps, pk = banks.next()
            for hh in range(8):
                h = g * 8 + hh
                o = ps[0:64, hh * 64:(hh + 1) * 64]
                p.op("pe", lambda e: e.matmul(o, lhsT=far[:, h, 0, :], rhs=Sb[:, h, :], start=True, stop=False),
                     reads=[kfar, ("Sb", g)], writes=[pk])
                p.op("pe", lambda e: e.matmul(o, lhsT=far[:, h, 0, :], rhs=Sl[:, h, :], start=False, stop=False),
                     reads=[kfar, ("Sl", g)], writes=[pk])
                p.op("pe", lambda e: e.matmul(o, lhsT=QK[g][:, hh, 0, :], rhs=tm[:, 0, h * 64:(h + 1) * 64], start=False, stop=True),
                     reads=[("QK", g), ktm], writes=[pk])
            p.op("act", lambda e: e.activation(out=Rb[g][:], in_=v3(ps[0:64, :]), func=AF.Copy), reads=[pk], writes=[("Rb", g)])
        for l in range(1, 6):
            for g in range(NHG):
                ps, pk = banks.next()
                for hh in range(8):
                    p.op("pe", lambda e: e.matmul(ps[0:64, hh * 64:(hh + 1) * 64], lhsT=NM[g][l][:, hh, :], rhs=Gb[g][:, hh, :], start=True, stop=True),
                         reads=[("NM", g, l), ("Gb", g)], writes=[pk])
                p.op("act", lambda e: e.activation(out=Wb[g][:], in_=v3(ps[0:64, :]), func=AF.Copy), reads=[pk], writes=[("Wb", g)])
            zz = []
            for g in range(NHG):
                psz = pkz = None
                if l < 5:
                    psz, pkz = banks.next()
                    for hh in range(8):
                        p.op("pe", lambda e: e.matmul(psz[0:64, hh * 64:(hh + 1) * 64], lhsT=Hb[g][:, hh, :], rhs=Wb[g][:, hh, :], start=True, stop=True),
                             reads=[("Hb", g), ("Wb", g)], writes=[pkz])
                ps2, pk2 = banks.next()
                for hh in range(8):
                    p.op("pe", lambda e: e.matmul(ps2[0:64, hh * 64:(hh + 1) * 64], lhsT=Wb[g][:, hh, :], rhs=Hb[g][:, hh, :], start=True, stop=True),
                         reads=[("Hb", g), ("Wb", g)], writes=[pk2])
                if l < 5:
                    p.op("dve", lambda e: e.tensor_tensor(out=Gf[g][:], in0=v3(psz[0:64, :]), in1=Gf[g][:], op=ALU.add),
                         reads=[pkz, ("Gf", g)], writes=[("Gf", g)])
                    p.op("act", lambda e: e.activation(out=Gb[g][:], in_=Gf[g][:], func=AF.Copy), reads=[("Gf", g)], writes=[("Gb", g)])
                p.op("dve", lambda e: e.tensor_tensor(out=Hf[g][:], in0=v3(ps2[0:64, :]), in1=Hf[g][:], op=ALU.add),
                     reads=[pk2, ("Hf", g)], writes=[("Hf", g)])
                p.op("act", lambda e: e.activation(out=Hb[g][:], in_=Hf[g][:], func=AF.Copy), reads=[("Hf", g)], writes=[("Hb", g)])
        for g in range(NHG):
            ps, pk = banks.next()
            for hh in range(8):
                p.op("pe", lambda e: e.matmul(ps[0:64, hh * 64:(hh + 1) * 64], lhsT=Hb[g][:, hh, :], rhs=Rb[g][:, hh, :], start=True, stop=True),
                     reads=[("Hb", g), ("Rb", g)], writes=[pk])
            p.op("act", lambda e: e.activation(out=Xb[g][:], in_=v3(ps[0:64, :]), func=AF.Copy), reads=[pk], writes=[("Xb", g)])
        ys = yst[s]
        for g in range(NHG):
            ps, pk = banks.next()
            for hh in range(8):
                h = g * 8 + hh
                o = ps[0:64, hh * 64:(hh + 1) * 64]
                p.op("pe", lambda e: e.matmul(o, lhsT=Sb[:, h, :], rhs=far[:, h, 1, :], start=True, stop=False),
                     reads=[("Sb", g), kfar], writes=[pk])
                p.op("pe", lambda e: e.matmul(o, lhsT=Sl[:, h, :], rhs=far[:, h, 1, :], start=False, stop=False),
                     reads=[("Sl", g), kfar], writes=[pk])
                p.op("pe", lambda e: e.matmul(o, lhsT=Xb[g][:, hh, :], rhs=QA[g][:, hh, 1, :], start=False, stop=False),
                     reads=[("Xb", g), ("QA", g)], writes=[pk])
                p.op("pe", lambda e: e.matmul(o, lhsT=tm[:, 0, h * 64:(h + 1) * 64], rhs=QK[g][:, hh, 1, :], start=False, stop=True),
                     reads=[ktm, ("QK", g)], writes=[pk])
            p.op("act", lambda e: e.activation(out=ys[:, g * 8:(g + 1) * 8, :], in_=v3(ps[0:64, :]), func=AF.Copy),
                 reads=[pk], writes=[("yst", s)])
        p.dma(y_d[ci], ys[:].rearrange("p a b -> p (a b)"), reads=[("yst", s)], q="act")
        for g in range(NHG):
            ps, pk = banks.next()
            for hh in range(8):
                h = g * 8 + hh
                o = ps[0:64, hh * 64:(hh + 1) * 64]
                p.op("pe", lambda e: e.matmul(o, lhsT=tm[:, 1, h * 64:(h + 1) * 64], rhs=Xb[g][:, hh, :], start=True, stop=False),
                     reads=[ktm, ("Xb", g)], writes=[pk])
                p.op("pe", lambda e: e.matmul(o, lhsT=tm[:, 2, h * 64:(h + 1) * 64], rhs=tm[:, 0, h * 64:(h + 1) * 64], start=False, stop=True),
                     reads=[ktm], writes=[pk])
            Sg = S[:, g * 8:(g + 1) * 8, :]
            Sdg = Sd[:, g * 8:(g + 1) * 8, :]
            p.op("dve", lambda e: e.tensor_tensor(out=Sg, in0=v3(ps[0:64, :]), in1=Sg, op=ALU.add), reads=[pk, ("S", g)], writes=[("S", g)])
            p.op("dve", lambda e: e.tensor_tensor(out=Sg, in0=Sg, in1=gam[:, g * 8:(g + 1) * 8, ci:ci + 1].broadcast_to([64, 8, 64]), op=ALU.mult),
                 reads=[("S", g), "gam"], writes=[("S", g)])
            p.op("act", lambda e: e.activation(out=Sb[:, g * 8:(g + 1) * 8, :], in_=Sg, func=AF.Copy), reads=[("S", g)], writes=[("Sb", g)])
            p.op(RW2_MASK_ENG, lambda e: e.tensor_tensor(out=Sdg, in0=Sg, in1=Sb[:, g * 8:(g + 1) * 8, :], op=ALU.subtract),
                 reads=[("S", g), ("Sb", g)], writes=[("Sd", g)])
            p.op("act", lambda e: e.activation(out=Sl[:, g * 8:(g + 1) * 8, :], in_=Sdg, func=AF.Copy), reads=[("Sd", g)], writes=[("Sl", g)])
    return p.build()


def rw2_masks():
    r = np.arange(64)[:, None]
    f = np.arange(64)[None, :]
    mka = np.stack([(f > r), (f >= r)], 1).astype(np.float32).reshape(64, 128)

    def m_level(l, i, j):
        return ((i >> (l + 1)) == (j >> (l + 1))) & (((i >> l) & 1) == 1) & (((j >> l) & 1) == 0)
    eye = (r == f)
    mkp = np.stack([m_level(0, r, f), eye], 1).astype(np.float32).reshape(64, 128)
    mkl = np.stack([m_level(l, f, r) for l in range(6)] + [eye], 1).astype(np.float32).reshape(64, 7 * 64)
    return mka, mkp, mkl


RW_GN_EPS = 64e-5


def build_rw3():
    p = Prog()
    W = WF
    h_d = p.dram("h", [D, W])
    modv_d = p.dram("modv", [128, 6 * KC, 2])
    g1_d = p.dram("g1", [128, KC])
    mask_d = p.dram("mask", [1, W])
    coef_d = p.dram("coef", [128, 2, 6, KC])
    yin_d = p.dram("yin", [4, D, 1152])
    g1w_d = p.dram("g1w", [D, RW_GATE])
    g2w_d = p.dram("g2w", [RW_GATE, D])
    lnv_d = p.dram("lnv", [128, 2, KC])
    wo_d = p.dram("wo", [D, D])
    bones_d = p.dram("bones", [128, 128])
    y_d = p.dram("y", [D, 1152], kind="ExternalOutput")
    cm = Common(p)
    banks = Banks(p)
    a = p.sb("a", [128, KC, W], BF16)
    hbuf = p.sb("hbuf", [128, 2, W], F32)
    tmp = p.sb("tmp", [128, 2, W], F32)
    sq = p.sb("sq", [128, 2, 512], BF16)
    rstd = p.sb("rstd", [128, W], F32)
    modv = load_small(p, "modv", [128, 6 * KC, 2], modv_d)
    gvec = load_small(p, "gvec", [128, KC], g1_d)
    mask = load_small(p, "mask", [128, W], mask_d[0].partition_broadcast(128))
    coef = load_coef(p, coef_d)
    lnv = load_small(p, "lnv", [128, 2, KC], lnv_d)
    gne = p.sb("gne", [128, 1], F32)
    p.op("dve", lambda e: e.memset(gne[:], RW_GN_EPS), writes=["gne"])
    bones = p.sb("bones", [128, 128], BF16)
    p.dma(bones[:], bones_d, writes=["bones"], q="pool", max_dma_last_dim=512)
    g1w = p.sb("g1w", [128, KC, RW_GATE], BF16)
    g2w = p.sb("g2w", [128, 2, D], BF16)
    p.dma(g1w[:], g1w_d.rearrange("(c p) f -> p c f", p=128), writes=["g1w"], q="pool")
    for c2 in range(2):
        p.dma(g2w[:, c2, :], g2w_d[c2 * 128:(c2 + 1) * 128, :], writes=["g2w"], q="pool")
    A_l = make_AB(p, "rl", modv, gvec, 0, 1, 0)
    A_c = make_AB(p, "rc", modv, gvec, 0, 1, 1)
    segs = [(LAT0, LAT1, A_l, lambda c: modv[:, c, 0:1]), (CTX0, CTX1, A_c, lambda c: modv[:, c, 1:2])]
    norm_mod_stream(p, cm, banks, h_d.rearrange("(c p) w -> p c w", p=128), W, segs, a, "a", rstd, sq, hbuf, tmp,
                    ["rl_A", "rc_A"], mask=mask)
    xg = p.sb("xg", [128, KC, 512], BF16)
    ggb = p.sb("ggb", [128, 2, 512], BF16)
    ob = p.sb("ob", [128, KC, 512], BF16)
    yt = [p.sb("yt%d" % i, [128, 4, 512], F32) for i in range(2)]
    ft = [p.sb("ft%d" % i, [128, 512], F32) for i in range(4)]
    fk = [("ft", i) for i in range(4)]
    sqb = p.sb("sqb", [128, 2, 512], BF16)
    st = p.sb("st", [128, 2, 512], F32)
    slabs = [p.sb("slab%d" % i, [128, KC, 256], BF16) for i in range(2)]
    yinv = yin_d.rearrange("n (c p) w -> p n c w", p=128)
    yov = y_d.rearrange("(c p) w -> p c w", p=128)
    cblocks = [(1, 512, 0), (513, 512, 512), (1027, 128, 1024)]
    cnt = [0]
    for (j0, n, o0) in cblocks:
        shift_mix(p, a, lambda c: ("a", c), xg, "xg", coef, 5, j0, n)
        for c2 in range(2):
            ps, pk = banks.next()
            for c in range(KC):
                p.op("pe", lambda e: e.matmul(ps[:, 0:n], lhsT=g1w[:, c, c2 * 128:(c2 + 1) * 128], rhs=xg[:, c, 0:n],
                                              start=(c == 0), stop=(c == KC - 1)),
                     reads=["g1w", ("xg", c)], writes=[pk])
            p.op("act", lambda e: e.activation(out=ggb[:, c2, 0:n], in_=ps[:, 0:n], func=AF.Sigmoid), reads=[pk], writes=[("ggb", c2)])
        for m in range(KC):
            s2 = m % 2
            y4 = yt[s2]
            p.dma(y4[:, :, 0:n], yinv[:, :, m, o0:o0 + n], writes=[("yt", s2)])
            psg, pkg = banks.next()
            for c2 in range(2):
                p.op("pe", lambda e: e.matmul(psg[:, 0:n], lhsT=g2w[:, c2, m * 128:(m + 1) * 128], rhs=ggb[:, c2, 0:n],
                                              start=(c2 == 0), stop=(c2 == 1)),
                     reads=["g2w", ("ggb", c2)], writes=[pkg])
            T_ = lambda i: ft[i][:, 0:n]
            p.op("dve", lambda e: e.tensor_tensor(out=T_(0), in0=y4[:, 0, 0:n], in1=y4[:, 1, 0:n], op=ALU.add), reads=[("yt", s2)], writes=[fk[0]])
            p.op("act", lambda e: e.activation(out=sqb[:, 0, 0:n], in_=T_(0), func=AF.Copy), reads=[fk[0]], writes=[("sqb", 0)])
            psm, pkm = banks.next()
            p.op("pe", lambda e: e.matmul(psm[:, 0:n], lhsT=bones[:], rhs=sqb[:, 0, 0:n], start=True, stop=True),
                 reads=["bones", ("sqb", 0)], writes=[pkm])
            p.op("dve", lambda e: e.scalar_tensor_tensor(out=T_(1), in0=psm[:, 0:n], scalar=-1.0 / 64, in1=T_(0), op0=ALU.mult, op1=ALU.add),
                 reads=[pkm, fk[0]], writes=[fk[1]])
            p.op("act", lambda e: e.activation(out=sqb[:, 1, 0:n], in_=T_(1), func=AF.Square), reads=[fk[1]], writes=[("sqb", 1)])
            psv, pkv = banks.next()
            p.op("pe", lambda e: e.matmul(psv[:, 0:n], lhsT=bones[:], rhs=sqb[:, 1, 0:n], start=True, stop=True),
                 reads=["bones", ("sqb", 1)], writes=[pkv])
            p.op("act", lambda e: e.activation(out=T_(2), in_=psv[:, 0:n], func=AF.Sqrt, bias=gne[:, 0:1], scale=1.0 / 64),
                 reads=[pkv, "gne"], writes=[fk[2]])
            p.op("dve", lambda e: e.reciprocal(out=T_(2), in_=T_(2)), reads=[fk[2]], writes=[fk[2]])
            p.op("dve", lambda e: e.tensor_tensor(out=T_(1), in0=T_(1), in1=T_(2), op=ALU.mult), reads=[fk[1], fk[2]], writes=[fk[1]])
            p.op("dve", lambda e: e.tensor_scalar(out=T_(1), in0=T_(1), scalar1=lnv[:, 0, m:m + 1], scalar2=lnv[:, 1, m:m + 1],
                                                  op0=ALU.mult, op1=ALU.add),
                 reads=[fk[1], "lnv"], writes=[fk[1]])
            p.op("dve", lambda e: e.tensor_tensor(out=T_(3), in0=y4[:, 2, 0:n], in1=y4[:, 3, 0:n], op=ALU.add), reads=[("yt", s2)], writes=[fk[3]])
            p.op("dve", lambda e: e.tensor_tensor(out=T_(1), in0=T_(1), in1=T_(3), op=ALU.add), reads=[fk[1], fk[3]], writes=[fk[1]])
            p.op("dve", lambda e: e.tensor_tensor(out=ob[:, m, 0:n], in0=psg[:, 0:n], in1=T_(1), op=ALU.mult),
                 reads=[pkg, fk[1]], writes=[("ob", m)])

        def evac_y(m, ps, pk, b0, b1):
            s = cnt[0] % 2
            cnt[0] += 1
            p.op("act", lambda e: e.activation(out=st[:, s, 0:b1 - b0], in_=ps[:, 0:b1 - b0], func=AF.Copy), reads=[pk], writes=[("st", s)])
            p.dma(yov[:, m, o0 + b0:o0 + b1], st[:, s, 0:b1 - b0], reads=[("st", s)], q="act")
        linear_fm2(p, banks, slabs, "slab", ob, "ob", KC, wo_d.rearrange("(c p) f -> p c f", p=128), D, [(0, n)], evac_y)
    return p.build()


_PROGS = {}
_DEBUG = None


def _prog(name, fn, *args):
    key = (name,) + args
    if key not in _PROGS:
        _PROGS[key] = fn(*args)
    return _PROGS[key]


def _run(nc, in_maps):
    res = _bu.run_bass_kernel_spmd(nc, in_maps, core_ids=list(range(8)))
    return res.results


def _f32(x):
    return np.ascontiguousarray(np.asarray(x, dtype=np.float32))


def _slab(hl_b, hc_b, s, halo):
    Dn = hl_b.shape[0]
    wl, wc = 1024 + 2 * halo, 128 + 2 * halo
    out = np.zeros((Dn, wl + wc), hl_b.dtype)
    mask = np.zeros((1, wl + wc), np.float32)
    for (src, n, base, o0) in ((hl_b, 1024, s * 1024, 0), (hc_b, 128, s * 128, wl)):
        lo, hi = base - halo, base + n + halo
        a0, a1 = max(lo, 0), min(hi, src.shape[1])
        out[:, o0 + (a0 - lo):o0 + (a1 - lo)] = src[:, a0:a1]
        mask[0, o0 + (a0 - lo):o0 + (a1 - lo)] = 1.0
    return out, mask


def _core_cols(hl_b, hc_b, s):
    return np.ascontiguousarray(np.concatenate([hl_b[:, s * 1024:(s + 1) * 1024], hc_b[:, s * 128:(s + 1) * 128]], 1))


def _pool_icnt(s):
    out = np.ones((4, WP), np.float32)
    for g, win in enumerate((2, 4, 8, 16)):
        left, right = win // 2, win - 1 - win // 2
        for (n, base, o0, Tn) in ((1024, s * 1024, 8, T), (128, s * 128, 1040 + 8, L)):
            t = np.arange(base, base + n)
            cnt = np.minimum(t + right, Tn - 1) - np.maximum(t - left, 0) + 1
            out[g, o0:o0 + n] = (1.0 / cnt.astype(np.float64)).astype(np.float32)
    return out


def rwkv_mixer(hl, hc, modv3, W3, dbg=None):
    import ml_dtypes
    norm1_g = W3["norm1_g"]; rw_mu = W3["rw_mu"]; rw_w_rkv = W3["rw_w_rkv"]; rw_w0 = W3["rw_w0"]; rw_w1 = W3["rw_w1"]
    rw_w2 = W3["rw_w2"]; rw_a0 = W3["rw_a0"]; rw_a1 = W3["rw_a1"]; rw_a2 = W3["rw_a2"]; rw_g1 = W3["rw_g1"]; rw_g2 = W3["rw_g2"]
    rw_k_k = W3["rw_k_k"]; rw_k_a = W3["rw_k_a"]; rw_r_k = W3["rw_r_k"]; rw_ln_g = W3["rw_ln_g"]; rw_ln_b = W3["rw_ln_b"]
    rw_w_o = W3["rw_w_o"]
    cores = [(b, s) for b in range(NB) for s in range(2)]
    mu = _f32(rw_mu[0])
    bones = np.kron(np.eye(2), np.ones((64, 64))).astype(np.float32)
    rmask = np.ones((1, 512), np.float32)
    rmask[0, ::64] = 0

    def coef_of(mu_prev, mu_next):
        return np.ascontiguousarray(np.stack([
            np.stack([_chunked(mu_prev[n]) for n in range(6)], 1),
            np.stack([_chunked(mu_next[n]) for n in range(6)], 1)], 1))

    bf = ml_dtypes.bfloat16
    mka, mkp, mkl = rw2_masks()
    rw2_in = {}
    bv_nat = {}
    vecs = np.ascontiguousarray(np.stack([_chunked(rw_w0[0][0]), _chunked(rw_w0[0][1]), _chunked(rw_a0[0][0]), _chunked(rw_a0[0][1]),
                                          _chunked(rw_k_k[0]), _chunked(rw_k_a[0]), _chunked(_f32(rw_r_k[0]).reshape(-1))], 1))
    ims = []
    for (b, s) in cores:
        hs, mk = _slab(hl[b], hc[b], s, 1)
        ims.append(dict(h=hs, modv=modv3[b], g1=_chunked(norm1_g[3]), mask=mk, coef=coef_of(mu[0], mu[1]),
                        wrkv=_f32(rw_w_rkv[0]), w1=_f32(rw_w1[0]), w2=_f32(rw_w2[0]), a1=_f32(rw_a1[0]),
                        a2=_f32(rw_a2[0]), vecs=vecs, bones=bones, rmask=rmask))
    res = _run(_prog("rw1", build_rw1), ims)
    for b in range(NB):
        r0, r1 = res[2 * b], res[2 * b + 1]
        for d in range(2):
            def seq(nm):
                lat = np.concatenate([np.asarray(r0[nm])[:, :1024], np.asarray(r1[nm])[:, :1024]], 1)
                cx = np.concatenate([np.asarray(r0[nm])[:, 1024:], np.asarray(r1[nm])[:, 1024:]], 1)
                if d == 1:
                    lat, cx = lat[:, ::-1], cx[:, ::-1]
                return np.concatenate([cx, lat], 1)
            at, bt, kt, rt = (seq("%s%d" % (nm, d)) for nm in ("at", "bt", "kt", "rt"))
            vt = seq("vt")
            hm = lambda z: z.reshape(32, 64, NCH, 64).transpose(2, 1, 0, 3)
            far = np.ascontiguousarray(np.stack([hm(at), hm(rt)], 3).reshape(NCH, 64, -1))
            fbk = np.ascontiguousarray(np.stack([hm(bt), hm(kt)], 2).reshape(NCH, 64, -1))
            tk = lambda z: np.ascontiguousarray(z.T).reshape(NCH, 64, D)
            tm = np.ascontiguousarray(np.stack([tk(vt), tk(bt), tk(kt)], 2).reshape(NCH, 64, -1))
            g0, g1_ = np.asarray(r0["gam%d" % d]), np.asarray(r1["gam%d" % d])
            glat = np.concatenate([g0[:, 0:16], g1_[:, 0:16]], 1)
            gcx = np.concatenate([g0[:, 16:18], g1_[:, 16:18]], 1)
            if d == 1:
                glat, gcx = glat[:, ::-1], gcx[:, ::-1]
            gam = np.concatenate([gcx, glat], 1)
            gam = np.ascontiguousarray(gam.reshape(32, 64, NCH).transpose(1, 0, 2))
            rw2_in[(b, d)] = dict(far=far.astype(bf, copy=False), fbk=fbk.astype(bf, copy=False), tm=tm.astype(bf, copy=False),
                                  gam=gam, mka=mka, mkp=mkp, mkl=mkl)
            bvs = np.concatenate([np.asarray(r0["bv%d" % d])[:, :1024], np.asarray(r1["bv%d" % d])[:, :1024]], 1)
            bvc = np.concatenate([np.asarray(r0["bv%d" % d])[:, 1024:], np.asarray(r1["bv%d" % d])[:, 1024:]], 1)
            bv_nat[(b, d)] = (np.ascontiguousarray(bvs), np.ascontiguousarray(bvc))
    res = _run(_prog("rw2", build_rw2), [rw2_in[(b, d)] for b in range(NB) for d in range(2)])
    y_nat = {}
    for b in range(NB):
        for d in range(2):
            y = res[2 * b + d]["y"].reshape(NCH, 64, 32, 64).transpose(2, 1, 0, 3).reshape(D, NTOK)
            yc_, yl_ = y[:, :L], y[:, L:]
            if d == 1:
                yc_, yl_ = yc_[:, ::-1], yl_[:, ::-1]
            y_nat[(b, d)] = (np.ascontiguousarray(yl_), np.ascontiguousarray(yc_))
    ims = []
    coef_n = coef_of(mu[0], mu[1])
    lnv = np.ascontiguousarray(np.stack([_chunked(rw_ln_g[0]), _chunked(rw_ln_b[0])], 1))
    for (b, s) in cores:
        hs, mk = _slab(hl[b], hc[b], s, 1)
        yin = np.stack([_core_cols(y_nat[(b, 0)][0], y_nat[(b, 0)][1], s), _core_cols(y_nat[(b, 1)][0], y_nat[(b, 1)][1], s),
                        _core_cols(bv_nat[(b, 0)][0], bv_nat[(b, 0)][1], s), _core_cols(bv_nat[(b, 1)][0], bv_nat[(b, 1)][1], s)], 0)
        ims.append(dict(h=hs, modv=modv3[b], g1=_chunked(norm1_g[3]), mask=mk, coef=coef_n, yin=np.ascontiguousarray(yin),
                        g1w=_f32(rw_g1[0]), g2w=_f32(rw_g2[0]), lnv=lnv, wo=_f32(rw_w_o[0]), bones=bones))
    res = _run(_prog("rw3", build_rw3), ims)
    if dbg is not None:
        dbg["rw2_in"] = rw2_in
        dbg["y_nat"] = y_nat
        dbg["bv_nat"] = bv_nat
    return res


def kernel(x, c, ctx, c_ctx, norm1_g, norm2_g, w_mod, b_mod, ffn_w_gate, ffn_w_up, ffn_conv_w, ffn_conv_b,
           ffn_w_down, final_norm_g, pool_w, pool_b, pool_scale, na_w_qkv, na_rpb, na_w_o, sg_w_in, sg_b_in,
           sg_norm_g, sg_w_s, sg_b_s, sg_w_o, rw_mu, rw_w_rkv, rw_w0, rw_w1, rw_w2, rw_a0, rw_a1, rw_a2, rw_g1,
           rw_g2, rw_k_k, rw_k_a, rw_r_k, rw_ln_g, rw_ln_b, rw_w_o):
    import ml_dtypes
    x, ctx = _f32(x), _f32(ctx)
    hl = [np.ascontiguousarray(x[b].T) for b in range(NB)]
    hc = [np.ascontiguousarray(ctx[b].T) for b in range(NB)]
    cores = [(b, s) for b in range(NB) for s in range(2)]

    cc = np.zeros((8, D), np.float32)
    cc[0:4] = _f32(c)
    cc[4] = _f32(c_ctx)
    ct = np.ascontiguousarray(cc.T.reshape(KC, 128, 8).transpose(1, 0, 2))
    w_mod = np.asarray(w_mod, np.float32)
    ims = []
    for core in range(8):
        i, half = core // 2, core % 2
        ims.append(dict(wm=np.ascontiguousarray(w_mod[i][:, half * 6144:(half + 1) * 6144]),
                        bm=_chunked(np.asarray(b_mod[i], np.float32)[half * 6144:(half + 1) * 6144]), ct=ct))
    res = _run(_prog("mod", build_mod), ims)
    modv = {}
    for i in range(4):
        mo = np.concatenate([res[2 * i]["mo"], res[2 * i + 1]["mo"]], 1)
        for b in range(NB):
            modv[(i, b)] = np.ascontiguousarray(mo[:, :, [b, 4]])

    def run_ffn(i, ys_l, ys_c, final):
        wg, wu, wd = _f32(ffn_w_gate[i]), _f32(ffn_w_up[i]), _f32(ffn_w_down[i])
        cw = np.ascontiguousarray(_f32(ffn_conv_w[i]).reshape(3, FC, 128).transpose(2, 1, 0))
        cb = _chunked(ffn_conv_b[i])
        g2 = _chunked(norm2_g[i])
        ims = []
        for (b, s) in cores:
            hs, mk = _slab(hl[b], hc[b], s, 1)
            y = np.zeros((len(ys_l), D, WF), np.float32)
            for n in range(len(ys_l)):
                y[n] = _slab(ys_l[n][b], ys_c[n][b], s, 1)[0]
            im = dict(h=hs, y=y, modv=modv[(i, b)], g2=g2, mask=np.ascontiguousarray(np.repeat(mk, 128, 0)),
                      wg=wg, wu=wu, wd=wd, cw=cw, cb=cb)
            if final:
                im["gf"] = _chunked(final_norm_g)
            ims.append(im)
        res = _run(_prog("ffn", build_ffn, final, len(ys_l)), ims)
        out = None
        if final:
            out = np.empty((NB, T, D), np.float32)
        for ci, (b, s) in enumerate(cores):
            ho = res[ci]["ho"]
            hl[b][:, s * 1024:(s + 1) * 1024] = ho[:, 0:1024]
            hc[b][:, s * 128:(s + 1) * 128] = ho[:, 1024:1152]
            if final:
                out[b, s * 1024:(s + 1) * 1024, :] = res[ci]["of"].T
        return out

    def split_y(res_y):
        yl = [np.empty((D, T), np.float32) for _ in range(NB)]
        yc = [np.empty((D, L), np.float32) for _ in range(NB)]
        for ci, (b, s) in enumerate(cores):
            yl[b][:, s * 1024:(s + 1) * 1024] = res_y[ci][:, 0:1024]
            yc[b][:, s * 128:(s + 1) * 128] = res_y[ci][:, 1024:1152]
        return yl, yc

    i = 0
    ims = []
    for (b, s) in cores:
        hs, mk = _slab(hl[b], hc[b], s, 8)
        ims.append(dict(h=hs, modv=modv[(i, b)], g1=_chunked(norm1_g[i]), mask=mk, icnt=_pool_icnt(s),
                        pw=_f32(pool_w[0]), pb=_chunked(pool_b[0]), psc=_chunked(pool_scale[0])))
    res = _run(_prog("pool", build_pool), ims)
    yl, yc = split_y([r["y"] for r in res])
    run_ffn(i, [yl], [yc], False)
    if _DEBUG is not None:
        _DEBUG.append(([a.copy() for a in hl], [a.copy() for a in hc]))

    i = 1
    pm = rope_perm()
    ims = []
    for (b, s) in cores:
        rc, rs = rope_tables(s * 1024, 1024)
        ims.append(dict(h=_core_cols(hl[b], hc[b], s), modv=modv[(i, b)], g1=_chunked(norm1_g[i]), wqkv=_f32(na_w_qkv[0]),
                        rcos=rc, rsin=rs, pm=pm))
    res = _run(_prog("na1", build_na1), ims)
    Q, Kt, V = [], [], []
    for b in range(NB):
        r0, r1 = res[2 * b], res[2 * b + 1]
        Q.append(np.concatenate([r0["qT"][:, :1024], r1["qT"][:, :1024], r0["qT"][:, 1024:], r1["qT"][:, 1024:]], 1))
        Kt.append(np.concatenate([r0["kT"][:, :1024], r1["kT"][:, :1024], r0["kT"][:, 1024:], r1["kT"][:, 1024:]], 1))
        V.append(np.concatenate([r0["v"][:1024], r1["v"][:1024], r0["v"][1024:], r1["v"][1024:]], 0))
    rpb = _f32(na_rpb[0])
    wo = _f32(na_w_o[0])
    ims = []
    for b in range(NB):
        for hh in range(2):
            sl = slice(hh * 1024, (hh + 1) * 1024)
            ims.append(dict(qT=np.ascontiguousarray(Q[b][sl]), kT=np.ascontiguousarray(Kt[b][sl]),
                            v=np.ascontiguousarray(V[b][:, sl]), bt=na_bias_tables(rpb, list(range(hh * 16, hh * 16 + 16))),
                            wo=np.ascontiguousarray(wo[sl])))
    res = _run(_prog("na2", build_na2), ims)
    yls, ycs = [], []
    for hh in range(2):
        yls.append([np.ascontiguousarray(res[2 * b + hh]["y"][:, :T]) for b in range(NB)])
        ycs.append([np.ascontiguousarray(res[2 * b + hh]["y"][:, T:]) for b in range(NB)])
    run_ffn(i, yls, ycs, False)
    if _DEBUG is not None:
        _DEBUG.append(([a.copy() for a in hl], [a.copy() for a in hc]))

    i = 2
    b_in = _f32(sg_b_in[0])
    ims = []
    for (b, s) in cores:
        ims.append(dict(h=_core_cols(hl[b], hc[b], s), modv=modv[(i, b)], g1=_chunked(norm1_g[i]), win=_f32(sg_w_in[0]),
                        bzu=_chunked(b_in[:D]), bzv=np.ascontiguousarray(b_in[None, D:]), ng=_f32(sg_norm_g[0])[None].copy(),
                        wst=np.ascontiguousarray(_f32(sg_w_s[0]).transpose(2, 0, 1)), bs=_f32(sg_b_s[0]).reshape(1, -1).copy(),
                        wo=_f32(sg_w_o[0])))
    res = _run(_prog("gmlp", build_gmlp), ims)
    yl, yc = split_y([r["y"] for r in res])
    run_ffn(i, [yl], [yc], False)
    if _DEBUG is not None:
        _DEBUG.append(([a.copy() for a in hl], [a.copy() for a in hc]))

    i = 3
    W3 = dict(norm1_g=norm1_g, rw_mu=rw_mu, rw_w_rkv=rw_w_rkv, rw_w0=rw_w0, rw_w1=rw_w1, rw_w2=rw_w2, rw_a0=rw_a0, rw_a1=rw_a1,
              rw_a2=rw_a2, rw_g1=rw_g1, rw_g2=rw_g2, rw_k_k=rw_k_k, rw_k_a=rw_k_a, rw_r_k=rw_r_k, rw_ln_g=rw_ln_g,
              rw_ln_b=rw_ln_b, rw_w_o=rw_w_o)
    res = rwkv_mixer(hl, hc, {b: modv[(i, b)] for b in range(NB)}, W3)
    yl, yc = split_y([r["y"] for r in res])
    return run_ffn(i, [yl], [yc], True)
```

```python
import contextlib
import os
import numpy as np
import concourse.bass as bass
import concourse.mybir as mybir

F32 = mybir.dt.float32
BF16 = mybir.dt.bfloat16
AF = mybir.ActivationFunctionType
ALU = mybir.AluOpType
AX = mybir.AxisListType

ENGS = ("sync", "pe", "dve", "act", "pool")
SAME_ENGINE_SYNC = True


class Prog:
    EPOCH = 16000
    NDMA = 24

    def __init__(self):
        self.nc = bass.Bass("TRN2", target_bir_lowering=False)
        self.stack = contextlib.ExitStack()
        self.ops = {e: [] for e in ENGS}
        self.cnt = {e: 0 for e in ENGS}
        self.last_w = {}
        self.readers = {}
        self.waited = {e: {} for e in ENGS}
        self.sems = {}
        self.dma_n = 0
        self.dma_tot = [0] * self.NDMA
        self.n_uid = 0

    def dram(self, name, shape, dt=F32, kind="ExternalInput"):
        return self.nc.dram_tensor(name, list(shape), dt, kind=kind).ap()

    def sb(self, name, shape, dt=F32):
        return self.stack.enter_context(self.nc.sbuf_tensor("sb_" + name, list(shape), dt))

    def ps(self, name, shape, dt=F32):
        return self.stack.enter_context(self.nc.psum_tensor("pp_" + name, list(shape), dt))

    def _sem(self, key):
        if key not in self.sems:
            self.sems[key] = self.stack.enter_context(
                self.nc.semaphore("s_%s_%s" % (key[0], key[1])))
        return self.sems[key]

    def _deps(self, eng, reads, writes, pe_sync=False):
        toks = []
        for k in reads:
            w = self.last_w.get(k)
            if w is not None:
                toks.append(w)
        for k in writes:
            w = self.last_w.get(k)
            if w is not None:
                toks.append(w)
            toks.extend(self.readers.get(k, ()))
        need = {}
        for t in toks:
            if t[0] == "eng":
                _, e, seq = t
                if e == eng and ((eng == "pe" and not pe_sync) or not SAME_ENGINE_SYNC):
                    continue
                sk = (e, seq // self.EPOCH)
                v = seq % self.EPOCH + 1
            else:
                _, s, v = t
                sk = ("dma", s)
            if need.get(sk, 0) < v:
                need[sk] = v
        out = []
        wd = self.waited[eng]
        for sk, v in need.items():
            if wd.get(sk, 0) >= v:
                continue
            wd[sk] = v
            out.append((sk, v))
        return out

    def _commit(self, tok, reads, writes):
        for k in reads:
            self.readers.setdefault(k, []).append(tok)
        for k in writes:
            self.last_w[k] = tok
            self.readers[k] = []

    @staticmethod
    def _psum_excl(reads, writes):
        r2, w2 = [], list(writes)
        for k in reads:
            if isinstance(k, tuple) and k and k[0] == "ps":
                w2.append(k)
            else:
                r2.append(k)
        return r2, w2

    def op(self, eng, fn, reads=(), writes=(), pe_sync=False):
        reads, writes = self._psum_excl(reads, writes)
        waits = self._deps(eng, reads, writes, pe_sync)
        seq = self.cnt[eng]
        self.cnt[eng] += 1
        tok = ("eng", eng, seq)
        self._emit(eng, waits, fn, ((eng, seq // self.EPOCH), 1))
        self._commit(tok, reads, writes)
        return tok

    def dma(self, out, in_, reads=(), writes=(), q="sync", **kw):
        if q == "pool" and "max_dma_last_dim" not in kw:
            kw["max_dma_last_dim"] = 2048
        s = self.dma_n % self.NDMA
        self.dma_n += 1
        waits = self._deps(q, reads, writes)
        prev = self.dma_tot[s]
        sk = ("dma", s)
        if prev and self.waited[q].get(sk, 0) < prev:
            self.waited[q][sk] = prev
            waits.append((sk, prev))
        self.dma_tot[s] = prev + 16
        tok = ("dma", s, prev + 16)
        fn = lambda e, out=out, in_=in_, kw=kw: e.dma_start(out=out, in_=in_, **kw)
        self._emit(q, waits, fn, (sk, 16))
        self._commit(tok, reads, writes)
        return tok

    def _emit(self, name, waits, fn, inc):
        nc = self.nc
        eng = {"sync": nc.sync, "pe": nc.tensor, "dve": nc.vector, "act": nc.scalar, "pool": nc.gpsimd}[name]
        for sk, v in waits:
            eng.wait_ge(self._sem(sk), v)
        fn(eng).then_inc(self._sem(inc[0]), inc[1])
        self.nops = getattr(self, "nops", 0) + 1 + len(waits)
        if not hasattr(self, "trace"):
            self.trace = {e: [] for e in ENGS}
        self.trace[name].append((list(waits), inc))

    def check_deadlock(self):
        tr = getattr(self, "trace", None)
        if tr is None:
            return
        val = {}
        pos = {e: 0 for e in ENGS}
        progress = True
        while progress:
            progress = False
            for e in ENGS:
                q = tr[e]
                while pos[e] < len(q):
                    waits, inc = q[pos[e]]
                    if all(val.get(sk, 0) >= v for sk, v in waits):
                        val[inc[0]] = val.get(inc[0], 0) + inc[1]
                        pos[e] += 1
                        progress = True
                    else:
                        break
        stuck = {e: (pos[e], len(tr[e])) for e in ENGS if pos[e] < len(tr[e])}
        if stuck:
            msg = []
            for e, (i, n) in stuck.items():
                waits, inc = tr[e][i]
                msg.append("%s stuck at %d/%d waiting %s (have %s)" % (
                    e, i, n, waits, [(sk, val.get(sk, 0)) for sk, v in waits]))
            raise RuntimeError("semaphore deadlock: " + "; ".join(msg))

    def build(self):
        nc = self.nc
        self.check_deadlock()
        for s in range(self.NDMA):
            if self.dma_tot[s]:
                nc.sync.wait_ge(self._sem(("dma", s)), self.dma_tot[s])
        for e in ENGS:
            if e != "sync" and self.cnt[e]:
                seq = self.cnt[e] - 1
                nc.sync.wait_ge(self._sem((e, seq // self.EPOCH)), seq % self.EPOCH + 1)
        self.stack.close()
        return nc


import concourse.bass_utils as _bu

D = 2048
KC = 16
T = 2048
L = 256
NB = 4
FF = 5504
FC = 43
EPS = 1e-6


class Banks:
    def __init__(self, p, n=8, prefix="psb"):
        self.t = [p.ps("%s%d" % (prefix, i), [128, 512], F32) for i in range(n)]
        self.keys = [("ps", prefix, i) for i in range(n)]
        self.i = 0

    def next(self):
        b = self.i % len(self.t)
        self.i += 1
        return self.t[b], self.keys[b]


def col_blocks(c0, c1, n=512):
    out = []
    while c0 < c1:
        out.append((c0, min(c1, c0 + n)))
        c0 += n
    return out


class Common:
    def __init__(self, p):
        self.ones = p.sb("c_ones", [128, 128], BF16)
        self.eps = p.sb("c_eps", [128, 1], F32)
        p.op("dve", lambda e: e.memset(self.ones[:], 1.0), writes=["c_ones"])
        p.op("dve", lambda e: e.memset(self.eps[:], EPS), writes=["c_eps"])


def load_small(p, name, shape, dram_ap, dt=F32):
    t = p.sb(name, shape, dt)
    p.dma(t[:], dram_ap, writes=[name])
    return t


def make_AB(p, name, modv, g, j_shift, j_scale, col):
    A = p.sb(name + "_A", [128, KC], F32)
    p.op("dve", lambda e: e.tensor_scalar(out=A[:], in0=modv[:, j_scale * KC:(j_scale + 1) * KC, col],
                                          scalar1=1.0, scalar2=None, op0=ALU.add),
         reads=["modv"], writes=[name + "_A"])
    p.op("dve", lambda e: e.tensor_tensor(out=A[:], in0=A[:], in1=g[:], op=ALU.mult),
         reads=[name + "_A", "gvec"], writes=[name + "_A"])
    return A


def rms_rstd(p, cm, banks, h, hkey, W, rstd, rkey, sq, nfeat_chunks=KC, dmodel=D, eps_ap=None):
    for (b0, b1) in col_blocks(0, W):
        ps, pk = banks.next()
        n = b1 - b0
        for c in range(nfeat_chunks):
            s = c % 2
            p.op("act", lambda e, c=c, s=s: e.activation(out=sq[:, s, 0:n], in_=h[:, c, b0:b1], func=AF.Square),
                 reads=[hkey], writes=[("sq", s)])
            p.op("pe", lambda e, c=c, s=s: e.matmul(ps[:, 0:n], lhsT=cm.ones[:], rhs=sq[:, s, 0:n],
                                                     start=(c == 0), stop=(c == nfeat_chunks - 1)),
                 reads=[("sq", s), "c_ones"], writes=[pk])
        p.op("act", lambda e: e.activation(out=rstd[:, b0:b1], in_=ps[:, 0:n], func=AF.Sqrt,
                                           bias=(eps_ap if eps_ap is not None else cm.eps)[:, 0:1], scale=1.0 / dmodel),
             reads=[pk, "c_eps"], writes=[rkey])
        p.op("dve", lambda e: e.reciprocal(out=rstd[:, b0:b1], in_=rstd[:, b0:b1]), reads=[rkey], writes=[rkey])


def norm_mod(p, cm, banks, h, hkey, W, segs, out_bf, okey, rstd, sq, tmp, mask=None):
    rms_rstd(p, cm, banks, h, hkey, W, rstd, "rstd", sq)
    for c in range(KC):
        s = c % 2
        p.op("dve", lambda e, c=c, s=s: e.tensor_tensor(out=tmp[:, s, 0:W], in0=h[:, c, 0:W], in1=rstd[:, 0:W], op=ALU.mult),
             reads=[hkey, "rstd"], writes=[("tmp", s)])
        for (c0, c1, A, Bfn) in segs:
            p.op("act", lambda e, c=c, s=s, c0=c0, c1=c1, A=A, Bfn=Bfn: e.activation(
                out=out_bf[:, c, c0:c1], in_=tmp[:, s, c0:c1], func=AF.Identity, scale=A[:, c:c + 1], bias=Bfn(c)),
                reads=[("tmp", s), "AB", "modv"], writes=[(okey, c)])
        if mask is not None:
            p.op("pool", lambda e, c=c: e.tensor_tensor(out=out_bf[:, c, 0:W], in0=out_bf[:, c, 0:W], in1=mask[:, 0:W], op=ALU.mult),
                 reads=[(okey, c), "mask"], writes=[(okey, c)])


WF = 1156
LAT0, LAT1 = 0, 1026
CTX0, CTX1 = 1026, 1156
FG = 6
HALO1 = ((0, 1), (1025, 1027), (1155, 1156))


def build_ffn(final, nparts=2):
    p = Prog()
    h_d = p.dram("h", [D, WF])
    y_d = p.dram("y", [nparts, D, WF])
    modv_d = p.dram("modv", [128, 6 * KC, 2])
    g2_d = p.dram("g2", [128, KC])
    mask_d = p.dram("mask", [128, WF])
    wg_d = p.dram("wg", [D, FF])
    wu_d = p.dram("wu", [D, FF])
    wd_d = p.dram("wd", [FF, D])
    cw_d = p.dram("cw", [128, FC, 3])
    cb_d = p.dram("cb", [128, FC])
    ho_d = p.dram("ho", [D, 1152], kind="ExternalOutput")
    if final:
        gf_d = p.dram("gf", [128, KC])
        of_d = p.dram("of", [D, 1024], kind="ExternalOutput")

    cm = Common(p)
    banks = Banks(p)
    h = p.sb("h", [128, KC, WF], F32)
    u = p.sb("u", [128, KC, WF], BF16)
    act = p.sb("actb", [128, FG, WF], BF16)
    gsb = p.sb("gsb", [128, 2, WF], F32)
    tsb = p.sb("tsb", [128, 2, WF], F32)
    sq = p.sb("sq", [128, 2, 512], BF16)
    rstd = p.sb("rstd", [128, WF], F32)
    modv = load_small(p, "modv", [128, 6 * KC, 2], modv_d)
    gvec = load_small(p, "gvec", [128, KC], g2_d)
    mask = p.sb("mask", [128, WF], BF16)
    p.dma(mask[:], mask_d, writes=["mask"], q="pool")
    cw = load_small(p, "cw", [128, FC, 3], cw_d)
    cb = load_small(p, "cb", [128, FC], cb_d)
    gus = [p.sb("gus%d" % i, [128, KC, 256], BF16) for i in range(4)]
    nev = [0]
    wds = p.sb("wds", [128, FG, D], BF16)

    hv = h_d.rearrange("(c p) w -> p c w", p=128)
    yv = y_d.rearrange("n (c p) w -> n p c w", p=128)
    for c4 in range(0, KC, 4):
        p.dma(h[:, c4:c4 + 4, :], hv[:, c4:c4 + 4, :], writes=[("h", c) for c in range(c4, c4 + 4)])
    for c in range(KC):
        for n in range(nparts):
            s = (c * 2 + n) % 2
            p.dma(gsb[:, s, :], yv[n, :, c, :], writes=[("gsb", s)])
            for (c0, c1, col) in ((LAT0, LAT1, 0), (CTX0, CTX1, 1)):
                p.op("dve", lambda e, c=c, s=s, c0=c0, c1=c1, col=col: e.scalar_tensor_tensor(
                    out=h[:, c, c0:c1], in0=gsb[:, s, c0:c1], scalar=modv[:, 2 * KC + c, col:col + 1],
                    in1=h[:, c, c0:c1], op0=ALU.mult, op1=ALU.add),
                    reads=[("gsb", s), "modv", ("h", c)], writes=[("h", c)])
    A_l = make_AB(p, "ffl", modv, gvec, 3, 4, 0)
    A_c = make_AB(p, "ffc", modv, gvec, 3, 4, 1)
    hkeys = [("h", c) for c in range(KC)]

    class HK:
        pass
    segs = [(LAT0, LAT1, A_l, lambda c: modv[:, 3 * KC + c, 0:1]), (CTX0, CTX1, A_c, lambda c: modv[:, 3 * KC + c, 1:2])]
    _norm_mod_chunked(p, cm, banks, h, W=WF, segs=segs, out_bf=u, okey="u", rstd=rstd, sq=sq, tmp=tsb, mask=mask,
                      keys_A=["ffl_A", "ffc_A"])

    wgv = wg_d.rearrange("(c p) f -> p c f", p=128)
    wuv = wu_d.rearrange("(c p) f -> p c f", p=128)
    wdv = wd_d.rearrange("(f p) d -> p f d", p=128)
    blocks = col_blocks(0, WF)
    dblocks = [(1, 513, 0), (513, 1025, 0), (1025, 1155, 1)]
    nslab = 0
    slab_of = {}
    ukeys = [("u", c) for c in range(KC)]

    def ensure_slab(fo):
        nonlocal nslab
        sidx = fo // 2
        if sidx in slab_of:
            return slab_of[sidx]
        i0 = (nslab % 2) * 2
        nslab += 1
        f0 = sidx * 256
        f1 = min(FF, f0 + 256)
        p.dma(gus[i0][:, :, 0:f1 - f0], wgv[:, :, f0:f1], writes=[("gus", i0)], q="pool")
        p.dma(gus[i0 + 1][:, :, 0:f1 - f0], wuv[:, :, f0:f1], writes=[("gus", i0 + 1)], q="pool")
        slab_of[sidx] = i0
        return i0

    ngroups = (FC + FG - 1) // FG
    for g in range(ngroups):
        fos = list(range(g * FG, min(FC, (g + 1) * FG)))
        for li_ in range(len(fos)):
            p.dma(wds[:, li_, :], wdv[:, fos[0] + li_, :], writes=["wds"], q="pool")
        for li, fo in enumerate(fos):
            i0 = ensure_slab(fo)
            off = (fo % 2) * 128
            s = fo % 2
            gps = []
            for (b0, b1) in blocks:
                ps, pk = banks.next()
                for c in range(KC):
                    p.op("pe", lambda e, ps=ps, c=c, b0=b0, b1=b1: e.matmul(
                        ps[:, 0:b1 - b0], lhsT=gus[i0][:, c, off:off + 128], rhs=u[:, c, b0:b1],
                        start=(c == 0), stop=(c == KC - 1)),
                        reads=[("gus", i0), ("u", c)], writes=[pk])
                p.op("act", lambda e, ps=ps, b0=b0, b1=b1: e.activation(out=gsb[:, s, b0:b1], in_=ps[:, 0:b1 - b0], func=AF.Copy),
                     reads=[pk], writes=[("gsb", s)])
            ups = []
            for (b0, b1) in blocks:
                ps, pk = banks.next()
                for c in range(KC):
                    p.op("pe", lambda e, ps=ps, c=c, b0=b0, b1=b1: e.matmul(
                        ps[:, 0:b1 - b0], lhsT=gus[i0 + 1][:, c, off:off + 128], rhs=u[:, c, b0:b1],
                        start=(c == 0), stop=(c == KC - 1)),
                        reads=[("gus", i0 + 1), ("u", c)], writes=[pk])
                ups.append((ps, pk, b0, b1))
            n = WF - 2
            p.op("dve", lambda e: e.tensor_scalar(out=tsb[:, s, 1:1 + n], in0=gsb[:, s, 0:n], scalar1=cw[:, fo, 0:1],
                                                  scalar2=None, op0=ALU.mult),
                 reads=[("gsb", s), "cw"], writes=[("tmp", s)])
            for k in (1, 2):
                p.op("dve", lambda e, k=k: e.scalar_tensor_tensor(out=tsb[:, s, 1:1 + n], in0=gsb[:, s, k:k + n],
                                                                  scalar=cw[:, fo, k:k + 1], in1=tsb[:, s, 1:1 + n],
                                                                  op0=ALU.mult, op1=ALU.add),
                     reads=[("gsb", s), "cw", ("tmp", s)], writes=[("tmp", s)])
            p.op("act", lambda e: e.activation(out=tsb[:, s, 1:1 + n], in_=tsb[:, s, 1:1 + n], func=AF.Silu,
                                               bias=cb[:, fo:fo + 1], scale=1.0),
                 reads=[("tmp", s), "cb"], writes=[("tmp", s)])
            for (ps, pk, b0, b1) in ups:
                a0 = max(b0, 1)
                a1 = min(b1, WF - 1)
                p.op("dve", lambda e, ps=ps, b0=b0, a0=a0, a1=a1: e.tensor_tensor(
                    out=act[:, li, a0:a1], in0=tsb[:, s, a0:a1], in1=ps[:, a0 - b0:a1 - b0], op=ALU.mult),
                    reads=[("tmp", s), pk], writes=[("act", li)])
        if g + 1 < ngroups:
            ensure_slab((g + 1) * FG)
        for m in range(KC):
            for (d0, d1, col) in dblocks:
                ps, pk = banks.next()
                for li in range(len(fos)):
                    p.op("pe", lambda e, ps=ps, li=li, d0=d0, d1=d1: e.matmul(
                        ps[:, 0:d1 - d0], lhsT=wds[:, li, m * 128:(m + 1) * 128], rhs=act[:, li, d0:d1],
                        start=(li == 0), stop=(li == len(fos) - 1)),
                        reads=["wds", ("act", li)], writes=[pk])
                if os.environ.get("FFN_EVAC", "dve") == "split":
                    es = nev[0] % 2
                    nev[0] += 1
                    p.op("act", lambda e: e.activation(out=evb[:, es, 0:d1 - d0], in_=ps[:, 0:d1 - d0], func=AF.Copy,
                                                       scale=modv[:, 5 * KC + m, col:col + 1]),
                         reads=[pk, "modv"], writes=[("evb", es)])
                    p.op("pool", lambda e: e.tensor_tensor(out=h[:, m, d0:d1], in0=h[:, m, d0:d1], in1=evb[:, es, 0:d1 - d0], op=ALU.add),
                         reads=[("evb", es), ("h", m)], writes=[("h", m)])
                else:
                    p.op("dve", lambda e, ps=ps, d0=d0, d1=d1, col=col: e.scalar_tensor_tensor(
                        out=h[:, m, d0:d1], in0=ps[:, 0:d1 - d0], scalar=modv[:, 5 * KC + m, col:col + 1],
                        in1=h[:, m, d0:d1], op0=ALU.mult, op1=ALU.add),
                        reads=[pk, "modv", ("h", m)], writes=[("h", m)])
    hov = ho_d.rearrange("(c p) w -> p c w", p=128)
    for c4 in range(0, KC, 4):
        ks = [("h", c) for c in range(c4, c4 + 4)]
        p.dma(hov[:, c4:c4 + 4, 0:1024], h[:, c4:c4 + 4, 1:1025], reads=ks)
        p.dma(hov[:, c4:c4 + 4, 1024:1152], h[:, c4:c4 + 4, 1027:1155], reads=ks)
    if final:
        gf = load_small(p, "gf", [128, KC], gf_d)
        rms_rstd_chunked(p, cm, banks, h, 1, 1025, rstd, sq)
        ofv = of_d.rearrange("(c p) w -> p c w", p=128)
        for c in range(KC):
            s = c % 2
            p.op("dve", lambda e, c=c, s=s: e.tensor_tensor(out=tsb[:, s, 1:1025], in0=h[:, c, 1:1025], in1=rstd[:, 1:1025], op=ALU.mult),
                 reads=[("h", c), "rstd"], writes=[("tmp", s)])
            p.op("act", lambda e, c=c, s=s: e.activation(out=tsb[:, s, 1:1025], in_=tsb[:, s, 1:1025], func=AF.Copy, scale=gf[:, c:c + 1]),
                 reads=[("tmp", s), "gf"], writes=[("tmp", s)])
            p.dma(ofv[:, c, :], tsb[:, s, 1:1025], reads=[("tmp", s)])
    return p.build()


def rms_rstd_chunked(p, cm, banks, h, w0, w1, rstd, sq, hname="h"):
    for (b0, b1) in col_blocks(w0, w1):
        ps, pk = banks.next()
        n = b1 - b0
        for c in range(KC):
            s = c % 2
            p.op("act", lambda e, c=c, s=s: e.activation(out=sq[:, s, 0:n], in_=h[:, c, b0:b1], func=AF.Square),
                 reads=[(hname, c)], writes=[("sq", s)])
            p.op("pe", lambda e, c=c, s=s: e.matmul(ps[:, 0:n], lhsT=cm.ones[:], rhs=sq[:, s, 0:n],
                                                     start=(c == 0), stop=(c == KC - 1)),
                 reads=[("sq", s), "c_ones"], writes=[pk])
        p.op("act", lambda e, ps=ps: e.activation(out=rstd[:, b0:b1], in_=ps[:, 0:n], func=AF.Sqrt,
                                                  bias=cm.eps[:, 0:1], scale=1.0 / D),
             reads=[pk, "c_eps"], writes=["rstd"])
        p.op("dve", lambda e: e.reciprocal(out=rstd[:, b0:b1], in_=rstd[:, b0:b1]), reads=["rstd"], writes=["rstd"])


def _norm_mod_chunked(p, cm, banks, h, W, segs, out_bf, okey, rstd, sq, tmp, mask, keys_A, hname="h", w0=0):
    rms_rstd_chunked(p, cm, banks, h, w0, W, rstd, sq, hname)
    for c in range(KC):
        s = c % 2
        p.op("dve", lambda e, c=c, s=s: e.tensor_tensor(out=tmp[:, s, w0:W], in0=h[:, c, w0:W], in1=rstd[:, w0:W], op=ALU.mult),
             reads=[(hname, c), "rstd"], writes=[("tmp", s)])
        for (c0, c1, A, Bfn) in segs:
            p.op("act", lambda e, c=c, s=s, c0=c0, c1=c1, A=A, Bfn=Bfn: e.activation(
                out=out_bf[:, c, c0:c1], in_=tmp[:, s, c0:c1], func=AF.Identity, scale=A[:, c:c + 1], bias=Bfn(c)),
                reads=[("tmp", s), "modv"] + keys_A, writes=[(okey, c)])
        if mask is not None:
            for (m0, m1) in HALO1:
                p.op("dve", lambda e, c=c: e.tensor_tensor(out=out_bf[:, c, m0:m1], in0=out_bf[:, c, m0:m1], in1=mask[:, m0:m1], op=ALU.mult),
                     reads=[(okey, c), "mask"], writes=[(okey, c)])


def linear_fm(p, banks, name, x, xkey, kcin, wview, dout, blocks, evac, sw=512, q="pool", nbuf=2, wdt=BF16):
    slabs = [p.sb("%s_w%d" % (name, i), [128, kcin, sw], wdt) for i in range(nbuf)]
    ns = (dout + sw - 1) // sw
    for si in range(ns):
        sl = slabs[si % nbuf]
        sk = (name + "_w", si % nbuf)
        f0 = si * sw
        f1 = min(dout, f0 + sw)
        half = kcin // 2 if kcin >= 8 else kcin
        for k0 in range(0, kcin, half):
            p.dma(sl[:, k0:k0 + half, 0:f1 - f0], wview[:, k0:k0 + half, f0:f1], writes=[sk], q=q)
        for mi in range((f1 - f0) // 128):
            m = (f0 // 128) + mi
            for (b0, b1) in blocks:
                ps, pk = banks.next()
                for c in range(kcin):
                    p.op("pe", lambda e: e.matmul(ps[:, 0:b1 - b0], lhsT=sl[:, c, mi * 128:(mi + 1) * 128],
                                                  rhs=x[:, c, b0:b1], start=(c == 0), stop=(c == kcin - 1)),
                         reads=[sk, (xkey, c)], writes=[pk])
                evac(m, ps, pk, b0, b1)


def build_mod():
    p = Prog()
    wm_d = p.dram("wm", [D, 6144])
    bm_d = p.dram("bm", [128, 48])
    ct_d = p.dram("ct", [128, KC, 8])
    mo_d = p.dram("mo", [128, 48, 8], kind="ExternalOutput")
    banks = Banks(p)
    ct = load_small(p, "ct", [128, KC, 8], ct_d)
    bm = load_small(p, "bm", [128, 48], bm_d)
    cb16 = p.sb("cb16", [128, KC, 8], F32)
    mo = p.sb("mo", [128, 48, 8], F32)
    p.op("act", lambda e: e.activation(out=cb16[:], in_=ct[:], func=AF.Silu), reads=["ct"],
         writes=[("cb16", c) for c in range(KC)])

    def evac(m, ps, pk, b0, b1):
        p.op("dve", lambda e: e.tensor_scalar(out=mo[:, m, :], in0=ps[:, 0:8], scalar1=bm[:, m:m + 1], scalar2=None, op0=ALU.add),
             reads=[pk, "bm"], writes=["mo"])
    linear_fm(p, banks, "mod", cb16, "cb16", KC, wm_d.rearrange("(c p) f -> p c f", p=128), 6144, [(0, 8)], evac,
              q="sync", wdt=F32)
    p.dma(mo_d, mo[:], reads=["mo"])
    return p.build()


WP = 1184
P_L0, P_L1, P_C0, P_C1 = 0, 1040, 1040, 1184


def build_pool():
    p = Prog()
    h_d = p.dram("h", [D, WP])
    modv_d = p.dram("modv", [128, 6 * KC, 2])
    g1_d = p.dram("g1", [128, KC])
    mask_d = p.dram("mask", [1, WP])
    icnt_d = p.dram("icnt", [4, WP])
    pw_d = p.dram("pw", [4, 512, 512])
    pb_d = p.dram("pb", [128, KC])
    psc_d = p.dram("psc", [128, KC])
    y_d = p.dram("y", [D, 1152], kind="ExternalOutput")
    cm = Common(p)
    banks = Banks(p)
    h = p.sb("h", [128, KC, WP], F32)
    pbf = p.sb("pbf", [128, KC, WP], BF16)
    tsb = p.sb("tsb", [128, 2, WP], F32)
    t2 = p.sb("t2", [128, 2, WP], F32)
    sq = p.sb("sq", [128, 2, 512], BF16)
    rstd = p.sb("rstd", [128, WP], F32)
    yo = p.sb("yo", [128, 2, WP], F32)
    modv = load_small(p, "modv", [128, 6 * KC, 2], modv_d)
    gvec = load_small(p, "gvec", [128, KC], g1_d)
    pb = load_small(p, "pb", [128, KC], pb_d)
    psc = load_small(p, "psc", [128, KC], psc_d)
    mask = load_small(p, "mask", [128, WP], mask_d[0].partition_broadcast(128))
    icnt = p.sb("icnt", [128, 4, WP], F32)
    for g in range(4):
        p.dma(icnt[:, g, :], icnt_d[g].partition_broadcast(128), writes=["icnt"])
    hv = h_d.rearrange("(c p) w -> p c w", p=128)
    for c4 in range(0, KC, 4):
        p.dma(h[:, c4:c4 + 4, :], hv[:, c4:c4 + 4, :], writes=[("h", c) for c in range(c4, c4 + 4)])
    A_l = make_AB(p, "pl", modv, gvec, 0, 1, 0)
    A_c = make_AB(p, "pc", modv, gvec, 0, 1, 1)
    segs = [(P_L0, P_L1, A_l, lambda c: modv[:, c, 0:1]), (P_C0, P_C1, A_c, lambda c: modv[:, c, 1:2])]
    rms_rstd_chunked(p, cm, banks, h, 0, WP, rstd, sq)
    for c in range(KC):
        s = c % 2
        p.op("dve", lambda e: e.tensor_tensor(out=tsb[:, s, :], in0=h[:, c, :], in1=rstd[:], op=ALU.mult),
             reads=[("h", c), "rstd"], writes=[("tmp", s)])
        for (c0, c1, A, Bfn) in segs:
            p.op("act", lambda e: e.activation(out=h[:, c, c0:c1], in_=tsb[:, s, c0:c1], func=AF.Identity,
                                               scale=A[:, c:c + 1], bias=Bfn(c)),
                 reads=[("tmp", s), "modv", "pl_A", "pc_A"], writes=[("h", c)])
        for (m0, m1) in ((0, 8), (1032, 1048), (1176, 1184)):
            p.op("dve", lambda e: e.tensor_tensor(out=h[:, c, m0:m1], in0=h[:, c, m0:m1], in1=mask[:, m0:m1], op=ALU.mult),
                 reads=[("h", c), "mask"], writes=[("h", c)])
        g = c // 4
        win = (2, 4, 8, 16)[g]
        right = win - 1 - win // 2
        src, skey = h[:, c, :], ("h", c)
        sh = 1
        bufs = [tsb[:, s, :], t2[:, s, :]]
        bkeys = [("tmp", s), ("t2", s)]
        bi = 0
        while sh < win:
            dst, dkey = bufs[bi], bkeys[bi]
            p.op("dve", lambda e: e.tensor_tensor(out=dst[:, sh:WP], in0=src[:, sh:WP], in1=src[:, 0:WP - sh], op=ALU.add),
                 reads=[skey], writes=[dkey])
            p.op("dve", lambda e: e.tensor_copy(out=dst[:, 0:sh], in_=src[:, 0:sh]), reads=[skey], writes=[dkey])
            src, skey = dst, dkey
            sh *= 2
            bi ^= 1
        dst, dkey = bufs[bi], bkeys[bi]
        n = WP - 16
        p.op("dve", lambda e: e.tensor_tensor(out=dst[:, 8:8 + n], in0=src[:, 8 + right:8 + right + n],
                                              in1=icnt[:, g, 8:8 + n], op=ALU.mult),
             reads=[skey, "icnt"], writes=[dkey])
        p.op("dve", lambda e: e.tensor_tensor(out=pbf[:, c, 8:8 + n], in0=dst[:, 8:8 + n], in1=h[:, c, 8:8 + n], op=ALU.subtract),
             reads=[dkey, ("h", c)], writes=[("pbf", c)])
    yv = y_d.rearrange("(c p) w -> p c w", p=128)
    vblocks = [(8, 520), (520, 1032), (1048, 1176)]
    for g in range(4):
        wsl = p.sb("pw%d" % g, [128, 4, 512], BF16)
        p.dma(wsl[:], pw_d[g].rearrange("(c p) f -> p c f", p=128), writes=[("pw", g)], q="pool")
        for mo_ in range(4):
            m = g * 4 + mo_
            s = m % 2
            for (b0, b1) in vblocks:
                ps, pk = banks.next()
                for ci in range(4):
                    p.op("pe", lambda e: e.matmul(ps[:, 0:b1 - b0], lhsT=wsl[:, ci, mo_ * 128:(mo_ + 1) * 128],
                                                  rhs=pbf[:, g * 4 + ci, b0:b1], start=(ci == 0), stop=(ci == 3)),
                         reads=[("pw", g), ("pbf", g * 4 + ci)], writes=[pk])
                p.op("dve", lambda e: e.tensor_scalar(out=yo[:, s, b0:b1], in0=ps[:, 0:b1 - b0], scalar1=pb[:, m:m + 1],
                                                      scalar2=psc[:, m:m + 1], op0=ALU.add, op1=ALU.mult),
                     reads=[pk, "pb", "psc"], writes=[("yo", s)])
            p.dma(yv[:, m, 0:1024], yo[:, s, 8:1032], reads=[("yo", s)])
            p.dma(yv[:, m, 1024:1152], yo[:, s, 1048:1176], reads=[("yo", s)])
    return p.build()


def norm_mod_stream(p, cm, banks, hview, W, segs, out, okey, rstd, sq, hbuf, tmp, akeys, mask=None, out_chunks=None):
    blocks = col_blocks(0, W)
    pss = [banks.next() for _ in blocks]
    for c in range(KC):
        s = c % 2
        p.dma(hbuf[:, s, 0:W], hview[:, c, :], writes=[("hbuf", s)])
        for bi, (b0, b1) in enumerate(blocks):
            ps, pk = pss[bi]
            p.op("act", lambda e: e.activation(out=sq[:, s, 0:b1 - b0], in_=hbuf[:, s, b0:b1], func=AF.Square),
                 reads=[("hbuf", s)], writes=[("sq", s)])
            p.op("pe", lambda e: e.matmul(ps[:, 0:b1 - b0], lhsT=cm.ones[:], rhs=sq[:, s, 0:b1 - b0],
                                          start=(c == 0), stop=(c == KC - 1)),
                 reads=[("sq", s), "c_ones"], writes=[pk])
    for bi, (b0, b1) in enumerate(blocks):
        ps, pk = pss[bi]
        p.op("act", lambda e: e.activation(out=rstd[:, b0:b1], in_=ps[:, 0:b1 - b0], func=AF.Sqrt,
                                           bias=cm.eps[:, 0:1], scale=1.0 / D),
             reads=[pk, "c_eps"], writes=["rstd"])
        p.op("dve", lambda e: e.reciprocal(out=rstd[:, b0:b1], in_=rstd[:, b0:b1]), reads=["rstd"], writes=["rstd"])
    for c in range(KC):
        s = c % 2
        p.dma(hbuf[:, s, 0:W], hview[:, c, :], writes=[("hbuf", s)])
        p.op("dve", lambda e: e.tensor_tensor(out=tmp[:, s, 0:W], in0=hbuf[:, s, 0:W], in1=rstd[:, 0:W], op=ALU.mult),
             reads=[("hbuf", s), "rstd"], writes=[("tmp", s)])
        for (c0, c1, A, Bfn) in segs:
            p.op("act", lambda e: e.activation(out=out[:, c, c0:c1], in_=tmp[:, s, c0:c1], func=AF.Identity,
                                               scale=A[:, c:c + 1], bias=Bfn(c)),
                 reads=[("tmp", s), "modv"] + akeys, writes=[(okey, c)])
        if mask is not None:
            for (m0, m1) in HALO1:
                p.op("dve", lambda e: e.tensor_tensor(out=out[:, c, m0:m1], in0=out[:, c, m0:m1], in1=mask[:, m0:m1], op=ALU.mult),
                     reads=[(okey, c), "mask"], writes=[(okey, c)])


def linear_fm2(p, banks, slabs, skey, x, xkey, kcin, wview, dout, blocks, evac, q="pool"):
    sw = slabs[0].shape[2]
    nbuf = len(slabs)
    ns = (dout + sw - 1) // sw
    for si in range(ns):
        sl = slabs[si % nbuf]
        sk = (skey, si % nbuf)
        f0 = si * sw
        f1 = min(dout, f0 + sw)
        half = kcin // 2 if kcin >= 8 else kcin
        for k0 in range(0, kcin, half):
            p.dma(sl[:, k0:k0 + half, 0:f1 - f0], wview[:, k0:k0 + half, f0:f1], writes=[sk], q=q)
        for mi in range((f1 - f0) // 128):
            m = (f0 // 128) + mi
            for (b0, b1) in blocks:
                ps, pk = banks.next()
                for c in range(kcin):
                    p.op("pe", lambda e: e.matmul(ps[:, 0:b1 - b0], lhsT=sl[:, c, mi * 128:(mi + 1) * 128],
                                                  rhs=x[:, c, b0:b1], start=(c == 0), stop=(c == kcin - 1)),
                         reads=[sk, (xkey, c)], writes=[pk])
                evac(m, ps, pk, b0, b1)


WG = 1152
NPC = 9


def build_gmlp():
    p = Prog()
    h_d = p.dram("h", [D, WG])
    modv_d = p.dram("modv", [128, 6 * KC, 2])
    g1_d = p.dram("g1", [128, KC])
    win_d = p.dram("win", [D, 2 * D])
    bzu_d = p.dram("bzu", [128, KC])
    bzv_d = p.dram("bzv", [1, D])
    ng_d = p.dram("ng", [1, D])
    wst_d = p.dram("wst", [128, 16, 128])
    bs_d = p.dram("bs", [1, 16 * 128])
    wo_d = p.dram("wo", [D, D])
    y_d = p.dram("y", [D, WG], kind="ExternalOutput")
    cm = Common(p)
    banks = Banks(p)
    aT = p.sb("aT", [128, KC, WG], BF16)
    zu = p.sb("zu", [128, KC, WG], BF16)
    zv = p.sb("zv", [128, NPC, D], BF16)
    hbuf = p.sb("hbuf", [128, 2, WG], F32)
    tmp = p.sb("tmp", [128, 2, WG], F32)
    sq = p.sb("sq", [128, 2, 512], BF16)
    rstd = p.sb("rstd", [128, WG], F32)
    slabs = [p.sb("slab%d" % i, [128, KC, 256], BF16) for i in range(2)]
    modv = load_small(p, "modv", [128, 6 * KC, 2], modv_d)
    gvec = load_small(p, "gvec", [128, KC], g1_d)
    bzu = load_small(p, "bzu", [128, KC], bzu_d)
    bzv = load_small(p, "bzv", [128, D], bzv_d[0].partition_broadcast(128))
    ng = load_small(p, "ng", [128, D], ng_d[0].partition_broadcast(128))
    bs = load_small(p, "bs", [128, 16 * 128], bs_d[0].partition_broadcast(128))
    wst = p.sb("wst", [128, 16, 128], BF16)
    p.dma(wst[:], wst_d, writes=["wst"], q="pool")
    ssq = p.sb("ssq", [128, NPC, 8], F32)
    rtok = p.sb("rtok", [128, NPC], F32)
    A_l = make_AB(p, "gl", modv, gvec, 0, 1, 0)
    A_c = make_AB(p, "gc", modv, gvec, 0, 1, 1)
    segs = [(0, 1024, A_l, lambda c: modv[:, c, 0:1]), (1024, WG, A_c, lambda c: modv[:, c, 1:2])]
    norm_mod_stream(p, cm, banks, h_d.rearrange("(c p) w -> p c w", p=128), WG, segs, aT, "aT", rstd, sq, hbuf, tmp,
                    ["gl_A", "gc_A"])
    blocks = col_blocks(0, WG)
    winv = win_d.rearrange("(c p) f -> p c f", p=128)
    import os
    stop = int(os.environ.get("GSTOP", "99"))
    if stop <= 0:
        return p.build()

    def evac_zu(m, ps, pk, b0, b1):
        p.op("act", lambda e: e.activation(out=zu[:, m, b0:b1], in_=ps[:, 0:b1 - b0], func=AF.Gelu_apprx_tanh,
                                           bias=bzu[:, m:m + 1], scale=1.0),
             reads=[pk, "bzu"], writes=[("zu", m)])
    linear_fm2(p, banks, slabs, "slab", aT, "aT", KC, winv[:, :, 0:D], D, blocks, evac_zu)

    if stop <= 1:
        return p.build()
    sw = 256
    for si in range(D // sw):
        sl = slabs[si % 2]
        sk = ("slab", si % 2)
        f0 = D + si * sw
        for k0 in (0, 8):
            p.dma(sl[:, k0:k0 + 8, :], winv[:, k0:k0 + 8, f0:f0 + sw], writes=[sk], q="pool")
        for n in range(NPC):
            ps, pk = banks.next()
            for c in range(KC):
                p.op("pe", lambda e: e.matmul(ps[:, 0:sw], lhsT=aT[:, c, n * 128:(n + 1) * 128], rhs=sl[:, c, :],
                                              start=(c == 0), stop=(c == KC - 1)),
                     reads=[sk, ("aT", c)], writes=[pk])
            s = (si * NPC + n) % 2
            p.op("dve", lambda e: e.tensor_tensor(out=tmp[:, s, 0:sw], in0=ps[:, 0:sw], in1=bzv[:, si * sw:(si + 1) * sw], op=ALU.add),
                 reads=[pk, "bzv"], writes=[("tmp", s)])
            p.op("act", lambda e: e.activation(out=tmp[:, s, 0:sw], in_=tmp[:, s, 0:sw], func=AF.Gelu_apprx_tanh),
                 reads=[("tmp", s)], writes=[("tmp", s)])
            p.op("act", lambda e: e.activation(out=tmp[:, s, 512:512 + sw], in_=tmp[:, s, 0:sw], func=AF.Square,
                                               accum_out=ssq[:, n, si:si + 1]),
                 reads=[("tmp", s)], writes=[("tmp", s), "ssq"])
            p.op("pool", lambda e: e.tensor_copy(out=zv[:, n, si * sw:(si + 1) * sw], in_=tmp[:, s, 0:sw]),
                 reads=[("tmp", s)], writes=[("zv", n)])
    if stop <= 2:
        return p.build()
    p.op("dve", lambda e: e.tensor_reduce(out=rtok[:], in_=ssq[:], axis=AX.X, op=ALU.add), reads=["ssq"], writes=["rtok"])
    p.op("act", lambda e: e.activation(out=rtok[:], in_=rtok[:], func=AF.Sqrt, bias=cm.eps[:, 0:1], scale=1.0 / D),
         reads=["rtok", "c_eps"], writes=["rtok"])
    p.op("dve", lambda e: e.reciprocal(out=rtok[:], in_=rtok[:]), reads=["rtok"], writes=["rtok"])
    for n in range(NPC):
        p.op("dve", lambda e: e.scalar_tensor_tensor(out=zv[:, n, :], in0=zv[:, n, :], scalar=rtok[:, n:n + 1], in1=ng[:],
                                                     op0=ALU.mult, op1=ALU.mult),
             reads=[("zv", n), "rtok", "ng"], writes=[("zv", n)])
    if stop <= 3:
        return p.build()
    bsv = bs[:].rearrange("p (g q) -> p g q", q=128)
    for n in range(NPC):
        for g4 in range(4):
            ps, pk = banks.next()
            for gi in range(4):
                g = g4 * 4 + gi
                p.op("pe", lambda e: e.matmul(ps[:, gi * 128:(gi + 1) * 128], lhsT=zv[:, n, g * 128:(g + 1) * 128],
                                              rhs=wst[:, g, :], start=True, stop=True),
                     reads=[("zv", n), "wst"], writes=[pk])
            s = (n * 4 + g4) % 2
            tv = tmp[:, s, 0:512].rearrange("p (g q) -> p g q", q=128)
            p.op("dve", lambda e: e.tensor_tensor(out=tv, in0=ps[:, 0:512].rearrange("p (g q) -> p g q", q=128),
                                                  in1=bsv[:, g4 * 4:(g4 + 1) * 4, :], op=ALU.add),
                 reads=[pk, "bs"], writes=[("tmp", s)])
            p.op("dve", lambda e: e.tensor_tensor(out=aT[:, g4 * 4:(g4 + 1) * 4, n * 128:(n + 1) * 128], in0=tv,
                                                  in1=zu[:, g4 * 4:(g4 + 1) * 4, n * 128:(n + 1) * 128], op=ALU.mult),
                 reads=[("tmp", s)] + [("zu", g4 * 4 + i) for i in range(4)],
                 writes=[("aT", g4 * 4 + i) for i in range(4)])
    if stop <= 4:
        return p.build()
    yv = y_d.rearrange("(c p) w -> p c w", p=128)
    cnt = [0]

    def evac_y(m, ps, pk, b0, b1):
        s = cnt[0] % 2
        cnt[0] += 1
        p.op("act", lambda e: e.activation(out=hbuf[:, s, b0:b1], in_=ps[:, 0:b1 - b0], func=AF.Copy),
             reads=[pk], writes=[("hbuf", s)])
        if os.environ.get("GNODMA") != "1":
            p.dma(yv[:, m, b0:b1], hbuf[:, s, b0:b1], reads=[("hbuf", s)], q="act")
    linear_fm2(p, banks, slabs, "slab", aT, "aT", KC, wo_d.rearrange("(c p) f -> p c f", p=128), D, blocks, evac_y)
    return p.build()


def build_na1():
    p = Prog()
    W = WG
    h_d = p.dram("h", [D, W])
    modv_d = p.dram("modv", [128, 6 * KC, 2])
    g1_d = p.dram("g1", [128, KC])
    w_d = p.dram("wqkv", [D, 3 * D])
    cos_d = p.dram("rcos", [128, 1024])
    sin_d = p.dram("rsin", [128, 1024])
    pm_d = p.dram("pm", [128, 128])
    q_d = p.dram("qT", [D, W], kind="ExternalOutput")
    k_d = p.dram("kT", [D, W], kind="ExternalOutput")
    v_d = p.dram("v", [W, D], kind="ExternalOutput")
    cm = Common(p)
    banks = Banks(p)
    aT = p.sb("aT", [128, KC, W], BF16)
    hbuf = p.sb("hbuf", [128, 2, W], F32)
    tmp = p.sb("tmp", [128, 2, W], F32)
    st = p.sb("st", [128, 2, W], F32)
    qb = p.sb("qb", [128, 2, 512], BF16)
    sq = p.sb("sq", [128, 2, 512], BF16)
    rstd = p.sb("rstd", [128, W], F32)
    slabs = [p.sb("slab%d" % i, [128, KC, 256], BF16) for i in range(2)]
    modv = load_small(p, "modv", [128, 6 * KC, 2], modv_d)
    gvec = load_small(p, "gvec", [128, KC], g1_d)
    rcos = load_small(p, "rcos", [128, 1024], cos_d)
    rsin = load_small(p, "rsin", [128, 1024], sin_d)
    pm = p.sb("pm", [128, 128], BF16)
    if os.environ.get("NOPM") != "1":
        p.dma(pm[:], pm_d, writes=["pm"], q="pool", max_dma_last_dim=int(os.environ.get("MDL", "512")))
    A_l = make_AB(p, "nl", modv, gvec, 0, 1, 0)
    A_c = make_AB(p, "ncx", modv, gvec, 0, 1, 1)
    segs = [(0, 1024, A_l, lambda c: modv[:, c, 0:1]), (1024, W, A_c, lambda c: modv[:, c, 1:2])]
    norm_mod_stream(p, cm, banks, h_d.rearrange("(c p) w -> p c w", p=128), W, segs, aT, "aT", rstd, sq, hbuf, tmp,
                    ["nl_A", "ncx_A"])
    wv = w_d.rearrange("(c p) f -> p c f", p=128)
    qv = q_d.rearrange("(c p) w -> p c w", p=128)
    kv = k_d.rearrange("(c p) w -> p c w", p=128)
    blocks = col_blocks(0, W)
    cnt = [0]
    stop = int(os.environ.get("GSTOP", "99"))
    if stop <= 0:
        return p.build()

    def evac_qk(m, ps, pk, b0, b1):
        isq = m < KC
        sc = 0.125 if isq else 1.0
        mm = m if isq else m - KC
        s = mm % 2
        n = b1 - b0
        ev = int(os.environ.get("EV", "9"))
        if b0 >= 1024 or ev == 0:
            p.op("act", lambda e: e.activation(out=st[:, s, b0:b1], in_=ps[:, 0:n], func=AF.Copy, scale=sc),
                 reads=[pk], writes=[("st", s, 2)])
        else:
            i = cnt[0] % 2
            cnt[0] += 1
            p.op("act", lambda e: e.activation(out=qb[:, i, 0:n], in_=ps[:, 0:n], func=AF.Copy, scale=sc),
                 reads=[pk], writes=[("qb", i)])
            ps2, pk2 = banks.next()
            p.op("pe", lambda e: e.matmul(ps2[:, 0:n], lhsT=pm[:], rhs=qb[:, i, 0:n], start=True, stop=True),
                 reads=["pm", ("qb", i)], writes=[pk2])
            if ev == 1:
                p.op("act", lambda e: e.activation(out=st[:, s, b0:b1], in_=ps2[:, 0:n], func=AF.Copy, scale=sc),
                     reads=[pk, pk2], writes=[("st", s, b0 // 512)])
                if b1 == W:
                    pass
                return
            p.op("dve", lambda e: e.scalar_tensor_tensor(out=st[:, s, b0:b1], in0=ps[:, 0:n], scalar=sc, in1=rcos[:, b0:b1],
                                                         op0=ALU.mult, op1=ALU.mult),
                 reads=[pk, "rcos"], writes=[("st", s, b0 // 512)])
            p.op("dve", lambda e: e.tensor_tensor(out=tmp[:, i, 0:n], in0=ps2[:, 0:n], in1=rsin[:, b0:b1], op=ALU.mult),
                 reads=[pk2, "rsin"], writes=[("tmp", i)])
            p.op("dve", lambda e: e.tensor_tensor(out=st[:, s, b0:b1], in0=st[:, s, b0:b1], in1=tmp[:, i, 0:n], op=ALU.add),
                 reads=[("tmp", i), ("st", s, b0 // 512)], writes=[("st", s, b0 // 512)])
        if b1 == W:
            dst = qv if isq else kv
            p.dma(dst[:, mm, :], st[:, s, :], reads=[("st", s, 0), ("st", s, 1), ("st", s, 2)], q="act")
    linear_fm2(p, banks, slabs, "slab", aT, "aT", KC, wv[:, :, 0:2 * D], 2 * D, blocks, evac_qk)
    if stop <= 1:
        return p.build()
    sw = 256
    for si in range(D // sw):
        sl = slabs[si % 2]
        sk = ("slab", si % 2)
        f0 = 2 * D + si * sw
        for k0 in (0, 8):
            p.dma(sl[:, k0:k0 + 8, :], wv[:, k0:k0 + 8, f0:f0 + sw], writes=[sk], q="pool")
        for n in range(NPC):
            ps, pk = banks.next()
            for c in range(KC):
                p.op("pe", lambda e: e.matmul(ps[:, 0:sw], lhsT=aT[:, c, n * 128:(n + 1) * 128], rhs=sl[:, c, :],
                                              start=(c == 0), stop=(c == KC - 1)),
                     reads=[sk, ("aT", c)], writes=[pk])
            s = (si * NPC + n) % 2
            p.op("act", lambda e: e.activation(out=tmp[:, s, 0:sw], in_=ps[:, 0:sw], func=AF.Copy),
                 reads=[pk], writes=[("tmp", s)])
            p.dma(v_d[n * 128:(n + 1) * 128, si * sw:(si + 1) * sw], tmp[:, s, 0:sw], reads=[("tmp", s)], q="act")
    return p.build()


NTOK = T + L


def na_rs(r):
    return min(max(r - 4, 0), 24)


def build_na2():
    p = Prog()
    q_d = p.dram("qT", [1024, NTOK])
    k_d = p.dram("kT", [1024, NTOK])
    v_d = p.dram("v", [NTOK, 1024])
    bt_d = p.dram("bt", [16, 128, 15 * 64])
    wo_d = p.dram("wo", [1024, D])
    y_d = p.dram("y", [D, NTOK], kind="ExternalOutput")
    sbanks = Banks(p, 4, "ps_s")
    abanks = Banks(p, 4, "ps_a")
    ones = p.sb("ones", [128, 64], BF16)
    p.op("dve", lambda e: e.memset(ones[:], 1.0), writes=["ones"])
    qT = p.sb("qT", [128, 8, NTOK], BF16)
    kT = p.sb("kT", [128, 8, NTOK], BF16)
    v = p.sb("v", [128, 18, 1024], BF16)
    at = p.sb("at", [128, 8, NTOK], BF16)
    bts = [p.sb("bt%d" % i, [128, 15 * 64], F32) for i in range(2)]
    pts = [p.sb("pt%d" % i, [128, 512], BF16) for i in range(4)]
    sbs = [p.sb("sbb%d" % i, [128, 512], F32) for i in range(2)]
    rec = p.sb("rec", [128, 2, 512], F32)
    st = p.sb("st", [128, 2, 512], F32)
    slabs = [p.sb("slab%d" % i, [128, 8, 512], BF16) for i in range(2)]
    qv = q_d.rearrange("(c p) w -> p c w", p=128)
    kvv = k_d.rearrange("(c p) w -> p c w", p=128)
    vv = v_d.rearrange("(n p) f -> p n f", p=128)
    for c in range(8):
        p.dma(qT[:, c, :], qv[:, c, :], writes=[("qT", c)], q="pool", max_dma_last_dim=4096)
        p.dma(kT[:, c, :], kvv[:, c, :], writes=[("kT", c)], q="pool", max_dma_last_dim=4096)
    for n in range(18):
        p.dma(v[:, n, :], vv[:, n, :], writes=[("v", n)], q="pool")
    npt = [0]
    nsb = [0]
    for h in range(16):
        c = h // 2
        base = (h % 2) * 64
        bt = bts[h % 2]
        bk = ("bt", h % 2)
        p.dma(bt[:], bt_d[h], writes=[bk])
        btv = bt[:].rearrange("p (i q) -> p i q", q=64)
        groups = [(g * 512, 512, g) for g in range(4)] + [(T, L, None)]
        for (q0, nq, g) in groups:
            O, ok = abanks.next()
            Dn, dk = abanks.next()
            items = []
            for j in range(2):
                items.append(("ctx", j))
            if g is not None:
                for kap in range(32):
                    rows = [r for r in range(8 * g, 8 * g + 8) if na_rs(r) <= kap < na_rs(r) + 8]
                    if rows:
                        items.append(("loc", kap, rows[0], rows[-1]))
            pend = None

            def stage2(st2):
                (kind, first, last, pt, ptk, args) = st2
                if kind == "ctx":
                    (j,) = args
                    p.op("pe", lambda e: e.matmul(O[base:base + 64, 0:nq], lhsT=v[:, 16 + j, h * 64:(h + 1) * 64],
                                                  rhs=pt[:, 0:nq], start=first, stop=last),
                         reads=[("v", 16 + j), ptk], writes=[ok], pe_sync=True)
                    p.op("pe", lambda e: e.matmul(Dn[base:base + 64, 0:nq], lhsT=ones[:, :], rhs=pt[:, 0:nq],
                                                  start=first, stop=last),
                         reads=["ones", ptk], writes=[dk], pe_sync=True)
                else:
                    (kap, kb, c0, nn) = args
                    p.op("pe", lambda e: e.matmul(O[base:base + 64, c0:c0 + nn], lhsT=v[kb:kb + 64, kap // 2, h * 64:(h + 1) * 64],
                                                  rhs=pt[kb:kb + 64, 0:nn], start=False, stop=last),
                         reads=[("v", kap // 2), ptk], writes=[ok], pe_sync=True)
                    p.op("pe", lambda e: e.matmul(Dn[base:base + 64, c0:c0 + nn], lhsT=ones[kb:kb + 64, :],
                                                  rhs=pt[kb:kb + 64, 0:nn], start=False, stop=last),
                         reads=["ones", ptk], writes=[dk], pe_sync=True)

            for ii, it in enumerate(items):
                first = ii == 0
                last = ii == len(items) - 1
                S, sk_ = sbanks.next()
                pt = pts[npt[0] % 4]
                ptk = ("pt", npt[0] % 4)
                npt[0] += 1
                if it[0] == "ctx":
                    j = it[1]
                    kc0 = T + j * 128
                    p.op("pe", lambda e: e.matmul(S[:, 0:nq], lhsT=kT[base:base + 64, c, kc0:kc0 + 128],
                                                  rhs=qT[base:base + 64, c, q0:q0 + nq], start=True, stop=True),
                         reads=[("kT", c), ("qT", c)], writes=[sk_])
                    p.op("act", lambda e: e.activation(out=pt[:, 0:nq], in_=S[:, 0:nq], func=AF.Exp),
                         reads=[sk_], writes=[ptk])
                    cur = ("ctx", first, last, pt, ptk, (j,))
                else:
                    _, kap, ra, rb = it
                    kb = (kap % 2) * 64
                    nr = rb - ra + 1
                    nn = nr * 64
                    c0 = ra * 64 - q0
                    idx0 = ra - kap + 7
                    sb_ = sbs[nsb[0] % 2]
                    sbk = ("sbb", nsb[0] % 2)
                    nsb[0] += 1
                    p.op("pe", lambda e: e.matmul(S[kb:kb + 64, 0:nn], lhsT=kT[base:base + 64, c, kap * 64:(kap + 1) * 64],
                                                  rhs=qT[base:base + 64, c, ra * 64:(rb + 1) * 64], start=True, stop=True),
                         reads=[("kT", c), ("qT", c)], writes=[sk_])
                    p.op("dve", lambda e: e.tensor_tensor(out=sb_[kb:kb + 64, 0:nn].rearrange("p (i q) -> p i q", q=64),
                                                          in0=S[kb:kb + 64, 0:nn].rearrange("p (i q) -> p i q", q=64),
                                                          in1=btv[kb:kb + 64, idx0:idx0 + nr, :], op=ALU.add),
                         reads=[sk_, bk], writes=[sbk])
                    p.op("act", lambda e: e.activation(out=pt[kb:kb + 64, 0:nn], in_=sb_[kb:kb + 64, 0:nn], func=AF.Exp),
                         reads=[sbk], writes=[ptk])
                    cur = ("loc", first, last, pt, ptk, (kap, kb, c0, nn))
                if pend is not None:
                    stage2(pend)
                pend = cur
            stage2(pend)
            ri = (h * 5 + (g if g is not None else 4)) % 2
            p.op("dve", lambda e: e.reciprocal(out=rec[base:base + 64, ri, 0:nq], in_=Dn[base:base + 64, 0:nq]),
                 reads=[dk], writes=[("rec", ri)])
            p.op("dve", lambda e: e.tensor_tensor(out=at[base:base + 64, c, q0:q0 + nq], in0=O[base:base + 64, 0:nq],
                                                  in1=rec[base:base + 64, ri, 0:nq], op=ALU.mult),
                 reads=[ok, ("rec", ri)], writes=[("at", c)])
    yv = y_d.rearrange("(c p) w -> p c w", p=128)
    cnt = [0]

    def evac_y(m, ps, pk, b0, b1):
        s = cnt[0] % 2
        cnt[0] += 1
        p.op("act", lambda e: e.activation(out=st[:, s, 0:b1 - b0], in_=ps[:, 0:b1 - b0], func=AF.Copy),
             reads=[pk], writes=[("st", s)])
        p.dma(yv[:, m, b0:b1], st[:, s, 0:b1 - b0], reads=[("st", s)], q="act")
    linear_fm2(p, sbanks, slabs, "slab", at, "at", 8, wo_d.rearrange("(c p) f -> p c f", p=128), D, col_blocks(0, NTOK), evac_y)
    return p.build()


def _chunked(vv):
    vv = np.asarray(vv, np.float32)
    return np.ascontiguousarray(vv.reshape(-1, 128).T)


def rope_tables(t0, n):
    half = 32
    inv_freq = (10000.0 ** (-np.arange(0, half, 2, dtype=np.float32) / half)).astype(np.float32)
    pos = np.arange(t0, t0 + n)
    row = (pos // 64).astype(np.float32)
    col = (pos % 64).astype(np.float32)
    cos = np.zeros((64, n), np.float32)
    sin = np.zeros((64, n), np.float32)
    for d in range(64):
        pp = row if d < 32 else col
        ang = pp * inv_freq[d % 16]
        cos[d] = np.cos(ang)
        sgn = -1.0 if (d % 32) < 16 else 1.0
        sin[d] = sgn * np.sin(ang)
    return np.concatenate([cos, cos], 0), np.concatenate([sin, sin], 0)


def rope_perm():
    pm = np.zeros((128, 128), np.float32)
    for m in range(128):
        d = m % 32
        partner = m + 16 if d < 16 else m - 16
        pm[partner, m] = 1.0
    return pm


def na_bias_tables(rpb, heads):
    cq = np.arange(64)
    ck = np.arange(64)
    cs = np.clip(cq - 8, 0, 48)
    ok = (ck[:, None] >= cs[None, :]) & (ck[:, None] < cs[None, :] + 16)
    dc = np.clip(ck[:, None] - cq[None, :], -15, 15) + 15
    out = np.empty((len(heads), 128, 15, 64), np.float32)
    for i, h in enumerate(heads):
        for idx in range(15):
            tab = rpb[h, 14 - idx][dc]
            tab = np.where(ok, tab, np.float32(-30000.0))
            out[i, 0:64, idx] = tab
            out[i, 64:128, idx] = tab
    return out.reshape(len(heads), 128, 15 * 64)


RW_LORA = 96
RW_GATE = 256
C64 = 64


def shift_mix(p, a, akey_fn, xo, xkey, coef, n_idx, j0, n):
    for c in range(KC):
        p.op("dve", lambda e: e.tensor_scalar(out=xo[:, c, 0:n], in0=a[:, c, j0:j0 + n], scalar1=coef[:, 0, n_idx, c:c + 1],
                                              scalar2=None, op0=ALU.mult),
             reads=[akey_fn(c), "coef"], writes=[(xkey, c)])
        p.op("dve", lambda e: e.scalar_tensor_tensor(out=xo[:, c, 0:n], in0=a[:, c, j0 - 1:j0 - 1 + n], scalar=coef[:, 1, n_idx, c:c + 1],
                                                     in1=xo[:, c, 0:n], op0=ALU.mult, op1=ALU.add),
             reads=[akey_fn(c), "coef", (xkey, c)], writes=[(xkey, c)])
        p.op("dve", lambda e: e.scalar_tensor_tensor(out=xo[:, c, 0:n], in0=a[:, c, j0 + 1:j0 + 1 + n], scalar=coef[:, 2, n_idx, c:c + 1],
                                                     in1=xo[:, c, 0:n], op0=ALU.mult, op1=ALU.add),
             reads=[akey_fn(c), "coef", (xkey, c)], writes=[(xkey, c)])


def load_coef(p, coef_d):
    coef = p.sb("coef", [128, 3, 6, KC], F32)
    p.dma(coef[:, 1:3, :, :], coef_d, writes=["coef"])
    p.op("dve", lambda e: e.tensor_scalar(out=coef[:, 0, :, :], in0=coef[:, 1, :, :], scalar1=-1.0, scalar2=1.0, op0=ALU.mult, op1=ALU.add),
         reads=["coef"], writes=["coef"])
    p.op("dve", lambda e: e.tensor_tensor(out=coef[:, 0, :, :], in0=coef[:, 0, :, :], in1=coef[:, 2, :, :], op=ALU.subtract),
         reads=["coef"], writes=["coef"])
    return coef


def build_rw1():
    p = Prog()
    W = WF
    h_d = p.dram("h", [D, W])
    modv_d = p.dram("modv", [128, 6 * KC, 2])
    g1_d = p.dram("g1", [128, KC])
    mask_d = p.dram("mask", [1, W])
    coef_d = p.dram("coef", [128, 2, 6, KC])
    wrkv_d = p.dram("wrkv", [3, D, D])
    w1_d = p.dram("w1", [2, D, RW_LORA])
    w2_d = p.dram("w2", [2, RW_LORA, D])
    a1_d = p.dram("a1", [2, D, RW_LORA])
    a2_d = p.dram("a2", [2, RW_LORA, D])
    vecs_d = p.dram("vecs", [128, 7, KC])
    bones_d = p.dram("bones", [128, 128])
    rmask_d = p.dram("rmask", [1, 512])
    outs = {}
    outs["vt"] = p.dram("vt", [D, 1152], BF16, kind="ExternalOutput")
    for d in range(2):
        for nm in ("at", "bt", "kt", "rt"):
            outs[(nm, d)] = p.dram("%s%d" % (nm, d), [D, 1152], BF16, kind="ExternalOutput")
    bv_d = [p.dram("bv%d" % d, [D, 1152], kind="ExternalOutput") for d in range(2)]
    gam_d = [p.dram("gam%d" % d, [D, 18], kind="ExternalOutput") for d in range(2)]
    cm = Common(p)
    banks = Banks(p)
    a = p.sb("a", [128, KC, W], BF16)
    hbuf = p.sb("hbuf", [128, 2, W], F32)
    tmp = p.sb("tmp", [128, 2, W], F32)
    sq = p.sb("sq", [128, 2, 512], BF16)
    rstd = p.sb("rstd", [128, W], F32)
    modv = load_small(p, "modv", [128, 6 * KC, 2], modv_d)
    gvec = load_small(p, "gvec", [128, KC], g1_d)
    mask = load_small(p, "mask", [128, W], mask_d[0].partition_broadcast(128))
    coef = load_coef(p, coef_d)
    vecs = load_small(p, "vecs", [128, 7, KC], vecs_d)
    rmask = load_small(p, "rmask", [128, 512], rmask_d[0].partition_broadcast(128))
    bones = p.sb("bones", [128, 128], BF16)
    p.dma(bones[:], bones_d, writes=["bones"], q="pool", max_dma_last_dim=512)
    w1 = [p.sb("w1_%d" % d, [128, KC, RW_LORA], BF16) for d in range(2)]
    a1 = [p.sb("a1_%d" % d, [128, KC, RW_LORA], BF16) for d in range(2)]
    w2 = [p.sb("w2_%d" % d, [RW_LORA, D], BF16) for d in range(2)]
    a2 = [p.sb("a2_%d" % d, [RW_LORA, D], BF16) for d in range(2)]
    for d in range(2):
        p.dma(w1[d][:], w1_d[d].rearrange("(c p) f -> p c f", p=128), writes=[("w1", d)], q="pool")
        p.dma(a1[d][:], a1_d[d].rearrange("(c p) f -> p c f", p=128), writes=[("a1", d)], q="pool")
        p.dma(w2[d][:], w2_d[d], writes=[("w2", d)], q="pool")
        p.dma(a2[d][:], a2_d[d], writes=[("a2", d)], q="pool")
    A_l = make_AB(p, "rl", modv, gvec, 0, 1, 0)
    A_c = make_AB(p, "rc", modv, gvec, 0, 1, 1)
    segs = [(LAT0, LAT1, A_l, lambda c: modv[:, c, 0:1]), (CTX0, CTX1, A_c, lambda c: modv[:, c, 1:2])]
    norm_mod_stream(p, cm, banks, h_d.rearrange("(c p) w -> p c w", p=128), W, segs, a, "a", rstd, sq, hbuf, tmp,
                    ["rl_A", "rc_A"], mask=mask)
    akey = lambda c: ("a", c)
    xs = {nm: p.sb("x_" + nm, [128, KC, 512], BF16) for nm in ("k", "v", "r")}
    tw = [p.sb("tw%d" % d, [RW_LORA, 512], BF16) for d in range(2)]
    ta = [p.sb("ta%d" % d, [RW_LORA, 512], BF16) for d in range(2)]
    NT_ = 14
    ft = [hbuf[:, 0, 0:512], hbuf[:, 0, 512:1024], hbuf[:, 1, 0:512], hbuf[:, 1, 512:1024],
          tmp[:, 0, 0:512], tmp[:, 0, 512:1024], tmp[:, 1, 0:512], tmp[:, 1, 512:1024]]
    ft += [p.sb("ft%d" % i, [128, 512], F32)[:] for i in range(8, NT_)]
    fk = [("ft", i) for i in range(NT_)]
    fence = p.sb("fence", [128, 1], F32)
    p.op("dve", lambda e: e.memset(fence[:], 0.0), reads=[("hbuf", 0), ("hbuf", 1), ("tmp", 0), ("tmp", 1)], writes=fk[0:8])
    ob = {nm: p.sb("ob_" + nm, [128, 2, 512], BF16) for nm in ("at", "bt", "kt", "rt", "vt")}
    sqb = p.sb("sqb", [128, 2, 512], BF16)
    gam = [p.sb("gamt%d" % d, [128, KC, 18], F32) for d in range(2)]
    slabs = {nm: [p.sb("sl_%s%d" % (nm, i), [128, KC, 128], BF16) for i in range(2)] for nm in ("r", "k", "v")}
    widx = {"r": 0, "k": 1, "v": 2}
    wv = wrkv_d.rearrange("n (c p) f -> n p c f", p=128)
    cblocks = [(1, 512, 0), (513, 512, 512), (1027, 128, 1024)]
    nslab = [0]
    for (j0, n, o0) in cblocks:
        nch = n // C64
        for (kind, n_idx, fn) in (("w", 1, AF.Tanh), ("a", 4, AF.Copy)):
            shift_mix(p, a, akey, xs["r"], "x_r", coef, n_idx, j0, n)
            for d in range(2):
                lw, lkey = (w1[d], ("w1", d)) if kind == "w" else (a1[d], ("a1", d))
                lt, ltkey = (tw[d], ("tw", d)) if kind == "w" else (ta[d], ("ta", d))
                ps, pk = banks.next()
                for c in range(KC):
                    p.op("pe", lambda e: e.matmul(ps[0:RW_LORA, 0:n], lhsT=lw[:, c, :], rhs=xs["r"][:, c, 0:n],
                                                  start=(c == 0), stop=(c == KC - 1)),
                         reads=[lkey, ("x_r", c)], writes=[pk])
                p.op("act", lambda e: e.activation(out=lt[:, 0:n], in_=ps[0:RW_LORA, 0:n], func=fn), reads=[pk], writes=[ltkey])
        shift_mix(p, a, akey, xs["k"], "x_k", coef, 2, j0, n)
        shift_mix(p, a, akey, xs["v"], "x_v", coef, 3, j0, n)
        shift_mix(p, a, akey, xs["r"], "x_r", coef, 0, j0, n)
        for m in range(KC):
            cur = {}
            for nm in ("r", "k", "v"):
                i = nslab[0] % 2
                sl = slabs[nm][i]
                sk = ("sl_" + nm, i)
                for k0 in (0, 8):
                    p.dma(sl[:, k0:k0 + 8, :], wv[widx[nm], :, k0:k0 + 8, m * 128:(m + 1) * 128], writes=[sk], q="pool")
                cur[nm] = (sl, sk)
            nslab[0] += 1
            s2 = m % 2
            T_ = lambda i: ft[i][:, 0:n]
            pss = {}
            for nm in ("k", "v", "r"):
                ps, pk = banks.next()
                sl, sk = cur[nm]
                for c in range(KC):
                    p.op("pe", lambda e: e.matmul(ps[:, 0:n], lhsT=sl[:, c, :], rhs=xs[nm][:, c, 0:n],
                                                  start=(c == 0), stop=(c == KC - 1)),
                         reads=[sk, ("x_" + nm, c)], writes=[pk])
                pss[nm] = (ps, pk)
            p.op("act", lambda e: e.activation(out=T_(0), in_=pss["k"][0][:, 0:n], func=AF.Copy), reads=[pss["k"][1]], writes=[fk[0]])
            p.op("act", lambda e: e.activation(out=T_(1), in_=pss["v"][0][:, 0:n], func=AF.Copy), reads=[pss["v"][1]], writes=[fk[1]])
            p.op("act", lambda e: e.activation(out=T_(2), in_=pss["r"][0][:, 0:n], func=AF.Copy), reads=[pss["r"][1]], writes=[fk[2]])
            p.op("pool", lambda e: e.tensor_copy(out=ob["vt"][:, s2, 0:n], in_=T_(1)), reads=[fk[1]], writes=[("ob_vt", s2)])
            ovv = outs["vt"].rearrange("(c p) w -> p c w", p=128)
            p.dma(ovv[:, m, o0:o0 + n], ob["vt"][:, s2, 0:n], reads=[("ob_vt", s2)])
            p.op("dve", lambda e: e.tensor_scalar(out=T_(5), in0=T_(0), scalar1=vecs[:, 4, m:m + 1], scalar2=None, op0=ALU.mult),
                 reads=[fk[0], "vecs"], writes=[fk[5]])
            p.op("act", lambda e: e.activation(out=sqb[:, 0, 0:n], in_=T_(5), func=AF.Square), reads=[fk[5]], writes=[("sqb", 0)])
            ps_n, pk_n = banks.next()
            p.op("pe", lambda e: e.matmul(ps_n[:, 0:n], lhsT=bones[:], rhs=sqb[:, 0, 0:n], start=True, stop=True),
                 reads=["bones", ("sqb", 0)], writes=[pk_n])
            p.op("act", lambda e: e.activation(out=T_(6), in_=ps_n[:, 0:n], func=AF.Sqrt), reads=[pk_n], writes=[fk[6]])
            p.op("dve", lambda e: e.tensor_scalar(out=T_(6), in0=T_(6), scalar1=1e-6, scalar2=None, op0=ALU.max),
                 reads=[fk[6]], writes=[fk[6]])
            p.op("dve", lambda e: e.reciprocal(out=T_(6), in_=T_(6)), reads=[fk[6]], writes=[fk[6]])
            p.op("dve", lambda e: e.tensor_tensor(out=T_(5), in0=T_(5), in1=T_(6), op=ALU.mult), reads=[fk[5], fk[6]], writes=[fk[5]])
            for d in range(2):
                ps_s, pk_s = banks.next()
                p.op("pe", lambda e: e.matmul(ps_s[:, 0:n], lhsT=w2[d][:, m * 128:(m + 1) * 128], rhs=tw[d][:, 0:n], start=True, stop=True),
                     reads=[("w2", d), ("tw", d)], writes=[pk_s])
                ps_a, pk_a = banks.next()
                p.op("pe", lambda e: e.matmul(ps_a[:, 0:n], lhsT=a2[d][:, m * 128:(m + 1) * 128], rhs=ta[d][:, 0:n], start=True, stop=True),
                     reads=[("a2", d), ("ta", d)], writes=[pk_a])
                p.op("act", lambda e: e.activation(out=T_(3), in_=ps_s[:, 0:n], func=AF.Sigmoid, bias=vecs[:, 0 + d, m:m + 1], scale=1.0),
                     reads=[pk_s, "vecs"], writes=[fk[3]])
                p.op("act", lambda e: e.activation(out=T_(4), in_=ps_a[:, 0:n], func=AF.Sigmoid, bias=vecs[:, 2 + d, m:m + 1], scale=1.0),
                     reads=[pk_a, "vecs"], writes=[fk[4]])
                p.op("dve", lambda e: e.tensor_scalar(out=T_(3), in0=T_(3), scalar1=-0.6065306597126334, scalar2=None, op0=ALU.mult),
                     reads=[fk[3]], writes=[fk[3]])
                p.op("dve", lambda e: e.tensor_tensor_scan(out=T_(7), data0=rmask[:, 0:n], data1=T_(3), initial=0.0,
                                                           op0=ALU.mult, op1=ALU.add),
                     reads=["rmask", fk[3]], writes=[fk[7]])
                if d == 0:
                    p.op("dve", lambda e: e.tensor_tensor(out=T_(8), in0=T_(7), in1=T_(3), op=ALU.subtract), reads=[fk[7], fk[3]], writes=[fk[8]])
                    p.op("pool", lambda e: e.tensor_copy(out=gam[d][:, m, o0 // C64:o0 // C64 + nch],
                                                         in_=ft[7][:, 0:n].rearrange("p (a b) -> p a b", b=C64)[:, :, C64 - 1]),
                         reads=[fk[7]], writes=[("gam", d)])
                else:
                    P3 = ft[7][:, 0:n].rearrange("p (a b) -> p a b", b=C64)
                    p.op("pool", lambda e: e.tensor_copy(out=gam[d][:, m, o0 // C64:o0 // C64 + nch], in_=P3[:, :, C64 - 1]),
                         reads=[fk[7]], writes=[("gam", d)])
                    tot = gam[d][:, m, o0 // C64:o0 // C64 + nch].unsqueeze(2).broadcast_to([128, nch, C64])
                    p.op("dve", lambda e: e.tensor_tensor(out=ft[8][:, 0:n].rearrange("p (a b) -> p a b", b=C64), in0=tot, in1=P3, op=ALU.subtract),
                         reads=[fk[7], ("gam", d)], writes=[fk[8]])
                    p.op("dve", lambda e: e.tensor_tensor(out=T_(7), in0=T_(8), in1=T_(3), op=ALU.add), reads=[fk[8], fk[3]], writes=[fk[7]])
                p.op("act", lambda e: e.activation(out=T_(8), in_=T_(8), func=AF.Exp), reads=[fk[8]], writes=[fk[8]])
                p.op("act", lambda e: e.activation(out=T_(9), in_=T_(7), func=AF.Exp, scale=-1.0), reads=[fk[7]], writes=[fk[9]])
                p.op("act", lambda e: e.activation(out=T_(7), in_=T_(7), func=AF.Exp), reads=[fk[7]], writes=[fk[7]])
                p.op("dve", lambda e: e.tensor_scalar(out=T_(10), in0=T_(4), scalar1=-1.0, scalar2=vecs[:, 5, m:m + 1], op0=ALU.add, op1=ALU.mult),
                     reads=[fk[4], "vecs"], writes=[fk[10]])
                p.op("dve", lambda e: e.scalar_tensor_tensor(out=T_(10), in0=T_(10), scalar=1.0, in1=T_(0), op0=ALU.add, op1=ALU.mult),
                     reads=[fk[10], fk[0]], writes=[fk[10]])
                p.op("dve", lambda e: e.scalar_tensor_tensor(out=ob["at"][:, d, 0:n], in0=T_(5), scalar=-1.0, in1=T_(8), op0=ALU.mult, op1=ALU.mult),
                     reads=[fk[5], fk[8]], writes=[("ob_at", d)])
                p.op("dve", lambda e: e.tensor_tensor(out=T_(11), in0=T_(5), in1=T_(4), op=ALU.mult), reads=[fk[5], fk[4]], writes=[fk[11]])
                p.op("dve", lambda e: e.tensor_tensor(out=ob["bt"][:, d, 0:n], in0=T_(11), in1=T_(9), op=ALU.mult),
                     reads=[fk[11], fk[9]], writes=[("ob_bt", d)])
                p.op("dve", lambda e: e.tensor_tensor(out=ob["kt"][:, d, 0:n], in0=T_(10), in1=T_(9), op=ALU.mult),
                     reads=[fk[10], fk[9]], writes=[("ob_kt", d)])
                p.op("dve", lambda e: e.tensor_tensor(out=ob["rt"][:, d, 0:n], in0=T_(2), in1=T_(7), op=ALU.mult),
                     reads=[fk[2], fk[7]], writes=[("ob_rt", d)])
                p.op("dve", lambda e: e.scalar_tensor_tensor(out=sqb[:, 1, 0:n], in0=T_(2), scalar=vecs[:, 6, m:m + 1], in1=T_(10), op0=ALU.mult, op1=ALU.mult),
                     reads=[fk[2], fk[10], "vecs"], writes=[("sqb", 1)])
                ps_b, pk_b = banks.next()
                p.op("pe", lambda e: e.matmul(ps_b[:, 0:n], lhsT=bones[:], rhs=sqb[:, 1, 0:n], start=True, stop=True),
                     reads=["bones", ("sqb", 1)], writes=[pk_b])
                i12 = 12 + d
                p.op("dve", lambda e: e.tensor_tensor(out=T_(i12), in0=ps_b[:, 0:n], in1=T_(1), op=ALU.mult), reads=[pk_b, fk[1]], writes=[fk[i12]])
                for nm in ("at", "bt", "kt", "rt"):
                    ov = outs[(nm, d)].rearrange("(c p) w -> p c w", p=128)
                    p.dma(ov[:, m, o0:o0 + n], ob[nm][:, d, 0:n], reads=[("ob_" + nm, d)])
                bvv = bv_d[d].rearrange("(c p) w -> p c w", p=128)
                p.dma(bvv[:, m, o0:o0 + n], T_(i12), reads=[fk[i12]])
    for d in range(2):
        p.op("act", lambda e: e.activation(out=gam[d][:], in_=gam[d][:], func=AF.Exp), reads=[("gam", d)], writes=[("gam", d)])
        p.dma(gam_d[d].rearrange("(c p) w -> p c w", p=128), gam[d][:], reads=[("gam", d)])
    return p.build()


NCH = NTOK // C64
RW2_MASK_ENG = os.environ.get("RW2_MASK_ENG", "dve")
NHG = 4


def build_rw2():
    p = Prog()
    far_d = p.dram("far", [NCH, 64, 32 * 2 * 64], BF16)
    fbk_d = p.dram("fbk", [NCH, 64, 2 * 32 * 64], BF16)
    tm_d = p.dram("tm", [NCH, 64, 3 * D], BF16)
    gam_d = p.dram("gam", [64, 32, NCH])
    mka_d = p.dram("mka", [64, 2 * 64])
    mkp_d = p.dram("mkp", [64, 2 * 64])
    mkl_d = p.dram("mkl", [64, 7 * 64])
    y_d = p.dram("y", [NCH, 64, 32 * 64], kind="ExternalOutput")
    banks = Banks(p)
    FAR = [p.sb("far%d" % i, [64, 32, 2, 64], BF16) for i in range(2)]
    FBK = [p.sb("fbk%d" % i, [64, 2, 32, 64], BF16) for i in range(2)]
    TM = [p.sb("tm%d" % i, [64, 3, D], BF16) for i in range(2)]
    gam = load_small(p, "gam", [64, 32, NCH], gam_d)
    mka = load_small(p, "mka", [64, 2, 64], mka_d.rearrange("p (a b) -> p a b", b=64))
    mkp = load_small(p, "mkp", [64, 2, 64], mkp_d.rearrange("p (a b) -> p a b", b=64))
    mkl = load_small(p, "mkl", [64, 7, 64], mkl_d.rearrange("p (a b) -> p a b", b=64))
    mklb = p.sb("mklb", [64, 7, 64], BF16)
    p.op("dve", lambda e: e.tensor_copy(out=mklb[:], in_=mkl[:]), reads=["mkl"], writes=["mklb"])
    S = p.sb("S", [64, 32, 64], F32)
    Sb = p.sb("Sb", [64, 32, 64], BF16)
    Sl = p.sb("Sl", [64, 32, 64], BF16)
    Sd = p.sb("Sd", [64, 32, 64], F32)
    yst = [p.sb("yst%d" % i, [64, 32, 64], F32) for i in range(2)]
    QA = [p.sb("QA%d" % g, [64, 8, 2, 64], BF16) for g in range(NHG)]
    QK = [p.sb("QK%d" % g, [64, 8, 2, 64], BF16) for g in range(NHG)]
    NM = [[p.sb("NM%d_%d" % (g, l), [64, 8, 64], BF16) for l in range(6)] for g in range(NHG)]
    Gf = [p.sb("Gf%d" % g, [64, 8, 64], F32) for g in range(NHG)]
    Hf = [p.sb("Hf%d" % g, [64, 8, 64], F32) for g in range(NHG)]
    Gb = [p.sb("Gb%d" % g, [64, 8, 64], BF16) for g in range(NHG)]
    Hb = [p.sb("Hb%d" % g, [64, 8, 64], BF16) for g in range(NHG)]
    Wb = [p.sb("Wb%d" % g, [64, 8, 64], BF16) for g in range(NHG)]
    Rb = [p.sb("Rb%d" % g, [64, 8, 64], BF16) for g in range(NHG)]
    Xb = [p.sb("Xb%d" % g, [64, 8, 64], BF16) for g in range(NHG)]
    p.op("dve", lambda e: e.memset(S[:], 0.0), writes=[("S", g) for g in range(NHG)])
    p.op("dve", lambda e: e.memset(Sb[:], 0.0), writes=[("Sb", g) for g in range(NHG)])
    p.op("dve", lambda e: e.memset(Sl[:], 0.0), writes=[("Sl", g) for g in range(NHG)])
    v3 = lambda ap: ap.rearrange("p (a b) -> p a b", b=64)
    bc8 = lambda ap2: ap2.unsqueeze(1).broadcast_to([64, 8, 64])
    for ci in range(NCH):
        s = ci % 2
        far, fbk, tm = FAR[s], FBK[s], TM[s]
        p.dma(far[:].rearrange("p a b c -> p (a b c)"), far_d[ci], writes=[("far", s)])
        p.dma(fbk[:].rearrange("p a b c -> p (a b c)"), fbk_d[ci], writes=[("fbk", s)])
        p.dma(tm[:].rearrange("p a b -> p (a b)"), tm_d[ci], writes=[("tm", s)])
        kfar, kfbk, ktm = ("far", s), ("fbk", s), ("tm", s)
        for g in range(NHG):
            ps, pk = banks.next()
            for hh in range(8):
                h = g * 8 + hh
                p.op("pe", lambda e: e.matmul(ps[0:64, hh * 64:(hh + 1) * 64], lhsT=far[:, h, 0, :], rhs=fbk[:, 0, h, :], start=True, stop=True),
                     reads=[kfar, kfbk], writes=[pk])
            p.op("dve", lambda e: e.tensor_tensor(out=Gf[g][:], in0=v3(ps[0:64, :]), in1=bc8(mkp[:, 0, :]), op=ALU.mult),
                 reads=[pk, "mkp"], writes=[("Gf", g)])
            p.op(RW2_MASK_ENG, lambda e: e.tensor_tensor(out=Gf[g][:], in0=Gf[g][:], in1=bc8(mkp[:, 1, :]), op=ALU.add),
                 reads=[("Gf", g), "mkp"], writes=[("Gf", g)])
            p.op("act", lambda e: e.activation(out=Gb[g][:], in_=Gf[g][:], func=AF.Copy), reads=[("Gf", g)], writes=[("Gb", g)])
            for (dst, dkey, which) in ((QA[g], ("QA", g), 0), (QK[g], ("QK", g), 1)):
                for half in range(2):
                    ps, pk = banks.next()
                    for hq in range(4):
                        h = g * 8 + half * 4 + hq
                        p.op("pe", lambda e: e.matmul(ps[0:64, hq * 128:(hq + 1) * 128], lhsT=fbk[:, which, h, :],
                                                      rhs=far[:, h, :, :].rearrange("p a b -> p (a b)"), start=True, stop=True),
                             reads=[kfar, kfbk], writes=[pk])
                    p.op("dve", lambda e: e.tensor_tensor(
                        out=dst[:, half * 4:(half + 1) * 4, :, :],
                        in0=ps[0:64, :].rearrange("p (h a b) -> p h a b", a=2, b=64),
                        in1=mka[:].unsqueeze(1).broadcast_to([64, 4, 2, 64]), op=ALU.mult),
                        reads=[pk, "mka"], writes=[dkey])
            NT0 = QA[g][:, :, 0, :]
            for l in range(6):
                p.op(RW2_MASK_ENG, lambda e: e.tensor_tensor(out=NM[g][l][:], in0=NT0, in1=bc8(mklb[:, l, :]), op=ALU.mult),
                     reads=[("QA", g), "mklb"], writes=[("NM", g, l)])
            p.op(RW2_MASK_ENG, lambda e: e.tensor_tensor(out=Hf[g][:], in0=NM[g][0][:], in1=bc8(mkl[:, 6, :]), op=ALU.add),
                 reads=[("NM", g, 0), "mkl"], writes=[("Hf", g)])
            p.op("act", lambda e: e.activation(out=Hb[g][:], in_=Hf[g][:], func=AF.Copy), reads=[("Hf", g)], writes=[("Hb", g)])
        for g in range(NHG):
            ps, pk = banks.next()
            for hh in range(8):
                h = g * 8 + hh
                o = ps[0:64, hh * 64:(hh + 1) * 64]
                p.op("pe", lambda e: e.matmul(o, lhsT=far[:, h, 0, :], rhs=Sb[:, h, :], start=True, stop=False),
                     reads=[kfar, ("Sb", g)], writes=[pk])
                p.op("pe", lambda e: e.matmul(o, lhsT=far[:, h, 0, :], rhs=Sl[:, h, :], start=False, stop=False),
                     reads=[kfar, ("Sl", g)], writes=[pk])
                p.op("pe", lambda e: e.matmul(o, lhsT=QK[g][:, hh, 0, :], rhs=tm[:, 0, h * 64:(h + 1) * 64], start=False, stop=True),
                     reads=[("QK", g), ktm], writes=[pk])
            p.op("act", lambda e: e.activation(out=Rb[g][:], in_=v3(ps[0:64, :]), func=AF.Copy), reads=[pk], writes=[("Rb", g)])
        for l in range(1, 6):
            for g in range(NHG):
                ps, pk = banks.next()
                for hh in range(8):
                    p.op("pe", lambda e: e.matmul(ps[0:64, hh * 64:(hh + 1) * 64], lhsT=NM[g][l][:, hh, :], rhs=Gb[g][:, hh, :], start=True, stop=True),
                         reads=[("NM", g, l), ("Gb", g)], writes=[pk])
                p.op("act", lambda e: e.activation(out=Wb[g][:], in_=v3(ps[0:64, :]), func=AF.Copy), reads=[pk], writes=[("Wb", g)])
            zz = []
            for g in range(NHG):
                psz = pkz = None
                if l < 5:
                    psz, pkz = banks.next()
                    for hh in range(8):
                        p.op("pe", lambda e: e.matmul(psz[0:64, hh * 64:(hh + 1) * 64], lhsT=Hb[g][:, hh, :], rhs=Wb[g][:, hh, :], start=True, stop=True),
                             reads=[("Hb", g), ("Wb", g)], writes=[pkz])
                ps2, pk2 = banks.next()
                for hh in range(8):
                    p.op("pe", lambda e: e.matmul(ps2[0:64, hh * 64:(hh + 1) * 64], lhsT=Wb[g][:, hh, :], rhs=Hb[g][:, hh, :], start=True, stop=True),
                         reads=[("Hb", g), ("Wb", g)], writes=[pk2])
                if l < 5:
                    p.op("dve", lambda e: e.tensor_tensor(out=Gf[g][:], in0=v3(psz[0:64, :]), in1=Gf[g][:], op=ALU.add),
                         reads=[pkz, ("Gf", g)], writes=[("Gf", g)])
                    p.op("act", lambda e: e.activation(out=Gb[g][:], in_=Gf[g][:], func=AF.Copy), reads=[("Gf", g)], writes=[("Gb", g)])
                p.op("dve", lambda e: e.tensor_tensor(out=Hf[g][:], in0=v3(ps2[0:64, :]), in1=Hf[g][:], op=ALU.add),
                     reads=[pk2, ("Hf", g)], writes=[("Hf", g)])
                p.op("act", lambda e: e.activation(out=Hb[g][:], in_=Hf[g][:], func=AF.Copy), reads=[("Hf", g)], writes=[("Hb", g)])
        for g in range(NHG):
            ps, pk = banks.next()
            for hh in range(8):
                p.op("pe", lambda e: e.matmul(ps[0:64, hh * 64:(hh + 1) * 64], lhsT=Hb[g][:, hh, :], rhs=Rb[g][:, hh, :], start=True, stop=True),
                     reads=[("Hb", g), ("Rb", g)], writes=[pk])
            p.op("act", lambda e: e.activation(out=Xb[g][:], in_=v3(ps[0:64, :]), func=AF.Copy), reads=[pk], writes=[("Xb", g)])
        ys = yst[s]
        for g in range(NHG):
            ps, pk = banks.next()
            for hh in range(8):
                h = g * 8 + hh
                o = ps[0:64, hh * 64:(hh + 1) * 64]
                p.op("pe", lambda e: e.matmul(o, lhsT=Sb[:, h, :], rhs=far[:, h, 1, :], start=True, stop=False),
                     reads=[("Sb", g), kfar], writes=[pk])
                p.op("pe", lambda e: e.matmul(o, lhsT=Sl[:, h, :], rhs=far[:, h, 1, :], start=False, stop=False),
                     reads=[("Sl", g), kfar], writes=[pk])
                p.op("pe", lambda e: e.matmul(o, lhsT=Xb[g][:, hh, :], rhs=QA[g][:, hh, 1, :], start=False, stop=False),
                     reads=[("Xb", g), ("QA", g)], writes=[pk])
                p.op("pe", lambda e: e.matmul(o, lhsT=tm[:, 0, h * 64:(h + 1) * 64], rhs=QK[g][:, hh, 1, :], start=False, stop=True),
                     reads=[ktm, ("QK", g)], writes=[pk])
            p.op("act", lambda e: e.activation(out=ys[:, g * 8:(g + 1) * 8, :], in_=v3(ps[0:64, :]), func=AF.Copy),
                 reads=[pk], writes=[("yst", s)])
        p.dma(y_d[ci], ys[:].rearrange("p a b -> p (a b)"), reads=[("yst", s)], q="act")
        for g in range(NHG):
            ps, pk = banks.next()
            for hh in range(8):
                h = g * 8 + hh
                o = ps[0:64, hh * 64:(hh + 1) * 64]
                p.op("pe", lambda e: e.matmul(o, lhsT=tm[:, 1, h * 64:(h + 1) * 64], rhs=Xb[g][:, hh, :], start=True, stop=False),
                     reads=[ktm, ("Xb", g)], writes=[pk])
                p.op("pe", lambda e: e.matmul(o, lhsT=tm[:, 2, h * 64:(h + 1) * 64], rhs=tm[:, 0, h * 64:(h + 1) * 64], start=False, stop=True),
                     reads=[ktm], writes=[pk])
            Sg = S[:, g * 8:(g + 1) * 8, :]
            Sdg = Sd[:, g * 8:(g + 1) * 8, :]
            p.op("dve", lambda e: e.tensor_tensor(out=Sg, in0=v3(ps[0:64, :]), in1=Sg, op=ALU.add), reads=[pk, ("S", g)], writes=[("S", g)])
            p.op("dve", lambda e: e.tensor_tensor(out=Sg, in0=Sg, in1=gam[:, g * 8:(g + 1) * 8, ci:ci + 1].broadcast_to([64, 8, 64]), op=ALU.mult),
                 reads=[("S", g), "gam"], writes=[("S", g)])
            p.op("act", lambda e: e.activation(out=Sb[:, g * 8:(g + 1) * 8, :], in_=Sg, func=AF.Copy), reads=[("S", g)], writes=[("Sb", g)])
            p.op(RW2_MASK_ENG, lambda e: e.tensor_tensor(out=Sdg, in0=Sg, in1=Sb[:, g * 8:(g + 1) * 8, :], op=ALU.subtract),
                 reads=[("S", g), ("Sb", g)], writes=[("Sd", g)])
            p.op("act", lambda e: e.activation(out=Sl[:, g * 8:(g + 1) * 8, :], in_=Sdg, func=AF.Copy), reads=[("Sd", g)], writes=[("Sl", g)])
    return p.build()


def rw2_masks():
    r = np.arange(64)[:, None]
    f = np.arange(64)[None, :]
    mka = np.stack([(f > r), (f >= r)], 1).astype(np.float32).reshape(64, 128)

    def m_level(l, i, j):
        return ((i >> (l + 1)) == (j >> (l + 1))) & (((i >> l) & 1) == 1) & (((j >> l) & 1) == 0)
    eye = (r == f)
    mkp = np.stack([m_level(0, r, f), eye], 1).astype(np.float32).reshape(64, 128)
    mkl = np.stack([m_level(l, f, r) for l in range(6)] + [eye], 1).astype(np.float32).reshape(64, 7 * 64)
    return mka, mkp, mkl


RW_GN_EPS = 64e-5


def build_rw3():
    p = Prog()
    W = WF
    h_d = p.dram("h", [D, W])
    modv_d = p.dram("modv", [128, 6 * KC, 2])
    g1_d = p.dram("g1", [128, KC])
    mask_d = p.dram("mask", [1, W])
    coef_d = p.dram("coef", [128, 2, 6, KC])
    yin_d = p.dram("yin", [4, D, 1152])
    g1w_d = p.dram("g1w", [D, RW_GATE])
    g2w_d = p.dram("g2w", [RW_GATE, D])
    lnv_d = p.dram("lnv", [128, 2, KC])
    wo_d = p.dram("wo", [D, D])
    bones_d = p.dram("bones", [128, 128])
    y_d = p.dram("y", [D, 1152], kind="ExternalOutput")
    cm = Common(p)
    banks = Banks(p)
    a = p.sb("a", [128, KC, W], BF16)
    hbuf = p.sb("hbuf", [128, 2, W], F32)
    tmp = p.sb("tmp", [128, 2, W], F32)
    sq = p.sb("sq", [128, 2, 512], BF16)
    rstd = p.sb("rstd", [128, W], F32)
    modv = load_small(p, "modv", [128, 6 * KC, 2], modv_d)
    gvec = load_small(p, "gvec", [128, KC], g1_d)
    mask = load_small(p, "mask", [128, W], mask_d[0].partition_broadcast(128))
    coef = load_coef(p, coef_d)
    lnv = load_small(p, "lnv", [128, 2, KC], lnv_d)
    gne = p.sb("gne", [128, 1], F32)
    p.op("dve", lambda e: e.memset(gne[:], RW_GN_EPS), writes=["gne"])
    bones = p.sb("bones", [128, 128], BF16)
    p.dma(bones[:], bones_d, writes=["bones"], q="pool", max_dma_last_dim=512)
    g1w = p.sb("g1w", [128, KC, RW_GATE], BF16)
    g2w = p.sb("g2w", [128, 2, D], BF16)
    p.dma(g1w[:], g1w_d.rearrange("(c p) f -> p c f", p=128), writes=["g1w"], q="pool")
    for c2 in range(2):
        p.dma(g2w[:, c2, :], g2w_d[c2 * 128:(c2 + 1) * 128, :], writes=["g2w"], q="pool")
    A_l = make_AB(p, "rl", modv, gvec, 0, 1, 0)
    A_c = make_AB(p, "rc", modv, gvec, 0, 1, 1)
    segs = [(LAT0, LAT1, A_l, lambda c: modv[:, c, 0:1]), (CTX0, CTX1, A_c, lambda c: modv[:, c, 1:2])]
    norm_mod_stream(p, cm, banks, h_d.rearrange("(c p) w -> p c w", p=128), W, segs, a, "a", rstd, sq, hbuf, tmp,
                    ["rl_A", "rc_A"], mask=mask)
    xg = p.sb("xg", [128, KC, 512], BF16)
    ggb = p.sb("ggb", [128, 2, 512], BF16)
    ob = p.sb("ob", [128, KC, 512], BF16)
    yt = [p.sb("yt%d" % i, [128, 4, 512], F32) for i in range(2)]
    ft = [p.sb("ft%d" % i, [128, 512], F32) for i in range(4)]
    fk = [("ft", i) for i in range(4)]
    sqb = p.sb("sqb", [128, 2, 512], BF16)
    st = p.sb("st", [128, 2, 512], F32)
    slabs = [p.sb("slab%d" % i, [128, KC, 256], BF16) for i in range(2)]
    yinv = yin_d.rearrange("n (c p) w -> p n c w", p=128)
    yov = y_d.rearrange("(c p) w -> p c w", p=128)
    cblocks = [(1, 512, 0), (513, 512, 512), (1027, 128, 1024)]
    cnt = [0]
    for (j0, n, o0) in cblocks:
        shift_mix(p, a, lambda c: ("a", c), xg, "xg", coef, 5, j0, n)
        for c2 in range(2):
            ps, pk = banks.next()
            for c in range(KC):
                p.op("pe", lambda e: e.matmul(ps[:, 0:n], lhsT=g1w[:, c, c2 * 128:(c2 + 1) * 128], rhs=xg[:, c, 0:n],
                                              start=(c == 0), stop=(c == KC - 1)),
                     reads=["g1w", ("xg", c)], writes=[pk])
            p.op("act", lambda e: e.activation(out=ggb[:, c2, 0:n], in_=ps[:, 0:n], func=AF.Sigmoid), reads=[pk], writes=[("ggb", c2)])
        for m in range(KC):
            s2 = m % 2
            y4 = yt[s2]
            p.dma(y4[:, :, 0:n], yinv[:, :, m, o0:o0 + n], writes=[("yt", s2)])
            psg, pkg = banks.next()
            for c2 in range(2):
                p.op("pe", lambda e: e.matmul(psg[:, 0:n], lhsT=g2w[:, c2, m * 128:(m + 1) * 128], rhs=ggb[:, c2, 0:n],
                                              start=(c2 == 0), stop=(c2 == 1)),
                     reads=["g2w", ("ggb", c2)], writes=[pkg])
            T_ = lambda i: ft[i][:, 0:n]
            p.op("dve", lambda e: e.tensor_tensor(out=T_(0), in0=y4[:, 0, 0:n], in1=y4[:, 1, 0:n], op=ALU.add), reads=[("yt", s2)], writes=[fk[0]])
            p.op("act", lambda e: e.activation(out=sqb[:, 0, 0:n], in_=T_(0), func=AF.Copy), reads=[fk[0]], writes=[("sqb", 0)])
            psm, pkm = banks.next()
            p.op("pe", lambda e: e.matmul(psm[:, 0:n], lhsT=bones[:], rhs=sqb[:, 0, 0:n], start=True, stop=True),
                 reads=["bones", ("sqb", 0)], writes=[pkm])
            p.op("dve", lambda e: e.scalar_tensor_tensor(out=T_(1), in0=psm[:, 0:n], scalar=-1.0 / 64, in1=T_(0), op0=ALU.mult, op1=ALU.add),
                 reads=[pkm, fk[0]], writes=[fk[1]])
            p.op("act", lambda e: e.activation(out=sqb[:, 1, 0:n], in_=T_(1), func=AF.Square), reads=[fk[1]], writes=[("sqb", 1)])
            psv, pkv = banks.next()
            p.op("pe", lambda e: e.matmul(psv[:, 0:n], lhsT=bones[:], rhs=sqb[:, 1, 0:n], start=True, stop=True),
                 reads=["bones", ("sqb", 1)], writes=[pkv])
            p.op("act", lambda e: e.activation(out=T_(2), in_=psv[:, 0:n], func=AF.Sqrt, bias=gne[:, 0:1], scale=1.0 / 64),
                 reads=[pkv, "gne"], writes=[fk[2]])
            p.op("dve", lambda e: e.reciprocal(out=T_(2), in_=T_(2)), reads=[fk[2]], writes=[fk[2]])
            p.op("dve", lambda e: e.tensor_tensor(out=T_(1), in0=T_(1), in1=T_(2), op=ALU.mult), reads=[fk[1], fk[2]], writes=[fk[1]])
            p.op("dve", lambda e: e.tensor_scalar(out=T_(1), in0=T_(1), scalar1=lnv[:, 0, m:m + 1], scalar2=lnv[:, 1, m:m + 1],
                                                  op0=ALU.mult, op1=ALU.add),
                 reads=[fk[1], "lnv"], writes=[fk[1]])
            p.op("dve", lambda e: e.tensor_tensor(out=T_(3), in0=y4[:, 2, 0:n], in1=y4[:, 3, 0:n], op=ALU.add), reads=[("yt", s2)], writes=[fk[3]])
            p.op("dve", lambda e: e.tensor_tensor(out=T_(1), in0=T_(1), in1=T_(3), op=ALU.add), reads=[fk[1], fk[3]], writes=[fk[1]])
            p.op("dve", lambda e: e.tensor_tensor(out=ob[:, m, 0:n], in0=psg[:, 0:n], in1=T_(1), op=ALU.mult),
                 reads=[pkg, fk[1]], writes=[("ob", m)])

        def evac_y(m, ps, pk, b0, b1):
            s = cnt[0] % 2
            cnt[0] += 1
            p.op("act", lambda e: e.activation(out=st[:, s, 0:b1 - b0], in_=ps[:, 0:b1 - b0], func=AF.Copy), reads=[pk], writes=[("st", s)])
            p.dma(yov[:, m, o0 + b0:o0 + b1], st[:, s, 0:b1 - b0], reads=[("st", s)], q="act")
        linear_fm2(p, banks, slabs, "slab", ob, "ob", KC, wo_d.rearrange("(c p) f -> p c f", p=128), D, [(0, n)], evac_y)
    return p.build()


_PROGS = {}
_DEBUG = None


def _prog(name, fn, *args):
    key = (name,) + args
    if key not in _PROGS:
        _PROGS[key] = fn(*args)
    return _PROGS[key]


def _run(nc, in_maps):
    res = _bu.run_bass_kernel_spmd(nc, in_maps, core_ids=list(range(8)))
    return res.results


def _f32(x):
    return np.ascontiguousarray(np.asarray(x, dtype=np.float32))


def _slab(hl_b, hc_b, s, halo):
    Dn = hl_b.shape[0]
    wl, wc = 1024 + 2 * halo, 128 + 2 * halo
    out = np.zeros((Dn, wl + wc), hl_b.dtype)
    mask = np.zeros((1, wl + wc), np.float32)
    for (src, n, base, o0) in ((hl_b, 1024, s * 1024, 0), (hc_b, 128, s * 128, wl)):
        lo, hi = base - halo, base + n + halo
        a0, a1 = max(lo, 0), min(hi, src.shape[1])
        out[:, o0 + (a0 - lo):o0 + (a1 - lo)] = src[:, a0:a1]
        mask[0, o0 + (a0 - lo):o0 + (a1 - lo)] = 1.0
    return out, mask


def _core_cols(hl_b, hc_b, s):
    return np.ascontiguousarray(np.concatenate([hl_b[:, s * 1024:(s + 1) * 1024], hc_b[:, s * 128:(s + 1) * 128]], 1))


def _pool_icnt(s):
    out = np.ones((4, WP), np.float32)
    for g, win in enumerate((2, 4, 8, 16)):
        left, right = win // 2, win - 1 - win // 2
        for (n, base, o0, Tn) in ((1024, s * 1024, 8, T), (128, s * 128, 1040 + 8, L)):
            t = np.arange(base, base + n)
            cnt = np.minimum(t + right, Tn - 1) - np.maximum(t - left, 0) + 1
            out[g, o0:o0 + n] = (1.0 / cnt.astype(np.float64)).astype(np.float32)
    return out


def rwkv_mixer(hl, hc, modv3, W3, dbg=None):
    import ml_dtypes
    norm1_g = W3["norm1_g"]; rw_mu = W3["rw_mu"]; rw_w_rkv = W3["rw_w_rkv"]; rw_w0 = W3["rw_w0"]; rw_w1 = W3["rw_w1"]
    rw_w2 = W3["rw_w2"]; rw_a0 = W3["rw_a0"]; rw_a1 = W3["rw_a1"]; rw_a2 = W3["rw_a2"]; rw_g1 = W3["rw_g1"]; rw_g2 = W3["rw_g2"]
    rw_k_k = W3["rw_k_k"]; rw_k_a = W3["rw_k_a"]; rw_r_k = W3["rw_r_k"]; rw_ln_g = W3["rw_ln_g"]; rw_ln_b = W3["rw_ln_b"]
    rw_w_o = W3["rw_w_o"]
    cores = [(b, s) for b in range(NB) for s in range(2)]
    mu = _f32(rw_mu[0])
    bones = np.kron(np.eye(2), np.ones((64, 64))).astype(np.float32)
    rmask = np.ones((1, 512), np.float32)
    rmask[0, ::64] = 0

    def coef_of(mu_prev, mu_next):
        return np.ascontiguousarray(np.stack([
            np.stack([_chunked(mu_prev[n]) for n in range(6)], 1),
            np.stack([_chunked(mu_next[n]) for n in range(6)], 1)], 1))

    bf = ml_dtypes.bfloat16
    mka, mkp, mkl = rw2_masks()
    rw2_in = {}
    bv_nat = {}
    vecs = np.ascontiguousarray(np.stack([_chunked(rw_w0[0][0]), _chunked(rw_w0[0][1]), _chunked(rw_a0[0][0]), _chunked(rw_a0[0][1]),
                                          _chunked(rw_k_k[0]), _chunked(rw_k_a[0]), _chunked(_f32(rw_r_k[0]).reshape(-1))], 1))
    ims = []
    for (b, s) in cores:
        hs, mk = _slab(hl[b], hc[b], s, 1)
        ims.append(dict(h=hs, modv=modv3[b], g1=_chunked(norm1_g[3]), mask=mk, coef=coef_of(mu[0], mu[1]),
                        wrkv=_f32(rw_w_rkv[0]), w1=_f32(rw_w1[0]), w2=_f32(rw_w2[0]), a1=_f32(rw_a1[0]),
                        a2=_f32(rw_a2[0]), vecs=vecs, bones=bones, rmask=rmask))
    res = _run(_prog("rw1", build_rw1), ims)
    for b in range(NB):
        r0, r1 = res[2 * b], res[2 * b + 1]
        for d in range(2):
            def seq(nm):
                lat = np.concatenate([np.asarray(r0[nm])[:, :1024], np.asarray(r1[nm])[:, :1024]], 1)
                cx = np.concatenate([np.asarray(r0[nm])[:, 1024:], np.asarray(r1[nm])[:, 1024:]], 1)
                if d == 1:
                    lat, cx = lat[:, ::-1], cx[:, ::-1]
                return np.concatenate([cx, lat], 1)
            at, bt, kt, rt = (seq("%s%d" % (nm, d)) for nm in ("at", "bt", "kt", "rt"))
            vt = seq("vt")
            hm = lambda z: z.reshape(32, 64, NCH, 64).transpose(2, 1, 0, 3)
            far = np.ascontiguousarray(np.stack([hm(at), hm(rt)], 3).reshape(NCH, 64, -1))
            fbk = np.ascontiguousarray(np.stack([hm(bt), hm(kt)], 2).reshape(NCH, 64, -1))
            tk = lambda z: np.ascontiguousarray(z.T).reshape(NCH, 64, D)
            tm = np.ascontiguousarray(np.stack([tk(vt), tk(bt), tk(kt)], 2).reshape(NCH, 64, -1))
            g0, g1_ = np.asarray(r0["gam%d" % d]), np.asarray(r1["gam%d" % d])
            glat = np.concatenate([g0[:, 0:16], g1_[:, 0:16]], 1)
            gcx = np.concatenate([g0[:, 16:18], g1_[:, 16:18]], 1)
            if d == 1:
                glat, gcx = glat[:, ::-1], gcx[:, ::-1]
            gam = np.concatenate([gcx, glat], 1)
            gam = np.ascontiguousarray(gam.reshape(32, 64, NCH).transpose(1, 0, 2))
            rw2_in[(b, d)] = dict(far=far.astype(bf, copy=False), fbk=fbk.astype(bf, copy=False), tm=tm.astype(bf, copy=False),
                                  gam=gam, mka=mka, mkp=mkp, mkl=mkl)
            bvs = np.concatenate([np.asarray(r0["bv%d" % d])[:, :1024], np.asarray(r1["bv%d" % d])[:, :1024]], 1)
            bvc = np.concatenate([np.asarray(r0["bv%d" % d])[:, 1024:], np.asarray(r1["bv%d" % d])[:, 1024:]], 1)
            bv_nat[(b, d)] = (np.ascontiguousarray(bvs), np.ascontiguousarray(bvc))
    res = _run(_prog("rw2", build_rw2), [rw2_in[(b, d)] for b in range(NB) for d in range(2)])
    y_nat = {}
    for b in range(NB):
        for d in range(2):
            y = res[2 * b + d]["y"].reshape(NCH, 64, 32, 64).transpose(2, 1, 0, 3).reshape(D, NTOK)
            yc_, yl_ = y[:, :L], y[:, L:]
            if d == 1:
                yc_, yl_ = yc_[:, ::-1], yl_[:, ::-1]
            y_nat[(b, d)] = (np.ascontiguousarray(yl_), np.ascontiguousarray(yc_))
    ims = []
    coef_n = coef_of(mu[0], mu[1])
    lnv = np.ascontiguousarray(np.stack([_chunked(rw_ln_g[0]), _chunked(rw_ln_b[0])], 1))
    for (b, s) in cores:
        hs, mk = _slab(hl[b], hc[b], s, 1)
        yin = np.stack([_core_cols(y_nat[(b, 0)][0], y_nat[(b, 0)][1], s), _core_cols(y_nat[(b, 1)][0], y_nat[(b, 1)][1], s),
                        _core_cols(bv_nat[(b, 0)][0], bv_nat[(b, 0)][1], s), _core_cols(bv_nat[(b, 1)][0], bv_nat[(b, 1)][1], s)], 0)
        ims.append(dict(h=hs, modv=modv3[b], g1=_chunked(norm1_g[3]), mask=mk, coef=coef_n, yin=np.ascontiguousarray(yin),
                        g1w=_f32(rw_g1[0]), g2w=_f32(rw_g2[0]), lnv=lnv, wo=_f32(rw_w_o[0]), bones=bones))
    res = _run(_prog("rw3", build_rw3), ims)
    if dbg is not None:
        dbg["rw2_in"] = rw2_in
        dbg["y_nat"] = y_nat
        dbg["bv_nat"] = bv_nat
    return res


def kernel(x, c, ctx, c_ctx, norm1_g, norm2_g, w_mod, b_mod, ffn_w_gate, ffn_w_up, ffn_conv_w, ffn_conv_b,
           ffn_w_down, final_norm_g, pool_w, pool_b, pool_scale, na_w_qkv, na_rpb, na_w_o, sg_w_in, sg_b_in,
           sg_norm_g, sg_w_s, sg_b_s, sg_w_o, rw_mu, rw_w_rkv, rw_w0, rw_w1, rw_w2, rw_a0, rw_a1, rw_a2, rw_g1,
           rw_g2, rw_k_k, rw_k_a, rw_r_k, rw_ln_g, rw_ln_b, rw_w_o):
    import ml_dtypes
    x, ctx = _f32(x), _f32(ctx)
    hl = [np.ascontiguousarray(x[b].T) for b in range(NB)]
    hc = [np.ascontiguousarray(ctx[b].T) for b in range(NB)]
    cores = [(b, s) for b in range(NB) for s in range(2)]

    cc = np.zeros((8, D), np.float32)
    cc[0:4] = _f32(c)
    cc[4] = _f32(c_ctx)
    ct = np.ascontiguousarray(cc.T.reshape(KC, 128, 8).transpose(1, 0, 2))
    w_mod = np.asarray(w_mod, np.float32)
    ims = []
    for core in range(8):
        i, half = core // 2, core % 2
        ims.append(dict(wm=np.ascontiguousarray(w_mod[i][:, half * 6144:(half + 1) * 6144]),
                        bm=_chunked(np.asarray(b_mod[i], np.float32)[half * 6144:(half + 1) * 6144]), ct=ct))
    res = _run(_prog("mod", build_mod), ims)
    modv = {}
    for i in range(4):
        mo = np.concatenate([res[2 * i]["mo"], res[2 * i + 1]["mo"]], 1)
        for b in range(NB):
            modv[(i, b)] = np.ascontiguousarray(mo[:, :, [b, 4]])

    def run_ffn(i, ys_l, ys_c, final):
        wg, wu, wd = _f32(ffn_w_gate[i]), _f32(ffn_w_up[i]), _f32(ffn_w_down[i])
        cw = np.ascontiguousarray(_f32(ffn_conv_w[i]).reshape(3, FC, 128).transpose(2, 1, 0))
        cb = _chunked(ffn_conv_b[i])
        g2 = _chunked(norm2_g[i])
        ims = []
        for (b, s) in cores:
            hs, mk = _slab(hl[b], hc[b], s, 1)
            y = np.zeros((len(ys_l), D, WF), np.float32)
            for n in range(len(ys_l)):
                y[n] = _slab(ys_l[n][b], ys_c[n][b], s, 1)[0]
            im = dict(h=hs, y=y, modv=modv[(i, b)], g2=g2, mask=np.ascontiguousarray(np.repeat(mk, 128, 0)),
                      wg=wg, wu=wu, wd=wd, cw=cw, cb=cb)
            if final:
                im["gf"] = _chunked(final_norm_g)
            ims.append(im)
        res = _run(_prog("ffn", build_ffn, final, len(ys_l)), ims)
        out = None
        if final:
            out = np.empty((NB, T, D), np.float32)
        for ci, (b, s) in enumerate(cores):
            ho = res[ci]["ho"]
            hl[b][:, s * 1024:(s + 1) * 1024] = ho[:, 0:1024]
            hc[b][:, s * 128:(s + 1) * 128] = ho[:, 1024:1152]
            if final:
                out[b, s * 1024:(s + 1) * 1024, :] = res[ci]["of"].T
        return out

    def split_y(res_y):
        yl = [np.empty((D, T), np.float32) for _ in range(NB)]
        yc = [np.empty((D, L), np.float32) for _ in range(NB)]
        for ci, (b, s) in enumerate(cores):
            yl[b][:, s * 1024:(s + 1) * 1024] = res_y[ci][:, 0:1024]
            yc[b][:, s * 128:(s + 1) * 128] = res_y[ci][:, 1024:1152]
        return yl, yc

    i = 0
    ims = []
    for (b, s) in cores:
        hs, mk = _slab(hl[b], hc[b], s, 8)
        ims.append(dict(h=hs, modv=modv[(i, b)], g1=_chunked(norm1_g[i]), mask=mk, icnt=_pool_icnt(s),
                        pw=_f32(pool_w[0]), pb=_chunked(pool_b[0]), psc=_chunked(pool_scale[0])))
    res = _run(_prog("pool", build_pool), ims)
    yl, yc = split_y([r["y"] for r in res])
    run_ffn(i, [yl], [yc], False)
    if _DEBUG is not None:
        _DEBUG.append(([a.copy() for a in hl], [a.copy() for a in hc]))

    i = 1
    pm = rope_perm()
    ims = []
    for (b, s) in cores:
        rc, rs = rope_tables(s * 1024, 1024)
        ims.append(dict(h=_core_cols(hl[b], hc[b], s), modv=modv[(i, b)], g1=_chunked(norm1_g[i]), wqkv=_f32(na_w_qkv[0]),
                        rcos=rc, rsin=rs, pm=pm))
    res = _run(_prog("na1", build_na1), ims)
    Q, Kt, V = [], [], []
    for b in range(NB):
        r0, r1 = res[2 * b], res[2 * b + 1]
        Q.append(np.concatenate([r0["qT"][:, :1024], r1["qT"][:, :1024], r0["qT"][:, 1024:], r1["qT"][:, 1024:]], 1))
        Kt.append(np.concatenate([r0["kT"][:, :1024], r1["kT"][:, :1024], r0["kT"][:, 1024:], r1["kT"][:, 1024:]], 1))
        V.append(np.concatenate([r0["v"][:1024], r1["v"][:1024], r0["v"][1024:], r1["v"][1024:]], 0))
    rpb = _f32(na_rpb[0])
    wo = _f32(na_w_o[0])
    ims = []
    for b in range(NB):
        for hh in range(2):
            sl = slice(hh * 1024, (hh + 1) * 1024)
            ims.append(dict(qT=np.ascontiguousarray(Q[b][sl]), kT=np.ascontiguousarray(Kt[b][sl]),
                            v=np.ascontiguousarray(V[b][:, sl]), bt=na_bias_tables(rpb, list(range(hh * 16, hh * 16 + 16))),
                            wo=np.ascontiguousarray(wo[sl])))
    res = _run(_prog("na2", build_na2), ims)
    yls, ycs = [], []
    for hh in range(2):
        yls.append([np.ascontiguousarray(res[2 * b + hh]["y"][:, :T]) for b in range(NB)])
        ycs.append([np.ascontiguousarray(res[2 * b + hh]["y"][:, T:]) for b in range(NB)])
    run_ffn(i, yls, ycs, False)
    if _DEBUG is not None:
        _DEBUG.append(([a.copy() for a in hl], [a.copy() for a in hc]))

    i = 2
    b_in = _f32(sg_b_in[0])
    ims = []
    for (b, s) in cores:
        ims.append(dict(h=_core_cols(hl[b], hc[b], s), modv=modv[(i, b)], g1=_chunked(norm1_g[i]), win=_f32(sg_w_in[0]),
                        bzu=_chunked(b_in[:D]), bzv=np.ascontiguousarray(b_in[None, D:]), ng=_f32(sg_norm_g[0])[None].copy(),
                        wst=np.ascontiguousarray(_f32(sg_w_s[0]).transpose(2, 0, 1)), bs=_f32(sg_b_s[0]).reshape(1, -1).copy(),
                        wo=_f32(sg_w_o[0])))
    res = _run(_prog("gmlp", build_gmlp), ims)
    yl, yc = split_y([r["y"] for r in res])
    run_ffn(i, [yl], [yc], False)
    if _DEBUG is not None:
        _DEBUG.append(([a.copy() for a in hl], [a.copy() for a in hc]))

    i = 3
    W3 = dict(norm1_g=norm1_g, rw_mu=rw_mu, rw_w_rkv=rw_w_rkv, rw_w0=rw_w0, rw_w1=rw_w1, rw_w2=rw_w2, rw_a0=rw_a0, rw_a1=rw_a1,
              rw_a2=rw_a2, rw_g1=rw_g1, rw_g2=rw_g2, rw_k_k=rw_k_k, rw_k_a=rw_k_a, rw_r_k=rw_r_k, rw_ln_g=rw_ln_g,
              rw_ln_b=rw_ln_b, rw_w_o=rw_w_o)
    res = rwkv_mixer(hl, hc, {b: modv[(i, b)] for b in range(NB)}, W3)
    yl, yc = split_y([r["y"] for r in res])
    return run_ffn(i, [yl], [yc], True)
```

```python
import contextlib
import os
import numpy as np
import concourse.bass as bass
import concourse.mybir as mybir

F32 = mybir.dt.float32
BF16 = mybir.dt.bfloat16
AF = mybir.ActivationFunctionType
ALU = mybir.AluOpType
AX = mybir.AxisListType

ENGS = ("sync", "pe", "dve", "act", "pool")
SAME_ENGINE_SYNC = True


class Prog:
    EPOCH = 16000
    NDMA = 24

    def __init__(self):
        self.nc = bass.Bass("TRN2", target_bir_lowering=False)
        self.stack = contextlib.ExitStack()
        self.ops = {e: [] for e in ENGS}
        self.cnt = {e: 0 for e in ENGS}
        self.last_w = {}
        self.readers = {}
        self.waited = {e: {} for e in ENGS}
        self.sems = {}
        self.dma_n = 0
        self.dma_tot = [0] * self.NDMA
        self.n_uid = 0

    def dram(self, name, shape, dt=F32, kind="ExternalInput"):
        return self.nc.dram_tensor(name, list(shape), dt, kind=kind).ap()

    def sb(self, name, shape, dt=F32):
        return self.stack.enter_context(self.nc.sbuf_tensor("sb_" + name, list(shape), dt))

    def ps(self, name, shape, dt=F32):
        return self.stack.enter_context(self.nc.psum_tensor("pp_" + name, list(shape), dt))

    def _sem(self, key):
        if key not in self.sems:
            self.sems[key] = self.stack.enter_context(
                self.nc.semaphore("s_%s_%s" % (key[0], key[1])))
        return self.sems[key]

    def _deps(self, eng, reads, writes, pe_sync=False):
        toks = []
        for k in reads:
            w = self.last_w.get(k)
            if w is not None:
                toks.append(w)
        for k in writes:
            w = self.last_w.get(k)
            if w is not None:
                toks.append(w)
            toks.extend(self.readers.get(k, ()))
        need = {}
        for t in toks:
            if t[0] == "eng":
                _, e, seq = t
                if e == eng and ((eng == "pe" and not pe_sync) or not SAME_ENGINE_SYNC):
                    continue
                sk = (e, seq // self.EPOCH)
                v = seq % self.EPOCH + 1
            else:
                _, s, v = t
                sk = ("dma", s)
            if need.get(sk, 0) < v:
                need[sk] = v
        out = []
        wd = self.waited[eng]
        for sk, v in need.items():
            if wd.get(sk, 0) >= v:
                continue
            wd[sk] = v
            out.append((sk, v))
        return out

    def _commit(self, tok, reads, writes):
        for k in reads:
            self.readers.setdefault(k, []).append(tok)
        for k in writes:
            self.last_w[k] = tok
            self.readers[k] = []

    @staticmethod
    def _psum_excl(reads, writes):
        r2, w2 = [], list(writes)
        for k in reads:
            if isinstance(k, tuple) and k and k[0] == "ps":
                w2.append(k)
            else:
                r2.append(k)
        return r2, w2

    def op(self, eng, fn, reads=(), writes=(), pe_sync=False):
        reads, writes = self._psum_excl(reads, writes)
        waits = self._deps(eng, reads, writes, pe_sync)
        seq = self.cnt[eng]
        self.cnt[eng] += 1
        tok = ("eng", eng, seq)
        self._emit(eng, waits, fn, ((eng, seq // self.EPOCH), 1))
        self._commit(tok, reads, writes)
        return tok

    def dma(self, out, in_, reads=(), writes=(), q="sync", **kw):
        if q == "pool" and "max_dma_last_dim" not in kw:
            kw["max_dma_last_dim"] = 2048
        s = self.dma_n % self.NDMA
        self.dma_n += 1
        waits = self._deps(q, reads, writes)
        prev = self.dma_tot[s]
        sk = ("dma", s)
        if prev and self.waited[q].get(sk, 0) < prev:
            self.waited[q][sk] = prev
            waits.append((sk, prev))
        self.dma_tot[s] = prev + 16
        tok = ("dma", s, prev + 16)
        fn = lambda e, out=out, in_=in_, kw=kw: e.dma_start(out=out, in_=in_, **kw)
        self._emit(q, waits, fn, (sk, 16))
        self._commit(tok, reads, writes)
        return tok

    def _emit(self, name, waits, fn, inc):
        nc = self.nc
        eng = {"sync": nc.sync, "pe": nc.tensor, "dve": nc.vector, "act": nc.scalar, "pool": nc.gpsimd}[name]
        for sk, v in waits:
            eng.wait_ge(self._sem(sk), v)
        fn(eng).then_inc(self._sem(inc[0]), inc[1])
        self.nops = getattr(self, "nops", 0) + 1 + len(waits)
        if not hasattr(self, "trace"):
            self.trace = {e: [] for e in ENGS}
        self.trace[name].append((list(waits), inc))

    def check_deadlock(self):
        tr = getattr(self, "trace", None)
        if tr is None:
            return
        val = {}
        pos = {e: 0 for e in ENGS}
        progress = True
        while progress:
            progress = False
            for e in ENGS:
                q = tr[e]
                while pos[e] < len(q):
                    waits, inc = q[pos[e]]
                    if all(val.get(sk, 0) >= v for sk, v in waits):
                        val[inc[0]] = val.get(inc[0], 0) + inc[1]
                        pos[e] += 1
                        progress = True
                    else:
                        break
        stuck = {e: (pos[e], len(tr[e])) for e in ENGS if pos[e] < len(tr[e])}
        if stuck:
            msg = []
            for e, (i, n) in stuck.items():
                waits, inc = tr[e][i]
                msg.append("%s stuck at %d/%d waiting %s (have %s)" % (
                    e, i, n, waits, [(sk, val.get(sk, 0)) for sk, v in waits]))
            raise RuntimeError("semaphore deadlock: " + "; ".join(msg))

    def build(self):
        nc = self.nc
        self.check_deadlock()
        for s in range(self.NDMA):
            if self.dma_tot[s]:
                nc.sync.wait_ge(self._sem(("dma", s)), self.dma_tot[s])
        for e in ENGS:
            if e != "sync" and self.cnt[e]:
                seq = self.cnt[e] - 1
                nc.sync.wait_ge(self._sem((e, seq // self.EPOCH)), seq % self.EPOCH + 1)
        self.stack.close()
        return nc


import concourse.bass_utils as _bu

D = 2048
KC = 16
T = 2048
L = 256
NB = 4
FF = 5504
FC = 43
EPS = 1e-6


class Banks:
    def __init__(self, p, n=8, prefix="psb"):
        self.t = [p.ps("%s%d" % (prefix, i), [128, 512], F32) for i in range(n)]
        self.keys = [("ps", prefix, i) for i in range(n)]
        self.i = 0

    def next(self):
        b = self.i % len(self.t)
        self.i += 1
        return self.t[b], self.keys[b]


def col_blocks(c0, c1, n=512):
    out = []
    while c0 < c1:
        out.append((c0, min(c1, c0 + n)))
        c0 += n
    return out


class Common:
    def __init__(self, p):
        self.ones = p.sb("c_ones", [128, 128], BF16)
        self.eps = p.sb("c_eps", [128, 1], F32)
        p.op("dve", lambda e: e.memset(self.ones[:], 1.0), writes=["c_ones"])
        p.op("dve", lambda e: e.memset(self.eps[:], EPS), writes=["c_eps"])


def load_small(p, name, shape, dram_ap, dt=F32):
    t = p.sb(name, shape, dt)
    p.dma(t[:], dram_ap, writes=[name])
    return t


def make_AB(p, name, modv, g, j_shift, j_scale, col):
    A = p.sb(name + "_A", [128, KC], F32)
    p.op("dve", lambda e: e.tensor_scalar(out=A[:], in0=modv[:, j_scale * KC:(j_scale + 1) * KC, col],
                                          scalar1=1.0, scalar2=None, op0=ALU.add),
         reads=["modv"], writes=[name + "_A"])
    p.op("dve", lambda e: e.tensor_tensor(out=A[:], in0=A[:], in1=g[:], op=ALU.mult),
         reads=[name + "_A", "gvec"], writes=[name + "_A"])
    return A


def rms_rstd(p, cm, banks, h, hkey, W, rstd, rkey, sq, nfeat_chunks=KC, dmodel=D, eps_ap=None):
    for (b0, b1) in col_blocks(0, W):
        ps, pk = banks.next()
        n = b1 - b0
        for c in range(nfeat_chunks):
            s = c % 2
            p.op("act", lambda e, c=c, s=s: e.activation(out=sq[:, s, 0:n], in_=h[:, c, b0:b1], func=AF.Square),
                 reads=[hkey], writes=[("sq", s)])
            p.op("pe", lambda e, c=c, s=s: e.matmul(ps[:, 0:n], lhsT=cm.ones[:], rhs=sq[:, s, 0:n],
                                                     start=(c == 0), stop=(c == nfeat_chunks - 1)),
                 reads=[("sq", s), "c_ones"], writes=[pk])
        p.op("act", lambda e: e.activation(out=rstd[:, b0:b1], in_=ps[:, 0:n], func=AF.Sqrt,
                                           bias=(eps_ap if eps_ap is not None else cm.eps)[:, 0:1], scale=1.0 / dmodel),
             reads=[pk, "c_eps"], writes=[rkey])
        p.op("dve", lambda e: e.reciprocal(out=rstd[:, b0:b1], in_=rstd[:, b0:b1]), reads=[rkey], writes=[rkey])


def norm_mod(p, cm, banks, h, hkey, W, segs, out_bf, okey, rstd, sq, tmp, mask=None):
    rms_rstd(p, cm, banks, h, hkey, W, rstd, "rstd", sq)
    for c in range(KC):
        s = c % 2
        p.op("dve", lambda e, c=c, s=s: e.tensor_tensor(out=tmp[:, s, 0:W], in0=h[:, c, 0:W], in1=rstd[:, 0:W], op=ALU.mult),
             reads=[hkey, "rstd"], writes=[("tmp", s)])
        for (c0, c1, A, Bfn) in segs:
            p.op("act", lambda e, c=c, s=s, c0=c0, c1=c1, A=A, Bfn=Bfn: e.activation(
                out=out_bf[:, c, c0:c1], in_=tmp[:, s, c0:c1], func=AF.Identity, scale=A[:, c:c + 1], bias=Bfn(c)),
                reads=[("tmp", s), "AB", "modv"], writes=[(okey, c)])
        if mask is not None:
            p.op("pool", lambda e, c=c: e.tensor_tensor(out=out_bf[:, c, 0:W], in0=out_bf[:, c, 0:W], in1=mask[:, 0:W], op=ALU.mult),
                 reads=[(okey, c), "mask"], writes=[(okey, c)])


WF = 1156
LAT0, LAT1 = 0, 1026
CTX0, CTX1 = 1026, 1156
FG = 6
HALO1 = ((0, 1), (1025, 1027), (1155, 1156))


def build_ffn(final, nparts=2):
    p = Prog()
    h_d = p.dram("h", [D, WF])
    y_d = p.dram("y", [nparts, D, WF])
    modv_d = p.dram("modv", [128, 6 * KC, 2])
    g2_d = p.dram("g2", [128, KC])
    mask_d = p.dram("mask", [128, WF])
    wg_d = p.dram("wg", [D, FF])
    wu_d = p.dram("wu", [D, FF])
    wd_d = p.dram("wd", [FF, D])
    cw_d = p.dram("cw", [128, FC, 3])
    cb_d = p.dram("cb", [128, FC])
    ho_d = p.dram("ho", [D, 1152], kind="ExternalOutput")
    if final:
        gf_d = p.dram("gf", [128, KC])
        of_d = p.dram("of", [D, 1024], kind="ExternalOutput")

    cm = Common(p)
    banks = Banks(p)
    h = p.sb("h", [128, KC, WF], F32)
    u = p.sb("u", [128, KC, WF], BF16)
    act = p.sb("actb", [128, FG, WF], BF16)
    gsb = p.sb("gsb", [128, 2, WF], F32)
    tsb = p.sb("tsb", [128, 2, WF], F32)
    sq = p.sb("sq", [128, 2, 512], BF16)
    rstd = p.sb("rstd", [128, WF], F32)
    modv = load_small(p, "modv", [128, 6 * KC, 2], modv_d)
    gvec = load_small(p, "gvec", [128, KC], g2_d)
    mask = p.sb("mask", [128, WF], BF16)
    p.dma(mask[:], mask_d, writes=["mask"], q="pool")
    cw = load_small(p, "cw", [128, FC, 3], cw_d)
    cb = load_small(p, "cb", [128, FC], cb_d)
    gus = [p.sb("gus%d" % i, [128, KC, 256], BF16) for i in range(4)]
    nev = [0]
    wds = p.sb("wds", [128, FG, D], BF16)

    hv = h_d.rearrange("(c p) w -> p c w", p=128)
    yv = y_d.rearrange("n (c p) w -> n p c w", p=128)
    for c4 in range(0, KC, 4):
        p.dma(h[:, c4:c4 + 4, :], hv[:, c4:c4 + 4, :], writes=[("h", c) for c in range(c4, c4 + 4)])
    for c in range(KC):
        for n in range(nparts):
            s = (c * 2 + n) % 2
            p.dma(gsb[:, s, :], yv[n, :, c, :], writes=[("gsb", s)])
            for (c0, c1, col) in ((LAT0, LAT1, 0), (CTX0, CTX1, 1)):
                p.op("dve", lambda e, c=c, s=s, c0=c0, c1=c1, col=col: e.scalar_tensor_tensor(
                    out=h[:, c, c0:c1], in0=gsb[:, s, c0:c1], scalar=modv[:, 2 * KC + c, col:col + 1],
                    in1=h[:, c, c0:c1], op0=ALU.mult, op1=ALU.add),
                    reads=[("gsb", s), "modv", ("h", c)], writes=[("h", c)])
    A_l = make_AB(p, "ffl", modv, gvec, 3, 4, 0)
    A_c = make_AB(p, "ffc", modv, gvec, 3, 4, 1)
    hkeys = [("h", c) for c in range(KC)]

    class HK:
        pass
    segs = [(LAT0, LAT1, A_l, lambda c: modv[:, 3 * KC + c, 0:1]), (CTX0, CTX1, A_c, lambda c: modv[:, 3 * KC + c, 1:2])]
    _norm_mod_chunked(p, cm, banks, h, W=WF, segs=segs, out_bf=u, okey="u", rstd=rstd, sq=sq, tmp=tsb, mask=mask,
                      keys_A=["ffl_A", "ffc_A"])

    wgv = wg_d.rearrange("(c p) f -> p c f", p=128)
    wuv = wu_d.rearrange("(c p) f -> p c f", p=128)
    wdv = wd_d.rearrange("(f p) d -> p f d", p=128)
    blocks = col_blocks(0, WF)
    dblocks = [(1, 513, 0), (513, 1025, 0), (1025, 1155, 1)]
    nslab = 0
    slab_of = {}
    ukeys = [("u", c) for c in range(KC)]

    def ensure_slab(fo):
        nonlocal nslab
        sidx = fo // 2
        if sidx in slab_of:
            return slab_of[sidx]
        i0 = (nslab % 2) * 2
        nslab += 1
        f0 = sidx * 256
        f1 = min(FF, f0 + 256)
        p.dma(gus[i0][:, :, 0:f1 - f0], wgv[:, :, f0:f1], writes=[("gus", i0)], q="pool")
        p.dma(gus[i0 + 1][:, :, 0:f1 - f0], wuv[:, :, f0:f1], writes=[("gus", i0 + 1)], q="pool")
        slab_of[sidx] = i0
        return i0

    ngroups = (FC + FG - 1) // FG
    for g in range(ngroups):
        fos = list(range(g * FG, min(FC, (g + 1) * FG)))
        for li_ in range(len(fos)):
            p.dma(wds[:, li_, :], wdv[:, fos[0] + li_, :], writes=["wds"], q="pool")
        for li, fo in enumerate(fos):
            i0 = ensure_slab(fo)
            off = (fo % 2) * 128
            s = fo % 2
            gps = []
            for (b0, b1) in blocks:
                ps, pk = banks.next()
                for c in range(KC):
                    p.op("pe", lambda e, ps=ps, c=c, b0=b0, b1=b1: e.matmul(
                        ps[:, 0:b1 - b0], lhsT=gus[i0][:, c, off:off + 128], rhs=u[:, c, b0:b1],
                        start=(c == 0), stop=(c == KC - 1)),
                        reads=[("gus", i0), ("u", c)], writes=[pk])
                p.op("act", lambda e, ps=ps, b0=b0, b1=b1: e.activation(out=gsb[:, s, b0:b1], in_=ps[:, 0:b1 - b0], func=AF.Copy),
                     reads=[pk], writes=[("gsb", s)])
            ups = []
            for (b0, b1) in blocks:
                ps, pk = banks.next()
                for c in range(KC):
                    p.op("pe", lambda e, ps=ps, c=c, b0=b0, b1=b1: e.matmul(
                        ps[:, 0:b1 - b0], lhsT=gus[i0 + 1][:, c, off:off + 128], rhs=u[:, c, b0:b1],
                        start=(c == 0), stop=(c == KC - 1)),
                        reads=[("gus", i0 + 1), ("u", c)], writes=[pk])
                ups.append((ps, pk, b0, b1))
            n = WF - 2
            p.op("dve", lambda e: e.tensor_scalar(out=tsb[:, s, 1:1 + n], in0=gsb[:, s, 0:n], scalar1=cw[:, fo, 0:1],
                                                  scalar2=None, op0=ALU.mult),
                 reads=[("gsb", s), "cw"], writes=[("tmp", s)])
            for k in (1, 2):
                p.op("dve", lambda e, k=k: e.scalar_tensor_tensor(out=tsb[:, s, 1:1 + n], in0=gsb[:, s, k:k + n],
                                                                  scalar=cw[:, fo, k:k + 1], in1=tsb[:, s, 1:1 + n],
                                                                  op0=ALU.mult, op1=ALU.add),
                     reads=[("gsb", s), "cw", ("tmp", s)], writes=[("tmp", s)])
            p.op("act", lambda e: e.activation(out=tsb[:, s, 1:1 + n], in_=tsb[:, s, 1:1 + n], func=AF.Silu,
                                               bias=cb[:, fo:fo + 1], scale=1.0),
                 reads=[("tmp", s), "cb"], writes=[("tmp", s)])
            for (ps, pk, b0, b1) in ups:
                a0 = max(b0, 1)
                a1 = min(b1, WF - 1)
                p.op("dve", lambda e, ps=ps, b0=b0, a0=a0, a1=a1: e.tensor_tensor(
                    out=act[:, li, a0:a1], in0=tsb[:, s, a0:a1], in1=ps[:, a0 - b0:a1 - b0], op=ALU.mult),
                    reads=[("tmp", s), pk], writes=[("act", li)])
        if g + 1 < ngroups:
            ensure_slab((g + 1) * FG)
        for m in range(KC):
            for (d0, d1, col) in dblocks:
                ps, pk = banks.next()
                for li in range(len(fos)):
                    p.op("pe", lambda e, ps=ps, li=li, d0=d0, d1=d1: e.matmul(
                        ps[:, 0:d1 - d0], lhsT=wds[:, li, m * 128:(m + 1) * 128], rhs=act[:, li, d0:d1],
                        start=(li == 0), stop=(li == len(fos) - 1)),
                        reads=["wds", ("act", li)], writes=[pk])
                if os.environ.get("FFN_EVAC", "dve") == "split":
                    es = nev[0] % 2
                    nev[0] += 1
                    p.op("act", lambda e: e.activation(out=evb[:, es, 0:d1 - d0], in_=ps[:, 0:d1 - d0], func=AF.Copy,
                                                       scale=modv[:, 5 * KC + m, col:col + 1]),
                         reads=[pk, "modv"], writes=[("evb", es)])
                    p.op("pool", lambda e: e.tensor_tensor(out=h[:, m, d0:d1], in0=h[:, m, d0:d1], in1=evb[:, es, 0:d1 - d0], op=ALU.add),
                         reads=[("evb", es), ("h", m)], writes=[("h", m)])
                else:
                    p.op("dve", lambda e, ps=ps, d0=d0, d1=d1, col=col: e.scalar_tensor_tensor(
                        out=h[:, m, d0:d1], in0=ps[:, 0:d1 - d0], scalar=modv[:, 5 * KC + m, col:col + 1],
                        in1=h[:, m, d0:d1], op0=ALU.mult, op1=ALU.add),
                        reads=[pk, "modv", ("h", m)], writes=[("h", m)])
    hov = ho_d.rearrange("(c p) w -> p c w", p=128)
    for c4 in range(0, KC, 4):
        ks = [("h", c) for c in range(c4, c4 + 4)]
        p.dma(hov[:, c4:c4 + 4, 0:1024], h[:, c4:c4 + 4, 1:1025], reads=ks)
        p.dma(hov[:, c4:c4 + 4, 1024:1152], h[:, c4:c4 + 4, 1027:1155], reads=ks)
    if final:
        gf = load_small(p, "gf", [128, KC], gf_d)
        rms_rstd_chunked(p, cm, banks, h, 1, 1025, rstd, sq)
        ofv = of_d.rearrange("(c p) w -> p c w", p=128)
        for c in range(KC):
            s = c % 2
            p.op("dve", lambda e, c=c, s=s: e.tensor_tensor(out=tsb[:, s, 1:1025], in0=h[:, c, 1:1025], in1=rstd[:, 1:1025], op=ALU.mult),
                 reads=[("h", c), "rstd"], writes=[("tmp", s)])
            p.op("act", lambda e, c=c, s=s: e.activation(out=tsb[:, s, 1:1025], in_=tsb[:, s, 1:1025], func=AF.Copy, scale=gf[:, c:c + 1]),
                 reads=[("tmp", s), "gf"], writes=[("tmp", s)])
            p.dma(ofv[:, c, :], tsb[:, s, 1:1025], reads=[("tmp", s)])
    return p.build()


def rms_rstd_chunked(p, cm, banks, h, w0, w1, rstd, sq, hname="h"):
    for (b0, b1) in col_blocks(w0, w1):
        ps, pk = banks.next()
        n = b1 - b0
        for c in range(KC):
            s = c % 2
            p.op("act", lambda e, c=c, s=s: e.activation(out=sq[:, s, 0:n], in_=h[:, c, b0:b1], func=AF.Square),
                 reads=[(hname, c)], writes=[("sq", s)])
            p.op("pe", lambda e, c=c, s=s: e.matmul(ps[:, 0:n], lhsT=cm.ones[:], rhs=sq[:, s, 0:n],
                                                     start=(c == 0), stop=(c == KC - 1)),
                 reads=[("sq", s), "c_ones"], writes=[pk])
        p.op("act", lambda e, ps=ps: e.activation(out=rstd[:, b0:b1], in_=ps[:, 0:n], func=AF.Sqrt,
                                                  bias=cm.eps[:, 0:1], scale=1.0 / D),
             reads=[pk, "c_eps"], writes=["rstd"])
        p.op("dve", lambda e: e.reciprocal(out=rstd[:, b0:b1], in_=rstd[:, b0:b1]), reads=["rstd"], writes=["rstd"])


def _norm_mod_chunked(p, cm, banks, h, W, segs, out_bf, okey, rstd, sq, tmp, mask, keys_A, hname="h", w0=0):
    rms_rstd_chunked(p, cm, banks, h, w0, W, rstd, sq, hname)
    for c in range(KC):
        s = c % 2
        p.op("dve", lambda e, c=c, s=s: e.tensor_tensor(out=tmp[:, s, w0:W], in0=h[:, c, w0:W], in1=rstd[:, w0:W], op=ALU.mult),
             reads=[(hname, c), "rstd"], writes=[("tmp", s)])
        for (c0, c1, A, Bfn) in segs:
            p.op("act", lambda e, c=c, s=s, c0=c0, c1=c1, A=A, Bfn=Bfn: e.activation(
                out=out_bf[:, c, c0:c1], in_=tmp[:, s, c0:c1], func=AF.Identity, scale=A[:, c:c + 1], bias=Bfn(c)),
                reads=[("tmp", s), "modv"] + keys_A, writes=[(okey, c)])
        if mask is not None:
            for (m0, m1) in HALO1:
                p.op("dve", lambda e, c=c: e.tensor_tensor(out=out_bf[:, c, m0:m1], in0=out_bf[:, c, m0:m1], in1=mask[:, m0:m1], op=ALU.mult),
                     reads=[(okey, c), "mask"], writes=[(okey, c)])


def linear_fm(p, banks, name, x, xkey, kcin, wview, dout, blocks, evac, sw=512, q="pool", nbuf=2, wdt=BF16):
    slabs = [p.sb("%s_w%d" % (name, i), [128, kcin, sw], wdt) for i in range(nbuf)]
    ns = (dout + sw - 1) // sw
    for si in range(ns):
        sl = slabs[si % nbuf]
        sk = (name + "_w", si % nbuf)
        f0 = si * sw
        f1 = min(dout, f0 + sw)
        half = kcin // 2 if kcin >= 8 else kcin
        for k0 in range(0, kcin, half):
            p.dma(sl[:, k0:k0 + half, 0:f1 - f0], wview[:, k0:k0 + half, f0:f1], writes=[sk], q=q)
        for mi in range((f1 - f0) // 128):
            m = (f0 // 128) + mi
            for (b0, b1) in blocks:
                ps, pk = banks.next()
                for c in range(kcin):
                    p.op("pe", lambda e: e.matmul(ps[:, 0:b1 - b0], lhsT=sl[:, c, mi * 128:(mi + 1) * 128],
                                                  rhs=x[:, c, b0:b1], start=(c == 0), stop=(c == kcin - 1)),
                         reads=[sk, (xkey, c)], writes=[pk])
                evac(m, ps, pk, b0, b1)


def build_mod():
    p = Prog()
    wm_d = p.dram("wm", [D, 6144])
    bm_d = p.dram("bm", [128, 48])
    ct_d = p.dram("ct", [128, KC, 8])
    mo_d = p.dram("mo", [128, 48, 8], kind="ExternalOutput")
    banks = Banks(p)
    ct = load_small(p, "ct", [128, KC, 8], ct_d)
    bm = load_small(p, "bm", [128, 48], bm_d)
    cb16 = p.sb("cb16", [128, KC, 8], F32)
    mo = p.sb("mo", [128, 48, 8], F32)
    p.op("act", lambda e: e.activation(out=cb16[:], in_=ct[:], func=AF.Silu), reads=["ct"],
         writes=[("cb16", c) for c in range(KC)])

    def evac(m, ps, pk, b0, b1):
        p.op("dve", lambda e: e.tensor_scalar(out=mo[:, m, :], in0=ps[:, 0:8], scalar1=bm[:, m:m + 1], scalar2=None, op0=ALU.add),
             reads=[pk, "bm"], writes=["mo"])
    linear_fm(p, banks, "mod", cb16, "cb16", KC, wm_d.rearrange("(c p) f -> p c f", p=128), 6144, [(0, 8)], evac,
              q="sync", wdt=F32)
    p.dma(mo_d, mo[:], reads=["mo"])
    return p.build()


WP = 1184
P_L0, P_L1, P_C0, P_C1 = 0, 1040, 1040, 1184


def build_pool():
    p = Prog()
    h_d = p.dram("h", [D, WP])
    modv_d = p.dram("modv", [128, 6 * KC, 2])
    g1_d = p.dram("g1", [128, KC])
    mask_d = p.dram("mask", [1, WP])
    icnt_d = p.dram("icnt", [4, WP])
    pw_d = p.dram("pw", [4, 512, 512])
    pb_d = p.dram("pb", [128, KC])
    psc_d = p.dram("psc", [128, KC])
    y_d = p.dram("y", [D, 1152], kind="ExternalOutput")
    cm = Common(p)
    banks = Banks(p)
    h = p.sb("h", [128, KC, WP], F32)
    pbf = p.sb("pbf", [128, KC, WP], BF16)
    tsb = p.sb("tsb", [128, 2, WP], F32)
    t2 = p.sb("t2", [128, 2, WP], F32)
    sq = p.sb("sq", [128, 2, 512], BF16)
    rstd = p.sb("rstd", [128, WP], F32)
    yo = p.sb("yo", [128, 2, WP], F32)
    modv = load_small(p, "modv", [128, 6 * KC, 2], modv_d)
    gvec = load_small(p, "gvec", [128, KC], g1_d)
    pb = load_small(p, "pb", [128, KC], pb_d)
    psc = load_small(p, "psc", [128, KC], psc_d)
    mask = load_small(p, "mask", [128, WP], mask_d[0].partition_broadcast(128))
    icnt = p.sb("icnt", [128, 4, WP], F32)
    for g in range(4):
        p.dma(icnt[:, g, :], icnt_d[g].partition_broadcast(128), writes=["icnt"])
    hv = h_d.rearrange("(c p) w -> p c w", p=128)
    for c4 in range(0, KC, 4):
        p.dma(h[:, c4:c4 + 4, :], hv[:, c4:c4 + 4, :], writes=[("h", c) for c in range(c4, c4 + 4)])
    A_l = make_AB(p, "pl", modv, gvec, 0, 1, 0)
    A_c = make_AB(p, "pc", modv, gvec, 0, 1, 1)
    segs = [(P_L0, P_L1, A_l, lambda c: modv[:, c, 0:1]), (P_C0, P_C1, A_c, lambda c: modv[:, c, 1:2])]
    rms_rstd_chunked(p, cm, banks, h, 0, WP, rstd, sq)
    for c in range(KC):
        s = c % 2
        p.op("dve", lambda e: e.tensor_tensor(out=tsb[:, s, :], in0=h[:, c, :], in1=rstd[:], op=ALU.mult),
             reads=[("h", c), "rstd"], writes=[("tmp", s)])
        for (c0, c1, A, Bfn) in segs:
            p.op("act", lambda e: e.activation(out=h[:, c, c0:c1], in_=tsb[:, s, c0:c1], func=AF.Identity,
                                               scale=A[:, c:c + 1], bias=Bfn(c)),
                 reads=[("tmp", s), "modv", "pl_A", "pc_A"], writes=[("h", c)])
        for (m0, m1) in ((0, 8), (1032, 1048), (1176, 1184)):
            p.op("dve", lambda e: e.tensor_tensor(out=h[:, c, m0:m1], in0=h[:, c, m0:m1], in1=mask[:, m0:m1], op=ALU.mult),
                 reads=[("h", c), "mask"], writes=[("h", c)])
        g = c // 4
        win = (2, 4, 8, 16)[g]
        right = win - 1 - win // 2
        src, skey = h[:, c, :], ("h", c)
        sh = 1
        bufs = [tsb[:, s, :], t2[:, s, :]]
        bkeys = [("tmp", s), ("t2", s)]
        bi = 0
        while sh < win:
            dst, dkey = bufs[bi], bkeys[bi]
            p.op("dve", lambda e: e.tensor_tensor(out=dst[:, sh:WP], in0=src[:, sh:WP], in1=src[:, 0:WP - sh], op=ALU.add),
                 reads=[skey], writes=[dkey])
            p.op("dve", lambda e: e.tensor_copy(out=dst[:, 0:sh], in_=src[:, 0:sh]), reads=[skey], writes=[dkey])
            src, skey = dst, dkey
            sh *= 2
            bi ^= 1
        dst, dkey = bufs[bi], bkeys[bi]
        n = WP - 16
        p.op("dve", lambda e: e.tensor_tensor(out=dst[:, 8:8 + n], in0=src[:, 8 + right:8 + right + n],
                                              in1=icnt[:, g, 8:8 + n], op=ALU.mult),
             reads=[skey, "icnt"], writes=[dkey])
        p.op("dve", lambda e: e.tensor_tensor(out=pbf[:, c, 8:8 + n], in0=dst[:, 8:8 + n], in1=h[:, c, 8:8 + n], op=ALU.subtract),
             reads=[dkey, ("h", c)], writes=[("pbf", c)])
    yv = y_d.rearrange("(c p) w -> p c w", p=128)
    vblocks = [(8, 520), (520, 1032), (1048, 1176)]
    for g in range(4):
        wsl = p.sb("pw%d" % g, [128, 4, 512], BF16)
        p.dma(wsl[:], pw_d[g].rearrange("(c p) f -> p c f", p=128), writes=[("pw", g)], q="pool")
        for mo_ in range(4):
            m = g * 4 + mo_
            s = m % 2
            for (b0, b1) in vblocks:
                ps, pk = banks.next()
                for ci in range(4):
                    p.op("pe", lambda e: e.matmul(ps[:, 0:b1 - b0], lhsT=wsl[:, ci, mo_ * 128:(mo_ + 1) * 128],
                                                  rhs=pbf[:, g * 4 + ci, b0:b1], start=(ci == 0), stop=(ci == 3)),
                         reads=[("pw", g), ("pbf", g * 4 + ci)], writes=[pk])
                p.op("dve", lambda e: e.tensor_scalar(out=yo[:, s, b0:b1], in0=ps[:, 0:b1 - b0], scalar1=pb[:, m:m + 1],
                                                      scalar2=psc[:, m:m + 1], op0=ALU.add, op1=ALU.mult),
                     reads=[pk, "pb", "psc"], writes=[("yo", s)])
            p.dma(yv[:, m, 0:1024], yo[:, s, 8:1032], reads=[("yo", s)])
            p.dma(yv[:, m, 1024:1152], yo[:, s, 1048:1176], reads=[("yo", s)])
    return p.build()


def norm_mod_stream(p, cm, banks, hview, W, segs, out, okey, rstd, sq, hbuf, tmp, akeys, mask=None, out_chunks=None):
    blocks = col_blocks(0, W)
    pss = [banks.next() for _ in blocks]
    for c in range(KC):
        s = c % 2
        p.dma(hbuf[:, s, 0:W], hview[:, c, :], writes=[("hbuf", s)])
        for bi, (b0, b1) in enumerate(blocks):
            ps, pk = pss[bi]
            p.op("act", lambda e: e.activation(out=sq[:, s, 0:b1 - b0], in_=hbuf[:, s, b0:b1], func=AF.Square),
                 reads=[("hbuf", s)], writes=[("sq", s)])
            p.op("pe", lambda e: e.matmul(ps[:, 0:b1 - b0], lhsT=cm.ones[:], rhs=sq[:, s, 0:b1 - b0],
                                          start=(c == 0), stop=(c == KC - 1)),
                 reads=[("sq", s), "c_ones"], writes=[pk])
    for bi, (b0, b1) in enumerate(blocks):
        ps, pk = pss[bi]
        p.op("act", lambda e: e.activation(out=rstd[:, b0:b1], in_=ps[:, 0:b1 - b0], func=AF.Sqrt,
                                           bias=cm.eps[:, 0:1], scale=1.0 / D),
             reads=[pk, "c_eps"], writes=["rstd"])
        p.op("dve", lambda e: e.reciprocal(out=rstd[:, b0:b1], in_=rstd[:, b0:b1]), reads=["rstd"], writes=["rstd"])
    for c in range(KC):
        s = c % 2
        p.dma(hbuf[:, s, 0:W], hview[:, c, :], writes=[("hbuf", s)])
        p.op("dve", lambda e: e.tensor_tensor(out=tmp[:, s, 0:W], in0=hbuf[:, s, 0:W], in1=rstd[:, 0:W], op=ALU.mult),
             reads=[("hbuf", s), "rstd"], writes=[("tmp", s)])
        for (c0, c1, A, Bfn) in segs:
            p.op("act", lambda e: e.activation(out=out[:, c, c0:c1], in_=tmp[:, s, c0:c1], func=AF.Identity,
                                               scale=A[:, c:c + 1], bias=Bfn(c)),
                 reads=[("tmp", s), "modv"] + akeys, writes=[(okey, c)])
        if mask is not None:
            for (m0, m1) in HALO1:
                p.op("dve", lambda e: e.tensor_tensor(out=out[:, c, m0:m1], in0=out[:, c, m0:m1], in1=mask[:, m0:m1], op=ALU.mult),
                     reads=[(okey, c), "mask"], writes=[(okey, c)])


def linear_fm2(p, banks, slabs, skey, x, xkey, kcin, wview, dout, blocks, evac, q="pool"):
    sw = slabs[0].shape[2]
    nbuf = len(slabs)
    ns = (dout + sw - 1) // sw
    for si in range(ns):
        sl = slabs[si % nbuf]
        sk = (skey, si % nbuf)
        f0 = si * sw
        f1 = min(dout, f0 + sw)
        half = kcin // 2 if kcin >= 8 else kcin
        for k0 in range(0, kcin, half):
            p.dma(sl[:, k0:k0 + half, 0:f1 - f0], wview[:, k0:k0 + half, f0:f1], writes=[sk], q=q)
        for mi in range((f1 - f0) // 128):
            m = (f0 // 128) + mi
            for (b0, b1) in blocks:
                ps, pk = banks.next()
                for c in range(kcin):
                    p.op("pe", lambda e: e.matmul(ps[:, 0:b1 - b0], lhsT=sl[:, c, mi * 128:(mi + 1) * 128],
                                                  rhs=x[:, c, b0:b1], start=(c == 0), stop=(c == kcin - 1)),
                         reads=[sk, (xkey, c)], writes=[pk])
                evac(m, ps, pk, b0, b1)


WG = 1152
NPC = 9


def build_gmlp():
    p = Prog()
    h_d = p.dram("h", [D, WG])
    modv_d = p.dram("modv", [128, 6 * KC, 2])
    g1_d = p.dram("g1", [128, KC])
    win_d = p.dram("win", [D, 2 * D])
    bzu_d = p.dram("bzu", [128, KC])
    bzv_d = p.dram("bzv", [1, D])
    ng_d = p.dram("ng", [1, D])
    wst_d = p.dram("wst", [128, 16, 128])
    bs_d = p.dram("bs", [1, 16 * 128])
    wo_d = p.dram("wo", [D, D])
    y_d = p.dram("y", [D, WG], kind="ExternalOutput")
    cm = Common(p)
    banks = Banks(p)
    aT = p.sb("aT", [128, KC, WG], BF16)
    zu = p.sb("zu", [128, KC, WG], BF16)
    zv = p.sb("zv", [128, NPC, D], BF16)
    hbuf = p.sb("hbuf", [128, 2, WG], F32)
    tmp = p.sb("tmp", [128, 2, WG], F32)
    sq = p.sb("sq", [128, 2, 512], BF16)
    rstd = p.sb("rstd", [128, WG], F32)
    slabs = [p.sb("slab%d" % i, [128, KC, 256], BF16) for i in range(2)]
    modv = load_small(p, "modv", [128, 6 * KC, 2], modv_d)
    gvec = load_small(p, "gvec", [128, KC], g1_d)
    bzu = load_small(p, "bzu", [128, KC], bzu_d)
    bzv = load_small(p, "bzv", [128, D], bzv_d[0].partition_broadcast(128))
    ng = load_small(p, "ng", [128, D], ng_d[0].partition_broadcast(128))
    bs = load_small(p, "bs", [128, 16 * 128], bs_d[0].partition_broadcast(128))
    wst = p.sb("wst", [128, 16, 128], BF16)
    p.dma(wst[:], wst_d, writes=["wst"], q="pool")
    ssq = p.sb("ssq", [128, NPC, 8], F32)
    rtok = p.sb("rtok", [128, NPC], F32)
    A_l = make_AB(p, "gl", modv, gvec, 0, 1, 0)
    A_c = make_AB(p, "gc", modv, gvec, 0, 1, 1)
    segs = [(0, 1024, A_l, lambda c: modv[:, c, 0:1]), (1024, WG, A_c, lambda c: modv[:, c, 1:2])]
    norm_mod_stream(p, cm, banks, h_d.rearrange("(c p) w -> p c w", p=128), WG, segs, aT, "aT", rstd, sq, hbuf, tmp,
                    ["gl_A", "gc_A"])
    blocks = col_blocks(0, WG)
    winv = win_d.rearrange("(c p) f -> p c f", p=128)
    import os
    stop = int(os.environ.get("GSTOP", "99"))
    if stop <= 0:
        return p.build()

    def evac_zu(m, ps, pk, b0, b1):
        p.op("act", lambda e: e.activation(out=zu[:, m, b0:b1], in_=ps[:, 0:b1 - b0], func=AF.Gelu_apprx_tanh,
                                           bias=bzu[:, m:m + 1], scale=1.0),
             reads=[pk, "bzu"], writes=[("zu", m)])
    linear_fm2(p, banks, slabs, "slab", aT, "aT", KC, winv[:, :, 0:D], D, blocks, evac_zu)

    if stop <= 1:
        return p.build()
    sw = 256
    for si in range(D // sw):
        sl = slabs[si % 2]
        sk = ("slab", si % 2)
        f0 = D + si * sw
        for k0 in (0, 8):
            p.dma(sl[:, k0:k0 + 8, :], winv[:, k0:k0 + 8, f0:f0 + sw], writes=[sk], q="pool")
        for n in range(NPC):
            ps, pk = banks.next()
            for c in range(KC):
                p.op("pe", lambda e: e.matmul(ps[:, 0:sw], lhsT=aT[:, c, n * 128:(n + 1) * 128], rhs=sl[:, c, :],
                                              start=(c == 0), stop=(c == KC - 1)),
                     reads=[sk, ("aT", c)], writes=[pk])
            s = (si * NPC + n) % 2
            p.op("dve", lambda e: e.tensor_tensor(out=tmp[:, s, 0:sw], in0=ps[:, 0:sw], in1=bzv[:, si * sw:(si + 1) * sw], op=ALU.add),
                 reads=[pk, "bzv"], writes=[("tmp", s)])
            p.op("act", lambda e: e.activation(out=tmp[:, s, 0:sw], in_=tmp[:, s, 0:sw], func=AF.Gelu_apprx_tanh),
                 reads=[("tmp", s)], writes=[("tmp", s)])
            p.op("act", lambda e: e.activation(out=tmp[:, s, 512:512 + sw], in_=tmp[:, s, 0:sw], func=AF.Square,
                                               accum_out=ssq[:, n, si:si + 1]),
                 reads=[("tmp", s)], writes=[("tmp", s), "ssq"])
            p.op("pool", lambda e: e.tensor_copy(out=zv[:, n, si * sw:(si + 1) * sw], in_=tmp[:, s, 0:sw]),
                 reads=[("tmp", s)], writes=[("zv", n)])
    if stop <= 2:
        return p.build()
    p.op("dve", lambda e: e.tensor_reduce(out=rtok[:], in_=ssq[:], axis=AX.X, op=ALU.add), reads=["ssq"], writes=["rtok"])
    p.op("act", lambda e: e.activation(out=rtok[:], in_=rtok[:], func=AF.Sqrt, bias=cm.eps[:, 0:1], scale=1.0 / D),
         reads=["rtok", "c_eps"], writes=["rtok"])
    p.op("dve", lambda e: e.reciprocal(out=rtok[:], in_=rtok[:]), reads=["rtok"], writes=["rtok"])
    for n in range(NPC):
        p.op("dve", lambda e: e.scalar_tensor_tensor(out=zv[:, n, :], in0=zv[:, n, :], scalar=rtok[:, n:n + 1], in1=ng[:],
                                                     op0=ALU.mult, op1=ALU.mult),
             reads=[("zv", n), "rtok", "ng"], writes=[("zv", n)])
    if stop <= 3:
        return p.build()
    bsv = bs[:].rearrange("p (g q) -> p g q", q=128)
    for n in range(NPC):
        for g4 in range(4):
            ps, pk = banks.next()
            for gi in range(4):
                g = g4 * 4 + gi
                p.op("pe", lambda e: e.matmul(ps[:, gi * 128:(gi + 1) * 128], lhsT=zv[:, n, g * 128:(g + 1) * 128],
                                              rhs=wst[:, g, :], start=True, stop=True),
                     reads=[("zv", n), "wst"], writes=[pk])
            s = (n * 4 + g4) % 2
            tv = tmp[:, s, 0:512].rearrange("p (g q) -> p g q", q=128)
            p.op("dve", lambda e: e.tensor_tensor(out=tv, in0=ps[:, 0:512].rearrange("p (g q) -> p g q", q=128),
                                                  in1=bsv[:, g4 * 4:(g4 + 1) * 4, :], op=ALU.add),
                 reads=[pk, "bs"], writes=[("tmp", s)])
            p.op("dve", lambda e: e.tensor_tensor(out=aT[:, g4 * 4:(g4 + 1) * 4, n * 128:(n + 1) * 128], in0=tv,
                                                  in1=zu[:, g4 * 4:(g4 + 1) * 4, n * 128:(n + 1) * 128], op=ALU.mult),
                 reads=[("tmp", s)] + [("zu", g4 * 4 + i) for i in range(4)],
                 writes=[("aT", g4 * 4 + i) for i in range(4)])
    if stop <= 4:
        return p.build()
    yv = y_d.rearrange("(c p) w -> p c w", p=128)
    cnt = [0]

    def evac_y(m, ps, pk, b0, b1):
        s = cnt[0] % 2
        cnt[0] += 1
        p.op("act", lambda e: e.activation(out=hbuf[:, s, b0:b1], in_=ps[:, 0:b1 - b0], func=AF.Copy),
             reads=[pk], writes=[("hbuf", s)])
        if os.environ.get("GNODMA") != "1":
            p.dma(yv[:, m, b0:b1], hbuf[:, s, b0:b1], reads=[("hbuf", s)], q="act")
    linear_fm2(p, banks, slabs, "slab", aT, "aT", KC, wo_d.rearrange("(c p) f -> p c f", p=128), D, blocks, evac_y)
    return p.build()


def build_na1():
    p = Prog()
    W = WG
    h_d = p.dram("h", [D, W])
    modv_d = p.dram("modv", [128, 6 * KC, 2])
    g1_d = p.dram("g1", [128, KC])
    w_d = p.dram("wqkv", [D, 3 * D])
    cos_d = p.dram("rcos", [128, 1024])
    sin_d = p.dram("rsin", [128, 1024])
    pm_d = p.dram("pm", [128, 128])
    q_d = p.dram("qT", [D, W], kind="ExternalOutput")
    k_d = p.dram("kT", [D, W], kind="ExternalOutput")
    v_d = p.dram("v", [W, D], kind="ExternalOutput")
    cm = Common(p)
    banks = Banks(p)
    aT = p.sb("aT", [128, KC, W], BF16)
    hbuf = p.sb("hbuf", [128, 2, W], F32)
    tmp = p.sb("tmp", [128, 2, W], F32)
    st = p.sb("st", [128, 2, W], F32)
    qb = p.sb("qb", [128, 2, 512], BF16)
    sq = p.sb("sq", [128, 2, 512], BF16)
    rstd = p.sb("rstd", [128, W], F32)
    slabs = [p.sb("slab%d" % i, [128, KC, 256], BF16) for i in range(2)]
    modv = load_small(p, "modv", [128, 6 * KC, 2], modv_d)
    gvec = load_small(p, "gvec", [128, KC], g1_d)
    rcos = load_small(p, "rcos", [128, 1024], cos_d)
    rsin = load_small(p, "rsin", [128, 1024], sin_d)
    pm = p.sb("pm", [128, 128], BF16)
    if os.environ.get("NOPM") != "1":
        p.dma(pm[:], pm_d, writes=["pm"], q="pool", max_dma_last_dim=int(os.environ.get("MDL", "512")))
    A_l = make_AB(p, "nl", modv, gvec, 0, 1, 0)
    A_c = make_AB(p, "ncx", modv, gvec, 0, 1, 1)
    segs = [(0, 1024, A_l, lambda c: modv[:, c, 0:1]), (1024, W, A_c, lambda c: modv[:, c, 1:2])]
    norm_mod_stream(p, cm, banks, h_d.rearrange("(c p) w -> p c w", p=128), W, segs, aT, "aT", rstd, sq, hbuf, tmp,
                    ["nl_A", "ncx_A"])
    wv = w_d.rearrange("(c p) f -> p c f", p=128)
    qv = q_d.rearrange("(c p) w -> p c w", p=128)
    kv = k_d.rearrange("(c p) w -> p c w", p=128)
    blocks = col_blocks(0, W)
    cnt = [0]
    stop = int(os.environ.get("GSTOP", "99"))
    if stop <= 0:
        return p.build()

    def evac_qk(m, ps, pk, b0, b1):
        isq = m < KC
        sc = 0.125 if isq else 1.0
        mm = m if isq else m - KC
        s = mm % 2
        n = b1 - b0
        ev = int(os.environ.get("EV", "9"))
        if b0 >= 1024 or ev == 0:
            p.op("act", lambda e: e.activation(out=st[:, s, b0:b1], in_=ps[:, 0:n], func=AF.Copy, scale=sc),
                 reads=[pk], writes=[("st", s, 2)])
        else:
            i = cnt[0] % 2
            cnt[0] += 1
            p.op("act", lambda e: e.activation(out=qb[:, i, 0:n], in_=ps[:, 0:n], func=AF.Copy, scale=sc),
                 reads=[pk], writes=[("qb", i)])
            ps2, pk2 = banks.next()
            p.op("pe", lambda e: e.matmul(ps2[:, 0:n], lhsT=pm[:], rhs=qb[:, i, 0:n], start=True, stop=True),
                 reads=["pm", ("qb", i)], writes=[pk2])
            if ev == 1:
                p.op("act", lambda e: e.activation(out=st[:, s, b0:b1], in_=ps2[:, 0:n], func=AF.Copy, scale=sc),
                     reads=[pk, pk2], writes=[("st", s, b0 // 512)])
                if b1 == W:
                    pass
                return
            p.op("dve", lambda e: e.scalar_tensor_tensor(out=st[:, s, b0:b1], in0=ps[:, 0:n], scalar=sc, in1=rcos[:, b0:b1],
                                                         op0=ALU.mult, op1=ALU.mult),
                 reads=[pk, "rcos"], writes=[("st", s, b0 // 512)])
            p.op("dve", lambda e: e.tensor_tensor(out=tmp[:, i, 0:n], in0=ps2[:, 0:n], in1=rsin[:, b0:b1], op=ALU.mult),
                 reads=[pk2, "rsin"], writes=[("tmp", i)])
            p.op("dve", lambda e: e.tensor_tensor(out=st[:, s, b0:b1], in0=st[:, s, b0:b1], in1=tmp[:, i, 0:n], op=ALU.add),
                 reads=[("tmp", i), ("st", s, b0 // 512)], writes=[("st", s, b0 // 512)])
        if b1 == W:
            dst = qv if isq else kv
            p.dma(dst[:, mm, :], st[:, s, :], reads=[("st", s, 0), ("st", s, 1), ("st", s, 2)], q="act")
    linear_fm2(p, banks, slabs, "slab", aT, "aT", KC, wv[:, :, 0:2 * D], 2 * D, blocks, evac_qk)
    if stop <= 1:
        return p.build()
    sw = 256
    for si in range(D // sw):
        sl = slabs[si % 2]
        sk = ("slab", si % 2)
        f0 = 2 * D + si * sw
        for k0 in (0, 8):
            p.dma(sl[:, k0:k0 + 8, :], wv[:, k0:k0 + 8, f0:f0 + sw], writes=[sk], q="pool")
        for n in range(NPC):
            ps, pk = banks.next()
            for c in range(KC):
                p.op("pe", lambda e: e.matmul(ps[:, 0:sw], lhsT=aT[:, c, n * 128:(n + 1) * 128], rhs=sl[:, c, :],
                                              start=(c == 0), stop=(c == KC - 1)),
                     reads=[sk, ("aT", c)], writes=[pk])
            s = (si * NPC + n) % 2
            p.op("act", lambda e: e.activation(out=tmp[:, s, 0:sw], in_=ps[:, 0:sw], func=AF.Copy),
                 reads=[pk], writes=[("tmp", s)])
            p.dma(v_d[n * 128:(n + 1) * 128, si * sw:(si + 1) * sw], tmp[:, s, 0:sw], reads=[("tmp", s)], q="act")
    return p.build()


NTOK = T + L


def na_rs(r):
    return min(max(r - 4, 0), 24)


def build_na2():
    p = Prog()
    q_d = p.dram("qT", [1024, NTOK])
    k_d = p.dram("kT", [1024, NTOK])
    v_d = p.dram("v", [NTOK, 1024])
    bt_d = p.dram("bt", [16, 128, 15 * 64])
    wo_d = p.dram("wo", [1024, D])
    y_d = p.dram("y", [D, NTOK], kind="ExternalOutput")
    sbanks = Banks(p, 4, "ps_s")
    abanks = Banks(p, 4, "ps_a")
    ones = p.sb("ones", [128, 64], BF16)
    p.op("dve", lambda e: e.memset(ones[:], 1.0), writes=["ones"])
    qT = p.sb("qT", [128, 8, NTOK], BF16)
    kT = p.sb("kT", [128, 8, NTOK], BF16)
    v = p.sb("v", [128, 18, 1024], BF16)
    at = p.sb("at", [128, 8, NTOK], BF16)
    bts = [p.sb("bt%d" % i, [128, 15 * 64], F32) for i in range(2)]
    pts = [p.sb("pt%d" % i, [128, 512], BF16) for i in range(4)]
    sbs = [p.sb("sbb%d" % i, [128, 512], F32) for i in range(2)]
    rec = p.sb("rec", [128, 2, 512], F32)
    st = p.sb("st", [128, 2, 512], F32)
    slabs = [p.sb("slab%d" % i, [128, 8, 512], BF16) for i in range(2)]
    qv = q_d.rearrange("(c p) w -> p c w", p=128)
    kvv = k_d.rearrange("(c p) w -> p c w", p=128)
    vv = v_d.rearrange("(n p) f -> p n f", p=128)
    for c in range(8):
        p.dma(qT[:, c, :], qv[:, c, :], writes=[("qT", c)], q="pool", max_dma_last_dim=4096)
        p.dma(kT[:, c, :], kvv[:, c, :], writes=[("kT", c)], q="pool", max_dma_last_dim=4096)
    for n in range(18):
        p.dma(v[:, n, :], vv[:, n, :], writes=[("v", n)], q="pool")
    npt = [0]
    nsb = [0]
    for h in range(16):
        c = h // 2
        base = (h % 2) * 64
        bt = bts[h % 2]
        bk = ("bt", h % 2)
        p.dma(bt[:], bt_d[h], writes=[bk])
        btv = bt[:].rearrange("p (i q) -> p i q", q=64)
        groups = [(g * 512, 512, g) for g in range(4)] + [(T, L, None)]
        for (q0, nq, g) in groups:
            O, ok = abanks.next()
            Dn, dk = abanks.next()
            items = []
            for j in range(2):
                items.append(("ctx", j))
            if g is not None:
                for kap in range(32):
                    rows = [r for r in range(8 * g, 8 * g + 8) if na_rs(r) <= kap < na_rs(r) + 8]
                    if rows:
                        items.append(("loc", kap, rows[0], rows[-1]))
            pend = None

            def stage2(st2):
                (kind, first, last, pt, ptk, args) = st2
                if kind == "ctx":
                    (j,) = args
                    p.op("pe", lambda e: e.matmul(O[base:base + 64, 0:nq], lhsT=v[:, 16 + j, h * 64:(h + 1) * 64],
                                                  rhs=pt[:, 0:nq], start=first, stop=last),
                         reads=[("v", 16 + j), ptk], writes=[ok], pe_sync=True)
                    p.op("pe", lambda e: e.matmul(Dn[base:base + 64, 0:nq], lhsT=ones[:, :], rhs=pt[:, 0:nq],
                                                  start=first, stop=last),
                         reads=["ones", ptk], writes=[dk], pe_sync=True)
                else:
                    (kap, kb, c0, nn) = args
                    p.op("pe", lambda e: e.matmul(O[base:base + 64, c0:c0 + nn], lhsT=v[kb:kb + 64, kap // 2, h * 64:(h + 1) * 64],
                                                  rhs=pt[kb:kb + 64, 0:nn], start=False, stop=last),
                         reads=[("v", kap // 2), ptk], writes=[ok], pe_sync=True)
                    p.op("pe", lambda e: e.matmul(Dn[base:base + 64, c0:c0 + nn], lhsT=ones[kb:kb + 64, :],
                                                  rhs=pt[kb:kb + 64, 0:nn], start=False, stop=last),
                         reads=["ones", ptk], writes=[dk], pe_sync=True)

            for ii, it in enumerate(items):
                first = ii == 0
                last = ii == len(items) - 1
                S, sk_ = sbanks.next()
                pt = pts[npt[0] % 4]
                ptk = ("pt", npt[0] % 4)
                npt[0] += 1
                if it[0] == "ctx":
                    j = it[1]
                    kc0 = T + j * 128
                    p.op("pe", lambda e: e.matmul(S[:, 0:nq], lhsT=kT[base:base + 64, c, kc0:kc0 + 128],
                                                  rhs=qT[base:base + 64, c, q0:q0 + nq], start=True, stop=True),
                         reads=[("kT", c), ("qT", c)], writes=[sk_])
                    p.op("act", lambda e: e.activation(out=pt[:, 0:nq], in_=S[:, 0:nq], func=AF.Exp),
                         reads=[sk_], writes=[ptk])
                    cur = ("ctx", first, last, pt, ptk, (j,))
                else:
                    _, kap, ra, rb = it
                    kb = (kap % 2) * 64
                    nr = rb - ra + 1
                    nn = nr * 64
                    c0 = ra * 64 - q0
                    idx0 = ra - kap + 7
                    sb_ = sbs[nsb[0] % 2]
                    sbk = ("sbb", nsb[0] % 2)
                    nsb[0] += 1
                    p.op("pe", lambda e: e.matmul(S[kb:kb + 64, 0:nn], lhsT=kT[base:base + 64, c, kap * 64:(kap + 1) * 64],
                                                  rhs=qT[base:base + 64, c, ra * 64:(rb + 1) * 64], start=True, stop=True),
                         reads=[("kT", c), ("qT", c)], writes=[sk_])
                    p.op("dve", lambda e: e.tensor_tensor(out=sb_[kb:kb + 64, 0:nn].rearrange("p (i q) -> p i q", q=64),
                                                          in0=S[kb:kb + 64, 0:nn].rearrange("p (i q) -> p i q", q=64),
                                                          in1=btv[kb:kb + 64, idx0:idx0 + nr, :], op=ALU.add),
                         reads=[sk_, bk], writes=[sbk])
                    p.op("act", lambda e: e.activation(out=pt[kb:kb + 64, 0:nn], in_=sb_[kb:kb + 64, 0:nn], func=AF.Exp),
                         reads=[sbk], writes=[ptk])
                    cur = ("loc", first, last, pt, ptk, (kap, kb, c0, nn))
                if pend is not None:
                    stage2(pend)
                pend = cur
            stage2(pend)
            ri = (h * 5 + (g if g is not None else 4)) % 2
            p.op("dve", lambda e: e.reciprocal(out=rec[base:base + 64, ri, 0:nq], in_=Dn[base:base + 64, 0:nq]),
                 reads=[dk], writes=[("rec", ri)])
            p.op("dve", lambda e: e.tensor_tensor(out=at[base:base + 64, c, q0:q0 + nq], in0=O[base:base + 64, 0:nq],
                                                  in1=rec[base:base + 64, ri, 0:nq], op=ALU.mult),
                 reads=[ok, ("rec", ri)], writes=[("at", c)])
    yv = y_d.rearrange("(c p) w -> p c w", p=128)
    cnt = [0]

    def evac_y(m, ps, pk, b0, b1):
        s = cnt[0] % 2
        cnt[0] += 1
        p.op("act", lambda e: e.activation(out=st[:, s, 0:b1 - b0], in_=ps[:, 0:b1 - b0], func=AF.Copy),
             reads=[pk], writes=[("st", s)])
        p.dma(yv[:, m, b0:b1], st[:, s, 0:b1 - b0], reads=[("st", s)], q="act")
    linear_fm2(p, sbanks, slabs, "slab", at, "at", 8, wo_d.rearrange("(c p) f -> p c f", p=128), D, col_blocks(0, NTOK), evac_y)
    return p.build()


def _chunked(vv):
    vv = np.asarray(vv, np.float32)
    return np.ascontiguousarray(vv.reshape(-1, 128).T)


def rope_tables(t0, n):
    half = 32
    inv_freq = (10000.0 ** (-np.arange(0, half, 2, dtype=np.float32) / half)).astype(np.float32)
    pos = np.arange(t0, t0 + n)
    row = (pos // 64).astype(np.float32)
    col = (pos % 64).astype(np.float32)
    cos = np.zeros((64, n), np.float32)
    sin = np.zeros((64, n), np.float32)
    for d in range(64):
        pp = row if d < 32 else col
        ang = pp * inv_freq[d % 16]
        cos[d] = np.cos(ang)
        sgn = -1.0 if (d % 32) < 16 else 1.0
        sin[d] = sgn * np.sin(ang)
    return np.concatenate([cos, cos], 0), np.concatenate([sin, sin], 0)


def rope_perm():
    pm = np.zeros((128, 128), np.float32)
    for m in range(128):
        d = m % 32
        partner = m + 16 if d < 16 else m - 16
        pm[partner, m] = 1.0
    return pm


def na_bias_tables(rpb, heads):
    cq = np.arange(64)
    ck = np.arange(64)
    cs = np.clip(cq - 8, 0, 48)
    ok = (ck[:, None] >= cs[None, :]) & (ck[:, None] < cs[None, :] + 16)
    dc = np.clip(ck[:, None] - cq[None, :], -15, 15) + 15
    out = np.empty((len(heads), 128, 15, 64), np.float32)
    for i, h in enumerate(heads):
        for idx in range(15):
            tab = rpb[h, 14 - idx][dc]
            tab = np.where(ok, tab, np.float32(-30000.0))
            out[i, 0:64, idx] = tab
            out[i, 64:128, idx] = tab
    return out.reshape(len(heads), 128, 15 * 64)


RW_LORA = 96
RW_GATE = 256
C64 = 64


def shift_mix(p, a, akey_fn, xo, xkey, coef, n_idx, j0, n):
    for c in range(KC):
        p.op("dve", lambda e: e.tensor_scalar(out=xo[:, c, 0:n], in0=a[:, c, j0:j0 + n], scalar1=coef[:, 0, n_idx, c:c + 1],
                                              scalar2=None, op0=ALU.mult),
             reads=[akey_fn(c), "coef"], writes=[(xkey, c)])
        p.op("dve", lambda e: e.scalar_tensor_tensor(out=xo[:, c, 0:n], in0=a[:, c, j0 - 1:j0 - 1 + n], scalar=coef[:, 1, n_idx, c:c + 1],
                                                     in1=xo[:, c, 0:n], op0=ALU.mult, op1=ALU.add),
             reads=[akey_fn(c), "coef", (xkey, c)], writes=[(xkey, c)])
        p.op("dve", lambda e: e.scalar_tensor_tensor(out=xo[:, c, 0:n], in0=a[:, c, j0 + 1:j0 + 1 + n], scalar=coef[:, 2, n_idx, c:c + 1],
                                                     in1=xo[:, c, 0:n], op0=ALU.mult, op1=ALU.add),
             reads=[akey_fn(c), "coef", (xkey, c)], writes=[(xkey, c)])


def load_coef(p, coef_d):
    coef = p.sb("coef", [128, 3, 6, KC], F32)
    p.dma(coef[:, 1:3, :, :], coef_d, writes=["coef"])
    p.op("dve", lambda e: e.tensor_scalar(out=coef[:, 0, :, :], in0=coef[:, 1, :, :], scalar1=-1.0, scalar2=1.0, op0=ALU.mult, op1=ALU.add),
         reads=["coef"], writes=["coef"])
    p.op("dve", lambda e: e.tensor_tensor(out=coef[:, 0, :, :], in0=coef[:, 0, :, :], in1=coef[:, 2, :, :], op=ALU.subtract),
         reads=["coef"], writes=["coef"])
    return coef


def build_rw1():
    p = Prog()
    W = WF
    h_d = p.dram("h", [D, W])
    modv_d = p.dram("modv", [128, 6 * KC, 2])
    g1_d = p.dram("g1", [128, KC])
    mask_d = p.dram("mask", [1, W])
    coef_d = p.dram("coef", [128, 2, 6, KC])
    wrkv_d = p.dram("wrkv", [3, KC, 128, KC * 128])
    w1_d = p.dram("w1", [2, D, RW_LORA])
    w2_d = p.dram("w2", [2, RW_LORA, D])
    a1_d = p.dram("a1", [2, D, RW_LORA])
    a2_d = p.dram("a2", [2, RW_LORA, D])
    vecs_d = p.dram("vecs", [128, 7, KC])
    bones_d = p.dram("bones", [128, 128])
    rmask_d = p.dram("rmask", [1, 512])
    outs = {}
    outs["vt"] = p.dram("vt", [D, 1152], BF16, kind="ExternalOutput")
    for d in range(2):
        for nm in ("at", "bt", "kt", "rt"):
            outs[(nm, d)] = p.dram("%s%d" % (nm, d), [D, 1152], BF16, kind="ExternalOutput")
    bv_d = [p.dram("bv%d" % d, [D, 1152], kind="ExternalOutput") for d in range(2)]
    gam_d = [p.dram("gam%d" % d, [D, 18], kind="ExternalOutput") for d in range(2)]
    cm = Common(p)
    banks = Banks(p)
    a = p.sb("a", [128, KC, W], BF16)
    hbuf = p.sb("hbuf", [128, 2, W], F32)
    tmp = p.sb("tmp", [128, 2, W], F32)
    sq = p.sb("sq", [128, 2, 512], BF16)
    rstd = p.sb("rstd", [128, W], F32)
    modv = load_small(p, "modv", [128, 6 * KC, 2], modv_d)
    gvec = load_small(p, "gvec", [128, KC], g1_d)
    mask = load_small(p, "mask", [128, W], mask_d[0].partition_broadcast(128))
    coef = load_coef(p, coef_d)
    vecs = load_small(p, "vecs", [128, 7, KC], vecs_d)
    rmask = load_small(p, "rmask", [128, 512], rmask_d[0].partition_broadcast(128))
    bones = p.sb("bones", [128, 128], BF16)
    p.dma(bones[:], bones_d, writes=["bones"], q="pool", max_dma_last_dim=512)
    w1 = [p.sb("w1_%d" % d, [128, KC, RW_LORA], BF16) for d in range(2)]
    a1 = [p.sb("a1_%d" % d, [128, KC, RW_LORA], BF16) for d in range(2)]
    w2 = [p.sb("w2_%d" % d, [RW_LORA, D], BF16) for d in range(2)]
    a2 = [p.sb("a2_%d" % d, [RW_LORA, D], BF16) for d in range(2)]
    for d in range(2):
        p.dma(w1[d][:], w1_d[d].rearrange("(c p) f -> p c f", p=128), writes=[("w1", d)], q="pool")
        p.dma(a1[d][:], a1_d[d].rearrange("(c p) f -> p c f", p=128), writes=[("a1", d)], q="pool")
        p.dma(w2[d][:], w2_d[d], writes=[("w2", d)], q="pool")
        p.dma(a2[d][:], a2_d[d], writes=[("a2", d)], q="pool")
    A_l = make_AB(p, "rl", modv, gvec, 0, 1, 0)
    A_c = make_AB(p, "rc", modv, gvec, 0, 1, 1)
    segs = [(LAT0, LAT1, A_l, lambda c: modv[:, c, 0:1]), (CTX0, CTX1, A_c, lambda c: modv[:, c, 1:2])]
    norm_mod_stream(p, cm, banks, h_d.rearrange("(c p) w -> p c w", p=128), W, segs, a, "a", rstd, sq, hbuf, tmp,
                    ["rl_A", "rc_A"], mask=mask)
    akey = lambda c: ("a", c)
    xs = {nm: p.sb("x_" + nm, [128, KC, 512], BF16) for nm in ("k", "v", "r")}
    tw = [p.sb("tw%d" % d, [RW_LORA, 512], BF16) for d in range(2)]
    ta = [p.sb("ta%d" % d, [RW_LORA, 512], BF16) for d in range(2)]
    NT_ = 14
    ft = [hbuf[:, 0, 0:512], hbuf[:, 0, 512:1024], hbuf[:, 1, 0:512], hbuf[:, 1, 512:1024],
          tmp[:, 0, 0:512], tmp[:, 0, 512:1024], tmp[:, 1, 0:512], tmp[:, 1, 512:1024]]
    ft += [p.sb("ft%d" % i, [128, 512], F32)[:] for i in range(8, NT_)]
    fk = [("ft", i) for i in range(NT_)]
    fence = p.sb("fence", [128, 1], F32)
    p.op("dve", lambda e: e.memset(fence[:], 0.0), reads=[("hbuf", 0), ("hbuf", 1), ("tmp", 0), ("tmp", 1)], writes=fk[0:8])
    ob = {nm: p.sb("ob_" + nm, [128, 2, 512], BF16) for nm in ("at", "bt", "kt", "rt", "vt")}
    sqb = p.sb("sqb", [128, 2, 512], BF16)
    gam = [p.sb("gamt%d" % d, [128, KC, 18], F32) for d in range(2)]
    slabs = {nm: [p.sb("sl_%s%d" % (nm, i), [128, KC, 128], BF16) for i in range(2)] for nm in ("r", "k", "v")}
    widx = {"r": 0, "k": 1, "v": 2}
    cblocks = [(1, 512, 0), (513, 512, 512), (1027, 128, 1024)]
    nslab = [0]
    for (j0, n, o0) in cblocks:
        nch = n // C64
        for (kind, n_idx, fn) in (("w", 1, AF.Tanh), ("a", 4, AF.Copy)):
            shift_mix(p, a, akey, xs["r"], "x_r", coef, n_idx, j0, n)
            for d in range(2):
                lw, lkey = (w1[d], ("w1", d)) if kind == "w" else (a1[d], ("a1", d))
                lt, ltkey = (tw[d], ("tw", d)) if kind == "w" else (ta[d], ("ta", d))
                ps, pk = banks.next()
                for c in range(KC):
                    p.op("pe", lambda e: e.matmul(ps[0:RW_LORA, 0:n], lhsT=lw[:, c, :], rhs=xs["r"][:, c, 0:n],
                                                  start=(c == 0), stop=(c == KC - 1)),
                         reads=[lkey, ("x_r", c)], writes=[pk])
                p.op("act", lambda e: e.activation(out=lt[:, 0:n], in_=ps[0:RW_LORA, 0:n], func=fn), reads=[pk], writes=[ltkey])
        shift_mix(p, a, akey, xs["k"], "x_k", coef, 2, j0, n)
        shift_mix(p, a, akey, xs["v"], "x_v", coef, 3, j0, n)
        shift_mix(p, a, akey, xs["r"], "x_r", coef, 0, j0, n)
        for m in range(KC):
            cur = {}
            for nm in ("r", "k", "v"):
                i = nslab[0] % 2
                sl = slabs[nm][i]
                sk = ("sl_" + nm, i)
                p.dma(sl[:].rearrange("p c f -> p (c f)"), wrkv_d[widx[nm], m], writes=[sk], q="pool")
                cur[nm] = (sl, sk)
            nslab[0] += 1
            s2 = m % 2
            T_ = lambda i: ft[i][:, 0:n]
            pss = {}
            for nm in ("k", "v", "r"):
                ps, pk = banks.next()
                sl, sk = cur[nm]
                for c in range(KC):
                    p.op("pe", lambda e: e.matmul(ps[:, 0:n], lhsT=sl[:, c, :], rhs=xs[nm][:, c, 0:n],
                                                  start=(c == 0), stop=(c == KC - 1)),
                         reads=[sk, ("x_" + nm, c)], writes=[pk])
                pss[nm] = (ps, pk)
            p.op("act", lambda e: e.activation(out=T_(0), in_=pss["k"][0][:, 0:n], func=AF.Copy), reads=[pss["k"][1]], writes=[fk[0]])
            p.op("act", lambda e: e.activation(out=T_(1), in_=pss["v"][0][:, 0:n], func=AF.Copy), reads=[pss["v"][1]], writes=[fk[1]])
            p.op("act", lambda e: e.activation(out=T_(2), in_=pss["r"][0][:, 0:n], func=AF.Copy), reads=[pss["r"][1]], writes=[fk[2]])
            p.op("pool", lambda e: e.tensor_copy(out=ob["vt"][:, s2, 0:n], in_=T_(1)), reads=[fk[1]], writes=[("ob_vt", s2)])
            ovv = outs["vt"].rearrange("(c p) w -> p c w", p=128)
            p.dma(ovv[:, m, o0:o0 + n], ob["vt"][:, s2, 0:n], reads=[("ob_vt", s2)])
            p.op("dve", lambda e: e.tensor_scalar(out=T_(5), in0=T_(0), scalar1=vecs[:, 4, m:m + 1], scalar2=None, op0=ALU.mult),
                 reads=[fk[0], "vecs"], writes=[fk[5]])
            p.op("act", lambda e: e.activation(out=sqb[:, 0, 0:n], in_=T_(5), func=AF.Square), reads=[fk[5]], writes=[("sqb", 0)])
            ps_n, pk_n = banks.next()
            p.op("pe", lambda e: e.matmul(ps_n[:, 0:n], lhsT=bones[:], rhs=sqb[:, 0, 0:n], start=True, stop=True),
                 reads=["bones", ("sqb", 0)], writes=[pk_n])
            p.op("act", lambda e: e.activation(out=T_(6), in_=ps_n[:, 0:n], func=AF.Sqrt), reads=[pk_n], writes=[fk[6]])
            p.op("dve", lambda e: e.tensor_scalar(out=T_(6), in0=T_(6), scalar1=1e-6, scalar2=None, op0=ALU.max),
                 reads=[fk[6]], writes=[fk[6]])
            p.op("dve", lambda e: e.reciprocal(out=T_(6), in_=T_(6)), reads=[fk[6]], writes=[fk[6]])
            p.op("dve", lambda e: e.tensor_tensor(out=T_(5), in0=T_(5), in1=T_(6), op=ALU.mult), reads=[fk[5], fk[6]], writes=[fk[5]])
            for d in range(2):
                ps_s, pk_s = banks.next()
                p.op("pe", lambda e: e.matmul(ps_s[:, 0:n], lhsT=w2[d][:, m * 128:(m + 1) * 128], rhs=tw[d][:, 0:n], start=True, stop=True),
                     reads=[("w2", d), ("tw", d)], writes=[pk_s])
                ps_a, pk_a = banks.next()
                p.op("pe", lambda e: e.matmul(ps_a[:, 0:n], lhsT=a2[d][:, m * 128:(m + 1) * 128], rhs=ta[d][:, 0:n], start=True, stop=True),
                     reads=[("a2", d), ("ta", d)], writes=[pk_a])
                p.op("act", lambda e: e.activation(out=T_(3), in_=ps_s[:, 0:n], func=AF.Sigmoid, bias=vecs[:, 0 + d, m:m + 1], scale=1.0),
                     reads=[pk_s, "vecs"], writes=[fk[3]])
                p.op("act", lambda e: e.activation(out=T_(4), in_=ps_a[:, 0:n], func=AF.Sigmoid, bias=vecs[:, 2 + d, m:m + 1], scale=1.0),
                     reads=[pk_a, "vecs"], writes=[fk[4]])
                p.op("dve", lambda e: e.tensor_scalar(out=T_(3), in0=T_(3), scalar1=-0.6065306597126334, scalar2=None, op0=ALU.mult),
                     reads=[fk[3]], writes=[fk[3]])
                p.op("dve", lambda e: e.tensor_tensor_scan(out=T_(7), data0=rmask[:, 0:n], data1=T_(3), initial=0.0,
                                                           op0=ALU.mult, op1=ALU.add),
                     reads=["rmask", fk[3]], writes=[fk[7]])
                if d == 0:
                    p.op("dve", lambda e: e.tensor_tensor(out=T_(8), in0=T_(7), in1=T_(3), op=ALU.subtract), reads=[fk[7], fk[3]], writes=[fk[8]])
                    p.op("pool", lambda e: e.tensor_copy(out=gam[d][:, m, o0 // C64:o0 // C64 + nch],
                                                         in_=ft[7][:, 0:n].rearrange("p (a b) -> p a b", b=C64)[:, :, C64 - 1]),
                         reads=[fk[7]], writes=[("gam", d)])
                else:
                    P3 = ft[7][:, 0:n].rearrange("p (a b) -> p a b", b=C64)
                    p.op("pool", lambda e: e.tensor_copy(out=gam[d][:, m, o0 // C64:o0 // C64 + nch], in_=P3[:, :, C64 - 1]),
                         reads=[fk[7]], writes=[("gam", d)])
                    tot = gam[d][:, m, o0 // C64:o0 // C64 + nch].unsqueeze(2).broadcast_to([128, nch, C64])
                    p.op("dve", lambda e: e.tensor_tensor(out=ft[8][:, 0:n].rearrange("p (a b) -> p a b", b=C64), in0=tot, in1=P3, op=ALU.subtract),
                         reads=[fk[7], ("gam", d)], writes=[fk[8]])
                    p.op("dve", lambda e: e.tensor_tensor(out=T_(7), in0=T_(8), in1=T_(3), op=ALU.add), reads=[fk[8], fk[3]], writes=[fk[7]])
                p.op("act", lambda e: e.activation(out=T_(8), in_=T_(8), func=AF.Exp), reads=[fk[8]], writes=[fk[8]])
                p.op("act", lambda e: e.activation(out=T_(9), in_=T_(7), func=AF.Exp, scale=-1.0), reads=[fk[7]], writes=[fk[9]])
                p.op("act", lambda e: e.activation(out=T_(7), in_=T_(7), func=AF.Exp), reads=[fk[7]], writes=[fk[7]])
                p.op("dve", lambda e: e.tensor_scalar(out=T_(10), in0=T_(4), scalar1=-1.0, scalar2=vecs[:, 5, m:m + 1], op0=ALU.add, op1=ALU.mult),
                     reads=[fk[4], "vecs"], writes=[fk[10]])
                p.op("dve", lambda e: e.scalar_tensor_tensor(out=T_(10), in0=T_(10), scalar=1.0, in1=T_(0), op0=ALU.add, op1=ALU.mult),
                     reads=[fk[10], fk[0]], writes=[fk[10]])
                p.op("dve", lambda e: e.scalar_tensor_tensor(out=ob["at"][:, d, 0:n], in0=T_(5), scalar=-1.0, in1=T_(8), op0=ALU.mult, op1=ALU.mult),
                     reads=[fk[5], fk[8]], writes=[("ob_at", d)])
                p.op("dve", lambda e: e.tensor_tensor(out=T_(11), in0=T_(5), in1=T_(4), op=ALU.mult), reads=[fk[5], fk[4]], writes=[fk[11]])
                p.op("dve", lambda e: e.tensor_tensor(out=ob["bt"][:, d, 0:n], in0=T_(11), in1=T_(9), op=ALU.mult),
                     reads=[fk[11], fk[9]], writes=[("ob_bt", d)])
                p.op("dve", lambda e: e.tensor_tensor(out=ob["kt"][:, d, 0:n], in0=T_(10), in1=T_(9), op=ALU.mult),
                     reads=[fk[10], fk[9]], writes=[("ob_kt", d)])
                p.op("dve", lambda e: e.tensor_tensor(out=ob["rt"][:, d, 0:n], in0=T_(2), in1=T_(7), op=ALU.mult),
                     reads=[fk[2], fk[7]], writes=[("ob_rt", d)])
                p.op("dve", lambda e: e.scalar_tensor_tensor(out=sqb[:, 1, 0:n], in0=T_(2), scalar=vecs[:, 6, m:m + 1], in1=T_(10), op0=ALU.mult, op1=ALU.mult),
                     reads=[fk[2], fk[10], "vecs"], writes=[("sqb", 1)])
                ps_b, pk_b = banks.next()
                p.op("pe", lambda e: e.matmul(ps_b[:, 0:n], lhsT=bones[:], rhs=sqb[:, 1, 0:n], start=True, stop=True),
                     reads=["bones", ("sqb", 1)], writes=[pk_b])
                i12 = 12 + d
                p.op("dve", lambda e: e.tensor_tensor(out=T_(i12), in0=ps_b[:, 0:n], in1=T_(1), op=ALU.mult), reads=[pk_b, fk[1]], writes=[fk[i12]])
                for nm in ("at", "bt", "kt", "rt"):
                    ov = outs[(nm, d)].rearrange("(c p) w -> p c w", p=128)
                    p.dma(ov[:, m, o0:o0 + n], ob[nm][:, d, 0:n], reads=[("ob_" + nm, d)])
                bvv = bv_d[d].rearrange("(c p) w -> p c w", p=128)
                p.dma(bvv[:, m, o0:o0 + n], T_(i12), reads=[fk[i12]])
    for d in range(2):
        p.op("act", lambda e: e.activation(out=gam[d][:], in_=gam[d][:], func=AF.Exp), reads=[("gam", d)], writes=[("gam", d)])
        p.dma(gam_d[d].rearrange("(c p) w -> p c w", p=128), gam[d][:], reads=[("gam", d)])
    return p.build()


NCH = NTOK // C64
RW2_MASK_ENG = os.environ.get("RW2_MASK_ENG", "dve")
NHG = 4


def build_rw2():
    p = Prog()
    far_d = p.dram("far", [NCH, 64, 32 * 2 * 64], BF16)
    fbk_d = p.dram("fbk", [NCH, 64, 2 * 32 * 64], BF16)
    tm_d = p.dram("tm", [NCH, 64, 3 * D], BF16)
    gam_d = p.dram("gam", [64, 32, NCH])
    mka_d = p.dram("mka", [64, 2 * 64])
    mkp_d = p.dram("mkp", [64, 2 * 64])
    mkl_d = p.dram("mkl", [64, 7 * 64])
    y_d = p.dram("y", [NCH, 64, 32 * 64], kind="ExternalOutput")
    banks = Banks(p)
    FAR = [p.sb("far%d" % i, [64, 32, 2, 64], BF16) for i in range(2)]
    FBK = [p.sb("fbk%d" % i, [64, 2, 32, 64], BF16) for i in range(2)]
    TM = [p.sb("tm%d" % i, [64, 3, D], BF16) for i in range(2)]
    gam = load_small(p, "gam", [64, 32, NCH], gam_d)
    mka = load_small(p, "mka", [64, 2, 64], mka_d.rearrange("p (a b) -> p a b", b=64))
    mkp = load_small(p, "mkp", [64, 2, 64], mkp_d.rearrange("p (a b) -> p a b", b=64))
    mkl = load_small(p, "mkl", [64, 7, 64], mkl_d.rearrange("p (a b) -> p a b", b=64))
    mklb = p.sb("mklb", [64, 7, 64], BF16)
    p.op("dve", lambda e: e.tensor_copy(out=mklb[:], in_=mkl[:]), reads=["mkl"], writes=["mklb"])
    S = p.sb("S", [64, 32, 64], F32)
    Sb = p.sb("Sb", [64, 32, 64], BF16)
    Sl = p.sb("Sl", [64, 32, 64], BF16)
    Sd = p.sb("Sd", [64, 32, 64], F32)
    yst = [p.sb("yst%d" % i, [64, 32, 64], F32) for i in range(2)]
    QA = [p.sb("QA%d" % g, [64, 8, 2, 64], BF16) for g in range(NHG)]
    QK = [p.sb("QK%d" % g, [64, 8, 2, 64], BF16) for g in range(NHG)]
    NM = [[p.sb("NM%d_%d" % (g, l), [64, 8, 64], BF16) for l in range(6)] for g in range(NHG)]
    Gf = [p.sb("Gf%d" % g, [64, 8, 64], F32) for g in range(NHG)]
    Hf = [p.sb("Hf%d" % g, [64, 8, 64], F32) for g in range(NHG)]
    Gb = [p.sb("Gb%d" % g, [64, 8, 64], BF16) for g in range(NHG)]
    Hb = [p.sb("Hb%d" % g, [64, 8, 64], BF16) for g in range(NHG)]
    Wb = [p.sb("Wb%d" % g, [64, 8, 64], BF16) for g in range(NHG)]
    Rb = [p.sb("Rb%d" % g, [64, 8, 64], BF16) for g in range(NHG)]
    Xb = [p.sb("Xb%d" % g, [64, 8, 64], BF16) for g in range(NHG)]
    p.op("dve", lambda e: e.memset(S[:], 0.0), writes=[("S", g) for g in range(NHG)])
    p.op("dve", lambda e: e.memset(Sb[:], 0.0), writes=[("Sb", g) for g in range(NHG)])
    p.op("dve", lambda e: e.memset(Sl[:], 0.0), writes=[("Sl", g) for g in range(NHG)])
    v3 = lambda ap: ap.rearrange("p (a b) -> p a b", b=64)
    bc8 = lambda ap2: ap2.unsqueeze(1).broadcast_to([64, 8, 64])
    for ci in range(NCH):
        s = ci % 2
        far, fbk, tm = FAR[s], FBK[s], TM[s]
        p.dma(far[:].rearrange("p a b c -> p (a b c)"), far_d[ci], writes=[("far", s)])
        p.dma(fbk[:].rearrange("p a b c -> p (a b c)"), fbk_d[ci], writes=[("fbk", s)])
        p.dma(tm[:].rearrange("p a b -> p (a b)"), tm_d[ci], writes=[("tm", s)])
        kfar, kfbk, ktm = ("far", s), ("fbk", s), ("tm", s)
        for g in range(NHG):
            ps, pk = banks.next()
            for hh in range(8):
                h = g * 8 + hh
                p.op("pe", lambda e: e.matmul(ps[0:64, hh * 64:(hh + 1) * 64], lhsT=far[:, h, 0, :], rhs=fbk[:, 0, h, :], start=True, stop=True),
                     reads=[kfar, kfbk], writes=[pk])
            p.op("dve", lambda e: e.tensor_tensor(out=Gf[g][:], in0=v3(ps[0:64, :]), in1=bc8(mkp[:, 0, :]), op=ALU.mult),
                 reads=[pk, "mkp"], writes=[("Gf", g)])
            p.op(RW2_MASK_ENG, lambda e: e.tensor_tensor(out=Gf[g][:], in0=Gf[g][:], in1=bc8(mkp[:, 1, :]), op=ALU.add),
                 reads=[("Gf", g), "mkp"], writes=[("Gf", g)])
            p.op("act", lambda e: e.activation(out=Gb[g][:], in_=Gf[g][:], func=AF.Copy), reads=[("Gf", g)], writes=[("Gb", g)])
            for (dst, dkey, which) in ((QA[g], ("QA", g), 0), (QK[g], ("QK", g), 1)):
                for half in range(2):
                    ps, pk = banks.next()
                    for hq in range(4):
                        h = g * 8 + half * 4 + hq
                        p.op("pe", lambda e: e.matmul(ps[0:64, hq * 128:(hq + 1) * 128], lhsT=fbk[:, which, h, :],
                                                      rhs=far[:, h, :, :].rearrange("p a b -> p (a b)"), start=True, stop=True),
                             reads=[kfar, kfbk], writes=[pk])
                    p.op("dve", lambda e: e.tensor_tensor(
                        out=dst[:, half * 4:(half + 1) * 4, :, :],
                        in0=ps[0:64, :].rearrange("p (h a b) -> p h a b", a=2, b=64),
                        in1=mka[:].unsqueeze(1).broadcast_to([64, 4, 2, 64]), op=ALU.mult),
                        reads=[pk, "mka"], writes=[dkey])
            NT0 = QA[g][:, :, 0, :]
            for l in range(6):
                p.op(RW2_MASK_ENG, lambda e: e.tensor_tensor(out=NM[g][l][:], in0=NT0, in1=bc8(mklb[:, l, :]), op=ALU.mult),
                     reads=[("QA", g), "mklb"], writes=[("NM", g, l)])
            p.op(RW2_MASK_ENG, lambda e: e.tensor_tensor(out=Hf[g][:], in0=NM[g][0][:], in1=bc8(mkl[:, 6, :]), op=ALU.add),
                 reads=[("NM", g, 0), "mkl"], writes=[("Hf", g)])
            p.op("act", lambda e: e.activation(out=Hb[g][:], in_=Hf[g][:], func=AF.Copy), reads=[("Hf", g)], writes=[("Hb", g)])
        for g in range(NHG):
            ps, pk = banks.next()
            for hh in range(8):
                h = g * 8 + hh
                o = ps[0:64, hh * 64:(hh + 1) * 64]
                p.op("pe", lambda e: e.matmul(o, lhsT=far[:, h, 0, :], rhs=Sb[:, h, :], start=True, stop=False),
                     reads=[kfar, ("Sb", g)], writes=[pk])
                p.op("pe", lambda e: e.matmul(o, lhsT=far[:, h, 0, :], rhs=Sl[:, h, :], start=False, stop=False),
                     reads=[kfar, ("Sl", g)], writes=[pk])
                p.op("pe", lambda e: e.matmul(o, lhsT=QK[g][:, hh, 0, :], rhs=tm[:, 0, h * 64:(h + 1) * 64], start=False, stop=True),
                     reads=[("QK", g), ktm], writes=[pk])
            p.op("act", lambda e: e.activation(out=Rb[g][:], in_=v3(ps[0:64, :]), func=AF.Copy), reads=[pk], writes=[("Rb", g)])
        for l in range(1, 6):
            for g in range(NHG):
                ps, pk = banks.next()
                for hh in range(8):
                    p.op("pe", lambda e: e.matmul(ps[0:64, hh * 64:(hh + 1) * 64], lhsT=NM[g][l][:, hh, :], rhs=Gb[g][:, hh, :], start=True, stop=True),
                         reads=[("NM", g, l), ("Gb", g)], writes=[pk])
                p.op("act", lambda e: e.activation(out=Wb[g][:], in_=v3(ps[0:64, :]), func=AF.Copy), reads=[pk], writes=[("Wb", g)])
            zz = []
            for g in range(NHG):
                psz = pkz = None
                if l < 5:
                    psz, pkz = banks.next()
                    for hh in range(8):
                        p.op("pe", lambda e: e.matmul(psz[0:64, hh * 64:(hh + 1) * 64], lhsT=Hb[g][:, hh, :], rhs=Wb[g][:, hh, :], start=True, stop=True),
                             reads=[("Hb", g), ("Wb", g)], writes=[pkz])
                ps2, pk2 = banks.next()
                for hh in range(8):
                    p.op("pe", lambda e: e.matmul(ps2[0:64, hh * 64:(hh + 1) * 64], lhsT=Wb[g][:, hh, :], rhs=Hb[g][:, hh, :], start=True, stop=True),
                         reads=[("Hb", g), ("Wb", g)], writes=[pk2])
                if l < 5:
                    p.op("dve", lambda e: e.tensor_tensor(out=Gf[g][:], in0=v3(psz[0:64, :]), in1=Gf[g][:], op=ALU.add),
                         reads=[pkz, ("Gf", g)], writes=[("Gf", g)])
                    p.op("act", lambda e: e.activation(out=Gb[g][:], in_=Gf[g][:], func=AF.Copy), reads=[("Gf", g)], writes=[("Gb", g)])
                p.op("dve", lambda e: e.tensor_tensor(out=Hf[g][:], in0=v3(ps2[0:64, :]), in1=Hf[g][:], op=ALU.add),
                     reads=[pk2, ("Hf", g)], writes=[("Hf", g)])
                p.op("act", lambda e: e.activation(out=Hb[g][:], in_=Hf[g][:], func=AF.Copy), reads=[("Hf", g)], writes=[("Hb", g)])
        for g in range(NHG):
            ps, pk = banks.next()
            for hh in range(8):
                p.op("pe", lambda e: e.matmul(ps[0:64, hh * 64:(hh + 1) * 64], lhsT=Hb[g][:, hh, :], rhs=Rb[g][:, hh, :], start=True, stop=True),
                     reads=[("Hb", g), ("Rb", g)], writes=[pk])
            p.op("act", lambda e: e.activation(out=Xb[g][:], in_=v3(ps[0:64, :]), func=AF.Copy), reads=[pk], writes=[("Xb", g)])
        ys = yst[s]
        for g in range(NHG):
            ps, pk = banks.next()
            for hh in range(8):
                h = g * 8 + hh
                o = ps[0:64, hh * 64:(hh + 1) * 64]
                p.op("pe", lambda e: e.matmul(o, lhsT=Sb[:, h, :], rhs=far[:, h, 1, :], start=True, stop=False),
                     reads=[("Sb", g), kfar], writes=[pk])
                p.op("pe", lambda e: e.matmul(o, lhsT=Sl[:, h, :], rhs=far[:, h, 1, :], start=False, stop=False),
                     reads=[("Sl", g), kfar], writes=[pk])
                p.op("pe", lambda e: e.matmul(o, lhsT=Xb[g][:, hh, :], rhs=QA[g][:, hh, 1, :], start=False, stop=False),
                     reads=[("Xb", g), ("QA", g)], writes=[pk])
                p.op("pe", lambda e: e.matmul(o, lhsT=tm[:, 0, h * 64:(h + 1) * 64], rhs=QK[g][:, hh, 1, :], start=False, stop=True),
                     reads=[ktm, ("QK", g)], writes=[pk])
            p.op("act", lambda e: e.activation(out=ys[:, g * 8:(g + 1) * 8, :], in_=v3(ps[0:64, :]), func=AF.Copy),
                 reads=[pk], writes=[("yst", s)])
        p.dma(y_d[ci], ys[:].rearrange("p a b -> p (a b)"), reads=[("yst", s)], q="act")
        for g in range(NHG):
            ps, pk = banks.next()
            for hh in range(8):
                h = g * 8 + hh
                o = ps[0:64, hh * 64:(hh + 1) * 64]
                p.op("pe", lambda e: e.matmul(o, lhsT=tm[:, 1, h * 64:(h + 1) * 64], rhs=Xb[g][:, hh, :], start=True, stop=False),
                     reads=[ktm, ("Xb", g)], writes=[pk])
                p.op("pe", lambda e: e.matmul(o, lhsT=tm[:, 2, h * 64:(h + 1) * 64], rhs=tm[:, 0, h * 64:(h + 1) * 64], start=False, stop=True),
                     reads=[ktm], writes=[pk])
            Sg = S[:, g * 8:(g + 1) * 8, :]
            Sdg = Sd[:, g * 8:(g + 1) * 8, :]
            p.op("dve", lambda e: e.tensor_tensor(out=Sg, in0=v3(ps[0:64, :]), in1=Sg, op=ALU.add), reads=[pk, ("S", g)], writes=[("S", g)])
            p.op("dve", lambda e: e.tensor_tensor(out=Sg, in0=Sg, in1=gam[:, g * 8:(g + 1) * 8, ci:ci + 1].broadcast_to([64, 8, 64]), op=ALU.mult),
                 reads=[("S", g), "gam"], writes=[("S", g)])
            p.op("act", lambda e: e.activation(out=Sb[:, g * 8:(g + 1) * 8, :], in_=Sg, func=AF.Copy), reads=[("S", g)], writes=[("Sb", g)])
            p.op(RW2_MASK_ENG, lambda e: e.tensor_tensor(out=Sdg, in0=Sg, in1=Sb[:, g * 8:(g + 1) * 8, :], op=ALU.subtract),
                 reads=[("S", g), ("Sb", g)], writes=[("Sd", g)])
            p.op("act", lambda e: e.activation(out=Sl[:, g * 8:(g + 1) * 8, :], in_=Sdg, func=AF.Copy), reads=[("Sd", g)], writes=[("Sl", g)])
    return p.build()


def rw2_masks():
    r = np.arange(64)[:, None]
    f = np.arange(64)[None, :]
    mka = np.stack([(f > r), (f >= r)], 1).astype(np.float32).reshape(64, 128)

    def m_level(l, i, j):
        return ((i >> (l + 1)) == (j >> (l + 1))) & (((i >> l) & 1) == 1) & (((j >> l) & 1) == 0)
    eye = (r == f)
    mkp = np.stack([m_level(0, r, f), eye], 1).astype(np.float32).reshape(64, 128)
    mkl = np.stack([m_level(l, f, r) for l in range(6)] + [eye], 1).astype(np.float32).reshape(64, 7 * 64)
    return mka, mkp, mkl


RW_GN_EPS = 64e-5


def build_rw3():
    p = Prog()
    W = WF
    h_d = p.dram("h", [D, W])
    modv_d = p.dram("modv", [128, 6 * KC, 2])
    g1_d = p.dram("g1", [128, KC])
    mask_d = p.dram("mask", [1, W])
    coef_d = p.dram("coef", [128, 2, 6, KC])
    yin_d = p.dram("yin", [4, D, 1152])
    g1w_d = p.dram("g1w", [D, RW_GATE])
    g2w_d = p.dram("g2w", [RW_GATE, D])
    lnv_d = p.dram("lnv", [128, 2, KC])
    wo_d = p.dram("wo", [D, D])
    bones_d = p.dram("bones", [128, 128])
    y_d = p.dram("y", [D, 1152], kind="ExternalOutput")
    cm = Common(p)
    banks = Banks(p)
    a = p.sb("a", [128, KC, W], BF16)
    hbuf = p.sb("hbuf", [128, 2, W], F32)
    tmp = p.sb("tmp", [128, 2, W], F32)
    sq = p.sb("sq", [128, 2, 512], BF16)
    rstd = p.sb("rstd", [128, W], F32)
    modv = load_small(p, "modv", [128, 6 * KC, 2], modv_d)
    gvec = load_small(p, "gvec", [128, KC], g1_d)
    mask = load_small(p, "mask", [128, W], mask_d[0].partition_broadcast(128))
    coef = load_coef(p, coef_d)
    lnv = load_small(p, "lnv", [128, 2, KC], lnv_d)
    gne = p.sb("gne", [128, 1], F32)
    p.op("dve", lambda e: e.memset(gne[:], RW_GN_EPS), writes=["gne"])
    bones = p.sb("bones", [128, 128], BF16)
    p.dma(bones[:], bones_d, writes=["bones"], q="pool", max_dma_last_dim=512)
    g1w = p.sb("g1w", [128, KC, RW_GATE], BF16)
    g2w = p.sb("g2w", [128, 2, D], BF16)
    p.dma(g1w[:], g1w_d.rearrange("(c p) f -> p c f", p=128), writes=["g1w"], q="pool")
    for c2 in range(2):
        p.dma(g2w[:, c2, :], g2w_d[c2 * 128:(c2 + 1) * 128, :], writes=["g2w"], q="pool")
    A_l = make_AB(p, "rl", modv, gvec, 0, 1, 0)
    A_c = make_AB(p, "rc", modv, gvec, 0, 1, 1)
    segs = [(LAT0, LAT1, A_l, lambda c: modv[:, c, 0:1]), (CTX0, CTX1, A_c, lambda c: modv[:, c, 1:2])]
    norm_mod_stream(p, cm, banks, h_d.rearrange("(c p) w -> p c w", p=128), W, segs, a, "a", rstd, sq, hbuf, tmp,
                    ["rl_A", "rc_A"], mask=mask)
    xg = p.sb("xg", [128, KC, 512], BF16)
    ggb = p.sb("ggb", [128, 2, 512], BF16)
    ob = p.sb("ob", [128, KC, 512], BF16)
    yt = [p.sb("yt%d" % i, [128, 4, 512], F32) for i in range(2)]
    ft = [p.sb("ft%d" % i, [128, 512], F32) for i in range(4)]
    fk = [("ft", i) for i in range(4)]
    sqb = p.sb("sqb", [128, 2, 512], BF16)
    st = p.sb("st", [128, 2, 512], F32)
    slabs = [p.sb("slab%d" % i, [128, KC, 256], BF16) for i in range(2)]
    yinv = yin_d.rearrange("n (c p) w -> p n c w", p=128)
    yov = y_d.rearrange("(c p) w -> p c w", p=128)
    cblocks = [(1, 512, 0), (513, 512, 512), (1027, 128, 1024)]
    cnt = [0]
    for (j0, n, o0) in cblocks:
        shift_mix(p, a, lambda c: ("a", c), xg, "xg", coef, 5, j0, n)
        for c2 in range(2):
            ps, pk = banks.next()
            for c in range(KC):
                p.op("pe", lambda e: e.matmul(ps[:, 0:n], lhsT=g1w[:, c, c2 * 128:(c2 + 1) * 128], rhs=xg[:, c, 0:n],
                                              start=(c == 0), stop=(c == KC - 1)),
                     reads=["g1w", ("xg", c)], writes=[pk])
            p.op("act", lambda e: e.activation(out=ggb[:, c2, 0:n], in_=ps[:, 0:n], func=AF.Sigmoid), reads=[pk], writes=[("ggb", c2)])
        for m in range(KC):
            s2 = m % 2
            y4 = yt[s2]
            p.dma(y4[:, :, 0:n], yinv[:, :, m, o0:o0 + n], writes=[("yt", s2)])
            psg, pkg = banks.next()
            for c2 in range(2):
                p.op("pe", lambda e: e.matmul(psg[:, 0:n], lhsT=g2w[:, c2, m * 128:(m + 1) * 128], rhs=ggb[:, c2, 0:n],
                                              start=(c2 == 0), stop=(c2 == 1)),
                     reads=["g2w", ("ggb", c2)], writes=[pkg])
            T_ = lambda i: ft[i][:, 0:n]
            p.op("dve", lambda e: e.tensor_tensor(out=T_(0), in0=y4[:, 0, 0:n], in1=y4[:, 1, 0:n], op=ALU.add), reads=[("yt", s2)], writes=[fk[0]])
            p.op("act", lambda e: e.activation(out=sqb[:, 0, 0:n], in_=T_(0), func=AF.Copy), reads=[fk[0]], writes=[("sqb", 0)])
            psm, pkm = banks.next()
            p.op("pe", lambda e: e.matmul(psm[:, 0:n], lhsT=bones[:], rhs=sqb[:, 0, 0:n], start=True, stop=True),
                 reads=["bones", ("sqb", 0)], writes=[pkm])
            p.op("dve", lambda e: e.scalar_tensor_tensor(out=T_(1), in0=psm[:, 0:n], scalar=-1.0 / 64, in1=T_(0), op0=ALU.mult, op1=ALU.add),
                 reads=[pkm, fk[0]], writes=[fk[1]])
            p.op("act", lambda e: e.activation(out=sqb[:, 1, 0:n], in_=T_(1), func=AF.Square), reads=[fk[1]], writes=[("sqb", 1)])
            psv, pkv = banks.next()
            p.op("pe", lambda e: e.matmul(psv[:, 0:n], lhsT=bones[:], rhs=sqb[:, 1, 0:n], start=True, stop=True),
                 reads=["bones", ("sqb", 1)], writes=[pkv])
            p.op("act", lambda e: e.activation(out=T_(2), in_=psv[:, 0:n], func=AF.Sqrt, bias=gne[:, 0:1], scale=1.0 / 64),
                 reads=[pkv, "gne"], writes=[fk[2]])
            p.op("dve", lambda e: e.reciprocal(out=T_(2), in_=T_(2)), reads=[fk[2]], writes=[fk[2]])
            p.op("dve", lambda e: e.tensor_tensor(out=T_(1), in0=T_(1), in1=T_(2), op=ALU.mult), reads=[fk[1], fk[2]], writes=[fk[1]])
            p.op("dve", lambda e: e.tensor_scalar(out=T_(1), in0=T_(1), scalar1=lnv[:, 0, m:m + 1], scalar2=lnv[:, 1, m:m + 1],
                                                  op0=ALU.mult, op1=ALU.add),
                 reads=[fk[1], "lnv"], writes=[fk[1]])
            p.op("dve", lambda e: e.tensor_tensor(out=T_(3), in0=y4[:, 2, 0:n], in1=y4[:, 3, 0:n], op=ALU.add), reads=[("yt", s2)], writes=[fk[3]])
            p.op("dve", lambda e: e.tensor_tensor(out=T_(1), in0=T_(1), in1=T_(3), op=ALU.add), reads=[fk[1], fk[3]], writes=[fk[1]])
            p.op("dve", lambda e: e.tensor_tensor(out=ob[:, m, 0:n], in0=psg[:, 0:n], in1=T_(1), op=ALU.mult),
                 reads=[pkg, fk[1]], writes=[("ob", m)])

        def evac_y(m, ps, pk, b0, b1):
            s = cnt[0] % 2
            cnt[0] += 1
            p.op("act", lambda e: e.activation(out=st[:, s, 0:b1 - b0], in_=ps[:, 0:b1 - b0], func=AF.Copy), reads=[pk], writes=[("st", s)])
            p.dma(yov[:, m, o0 + b0:o0 + b1], st[:, s, 0:b1 - b0], reads=[("st", s)], q="act")
        linear_fm2(p, banks, slabs, "slab", ob, "ob", KC, wo_d.rearrange("(c p) f -> p c f", p=128), D, [(0, n)], evac_y)
    return p.build()


_PROGS = {}
_DEBUG = None


def _prog(name, fn, *args):
    key = (name,) + args
    if key not in _PROGS:
        _PROGS[key] = fn(*args)
    return _PROGS[key]


def _run(nc, in_maps):
    res = _bu.run_bass_kernel_spmd(nc, in_maps, core_ids=list(range(8)))
    return res.results


def _f32(x):
    return np.ascontiguousarray(np.asarray(x, dtype=np.float32))


def _slab(hl_b, hc_b, s, halo):
    Dn = hl_b.shape[0]
    wl, wc = 1024 + 2 * halo, 128 + 2 * halo
    out = np.zeros((Dn, wl + wc), hl_b.dtype)
    mask = np.zeros((1, wl + wc), np.float32)
    for (src, n, base, o0) in ((hl_b, 1024, s * 1024, 0), (hc_b, 128, s * 128, wl)):
        lo, hi = base - halo, base + n + halo
        a0, a1 = max(lo, 0), min(hi, src.shape[1])
        out[:, o0 + (a0 - lo):o0 + (a1 - lo)] = src[:, a0:a1]
        mask[0, o0 + (a0 - lo):o0 + (a1 - lo)] = 1.0
    return out, mask


def _core_cols(hl_b, hc_b, s):
    return np.ascontiguousarray(np.concatenate([hl_b[:, s * 1024:(s + 1) * 1024], hc_b[:, s * 128:(s + 1) * 128]], 1))


def _pool_icnt(s):
    out = np.ones((4, WP), np.float32)
    for g, win in enumerate((2, 4, 8, 16)):
        left, right = win // 2, win - 1 - win // 2
        for (n, base, o0, Tn) in ((1024, s * 1024, 8, T), (128, s * 128, 1040 + 8, L)):
            t = np.arange(base, base + n)
            cnt = np.minimum(t + right, Tn - 1) - np.maximum(t - left, 0) + 1
            out[g, o0:o0 + n] = (1.0 / cnt.astype(np.float64)).astype(np.float32)
    return out


def rwkv_mixer(hl, hc, modv3, W3, dbg=None):
    import ml_dtypes
    norm1_g = W3["norm1_g"]; rw_mu = W3["rw_mu"]; rw_w_rkv = W3["rw_w_rkv"]; rw_w0 = W3["rw_w0"]; rw_w1 = W3["rw_w1"]
    rw_w2 = W3["rw_w2"]; rw_a0 = W3["rw_a0"]; rw_a1 = W3["rw_a1"]; rw_a2 = W3["rw_a2"]; rw_g1 = W3["rw_g1"]; rw_g2 = W3["rw_g2"]
    rw_k_k = W3["rw_k_k"]; rw_k_a = W3["rw_k_a"]; rw_r_k = W3["rw_r_k"]; rw_ln_g = W3["rw_ln_g"]; rw_ln_b = W3["rw_ln_b"]
    rw_w_o = W3["rw_w_o"]
    cores = [(b, s) for b in range(NB) for s in range(2)]
    mu = _f32(rw_mu[0])
    bones = np.kron(np.eye(2), np.ones((64, 64))).astype(np.float32)
    rmask = np.ones((1, 512), np.float32)
    rmask[0, ::64] = 0

    def coef_of(mu_prev, mu_next):
        return np.ascontiguousarray(np.stack([
            np.stack([_chunked(mu_prev[n]) for n in range(6)], 1),
            np.stack([_chunked(mu_next[n]) for n in range(6)], 1)], 1))

    bf = ml_dtypes.bfloat16
    mka, mkp, mkl = rw2_masks()
    rw2_in = {}
    bv_nat = {}
    vecs = np.ascontiguousarray(np.stack([_chunked(rw_w0[0][0]), _chunked(rw_w0[0][1]), _chunked(rw_a0[0][0]), _chunked(rw_a0[0][1]),
                                          _chunked(rw_k_k[0]), _chunked(rw_k_a[0]), _chunked(_f32(rw_r_k[0]).reshape(-1))], 1))
    ims = []
    wrkv_l = np.ascontiguousarray(_f32(rw_w_rkv[0]).reshape(3, KC, 128, KC, 128).transpose(0, 3, 2, 1, 4)).reshape(3, KC, 128, KC * 128)
    for (b, s) in cores:
        hs, mk = _slab(hl[b], hc[b], s, 1)
        ims.append(dict(h=hs, modv=modv3[b], g1=_chunked(norm1_g[3]), mask=mk, coef=coef_of(mu[0], mu[1]),
                        wrkv=wrkv_l, w1=_f32(rw_w1[0]), w2=_f32(rw_w2[0]), a1=_f32(rw_a1[0]),
                        a2=_f32(rw_a2[0]), vecs=vecs, bones=bones, rmask=rmask))
    res = _run(_prog("rw1", build_rw1), ims)
    for b in range(NB):
        r0, r1 = res[2 * b], res[2 * b + 1]
        for d in range(2):
            def seq(nm):
                lat = np.concatenate([np.asarray(r0[nm])[:, :1024], np.asarray(r1[nm])[:, :1024]], 1)
                cx = np.concatenate([np.asarray(r0[nm])[:, 1024:], np.asarray(r1[nm])[:, 1024:]], 1)
                if d == 1:
                    lat, cx = lat[:, ::-1], cx[:, ::-1]
                return np.concatenate([cx, lat], 1)
            at, bt, kt, rt = (seq("%s%d" % (nm, d)) for nm in ("at", "bt", "kt", "rt"))
            vt = seq("vt")
            hm = lambda z: z.reshape(32, 64, NCH, 64).transpose(2, 1, 0, 3)
            far = np.ascontiguousarray(np.stack([hm(at), hm(rt)], 3).reshape(NCH, 64, -1))
            fbk = np.ascontiguousarray(np.stack([hm(bt), hm(kt)], 2).reshape(NCH, 64, -1))
            tk = lambda z: np.ascontiguousarray(z.T).reshape(NCH, 64, D)
            tm = np.ascontiguousarray(np.stack([tk(vt), tk(bt), tk(kt)], 2).reshape(NCH, 64, -1))
            g0, g1_ = np.asarray(r0["gam%d" % d]), np.asarray(r1["gam%d" % d])
            glat = np.concatenate([g0[:, 0:16], g1_[:, 0:16]], 1)
            gcx = np.concatenate([g0[:, 16:18], g1_[:, 16:18]], 1)
            if d == 1:
                glat, gcx = glat[:, ::-1], gcx[:, ::-1]
            gam = np.concatenate([gcx, glat], 1)
            gam = np.ascontiguousarray(gam.reshape(32, 64, NCH).transpose(1, 0, 2))
            rw2_in[(b, d)] = dict(far=far.astype(bf, copy=False), fbk=fbk.astype(bf, copy=False), tm=tm.astype(bf, copy=False),
                                  gam=gam, mka=mka, mkp=mkp, mkl=mkl)
            bvs = np.concatenate([np.asarray(r0["bv%d" % d])[:, :1024], np.asarray(r1["bv%d" % d])[:, :1024]], 1)
            bvc = np.concatenate([np.asarray(r0["bv%d" % d])[:, 1024:], np.asarray(r1["bv%d" % d])[:, 1024:]], 1)
            bv_nat[(b, d)] = (np.ascontiguousarray(bvs), np.ascontiguousarray(bvc))
    res = _run(_prog("rw2", build_rw2), [rw2_in[(b, d)] for b in range(NB) for d in range(2)])
    y_nat = {}
    for b in range(NB):
        for d in range(2):
            y = res[2 * b + d]["y"].reshape(NCH, 64, 32, 64).transpose(2, 1, 0, 3).reshape(D, NTOK)
            yc_, yl_ = y[:, :L], y[:, L:]
            if d == 1:
                yc_, yl_ = yc_[:, ::-1], yl_[:, ::-1]
            y_nat[(b, d)] = (np.ascontiguousarray(yl_), np.ascontiguousarray(yc_))
    ims = []
    coef_n = coef_of(mu[0], mu[1])
    lnv = np.ascontiguousarray(np.stack([_chunked(rw_ln_g[0]), _chunked(rw_ln_b[0])], 1))
    for (b, s) in cores:
        hs, mk = _slab(hl[b], hc[b], s, 1)
        yin = np.stack([_core_cols(y_nat[(b, 0)][0], y_nat[(b, 0)][1], s), _core_cols(y_nat[(b, 1)][0], y_nat[(b, 1)][1], s),
                        _core_cols(bv_nat[(b, 0)][0], bv_nat[(b, 0)][1], s), _core_cols(bv_nat[(b, 1)][0], bv_nat[(b, 1)][1], s)], 0)
        ims.append(dict(h=hs, modv=modv3[b], g1=_chunked(norm1_g[3]), mask=mk, coef=coef_n, yin=np.ascontiguousarray(yin),
                        g1w=_f32(rw_g1[0]), g2w=_f32(rw_g2[0]), lnv=lnv, wo=_f32(rw_w_o[0]), bones=bones))
    res = _run(_prog("rw3", build_rw3), ims)
    if dbg is not None:
        dbg["rw2_in"] = rw2_in
        dbg["y_nat"] = y_nat
        dbg["bv_nat"] = bv_nat
    return res


def kernel(x, c, ctx, c_ctx, norm1_g, norm2_g, w_mod, b_mod, ffn_w_gate, ffn_w_up, ffn_conv_w, ffn_conv_b,
           ffn_w_down, final_norm_g, pool_w, pool_b, pool_scale, na_w_qkv, na_rpb, na_w_o, sg_w_in, sg_b_in,
           sg_norm_g, sg_w_s, sg_b_s, sg_w_o, rw_mu, rw_w_rkv, rw_w0, rw_w1, rw_w2, rw_a0, rw_a1, rw_a2, rw_g1,
           rw_g2, rw_k_k, rw_k_a, rw_r_k, rw_ln_g, rw_ln_b, rw_w_o):
    import ml_dtypes
    x, ctx = _f32(x), _f32(ctx)
    hl = [np.ascontiguousarray(x[b].T) for b in range(NB)]
    hc = [np.ascontiguousarray(ctx[b].T) for b in range(NB)]
    cores = [(b, s) for b in range(NB) for s in range(2)]

    cc = np.zeros((8, D), np.float32)
    cc[0:4] = _f32(c)
    cc[4] = _f32(c_ctx)
    ct = np.ascontiguousarray(cc.T.reshape(KC, 128, 8).transpose(1, 0, 2))
    w_mod = np.asarray(w_mod, np.float32)
    ims = []
    for core in range(8):
        i, half = core // 2, core % 2
        ims.append(dict(wm=np.ascontiguousarray(w_mod[i][:, half * 6144:(half + 1) * 6144]),
                        bm=_chunked(np.asarray(b_mod[i], np.float32)[half * 6144:(half + 1) * 6144]), ct=ct))
    res = _run(_prog("mod", build_mod), ims)
    modv = {}
    for i in range(4):
        mo = np.concatenate([res[2 * i]["mo"], res[2 * i + 1]["mo"]], 1)
        for b in range(NB):
            modv[(i, b)] = np.ascontiguousarray(mo[:, :, [b, 4]])

    def run_ffn(i, ys_l, ys_c, final):
        wg, wu, wd = _f32(ffn_w_gate[i]), _f32(ffn_w_up[i]), _f32(ffn_w_down[i])
        cw = np.ascontiguousarray(_f32(ffn_conv_w[i]).reshape(3, FC, 128).transpose(2, 1, 0))
        cb = _chunked(ffn_conv_b[i])
        g2 = _chunked(norm2_g[i])
        ims = []
        for (b, s) in cores:
            hs, mk = _slab(hl[b], hc[b], s, 1)
            y = np.zeros((len(ys_l), D, WF), np.float32)
            for n in range(len(ys_l)):
                y[n] = _slab(ys_l[n][b], ys_c[n][b], s, 1)[0]
            im = dict(h=hs, y=y, modv=modv[(i, b)], g2=g2, mask=np.ascontiguousarray(np.repeat(mk, 128, 0)),
                      wg=wg, wu=wu, wd=wd, cw=cw, cb=cb)
            if final:
                im["gf"] = _chunked(final_norm_g)
            ims.append(im)
        res = _run(_prog("ffn", build_ffn, final, len(ys_l)), ims)
        out = None
        if final:
            out = np.empty((NB, T, D), np.float32)
        for ci, (b, s) in enumerate(cores):
            ho = res[ci]["ho"]
            hl[b][:, s * 1024:(s + 1) * 1024] = ho[:, 0:1024]
            hc[b][:, s * 128:(s + 1) * 128] = ho[:, 1024:1152]
            if final:
                out[b, s * 1024:(s + 1) * 1024, :] = res[ci]["of"].T
        return out

    def split_y(res_y):
        yl = [np.empty((D, T), np.float32) for _ in range(NB)]
        yc = [np.empty((D, L), np.float32) for _ in range(NB)]
        for ci, (b, s) in enumerate(cores):
            yl[b][:, s * 1024:(s + 1) * 1024] = res_y[ci][:, 0:1024]
            yc[b][:, s * 128:(s + 1) * 128] = res_y[ci][:, 1024:1152]
        return yl, yc

    i = 0
    ims = []
    for (b, s) in cores:
        hs, mk = _slab(hl[b], hc[b], s, 8)
        ims.append(dict(h=hs, modv=modv[(i, b)], g1=_chunked(norm1_g[i]), mask=mk, icnt=_pool_icnt(s),
                        pw=_f32(pool_w[0]), pb=_chunked(pool_b[0]), psc=_chunked(pool_scale[0])))
    res = _run(_prog("pool", build_pool), ims)
    yl, yc = split_y([r["y"] for r in res])
    run_ffn(i, [yl], [yc], False)
    if _DEBUG is not None:
        _DEBUG.append(([a.copy() for a in hl], [a.copy() for a in hc]))

    i = 1
    pm = rope_perm()
    ims = []
    for (b, s) in cores:
        rc, rs = rope_tables(s * 1024, 1024)
        ims.append(dict(h=_core_cols(hl[b], hc[b], s), modv=modv[(i, b)], g1=_chunked(norm1_g[i]), wqkv=_f32(na_w_qkv[0]),
                        rcos=rc, rsin=rs, pm=pm))
    res = _run(_prog("na1", build_na1), ims)
    Q, Kt, V = [], [], []
    for b in range(NB):
        r0, r1 = res[2 * b], res[2 * b + 1]
        Q.append(np.concatenate([r0["qT"][:, :1024], r1["qT"][:, :1024], r0["qT"][:, 1024:], r1["qT"][:, 1024:]], 1))
        Kt.append(np.concatenate([r0["kT"][:, :1024], r1["kT"][:, :1024], r0["kT"][:, 1024:], r1["kT"][:, 1024:]], 1))
        V.append(np.concatenate([r0["v"][:1024], r1["v"][:1024], r0["v"][1024:], r1["v"][1024:]], 0))
    rpb = _f32(na_rpb[0])
    wo = _f32(na_w_o[0])
    ims = []
    for b in range(NB):
        for hh in range(2):
            sl = slice(hh * 1024, (hh + 1) * 1024)
            ims.append(dict(qT=np.ascontiguousarray(Q[b][sl]), kT=np.ascontiguousarray(Kt[b][sl]),
                            v=np.ascontiguousarray(V[b][:, sl]), bt=na_bias_tables(rpb, list(range(hh * 16, hh * 16 + 16))),
                            wo=np.ascontiguousarray(wo[sl])))
    res = _run(_prog("na2", build_na2), ims)
    yls, ycs = [], []
    for hh in range(2):
        yls.append([np.ascontiguousarray(res[2 * b + hh]["y"][:, :T]) for b in range(NB)])
        ycs.append([np.ascontiguousarray(res[2 * b + hh]["y"][:, T:]) for b in range(NB)])
    run_ffn(i, yls, ycs, False)
    if _DEBUG is not None:
        _DEBUG.append(([a.copy() for a in hl], [a.copy() for a in hc]))

    i = 2
    b_in = _f32(sg_b_in[0])
    ims = []
    for (b, s) in cores:
        ims.append(dict(h=_core_cols(hl[b], hc[b], s), modv=modv[(i, b)], g1=_chunked(norm1_g[i]), win=_f32(sg_w_in[0]),
                        bzu=_chunked(b_in[:D]), bzv=np.ascontiguousarray(b_in[None, D:]), ng=_f32(sg_norm_g[0])[None].copy(),
                        wst=np.ascontiguousarray(_f32(sg_w_s[0]).transpose(2, 0, 1)), bs=_f32(sg_b_s[0]).reshape(1, -1).copy(),
                        wo=_f32(sg_w_o[0])))
    res = _run(_prog("gmlp", build_gmlp), ims)
    yl, yc = split_y([r["y"] for r in res])
    run_ffn(i, [yl], [yc], False)
    if _DEBUG is not None:
        _DEBUG.append(([a.copy() for a in hl], [a.copy() for a in hc]))

    i = 3
    W3 = dict(norm1_g=norm1_g, rw_mu=rw_mu, rw_w_rkv=rw_w_rkv, rw_w0=rw_w0, rw_w1=rw_w1, rw_w2=rw_w2, rw_a0=rw_a0, rw_a1=rw_a1,
              rw_a2=rw_a2, rw_g1=rw_g1, rw_g2=rw_g2, rw_k_k=rw_k_k, rw_k_a=rw_k_a, rw_r_k=rw_r_k, rw_ln_g=rw_ln_g,
              rw_ln_b=rw_ln_b, rw_w_o=rw_w_o)
    res = rwkv_mixer(hl, hc, {b: modv[(i, b)] for b in range(NB)}, W3)
    yl, yc = split_y([r["y"] for r in res])
    return run_ffn(i, [yl], [yc], True)
```
